# Optimizing a Trainium2 kernel written in Bass

```python
import math
import jax, jax.numpy as jnp
from jax import lax
import numpy as np

D_MODEL = 1024
BATCH = 4
SEQ = 4096
DEPTH = 4

A_HEADS = 4
A_DK = 128
A_DV = 128
A_CHUNK = 64
A_W = A_HEADS * A_DK
A_V = A_HEADS * A_DV
B_HEADS = 8
B_GROUPS = 2
B_HD = 64
B_Q = B_HEADS * B_HD
B_KV = B_GROUPS * B_HD
CMP_LEN = 32
CMP_STRIDE = 16
CMP_HIDDEN = 256
SEL_LEN = 64
SEL_TOP = 16
N_LOCAL = 2
WINDOW = 512
Q_BLOCK = 128
C_HEADS = 8
C_HD = 64
C_W = C_HEADS * C_HD
C_DECAY_LORA = 64
C_AAA_LORA = 64
C_GATE_LORA = 128
LNX_EPS = 64e-5
REL_BUCKETS = 32
REL_MAX_DIST = 128
D_FF = -(-8 * D_MODEL // (3 * 256)) * 256
N_BRANCH = 3
BRANCH_W = 512

A_SPLITS = (A_W, A_W, A_V, A_V)
B_SPLITS = (B_Q, B_KV, B_KV, B_KV, B_KV, B_KV, B_KV, 3 * B_HEADS)
C_SPLITS = (C_W, C_W, C_W, C_DECAY_LORA, C_AAA_LORA, C_GATE_LORA)
A_IN = sum(A_SPLITS)
B_IN = sum(B_SPLITS)
C_IN = sum(C_SPLITS)
GATE_W = N_BRANCH * D_MODEL
IN_GROUPS = (A_IN, B_IN, C_IN, GATE_W)
N_IN = sum(IN_GROUPS)

kernel_name = "hybrid_hgrn2_nsa_rwkv7_trunk"


def split_cols(p, sizes):
    return jnp.split(p, [int(s) for s in np.cumsum(sizes)[:-1]], axis=-1)


def rmsnorm(x, w, eps=1e-6):
    xf = x.astype(jnp.float32)
    y = xf * lax.rsqrt(jnp.mean(xf * xf, axis=-1, keepdims=True) + eps)
    return (y * w.astype(jnp.float32)).astype(x.dtype)


def modulate(h, shift, scale):
    return h * (1 + scale) + shift


def masked_softmax(s, mask):
    s = jnp.where(mask, s.astype(jnp.float32), -jnp.inf)
    m = jnp.max(s, axis=-1, keepdims=True)
    m = jnp.where(jnp.isfinite(m), m, 0.0)
    e = jnp.exp(s - m)
    return e / jnp.maximum(jnp.sum(e, axis=-1, keepdims=True), 1e-30)


def t5_bucket(dist):
    n = jnp.maximum(dist, 0)
    max_exact = REL_BUCKETS // 2
    nf = jnp.maximum(n, 1).astype(jnp.float32)
    large = max_exact + (jnp.log(nf / max_exact) / math.log(REL_MAX_DIST / max_exact)
                         * (REL_BUCKETS - max_exact)).astype(jnp.int32)
    large = jnp.minimum(large, REL_BUCKETS - 1)
    return jnp.where(n < max_exact, n, large)


def hgrn2_mixer(q, f_raw, i, g, lb, norm_w):
    Bsz, T, _ = q.shape
    n_c = T // A_CHUNK
    q = jax.nn.silu(q.astype(jnp.float32))
    f_raw = f_raw.astype(jnp.float32)
    log_f = jnp.logaddexp(jnp.log(lb), jnp.log1p(-lb) + jax.nn.log_sigmoid(f_raw))
    k = (1 - lb) * jax.nn.sigmoid(-f_raw)

    def heads(t, d):
        return t.reshape(Bsz, n_c, A_CHUNK, A_HEADS, d).transpose(1, 0, 3, 2, 4)

    qh, kh = heads(q, A_DK), heads(k, A_DK)
    vh = heads(i.astype(jnp.float32), A_DV)
    bcum = jnp.cumsum(heads(log_f, A_DK), axis=3)
    tril = jnp.tril(jnp.ones((A_CHUNK, A_CHUNK), bool))[:, :, None]

    def step(S, xs):
        qc, kc, vc, bc = xs
        o_inter = jnp.einsum('bhtd,bhde->bhte', qc * jnp.exp(bc), S)
        diff = bc[:, :, :, None, :] - bc[:, :, None, :, :]
        decay = jnp.where(tril, jnp.exp(jnp.where(tril, diff, 0.0)), 0.0)
        att = jnp.einsum('bhtd,bhtsd,bhsd->bhts', qc, decay, kc)
        o = o_inter + jnp.einsum('bhts,bhse->bhte', att, vc)
        b_last = bc[:, :, -1:, :]
        S = jnp.exp(b_last[:, :, 0, :, None]) * S + jnp.einsum('bhsd,bhse->bhde', kc * jnp.exp(b_last - bc), vc)
        return S, o

    S0 = jnp.zeros((Bsz, A_HEADS, A_DK, A_DV), jnp.float32)
    _, o = lax.scan(step, S0, (qh, kh, vh, bcum))
    o = o.transpose(1, 0, 3, 2, 4).reshape(Bsz, T, A_HEADS, A_DV)
    o = o * lax.rsqrt(jnp.mean(o * o, axis=-1, keepdims=True) + 1e-5) * norm_w.reshape(A_HEADS, A_DV)
    return o.reshape(Bsz, T, A_V) * jax.nn.sigmoid(g.astype(jnp.float32))


def nsa_mixer(q, k_cmp, v_cmp, k_sel, v_sel, k_win, v_win, gate_raw,
              pe_k, w1_k, w2_k, pe_v, w1_v, w2_v, rel_bias):
    f32 = jnp.float32
    q, k_cmp, v_cmp, k_sel, v_sel, k_win, v_win, gate_raw = (
        t.astype(f32) for t in (q, k_cmp, v_cmp, k_sel, v_sel, k_win, v_win, gate_raw))
    Bsz, T, _ = q.shape
    J = B_HEADS // B_GROUPS
    n_q = T // Q_BLOCK
    n_cmp = (T - CMP_LEN) // CMP_STRIDE + 1
    n_sel = T // SEL_LEN
    n_top = min(SEL_TOP, n_sel)
    scale = B_HD ** -0.5

    def kv_heads(t):
        return t.reshape(Bsz, T, B_GROUPS, B_HD).transpose(0, 2, 1, 3)

    cmp_start = jnp.arange(n_cmp) * CMP_STRIDE
    cmp_idx = cmp_start[:, None] + jnp.arange(CMP_LEN)[None, :]
    cmp_end = cmp_start + CMP_LEN - 1

    def compress(t, pe, w1, w2):
        blk = kv_heads(t)[:, :, cmp_idx] + pe
        blk = blk.reshape(Bsz, B_GROUPS, n_cmp, CMP_LEN * B_HD)
        return jax.nn.silu(blk @ w1) @ w2

    kc = compress(k_cmp, pe_k, w1_k, w2_k)
    vc = compress(v_cmp, pe_v, w1_v, w2_v)
    ks = kv_heads(k_sel).reshape(Bsz, B_GROUPS, n_sel, SEL_LEN, B_HD)
    vs = kv_heads(v_sel).reshape(Bsz, B_GROUPS, n_sel, SEL_LEN, B_HD)
    pad = ((0, 0), (0, 0), (WINDOW, 0), (0, 0))
    kw = jnp.pad(kv_heads(k_win), pad)
    vw = jnp.pad(kv_heads(v_win), pad)
    sel_start = jnp.arange(n_sel) * SEL_LEN
    cover = ((cmp_start[:, None] < sel_start[None, :] + SEL_LEN)
             & (cmp_start[:, None] + CMP_LEN > sel_start[None, :])).astype(f32)
    bias_gj = rel_bias.reshape(REL_BUCKETS, B_GROUPS, J).transpose(1, 0, 2)
    bi = jnp.arange(Bsz)[:, None, None, None]
    gi = jnp.arange(B_GROUPS)[None, :, None, None]
    win_off = jnp.arange(Q_BLOCK + WINDOW) - WINDOW
    blk_pos = jnp.arange(SEL_LEN)
    blk_ids = jnp.arange(n_sel)

    def head_bias(dist):
        return rel_bias[t5_bucket(dist)].reshape(*dist.shape, B_GROUPS, J).transpose(2, 3, 0, 1)

    def block(args):
        qb, gb, t0 = args
        t = t0 + jnp.arange(Q_BLOCK)
        s = jnp.einsum('bgjqd,bgnd->bgjqn', qb, kc) * scale + head_bias(t[:, None] - cmp_end[None, :])
        p_c = masked_softmax(s, cmp_end[None, :] <= t[:, None])
        o_c = jnp.einsum('bgjqn,bgnd->bgjqd', p_c, vc)
        imp = jnp.einsum('bgjqn,nm->bgqm', p_c, cover)
        cur = t // SEL_LEN
        causal = blk_ids[None, :] <= cur[:, None]
        forced = (blk_ids[None, :] == 0) | ((blk_ids[None, :] > cur[:, None] - N_LOCAL) & causal)
        score = jnp.where(causal, jnp.where(forced, jnp.inf, imp), -jnp.inf)
        _, idx = lax.top_k(score, n_top)
        k_g = ks[bi, gi, idx].reshape(Bsz, B_GROUPS, Q_BLOCK, n_top * SEL_LEN, B_HD)
        v_g = vs[bi, gi, idx].reshape(Bsz, B_GROUPS, Q_BLOCK, n_top * SEL_LEN, B_HD)
        pos = (idx[..., None] * SEL_LEN + blk_pos).reshape(Bsz, B_GROUPS, Q_BLOCK, n_top * SEL_LEN)
        dist = t[:, None] - pos
        b_s = jnp.moveaxis(bias_gj[gi, t5_bucket(dist)], -1, 2)
        s = jnp.einsum('bgjqd,bgqkd->bgjqk', qb, k_g) * scale + b_s
        p_s = masked_softmax(s, (dist >= 0)[:, :, None])
        o_s = jnp.einsum('bgjqk,bgqkd->bgjqd', p_s, v_g)
        kwb = lax.dynamic_slice_in_dim(kw, t0, Q_BLOCK + WINDOW, axis=2)
        vwb = lax.dynamic_slice_in_dim(vw, t0, Q_BLOCK + WINDOW, axis=2)
        kpos = t0 + win_off
        dist = t[:, None] - kpos[None, :]
        mask = (dist >= 0) & (dist < WINDOW) & (kpos >= 0)[None, :]
        s = jnp.einsum('bgjqd,bgkd->bgjqk', qb, kwb) * scale + head_bias(dist)
        p_w = masked_softmax(s, mask)
        o_w = jnp.einsum('bgjqk,bgkd->bgjqd', p_w, vwb)
        g = jax.nn.sigmoid(gb)
        return g[..., 0:1] * o_c + g[..., 1:2] * o_s + g[..., 2:3] * o_w

    qh = q.reshape(Bsz, T, B_GROUPS, J, B_HD).transpose(0, 2, 3, 1, 4)
    q_blocks = jnp.moveaxis(qh.reshape(Bsz, B_GROUPS, J, n_q, Q_BLOCK, B_HD), 3, 0)
    gh = gate_raw.reshape(Bsz, T, B_GROUPS, J, 3).transpose(0, 2, 3, 1, 4)
    g_blocks = jnp.moveaxis(gh.reshape(Bsz, B_GROUPS, J, n_q, Q_BLOCK, 3), 3, 0)
    starts = jnp.arange(n_q, dtype=jnp.int32) * Q_BLOCK
    o = lax.map(block, (q_blocks, g_blocks, starts))
    return o.transpose(1, 0, 4, 2, 3, 5).reshape(Bsz, T, B_Q)


def rwkv7_mixer(p, mu, w0, w2, a0, a2, g2, k_k, k_a, r_k, lnx_w, lnx_b):
    Bsz, T, _ = p.shape
    p = p.astype(jnp.float32)
    p_prev = jnp.pad(p[:, :-1], ((0, 0), (1, 0), (0, 0)))
    p = p + (p_prev - p) * mu
    r, k, v, xw, xa, xg = split_cols(p, C_SPLITS)
    w = -jax.nn.softplus(-(w0 + jnp.tanh(xw) @ w2)) - 0.5
    decay = jnp.exp(-jnp.exp(w))
    a = jax.nn.sigmoid(a0 + xa @ a2)
    g = jax.nn.sigmoid(xg) @ g2

    def hd(t):
        return t.reshape(Bsz, T, C_HEADS, C_HD)

    kk = hd(k * k_k)
    kk = kk * lax.rsqrt(jnp.maximum(jnp.sum(kk * kk, axis=-1, keepdims=True), 1e-24))
    k = k * (1 + (a - 1) * k_a)
    r_h, k_h, v_h, a_h, w_h = hd(r), hd(k), hd(v), hd(a), hd(decay)

    def step(S, xs):
        r_t, w_t, k_t, v_t, kk_t, akk_t = xs
        sa = jnp.einsum('bhvk,bhk->bhv', S, kk_t)
        S = S * w_t[:, :, None, :] - sa[..., None] * akk_t[:, :, None, :] + v_t[..., None] * k_t[:, :, None, :]
        return S, jnp.einsum('bhvk,bhk->bhv', S, r_t)

    S0 = jnp.zeros((Bsz, C_HEADS, C_HD, C_HD), jnp.float32)
    xs = tuple(jnp.moveaxis(t, 1, 0) for t in (r_h, w_h, k_h, v_h, kk, a_h * kk))
    _, y = lax.scan(step, S0, xs)
    y = jnp.moveaxis(y, 0, 1)
    mean = jnp.mean(y, axis=-1, keepdims=True)
    var = jnp.mean(jnp.square(y - mean), axis=-1, keepdims=True)
    y = (y - mean) * lax.rsqrt(var + LNX_EPS) * lnx_w.reshape(C_HEADS, C_HD) + lnx_b.reshape(C_HEADS, C_HD)
    y = y + jnp.sum(r_h * k_h * r_k, axis=-1, keepdims=True) * v_h
    return y.reshape(Bsz, T, C_W) * g


def setup_inputs(seed: int = 0) -> dict:
    key = jax.random.key(seed)
    keys = iter(jax.random.split(key, 48))

    def nrm(shape, scale):
        return jax.random.normal(next(keys), shape, jnp.float32) * scale

    def gain(shape):
        return 1.0 + nrm(shape, 0.02)

    L = DEPTH
    return {
        "x": nrm((BATCH, SEQ, D_MODEL), 1.0),
        "c": nrm((BATCH, D_MODEL), 1.0),
        "ada_w": nrm((L, D_MODEL, 6 * D_MODEL), 0.5 * D_MODEL ** -0.5),
        "ada_b": nrm((L, 6 * D_MODEL), 0.02),
        "norm1_w": gain((L, D_MODEL)),
        "norm2_w": gain((L, D_MODEL)),
        "w_in": nrm((L, D_MODEL, N_IN), D_MODEL ** -0.5),
        "hgrn_lb_logits": nrm((L, A_W), 0.5),
        "hgrn_norm_w": gain((L, A_V)),
        "nsa_pe_k": nrm((L, CMP_LEN, B_HD), 0.1),
        "nsa_cmp_w1_k": nrm((L, CMP_LEN * B_HD, CMP_HIDDEN), (CMP_LEN * B_HD) ** -0.5),
        "nsa_cmp_w2_k": nrm((L, CMP_HIDDEN, B_HD), CMP_HIDDEN ** -0.5),
        "nsa_pe_v": nrm((L, CMP_LEN, B_HD), 0.1),
        "nsa_cmp_w1_v": nrm((L, CMP_LEN * B_HD, CMP_HIDDEN), (CMP_LEN * B_HD) ** -0.5),
        "nsa_cmp_w2_v": nrm((L, CMP_HIDDEN, B_HD), CMP_HIDDEN ** -0.5),
        "rel_bias": nrm((REL_BUCKETS, B_HEADS), 0.5),
        "rw_mu": jax.random.uniform(next(keys), (L, C_IN), jnp.float32),
        "rw_w0": nrm((L, C_W), 1.0),
        "rw_w2": nrm((L, C_DECAY_LORA, C_W), 0.1),
        "rw_a0": nrm((L, C_W), 0.5),
        "rw_a2": nrm((L, C_AAA_LORA, C_W), 0.1),
        "rw_g2": nrm((L, C_GATE_LORA, C_W), C_GATE_LORA ** -0.5),
        "rw_k_k": 1.0 + nrm((L, C_W), 0.1),
        "rw_k_a": 1.0 + nrm((L, C_W), 0.1),
        "rw_r_k": nrm((L, C_HEADS, C_HD), 0.1),
        "rw_lnx_w": gain((L, C_W)),
        "rw_lnx_b": nrm((L, C_W), 0.02),
        "w_branch": nrm((L, N_BRANCH, BRANCH_W, D_MODEL), BRANCH_W ** -0.5),
        "w_out": nrm((L, D_MODEL, D_MODEL), D_MODEL ** -0.5),
        "ffn_w1": nrm((L, D_MODEL, D_FF), D_MODEL ** -0.5),
        "ffn_w3": nrm((L, D_MODEL, D_FF), D_MODEL ** -0.5),
        "ffn_w2": nrm((L, D_FF, D_MODEL), D_FF ** -0.5),
        "final_norm_w": gain((D_MODEL,)),
    }


def reference(x, c, ada_w, ada_b, norm1_w, norm2_w, w_in, hgrn_lb_logits, hgrn_norm_w,
              nsa_pe_k, nsa_cmp_w1_k, nsa_cmp_w2_k, nsa_pe_v, nsa_cmp_w1_v, nsa_cmp_w2_v, rel_bias,
              rw_mu, rw_w0, rw_w2, rw_a0, rw_a2, rw_g2, rw_k_k, rw_k_a, rw_r_k, rw_lnx_w, rw_lnx_b,
              w_branch, w_out, ffn_w1, ffn_w3, ffn_w2, final_norm_w):
    dt = x.dtype
    lb = jnp.cumsum(jax.nn.softmax(hgrn_lb_logits.astype(jnp.float32), axis=0), axis=0)
    lb = lb - lb[0]
    cond = jax.nn.silu(c)
    for l in range(DEPTH):
        ada = (cond @ ada_w[l] + ada_b[l])[:, None, :]
        sh1, sc1, gt1, sh2, sc2, gt2 = jnp.split(ada, 6, axis=-1)
        h = modulate(rmsnorm(x, norm1_w[l]), sh1, sc1)
        p = h @ w_in[l]
        p_a, p_b, p_c, p_gate = split_cols(p, IN_GROUPS)
        y_a = hgrn2_mixer(*split_cols(p_a, A_SPLITS), lb[l], hgrn_norm_w[l])
        y_b = nsa_mixer(*split_cols(p_b, B_SPLITS), nsa_pe_k[l], nsa_cmp_w1_k[l], nsa_cmp_w2_k[l],
                        nsa_pe_v[l], nsa_cmp_w1_v[l], nsa_cmp_w2_v[l], rel_bias)
        y_c = rwkv7_mixer(p_c, rw_mu[l], rw_w0[l], rw_w2[l], rw_a0[l], rw_a2[l], rw_g2[l],
                          rw_k_k[l], rw_k_a[l], rw_r_k[l], rw_lnx_w[l], rw_lnx_b[l])
        ga, gb, gc = jnp.split(jax.nn.sigmoid(p_gate), 3, axis=-1)
        merged = (ga * (y_a.astype(dt) @ w_branch[l, 0])
                  + gb * (y_b.astype(dt) @ w_branch[l, 1])
                  + gc * (y_c.astype(dt) @ w_branch[l, 2]))
        x = x + gt1 * (merged @ w_out[l])
        h = modulate(rmsnorm(x, norm2_w[l]), sh2, sc2)
        x = x + gt2 * ((jax.nn.silu(h @ ffn_w1[l]) * (h @ ffn_w3[l])) @ ffn_w2[l])
    return rmsnorm(x, final_norm_w)
```

```python
import numpy as np
from contextlib import ExitStack, contextmanager
import concourse.bass as bass
import concourse.mybir as mybir
from concourse.bass_utils import run_bass_kernel_spmd

F32 = mybir.dt.float32
BF16 = mybir.dt.bfloat16
AF = mybir.ActivationFunctionType
ALU = mybir.AluOpType
AX = mybir.AxisListType

SAME_ENGINE_SYNC = True
N_DMA_SLOTS = 12
DMA_Q_MAP = {"pool": "sp", "act": "sp"}


class Buf:
    __slots__ = ("t", "w", "r", "name")

    def __init__(self, t=None, name=""):
        self.t = t
        self.w = None
        self.r = []
        self.name = name

    def __getitem__(self, k):
        return self.t[k]


class FW:
    def __init__(self, nc, es):
        self.nc = nc
        self.es = es
        self.eng = {"pe": nc.tensor, "act": nc.scalar, "dve": nc.vector, "pool": nc.gpsimd, "sp": nc.sync}
        self.sem = {k: es.enter_context(nc.semaphore("s_" + k)) for k in self.eng}
        self.cnt = {k: 0 for k in self.eng}
        self.seen = {k: {} for k in self.eng}
        self.dsem = {}
        self.dslot_use = {}
        self.dnext = {}
        for q in ("sp", "act", "pool"):
            self.dsem[q] = [es.enter_context(nc.semaphore(f"d_{q}{i}")) for i in range(N_DMA_SLOTS)]
            self.dslot_use[q] = [0] * N_DMA_SLOTS
            self.dnext[q] = 0
        self.n_inst = 0

    def sbuf(self, name, shape, dt=F32):
        self.n_alloc = getattr(self, "n_alloc", 0) + 1
        name = f"{name}_u{self.n_alloc}"
        return Buf(self.es.enter_context(self.nc.sbuf_tensor(name, list(shape), dt)), name)

    def psum(self, name, shape, dt=F32):
        return Buf(self.es.enter_context(self.nc.psum_tensor(name, list(shape), dt)), name)

    def dram(self, name, shape, dt=F32, kind="Internal"):
        return Buf(self.nc.dram_tensor(name, list(shape), dt, kind=kind), name)

    def _wait(self, e, ticket):
        if ticket is None:
            return
        kind = ticket[0]
        if kind == "e":
            _, src, n = ticket
            if src == e and (e == "pe" or not SAME_ENGINE_SYNC):
                return
            key = ("e", src)
            if self.seen[e].get(key, 0) >= n:
                return
            self.eng[e].wait_ge(self.sem[src], n)
            self.seen[e][key] = n
        else:
            _, q, slot, val = ticket
            key = ("d", q, slot)
            if self.seen[e].get(key, 0) >= val:
                return
            self.eng[e].wait_ge(self.dsem[q][slot], val)
            self.seen[e][key] = val

    def _deps(self, e, reads, writes):
        for b in reads:
            self._wait(e, b.w)
        for b in writes:
            self._wait(e, b.w)
            for t in b.r:
                self._wait(e, t)

    def _record(self, ticket, reads, writes):
        for b in reads:
            if ticket[0] == "e":
                b.r = [t for t in b.r if not (t[0] == "e" and t[1] == ticket[1])]
            b.r.append(ticket)
        for b in writes:
            b.w = ticket
            b.r = []

    def op(self, e, fn, reads=(), writes=()):
        self._deps(e, reads, writes)
        inst = fn()
        self.cnt[e] += 1
        inst.then_inc(self.sem[e], 1)
        self._record(("e", e, self.cnt[e]), reads, writes)
        self.n_inst += 1
        return inst

    def dma(self, q, out, in_, reads=(), writes=(), **kw):
        q = DMA_Q_MAP.get(q, q)
        self._deps(q, reads, writes)
        slot = self.dnext[q]
        self.dnext[q] = (slot + 1) % N_DMA_SLOTS
        uses = self.dslot_use[q][slot]
        if uses > 0:
            self._wait(q, ("d", q, slot, 16 * uses))
        inst = self.eng[q].dma_start(out=out, in_=in_, **kw)
        inst.then_inc(self.dsem[q][slot], 16)
        self.dslot_use[q][slot] = uses + 1
        self._record(("d", q, slot, 16 * (uses + 1)), reads, writes)
        self.n_inst += 1
        return inst

    @contextmanager
    def scope(self):
        old = self.es
        with ExitStack() as es2:
            self.es = es2
            try:
                yield
            finally:
                self.es = old
            self.barrier()

    def barrier(self):
        for e in self.eng:
            for k2 in self.eng:
                if k2 != e and self.cnt[k2] > 0:
                    self._wait(e, ("e", k2, self.cnt[k2]))
            for q in self.dsem:
                for slot in range(N_DMA_SLOTS):
                    u = self.dslot_use[q][slot]
                    if u:
                        self._wait(e, ("d", q, slot, 16 * u))

    def finish(self, bufs):
        for b in bufs:
            self._wait("sp", b.w)
        for k in self.eng:
            if k != "sp" and self.cnt[k] > 0:
                self._wait("sp", ("e", k, self.cnt[k]))
        for q in self.dsem:
            for slot in range(N_DMA_SLOTS):
                u = self.dslot_use[q][slot]
                if u:
                    self._wait("sp", ("d", q, slot, 16 * u))


import numpy as np

T = 4096
D = 1024
NCH = 65
NP = NCH * 128
TB = 512
NTB = T // TB


def host_consts():
    c = {}
    c["ident"] = np.eye(128, dtype=np.float32)
    c["ones"] = np.ones((128, 128), np.float32)
    return c


class K:
    pass


def setup_common(nc, fw, k, n_layers):
    def din(name, shape, dt=F32):
        return fw.dram(name, shape, dt, kind="ExternalInput")
    k.xT_in = din("xT", [D, T])
    k.c8 = din("c8", [128, 8])
    k.ada_w = din("ada_w", [4, D, 6 * D])
    k.ada_b = din("ada_b", [4, 6 * D])
    k.norm1_w = din("norm1_w", [4, D])
    k.norm2_w = din("norm2_w", [4, D])
    k.w_in = din("w_in_p", [4, D, NP])
    k.identD = din("ident", [128, 128])
    k.onesD = din("ones", [128, 128])

    k.ident = fw.sbuf("ident_s", [128, 128])
    k.ones = fw.sbuf("ones_s", [128, 128])
    fw.dma("sp", k.ident[:], k.identD.t.ap()[:, :], writes=[k.ident])
    fw.dma("sp", k.ones[:], k.onesD.t.ap()[:, :], writes=[k.ones])
    k.identb = fw.sbuf("ident_b", [128, 128], BF16)
    k.onesb = fw.sbuf("ones_b", [128, 128], BF16)
    fw.op("dve", lambda: nc.vector.tensor_copy(out=k.identb[:], in_=k.ident[:]), reads=[k.ident], writes=[k.identb])
    fw.op("dve", lambda: nc.vector.tensor_copy(out=k.onesb[:], in_=k.ones[:]), reads=[k.ones], writes=[k.onesb])

    k.ps = [fw.psum(f"ps{i}", [128, 512]) for i in range(8)]

    cs = fw.sbuf("c_s", [128, 8])
    fw.dma("sp", cs[:], k.c8.t.ap()[:, :], writes=[cs])
    cond = fw.sbuf("cond_s", [128, 8])
    fw.op("act", lambda: nc.scalar.activation(out=cond[:], in_=cs[:], func=AF.Silu), reads=[cs], writes=[cond])
    k.ada = fw.sbuf("ada_s", [128, 4, 48])
    adab = fw.sbuf("adab_s", [128, 4, 48])
    fw.dma("sp", adab[:], k.ada_b.t.ap().rearrange("l (j p) -> p l j", p=128), writes=[adab], allow_slow_non_contiguous=True)
    k.n1 = fw.sbuf("n1_s", [128, 4, 8])
    k.n2 = fw.sbuf("n2_s", [128, 4, 8])
    fw.dma("sp", k.n1[:], k.norm1_w.t.ap().rearrange("l (j p) -> p l j", p=128), writes=[k.n1], allow_slow_non_contiguous=True)
    fw.dma("sp", k.n2[:], k.norm2_w.t.ap().rearrange("l (j p) -> p l j", p=128), writes=[k.n2], allow_slow_non_contiguous=True)
    with fw.scope():
      wst = [fw.sbuf(f"adaw_st{i}", [128, 6 * D]) for i in range(2)]
      for l in range(n_layers):
        pst = k.ps[l % 2]
        for kc in range(8):
            w = wst[kc % 2]
            fw.dma("sp" if kc % 2 == 0 else "act", w[:], k.ada_w.t.ap()[l, kc * 128:(kc + 1) * 128, :], writes=[w])
            for j in range(48):
                col = kc * 48 + j
                fw.op("pe", lambda: nc.tensor.matmul(pst[:, col:col + 1], lhsT=w[:, j * 128:(j + 1) * 128], rhs=cond[:, kc:kc + 1],
                                                     start=True, stop=True), reads=[w, cond], writes=[pst])
        fw.op("dve", lambda: nc.vector.tensor_reduce(out=k.ada[:, l, :], in_=pst[:, 0:384].rearrange("p (k j) -> p j k", k=8),
                                                     axis=AX.X, op=ALU.add), reads=[pst], writes=[k.ada])
        fw.op("dve", lambda: nc.vector.tensor_tensor(out=k.ada[:, l, :], in0=k.ada[:, l, :], in1=adab[:, l, :], op=ALU.add),
              reads=[k.ada, adab], writes=[k.ada])
    k.g1 = fw.sbuf("g1_s", [128, 4, 8])
    k.g2 = fw.sbuf("g2_s", [128, 4, 8])
    for l in range(n_layers):
        fw.op("dve", lambda: nc.vector.scalar_tensor_tensor(out=k.g1[:, l, :], in0=k.ada[:, l, 8:16], scalar=1.0, in1=k.n1[:, l, :],
                                                            op0=ALU.add, op1=ALU.mult), reads=[k.ada, k.n1], writes=[k.g1])
        fw.op("dve", lambda: nc.vector.scalar_tensor_tensor(out=k.g2[:, l, :], in0=k.ada[:, l, 32:40], scalar=1.0, in1=k.n2[:, l, :],
                                                            op0=ALU.add, op1=ALU.mult), reads=[k.ada, k.n2], writes=[k.g2])


def norm_mod(nc, fw, k, XT, g, s, hT, tag):
    xs_b = k.xs_bufs
    for tb in range(NTB):
        xs = xs_b[tb % 2]
        fw.dma("sp" if tb % 2 == 0 else "act", xs[:], XT.t.ap()[:, tb * TB:(tb + 1) * TB].rearrange("(k p) t -> p k t", p=128),
               reads=[XT], writes=[xs])
        sq = k.sq_buf
        fw.op("act", lambda: nc.scalar.activation(out=sq[:], in_=xs[:], func=AF.Square), reads=[xs], writes=[sq])
        pst = k.ps[7]
        for kc in range(8):
            fw.op("pe", lambda: nc.tensor.matmul(pst[:], lhsT=k.ones[:], rhs=sq[:, kc, :], start=(kc == 0), stop=(kc == 7)),
                  reads=[k.ones, sq], writes=[pst])
        rstd = k.rstd_buf
        fw.op("dve", lambda: nc.vector.tensor_scalar(out=rstd[:], in0=pst[:], scalar1=1.0 / D, scalar2=1e-6, op0=ALU.mult, op1=ALU.add),
              reads=[pst], writes=[rstd])
        fw.op("act", lambda: nc.scalar.activation(out=rstd[:], in_=rstd[:], func=AF.Sqrt), reads=[rstd], writes=[rstd])
        fw.op("dve", lambda: nc.vector.reciprocal(out=rstd[:], in_=rstd[:]), reads=[rstd], writes=[rstd])
        for kc in range(8):
            tmp = k.tmp_bufs[kc % 2]
            fw.op("dve", lambda: nc.vector.scalar_tensor_tensor(out=tmp[:], in0=xs[:, kc, :], scalar=g[:, kc:kc + 1], in1=rstd[:],
                                                                op0=ALU.mult, op1=ALU.mult), reads=[xs, rstd], writes=[tmp])
            fw.op("act", lambda: nc.scalar.activation(out=hT[:, kc, tb * TB:(tb + 1) * TB], in_=tmp[:], func=AF.Identity,
                                                      bias=s[:, kc:kc + 1], scale=1.0), reads=[tmp], writes=[hT])


def in_proj(nc, fw, k, l, hT, PT):
    GC = 5
    NG = NCH // GC
    ev = 0
    for cg in range(NG):
        wst = k.w_st[cg % 2]
        wb = k.w_bf[cg % 2]
        fw.dma("sp" if cg % 2 == 0 else "act", wst[:], k.w_in.t.ap()[l, :, cg * 640:(cg + 1) * 640].rearrange("(k p) n -> p k n", p=128),
               reads=[], writes=[wst])
        if cg % 2 == 0:
            fw.op("pool", lambda: nc.gpsimd.tensor_copy(out=wb[:], in_=wst[:]), reads=[wst], writes=[wb])
        else:
            fw.op("dve", lambda: nc.vector.tensor_copy(out=wb[:], in_=wst[:]), reads=[wst], writes=[wb])
        for tb in range(NTB):
            for ch in range(GC):
                pst = k.ps[(0, 3, 1, 4)[ev % 4]]
                for kc in range(8):
                    fw.op("pe", lambda: nc.tensor.matmul(pst[:], lhsT=wb[:, kc, ch * 128:(ch + 1) * 128], rhs=hT[:, kc, tb * TB:(tb + 1) * TB],
                                                         start=(kc == 0), stop=(kc == 7)), reads=[wb, hT], writes=[pst])
                o = k.ev_bufs[ev % 4]
                if ev % 2 == 0:
                    fw.op("act", lambda: nc.scalar.copy(out=o[:], in_=pst[:]), reads=[pst], writes=[o])
                else:
                    fw.op("dve", lambda: nc.vector.tensor_copy(out=o[:], in_=pst[:]), reads=[pst], writes=[o])
                row = (cg * GC + ch) * 128
                fw.dma("pool" if ev % 2 == 0 else "sp", PT.t.ap()[row:row + 128, tb * TB:(tb + 1) * TB], o[:], reads=[o], writes=[k.PTtok[cg * GC + ch][tb]])
                ev += 1


def alloc_p1(fw, k):
    k.xs_bufs = [fw.sbuf(f"xs{i}", [128, 8, TB]) for i in range(2)]
    k.sq_buf = fw.sbuf("sq", [128, 8, TB])
    k.rstd_buf = fw.sbuf("rstd", [128, TB])
    k.tmp_bufs = [fw.sbuf(f"tmp{i}", [128, TB]) for i in range(2)]
    k.hT = fw.sbuf("hT", [128, 8, T], BF16)
    k.w_st = [fw.sbuf(f"wst{i}", [128, 8, 640]) for i in range(2)]
    k.w_bf = [fw.sbuf(f"wbf{i}", [128, 8, 640], BF16) for i in range(2)]
    k.ev_bufs = [fw.sbuf(f"ev{i}", [128, TB]) for i in range(4)]
    k.PTtok = [[Buf(None, f"pt{c}_{t}") for t in range(NTB)] for c in range(NCH)]


def pad_w_in(w_in):
    L = w_in.shape[0]
    out = np.zeros((L, D, NP), np.float32)
    out[:, :, :3352] = w_in[:, :, :3352]
    out[:, :, 3456:] = w_in[:, :, 3352:]
    return out


import numpy as np


def hgrn_consts():
    c = {}
    s = np.arange(128)
    c["mask_bd"] = ((s[:, None] // 32 == s[None, :] // 32) & (s[:, None] <= s[None, :])).astype(np.float32)
    c["rowmask"] = (s[:, None] // 32 == np.arange(4)[None, :]).astype(np.float32)
    m = np.ones((128, 512), np.float32)
    m[:, ::32] = 0.0
    c["scanmask32"] = m
    return c


def setup_hgrn(nc, fw, k, n_layers):
    def din(name, shape, dt=F32):
        return fw.dram(name, shape, dt, kind="ExternalInput")
    k.lb_logits = din("hgrn_lb_logits", [4, 512])
    k.hgrn_nw = din("hgrn_norm_w", [4, 512])
    md = din("mask_bd", [128, 128]); rm = din("rowmask", [128, 4]); sm = din("scanmask32", [128, 512])
    k.mask_bd = fw.sbuf("mask_bd_s", [128, 128]); k.rowmask = fw.sbuf("rowmask_s", [128, 4]); k.scanmask32 = fw.sbuf("scanmask32_s", [128, 512])
    fw.dma("sp", k.mask_bd[:], md.t.ap()[:, :], writes=[k.mask_bd])
    fw.dma("sp", k.rowmask[:], rm.t.ap()[:, :], writes=[k.rowmask])
    fw.dma("sp", k.scanmask32[:], sm.t.ap()[:, :], writes=[k.scanmask32])
    k.hnw = fw.sbuf("hnw_s", [128, 4, 4])
    fw.dma("sp", k.hnw[:], k.hgrn_nw.t.ap().rearrange("l (h p) -> p l h", p=128), writes=[k.hnw], allow_slow_non_contiguous=True)
    lbl = fw.sbuf("lbl_s", [128, 4, 4])
    fw.dma("sp", lbl[:], k.lb_logits.t.ap().rearrange("l (h p) -> p l h", p=128), writes=[lbl], allow_slow_non_contiguous=True)
    e = fw.sbuf("lbe_s", [128, 4, 4])
    fw.op("act", lambda: nc.scalar.activation(out=e[:], in_=lbl[:], func=AF.Exp), reads=[lbl], writes=[e])
    ssum = fw.sbuf("lbsum_s", [128, 4])
    fw.op("dve", lambda: nc.vector.tensor_tensor(out=ssum[:], in0=e[:, 0, :], in1=e[:, 1, :], op=ALU.add), reads=[e], writes=[ssum])
    fw.op("dve", lambda: nc.vector.tensor_tensor(out=ssum[:], in0=ssum[:], in1=e[:, 2, :], op=ALU.add), reads=[e, ssum], writes=[ssum])
    fw.op("dve", lambda: nc.vector.tensor_tensor(out=ssum[:], in0=ssum[:], in1=e[:, 3, :], op=ALU.add), reads=[e, ssum], writes=[ssum])
    fw.op("dve", lambda: nc.vector.reciprocal(out=ssum[:], in_=ssum[:]), reads=[ssum], writes=[ssum])
    k.lb = fw.sbuf("lb_s", [128, 4, 4])
    k.oml = fw.sbuf("oml_s", [128, 4, 4])
    k.noml = fw.sbuf("noml_s", [128, 4, 4])
    fw.op("dve", lambda: nc.vector.memset(k.lb[:], 0.0), writes=[k.lb])
    for l in range(1, 4):
        fw.op("dve", lambda: nc.vector.tensor_tensor(out=e[:, l, :], in0=e[:, l, :], in1=ssum[:], op=ALU.mult), reads=[e, ssum], writes=[e])
        fw.op("dve", lambda: nc.vector.tensor_tensor(out=k.lb[:, l, :], in0=k.lb[:, l - 1, :], in1=e[:, l, :], op=ALU.add), reads=[e, k.lb], writes=[k.lb])
    fw.op("dve", lambda: nc.vector.tensor_scalar(out=k.oml[:], in0=k.lb[:], scalar1=-1.0, scalar2=1.0, op0=ALU.mult, op1=ALU.add), reads=[k.lb], writes=[k.oml])
    fw.op("dve", lambda: nc.vector.tensor_scalar(out=k.noml[:], in0=k.oml[:], scalar1=-1.0, scalar2=None, op0=ALU.mult), reads=[k.oml], writes=[k.noml])


def load_pt(fw, k, PT, q, dst, ch, tb, rows=128, row0=0):
    r = ch * 128 + row0
    fw.dma(q, dst, PT.t.ap()[r:r + rows, tb * TB:(tb + 1) * TB], reads=[k.PTtok[ch][tb]], writes=[])


def hgrn(nc, fw, k, l, PT, YT):
    with fw.scope():
        f32t = lambda n: fw.sbuf(n, [128, TB])
        bft = lambda n: fw.sbuf(n, [128, TB], BF16)
        zq = [f32t(f"h_zq{i}") for i in range(2)]; zf = [f32t(f"h_zf{i}") for i in range(2)]
        zi = [f32t(f"h_zi{i}") for i in range(2)]; zg = [f32t(f"h_zg{i}") for i in range(2)]
        q = f32t("h_q"); sg = f32t("h_sg"); kk = f32t("h_k"); b = f32t("h_b"); t1 = f32t("h_t1"); t2 = f32t("h_t2")
        kh = f32t("h_kh"); gam = fw.sbuf("h_gam", [128, 16])
        qt = bft("h_qt"); kt = bft("h_kt"); khb = bft("h_khb"); vb = bft("h_vb")
        AT = [fw.sbuf(f"h_AT{i}", [128, 128], BF16) for i in range(2)]
        Vt = [fw.sbuf(f"h_Vt{i}", [128, 128], BF16) for i in range(2)]
        khz = [[fw.sbuf(f"h_khz{i}_{c}", [128, 128], BF16) for c in range(4)] for i in range(2)]
        S = fw.sbuf("h_S", [128, 128])
        Sb = [fw.sbuf(f"h_Sb{i}", [128, 128], BF16) for i in range(8)]
        ob = f32t("h_ob"); rs = f32t("h_rs"); yb = [bft(f"h_yb{i}") for i in range(2)]
        P_sc, P_vt, P_kt, P_o, P_u0, P_u1, P_n = (k.ps[i] for i in (3, 0, 4, 1, 5, 6, 7))
        it = 0
        sbi = 0
        import os
        SUB = int(os.environ.get('SUB', '9')); STAGE = int(os.environ.get('STAGE', '9')); NHX = int(os.environ.get('NHX', '4')); NTBX = int(os.environ.get('NTBX', '8'))
        for h in range(NHX):
            fw.op("dve", lambda: nc.vector.memset(S[:], 0.0), writes=[S])
            lbc = k.lb[:, l, h:h + 1]; omlc = k.oml[:, l, h:h + 1]; nomlc = k.noml[:, l, h:h + 1]
            for tb in range(NTBX):
                z_q, z_f, z_i, z_g = zq[it % 2], zf[it % 2], zi[it % 2], zg[it % 2]
                for (dst, ch, qq) in ((z_q, h, "sp"), (z_f, 4 + h, "act"), (z_i, 8 + h, "sp"), (z_g, 12 + h, "act")):
                    r = ch * 128
                    fw.dma(qq, dst[:], PT.t.ap()[r:r + 128, tb * TB:(tb + 1) * TB], reads=[k.PTtok[ch][tb]], writes=[dst])
                fw.op("act", lambda: nc.scalar.activation(out=q[:], in_=z_q[:], func=AF.Silu), reads=[z_q], writes=[q])
                fw.op("act", lambda: nc.scalar.activation(out=sg[:], in_=z_f[:], func=AF.Sigmoid), reads=[z_f], writes=[sg])
                fw.op("dve", lambda: nc.vector.tensor_scalar(out=t1[:], in0=sg[:], scalar1=omlc, scalar2=lbc, op0=ALU.mult, op1=ALU.add), reads=[sg], writes=[t1])
                fw.op("act", lambda: nc.scalar.activation(out=t1[:], in_=t1[:], func=AF.Ln), reads=[t1], writes=[t1])
                fw.op("dve", lambda: nc.vector.tensor_scalar(out=kk[:], in0=sg[:], scalar1=nomlc, scalar2=omlc, op0=ALU.mult, op1=ALU.add), reads=[sg], writes=[kk])
                fw.op("dve", lambda: nc.vector.tensor_tensor_scan(out=b[:], data0=k.scanmask32[:], data1=t1[:], initial=0.0, op0=ALU.mult, op1=ALU.add),
                      reads=[k.scanmask32, t1], writes=[b])
                fw.op("act", lambda: nc.scalar.activation(out=t2[:], in_=b[:], func=AF.Exp), reads=[b], writes=[t2])
                fw.op("dve", lambda: nc.vector.tensor_tensor(out=qt[:], in0=q[:], in1=t2[:], op=ALU.mult), reads=[q, t2], writes=[qt])
                fw.op("act", lambda: nc.scalar.activation(out=t2[:], in_=b[:], func=AF.Exp, scale=-1.0), reads=[b], writes=[t2])
                fw.op("dve", lambda: nc.vector.tensor_tensor(out=kt[:], in0=kk[:], in1=t2[:], op=ALU.mult), reads=[kk, t2], writes=[kt])
                b3 = b.t[:].rearrange("p (c t) -> p c t", t=32)
                fw.op("dve", lambda: nc.vector.tensor_tensor(out=t2.t[:].rearrange("p (c t) -> p c t", t=32), in0=b3[:, :, 31:32].to_broadcast([128, 16, 32]), in1=b3,
                                                             op=ALU.subtract), reads=[b], writes=[t2])
                fw.op("act", lambda: nc.scalar.activation(out=t2[:], in_=t2[:], func=AF.Exp), reads=[t2], writes=[t2])
                fw.op("dve", lambda: nc.vector.tensor_tensor(out=khb[:], in0=kk[:], in1=t2[:], op=ALU.mult), reads=[kk, t2], writes=[khb])
                fw.op("pool", lambda: nc.gpsimd.tensor_copy(out=vb[:], in_=z_i[:]), reads=[z_i], writes=[vb])
                fw.op("act", lambda: nc.scalar.activation(out=gam[:], in_=b3[:, :, 31], func=AF.Exp), reads=[b], writes=[gam])
                if os.environ.get('DBG') and it == 0:
                    for di, src in enumerate((q, kk, b, t2, t1, sg)):
                        fw.dma('sp', k.DBG.t.ap()[:, di, :], src[:], reads=[src], writes=[k.DBGtok])
                if STAGE < 2:
                    continue
                for tl in range(4):
                    cs = slice(tl * 128, (tl + 1) * 128)
                    a_t = AT[tl % 2]; v_t = Vt[tl % 2]; kz = khz[tl % 2]
                    fw.op("pe", lambda: nc.tensor.matmul(P_sc[:, cs], lhsT=kt[:, cs], rhs=qt[:, cs], start=True, stop=True), reads=[kt, qt], writes=[P_sc])
                    fw.op("dve", lambda: nc.vector.tensor_tensor(out=a_t[:], in0=P_sc[:, cs], in1=k.mask_bd[:], op=ALU.mult), reads=[P_sc, k.mask_bd], writes=[a_t])
                    if SUB < 2: continue
                    fw.op("pe", lambda: nc.tensor.matmul(P_vt[:, cs], lhsT=vb[:, cs], rhs=k.identb[:], start=True, stop=True), reads=[vb, k.identb], writes=[P_vt])
                    fw.op("act", lambda: nc.scalar.copy(out=v_t[:], in_=P_vt[:, cs]), reads=[P_vt], writes=[v_t])
                    if SUB < 3: continue
                    ksrc = {'khb': khb, 'vb': vb, 'qt': qt}[os.environ.get('KSRC', 'khb')]
                    fw.op("pe", lambda: nc.tensor.matmul(P_kt[:, cs], lhsT=ksrc[:, cs], rhs=k.identb[:], start=True, stop=True), reads=[ksrc, k.identb], writes=[P_kt])
                    for c in range(int(os.environ.get('NEV', '4'))):
                        if c % 2 == 0 or True:
                            fw.op("dve", lambda: nc.vector.tensor_scalar(out=kz[c][:], in0=P_kt[:, cs], scalar1=k.rowmask[:, c:c + 1], scalar2=None, op0=ALU.mult),
                                  reads=[P_kt, k.rowmask], writes=[kz[c]])
                        else:
                            fw.op("act", lambda: nc.scalar.activation(out=kz[c][:], in_=P_kt[:, cs], func=AF.Identity, scale=k.rowmask[:, c:c + 1]),
                                  reads=[P_kt, k.rowmask], writes=[kz[c]])
                    if SUB < 4: continue
                    if SUB < 5: continue
                    for c in range(4):
                        s_b = Sb[sbi % 8]; sbi += 1
                        fw.op("act", lambda: nc.scalar.copy(out=s_b[:], in_=S[:]), reads=[S], writes=[s_b])
                        c0 = tl * 128 + c * 32
                        fw.op("pe", lambda: nc.tensor.matmul(P_o[:, c0:c0 + 32], lhsT=v_t[:], rhs=a_t[:, c * 32:(c + 1) * 32], start=True, stop=False), reads=[v_t, a_t], writes=[P_o])
                        fw.op("pe", lambda: nc.tensor.matmul(P_o[:, c0:c0 + 32], lhsT=s_b[:], rhs=qt[:, c0:c0 + 32], start=False, stop=True), reads=[s_b, qt], writes=[P_o])
                        P_u = P_u0 if c % 2 == 0 else P_u1
                        fw.op("pe", lambda: nc.tensor.matmul(P_u[:, 0:128], lhsT=kz[c][:], rhs=v_t[:], start=True, stop=True), reads=[kz[c], v_t], writes=[P_u])
                        gi = tl * 4 + c
                        fw.op("dve", lambda: nc.vector.scalar_tensor_tensor(out=S[:], in0=S[:], scalar=gam[:, gi:gi + 1], in1=P_u[:, 0:128], op0=ALU.mult, op1=ALU.add),
                              reads=[S, gam, P_u], writes=[S])
                    fw.op("act", lambda: nc.scalar.copy(out=ob[:, cs], in_=P_o[:, cs]), reads=[P_o], writes=[ob])
                if STAGE < 3:
                    continue
                fw.op("act", lambda: nc.scalar.activation(out=t2[:], in_=ob[:], func=AF.Square), reads=[ob], writes=[t2])
                fw.op("pe", lambda: nc.tensor.matmul(P_n[:], lhsT=k.ones[:], rhs=t2[:], start=True, stop=True), reads=[k.ones, t2], writes=[P_n])
                S3 = int(os.environ.get('S3', '9'))
                if S3 < 2: continue
                fw.op("dve", lambda: nc.vector.tensor_scalar(out=rs[:], in0=P_n[:], scalar1=1.0 / 128, scalar2=1e-5, op0=ALU.mult, op1=ALU.add), reads=[P_n], writes=[rs])
                fw.op("act", lambda: nc.scalar.activation(out=rs[:], in_=rs[:], func=AF.Sqrt), reads=[rs], writes=[rs])
                fw.op("dve", lambda: nc.vector.reciprocal(out=rs[:], in_=rs[:]), reads=[rs], writes=[rs])
                if S3 < 3: continue
                fw.op("act", lambda: nc.scalar.activation(out=t1[:], in_=z_g[:], func=AF.Sigmoid), reads=[z_g], writes=[t1])
                fw.op("dve", lambda: nc.vector.scalar_tensor_tensor(out=rs[:], in0=ob[:], scalar=k.hnw[:, l, h:h + 1], in1=rs[:], op0=ALU.mult, op1=ALU.mult),
                      reads=[ob, rs, k.hnw], writes=[rs])
                y_b = yb[it % 2]
                fw.op("dve", lambda: nc.vector.tensor_tensor(out=y_b[:], in0=rs[:], in1=t1[:], op=ALU.mult), reads=[rs, t1], writes=[y_b])
                if S3 < 4: continue
                fw.dma("sp", YT.t.ap()[0, h * 128:(h + 1) * 128, tb * TB:(tb + 1) * TB], y_b[:], reads=[y_b], writes=[k.YTtok[0][h][tb]])
                it += 1


def alloc_tokens(k):
    k.PTtok = [[Buf(None, f"pt{c}_{t}") for t in range(NTB)] for c in range(NCH)]
    k.YTtok = [[[Buf(None, f"yt{m}_{c}_{t}") for t in range(NTB)] for c in range(4)] for m in range(3)]
    k.UTtok = [[Buf(None, f"ut{c}_{t}") for t in range(NTB)] for c in range(22)]


import numpy as np, os

CH_R, CH_K, CH_V, CH_WA, CH_G = 27, 31, 35, 39, 40


def rwkv_consts():
    c = {}
    i = np.arange(128)
    same = (i[:, None] // 64 == i[None, :] // 64)
    su = (same & (i[:, None] < i[None, :])).astype(np.float32)
    iu = (same & (i[:, None] <= i[None, :])).astype(np.float32)
    sl = (same & (i[:, None] > i[None, :])).astype(np.float32)
    c["rw_mask_su_iu"] = np.concatenate([su, iu], axis=1)
    c["rw_mask_negsu"] = -su
    c["rw_mask_negsl"] = -sl
    m = np.ones((64, 512), np.float32)
    m[:, ::64] = 0.0
    c["scanmask64"] = m
    c["ones64"] = np.ones((64, 64), np.float32)
    c["rowmask64"] = np.concatenate([(i[:, None] // 64 == np.arange(2)[None, :]).astype(np.float32), -(i[:, None] // 64 == np.arange(2)[None, :]).astype(np.float32)], axis=1)
    return c


def setup_rwkv(nc, fw, k):
    def din(name, shape, dt=F32):
        return fw.dram(name, shape, dt, kind="ExternalInput")
    k.rw = {}
    for nm, shp in (("rw_mu", [4, 1792]), ("rw_w0", [4, 512]), ("rw_w2", [4, 64, 512]), ("rw_a0", [4, 512]), ("rw_a2", [4, 64, 512]),
                    ("rw_g2", [4, 128, 512]), ("rw_k_k", [4, 512]), ("rw_k_a", [4, 512]), ("rw_r_k", [4, 512]), ("rw_lnx_w", [4, 512]), ("rw_lnx_b", [4, 512])):
        k.rw[nm] = din(nm, shp)
    cm = {}
    for nm, shp in (("rw_mask_su_iu", [128, 256]), ("rw_mask_negsu", [128, 128]), ("rw_mask_negsl", [128, 128]), ("scanmask64", [64, 512]), ("ones64", [64, 64]), ("rowmask64", [128, 4])):
        d = din(nm, shp)
        t = fw.sbuf(nm + "_s", shp)
        fw.dma("sp", t[:], d.t.ap()[:, :], writes=[t])
        cm[nm] = t
    k.rwc = cm


def rwkv(nc, fw, k, l, PT, YT, NHX=8, NTBX=NTB):
    with fw.scope():
        W = k.rw
        def pvec(nm):
            t = fw.sbuf("rp_" + nm, [64, 8])
            fw.dma("sp", t[:], W[nm].t.ap()[l, :].rearrange("(h p) -> p h", p=64), writes=[t], allow_slow_non_contiguous=True)
            return t
        w0 = pvec("rw_w0"); a0 = pvec("rw_a0"); k_k = pvec("rw_k_k"); k_a = pvec("rw_k_a"); r_k = pvec("rw_r_k"); lnw = pvec("rw_lnx_w"); lnb = pvec("rw_lnx_b")
        mu_rkv = fw.sbuf("rp_mu", [64, 24])
        fw.dma("sp", mu_rkv[:], W["rw_mu"].t.ap()[l, 0:1536].rearrange("(h p) -> p h", p=64), writes=[mu_rkv], allow_slow_non_contiguous=True)
        mu_wa = fw.sbuf("rp_muwa", [64, 2])
        fw.dma("sp", mu_wa[:], W["rw_mu"].t.ap()[l, 1536:1664].rearrange("(h p) -> p h", p=64), writes=[mu_wa], allow_slow_non_contiguous=True)
        mu_g = fw.sbuf("rp_mug", [128, 1])
        fw.dma("sp", mu_g[:], W["rw_mu"].t.ap()[l, 1664:1792].rearrange("(h p) -> p h", p=128), writes=[mu_g], allow_slow_non_contiguous=True)
        w2 = fw.sbuf("rp_w2", [64, 512]); a2 = fw.sbuf("rp_a2", [64, 512]); g2 = fw.sbuf("rp_g2", [128, 512])
        fw.dma("sp", w2[:], W["rw_w2"].t.ap()[l], writes=[w2]); fw.dma("sp", a2[:], W["rw_a2"].t.ap()[l], writes=[a2]); fw.dma("sp", g2[:], W["rw_g2"].t.ap()[l], writes=[g2])
        C = k.rwc
        ident = k.ident
        t64 = lambda n: fw.sbuf(n, [64, TB])
        xw_in = fw.sbuf("r_xwin", [64, TB + 1]); xa_in = fw.sbuf("r_xain", [64, TB + 1]); xg_in = fw.sbuf("r_xgin", [128, TB + 1])
        th = t64("r_th"); xa = t64("r_xa"); sgg = fw.sbuf("r_sgg", [128, TB]); dtmp = fw.sbuf("r_dtmp", [128, TB])
        rin = fw.sbuf("r_rin", [64, TB + 1]); kin = fw.sbuf("r_kin", [64, TB + 1]); vin = fw.sbuf("r_vin", [64, TB + 1])
        k_s = t64("r_ks"); ld = t64("r_ld"); a_ = t64("r_a"); kap = t64("r_kap"); b_ = t64("r_b"); L = t64("r_L")
        e1 = t64("r_e1"); e2 = t64("r_e2"); e3 = t64("r_e3"); t1 = t64("r_t1"); t2 = t64("r_t2")
        M = [fw.sbuf(f"r_M{h}", [128, 64]) for h in range(8)]
        S = []
        for s_ in range(2):
            B_ = {}
            for n_ in ("r_s", "v_s", "kp", "g_", "kapt", "kt", "bt", "kh", "bh", "yo"):
                B_[n_] = t64(f"r_{n_}{s_}")
            B_["rt"] = fw.sbuf(f"r_rt{s_}", [128, TB]); B_["gam"] = fw.sbuf(f"r_gam{s_}", [64, 8]); B_["dg"] = fw.sbuf(f"r_dg{s_}", [128, 64])
            B_["SCa"] = fw.sbuf(f"r_SCa{s_}", [128, 256]); B_["SCb"] = fw.sbuf(f"r_SCb{s_}", [128, 256])
            B_["Y"] = [fw.sbuf(f"r_Y{s_}{i}", [128, 128]) for i in range(2)]; B_["Z"] = [fw.sbuf(f"r_Z{s_}{i}", [128, 128]) for i in range(2)]
            B_["P"] = [fw.sbuf(f"r_P{s_}{i}", [128, 128]) for i in range(2)]
            B_["ktok"] = fw.sbuf(f"r_ktok{s_}", [128, 128]); B_["vtok"] = fw.sbuf(f"r_vtok{s_}", [128, 64]); B_["bhtok"] = fw.sbuf(f"r_bhtok{s_}", [128, 64])
            B_["khc"] = [fw.sbuf(f"r_khc{s_}{i}", [128, 64]) for i in range(2)]; B_["bhc"] = [fw.sbuf(f"r_bhc{s_}{i}", [128, 64]) for i in range(2)]
            B_["nWc"] = [fw.sbuf(f"r_nWc{s_}{i}", [128, 64]) for i in range(2)]
            B_["WU"] = fw.sbuf(f"r_WU{s_}", [128, 128]); B_["nWU"] = fw.sbuf(f"r_nWU{s_}", [128, 128]); B_["Rp"] = fw.sbuf(f"r_Rp{s_}", [128, 128])
            B_["PTm"] = [fw.sbuf(f"r_PTm{s_}{i}", [64, 64]) for i in range(2)]; B_["Qm"] = [fw.sbuf(f"r_Qm{s_}{i}", [64, 64]) for i in range(2)]
            B_["ybuf"] = fw.sbuf(f"r_yb{s_}", [64, TB], BF16)
            bk = (0, 3, 4, 5) if s_ == 0 else (1, 6, 7, 2)
            B_["X"], B_["E0"], B_["E1"], B_["E2"] = (k.ps[i] for i in bk)
            S.append(B_)
        ps = k.ps
        for h in range(8):
            fw.op("dve", lambda: nc.vector.memset(M[h][:], 0.0), writes=[M[h]])
        for B_ in S:
            fw.op("dve", lambda: nc.vector.memset(B_["rt"][:], 0.0), writes=[B_["rt"]])
            fw.op("dve", lambda: nc.vector.memset(B_["dg"][:], 0.0), writes=[B_["dg"]])
            fw.op("dve", lambda: nc.vector.memset(B_["Rp"][:], 0.0), writes=[B_["Rp"]])
        RM = C["rowmask64"]

        def load_shift(dst, ch, row0, rows, tb):
            r0 = ch * 128 + row0
            if tb == 0:
                fw.op("dve", lambda: nc.vector.memset(dst[0:rows, 0:1], 0.0), writes=[dst])
                fw.dma("sp", dst[0:rows, 1:TB + 1], PT.t.ap()[r0:r0 + rows, 0:TB], reads=[k.PTtok[ch][0]], writes=[dst])
            else:
                fw.dma("sp", dst[0:rows, 0:TB + 1], PT.t.ap()[r0:r0 + rows, tb * TB - 1:(tb + 1) * TB], reads=[k.PTtok[ch][tb], k.PTtok[ch][tb - 1]], writes=[dst])

        def shift_mix(out, src, rows, mu_ap, tmp, mub):
            fw.op("dve", lambda: nc.vector.tensor_tensor(out=tmp[0:rows, :], in0=src[0:rows, 0:TB], in1=src[0:rows, 1:TB + 1], op=ALU.subtract), reads=[src], writes=[tmp])
            fw.op("dve", lambda: nc.vector.scalar_tensor_tensor(out=out[0:rows, :], in0=tmp[0:rows, :], scalar=mu_ap, in1=src[0:rows, 1:TB + 1], op0=ALU.mult, op1=ALU.add),
                  reads=[tmp, src, mub], writes=[out])

        it = 0
        for tb in range(NTBX):
            load_shift(xw_in, CH_WA, 0, 64, tb); load_shift(xa_in, CH_WA, 64, 64, tb); load_shift(xg_in, CH_G, 0, 128, tb)
            shift_mix(th, xw_in, 64, mu_wa[:, 0:1], dtmp, mu_wa)
            fw.op("act", lambda: nc.scalar.activation(out=th[:], in_=th[:], func=AF.Tanh), reads=[th], writes=[th])
            shift_mix(xa, xa_in, 64, mu_wa[:, 1:2], dtmp, mu_wa)
            shift_mix(sgg, xg_in, 128, mu_g[:, 0:1], dtmp, mu_g)
            fw.op("act", lambda: nc.scalar.activation(out=sgg[:], in_=sgg[:], func=AF.Sigmoid), reads=[sgg], writes=[sgg])
            def unit(s, h):
                B_ = S[s]
                X, E0, E1, E2 = B_["X"], B_["E0"], B_["E1"], B_["E2"]
                r_s, v_s, kp, g_, kapt, kt, bt, rt, kh, bh, gam, dg, yo = (B_[n_] for n_ in ("r_s", "v_s", "kp", "g_", "kapt", "kt", "bt", "rt", "kh", "bh", "gam", "dg", "yo"))
                SCa, SCb, Y, Z, P, ktok, vtok, bhtok, khc, bhc, nWc, WU, nWU, Rp, PTm, Qm, ybuf = (B_[n_] for n_ in ("SCa", "SCb", "Y", "Z", "P", "ktok", "vtok", "bhtok", "khc", "bhc", "nWc", "WU", "nWU", "Rp", "PTm", "Qm", "ybuf"))
                j, hh = h // 2, h % 2
                load_shift(rin, CH_R + j, hh * 64, 64, tb); load_shift(kin, CH_K + j, hh * 64, 64, tb); load_shift(vin, CH_V + j, hh * 64, 64, tb)
                shift_mix(r_s, rin, 64, mu_rkv[:, h:h + 1], dtmp, mu_rkv)
                shift_mix(k_s, kin, 64, mu_rkv[:, 8 + h:9 + h], dtmp, mu_rkv)
                shift_mix(v_s, vin, 64, mu_rkv[:, 16 + h:17 + h], dtmp, mu_rkv)
                hs = slice(h * 64, (h + 1) * 64)
                fw.op("pe", lambda: nc.tensor.matmul(X[0:64, :], lhsT=w2[:, hs], rhs=th[:], start=True, stop=True), reads=[w2, th], writes=[X])
                fw.op("act", lambda: nc.scalar.activation(out=ld[:], in_=X[0:64, :], func=AF.Sigmoid, bias=w0[:, h:h + 1], scale=1.0), reads=[X, w0], writes=[ld])
                fw.op("dve", lambda: nc.vector.tensor_scalar(out=ld[:], in0=ld[:], scalar1=-0.6065306597126334, scalar2=None, op0=ALU.mult), reads=[ld], writes=[ld])
                fw.op("pe", lambda: nc.tensor.matmul(X[0:64, :], lhsT=a2[:, hs], rhs=xa[:], start=True, stop=True), reads=[a2, xa], writes=[X])
                fw.op("act", lambda: nc.scalar.activation(out=a_[:], in_=X[0:64, :], func=AF.Sigmoid, bias=a0[:, h:h + 1], scale=1.0), reads=[X, a0], writes=[a_])
                fw.op("pe", lambda: nc.tensor.matmul(X[0:64, :], lhsT=g2[:, hs], rhs=sgg[:], start=True, stop=True), reads=[g2, sgg], writes=[X])
                fw.op("act", lambda: nc.scalar.copy(out=g_[:], in_=X[0:64, :]), reads=[X], writes=[g_])
                fw.op("dve", lambda: nc.vector.tensor_scalar(out=kap[:], in0=k_s[:], scalar1=k_k[:, h:h + 1], scalar2=None, op0=ALU.mult), reads=[k_s, k_k], writes=[kap])
                fw.op("act", lambda: nc.scalar.activation(out=t1[:], in_=kap[:], func=AF.Square), reads=[kap], writes=[t1])
                fw.op("pe", lambda: nc.tensor.matmul(E0[0:64, :], lhsT=C["ones64"][:], rhs=t1[:], start=True, stop=True), reads=[C["ones64"], t1], writes=[E0])
                fw.op("dve", lambda: nc.vector.tensor_scalar(out=t1[:], in0=E0[0:64, :], scalar1=1e-24, scalar2=None, op0=ALU.max), reads=[E0], writes=[t1])
                fw.op("act", lambda: nc.scalar.activation(out=t1[:], in_=t1[:], func=AF.Sqrt), reads=[t1], writes=[t1])
                fw.op("dve", lambda: nc.vector.reciprocal(out=t1[:], in_=t1[:]), reads=[t1], writes=[t1])
                fw.op("dve", lambda: nc.vector.tensor_tensor(out=kap[:], in0=kap[:], in1=t1[:], op=ALU.mult), reads=[kap, t1], writes=[kap])
                fw.op("dve", lambda: nc.vector.tensor_scalar(out=t1[:], in0=a_[:], scalar1=-1.0, scalar2=k_a[:, h:h + 1], op0=ALU.add, op1=ALU.mult), reads=[a_, k_a], writes=[t1])
                fw.op("dve", lambda: nc.vector.scalar_tensor_tensor(out=kp[:], in0=t1[:], scalar=1.0, in1=k_s[:], op0=ALU.add, op1=ALU.mult), reads=[t1, k_s], writes=[kp])
                fw.op("dve", lambda: nc.vector.tensor_tensor(out=b_[:], in0=a_[:], in1=kap[:], op=ALU.mult), reads=[a_, kap], writes=[b_])
                fw.op("dve", lambda: nc.vector.tensor_tensor_scan(out=L[:], data0=C["scanmask64"][:], data1=ld[:], initial=0.0, op0=ALU.mult, op1=ALU.add),
                      reads=[C["scanmask64"], ld], writes=[L])
                fw.op("act", lambda: nc.scalar.activation(out=e1[:], in_=L[:], func=AF.Exp), reads=[L], writes=[e1])
                fw.op("act", lambda: nc.scalar.activation(out=e2[:], in_=L[:], func=AF.Exp, scale=-1.0), reads=[L], writes=[e2])
                L3 = L.t[:].rearrange("p (c t) -> p c t", t=64)
                fw.op("dve", lambda: nc.vector.tensor_tensor(out=t2.t[:].rearrange("p (c t) -> p c t", t=64), in0=L3[:, :, 63:64].to_broadcast([64, 8, 64]), in1=L3, op=ALU.subtract),
                      reads=[L], writes=[t2])
                fw.op("act", lambda: nc.scalar.activation(out=e3[:], in_=t2[:], func=AF.Exp), reads=[t2], writes=[e3])
                fw.op("act", lambda: nc.scalar.activation(out=gam[:], in_=L3[:, :, 63], func=AF.Exp), reads=[L], writes=[gam])
                fw.op("dve", lambda: nc.vector.tensor_tensor(out=t1[:], in0=L[:], in1=ld[:], op=ALU.subtract), reads=[L, ld], writes=[t1])
                fw.op("act", lambda: nc.scalar.activation(out=t1[:], in_=t1[:], func=AF.Exp), reads=[t1], writes=[t1])
                fw.op("dve", lambda: nc.vector.tensor_tensor(out=kapt[:], in0=kap[:], in1=t1[:], op=ALU.mult), reads=[kap, t1], writes=[kapt])
                fw.op("dve", lambda: nc.vector.tensor_tensor(out=kt[:], in0=kp[:], in1=e2[:], op=ALU.mult), reads=[kp, e2], writes=[kt])
                fw.op("dve", lambda: nc.vector.tensor_tensor(out=bt[:], in0=b_[:], in1=e2[:], op=ALU.mult), reads=[b_, e2], writes=[bt])
                fw.op("dve", lambda: nc.vector.tensor_tensor(out=rt[0:64, :], in0=r_s[:], in1=e1[:], op=ALU.mult), reads=[r_s, e1], writes=[rt])
                fw.op("dve", lambda: nc.vector.tensor_tensor(out=kh[:], in0=kp[:], in1=e3[:], op=ALU.mult), reads=[kp, e3], writes=[kh])
                fw.op("dve", lambda: nc.vector.tensor_tensor(out=bh[:], in0=b_[:], in1=e3[:], op=ALU.mult), reads=[b_, e3], writes=[bh])
                Mh = M[h]
                yield
                for tl in range(4):
                    cs = slice(tl * 128, (tl + 1) * 128)
                    fw.op("pe", lambda: nc.tensor.matmul(E0[:, 0:128], lhsT=bt[:, cs], rhs=kapt[:, cs], start=True, stop=True), reads=[bt, kapt], writes=[E0])
                    fw.op("pe", lambda: nc.tensor.matmul(E0[:, 128:256], lhsT=bt[:, cs], rhs=rt[0:64, cs], start=True, stop=True), reads=[bt, rt], writes=[E0])
                    fw.op("pe", lambda: nc.tensor.matmul(E0[:, 256:384], lhsT=kt[:, cs], rhs=kapt[:, cs], start=True, stop=True), reads=[kt, kapt], writes=[E0])
                    fw.op("pe", lambda: nc.tensor.matmul(E0[:, 384:512], lhsT=kt[:, cs], rhs=rt[0:64, cs], start=True, stop=True), reads=[kt, rt], writes=[E0])
                    fw.op("pe", lambda: nc.tensor.matmul(E1[:, 0:128], lhsT=kapt[:, cs], rhs=bt[:, cs], start=True, stop=True), reads=[kapt, bt], writes=[E1])
                    fw.op("dve", lambda: nc.vector.tensor_tensor(out=SCa[:], in0=E0[:, 0:256], in1=C["rw_mask_su_iu"][:], op=ALU.mult), reads=[E0, C["rw_mask_su_iu"]], writes=[SCa])
                    fw.op("dve", lambda: nc.vector.tensor_tensor(out=SCb[:], in0=E0[:, 256:512], in1=C["rw_mask_su_iu"][:], op=ALU.mult), reads=[E0, C["rw_mask_su_iu"]], writes=[SCb])
                    fw.op("dve", lambda: nc.vector.tensor_tensor(out=Y[0][:], in0=E0[:, 0:128], in1=C["rw_mask_negsu"][:], op=ALU.mult), reads=[E0, C["rw_mask_negsu"]], writes=[Y[0]])
                    fw.op("dve", lambda: nc.vector.tensor_tensor(out=Z[0][:], in0=E1[:, 0:128], in1=C["rw_mask_negsl"][:], op=ALU.mult), reads=[E1, C["rw_mask_negsl"]], writes=[Z[0]])
                    yield
                    fw.op("dve", lambda: nc.vector.tensor_tensor(out=P[0][:], in0=Y[0][:], in1=ident[:], op=ALU.add), reads=[Y[0], ident], writes=[P[0]])
                    cur = 0
                    for lev in range(1, 6):
                        nxt = 1 - cur
                        fw.op("pe", lambda: nc.tensor.matmul(X[:, 0:128], lhsT=Y[cur][:], rhs=Z[cur][:], start=True, stop=True), reads=[Y[cur], Z[cur]], writes=[X])
                        if lev < 5:
                            fw.op("pe", lambda: nc.tensor.matmul(E1[:, 128:256], lhsT=Z[cur][:], rhs=Y[cur][:], start=True, stop=True), reads=[Y[cur], Z[cur]], writes=[E1])
                        fw.op("act", lambda: nc.scalar.copy(out=Z[nxt][:], in_=X[:, 0:128]), reads=[X], writes=[Z[nxt]])
                        if lev < 5:
                            fw.op("dve", lambda: nc.vector.tensor_copy(out=Y[nxt][:], in_=E1[:, 128:256]), reads=[E1], writes=[Y[nxt]])
                        fw.op("pe", lambda: nc.tensor.matmul(E1[:, 256:384], lhsT=Z[nxt][:], rhs=P[cur][:], start=True, stop=True), reads=[Z[nxt], P[cur]], writes=[E1])
                        fw.op("dve", lambda: nc.vector.tensor_tensor(out=P[nxt][:], in0=E1[:, 256:384], in1=P[cur][:], op=ALU.add), reads=[E1, P[cur]], writes=[P[nxt]])
                        cur = nxt
                        yield
                    TT = P[cur]
                    fw.op("pe", lambda: nc.tensor.matmul(X[:, 128:192], lhsT=kapt[:, cs], rhs=ident[0:64, 0:64], start=True, stop=True), reads=[kapt, ident], writes=[X])
                    fw.op("pe", lambda: nc.tensor.matmul(X[:, 192:256], lhsT=v_s[:, cs], rhs=ident[0:64, 0:64], start=True, stop=True), reads=[v_s, ident], writes=[X])
                    fw.op("pe", lambda: nc.tensor.matmul(X[:, 256:320], lhsT=kh[:, cs], rhs=ident[0:64, 0:64], start=True, stop=True), reads=[kh, ident], writes=[X])
                    fw.op("pe", lambda: nc.tensor.matmul(X[:, 320:384], lhsT=bh[:, cs], rhs=ident[0:64, 0:64], start=True, stop=True), reads=[bh, ident], writes=[X])
                    fw.op("act", lambda: nc.scalar.copy(out=ktok[:, 0:64], in_=X[:, 128:192]), reads=[X], writes=[ktok])
                    fw.op("act", lambda: nc.scalar.copy(out=vtok[:], in_=X[:, 192:256]), reads=[X], writes=[vtok])
                    fw.op("act", lambda: nc.scalar.copy(out=bhtok[:], in_=X[:, 320:384]), reads=[X], writes=[bhtok])
                    for c in range(2):
                        fw.op("act", lambda: nc.scalar.activation(out=khc[c][:], in_=X[:, 256:320], func=AF.Identity, scale=RM[:, c:c + 1]), reads=[X, RM], writes=[khc[c]])
                        fw.op("act", lambda: nc.scalar.activation(out=bhc[c][:], in_=X[:, 320:384], func=AF.Identity, scale=RM[:, c:c + 1]), reads=[X, RM], writes=[bhc[c]])
                    fw.op("pe", lambda: nc.tensor.matmul(E1[:, 384:448], lhsT=SCb[:, 0:128], rhs=vtok[:], start=True, stop=True), reads=[SCb, vtok], writes=[E1])
                    fw.op("dve", lambda: nc.vector.tensor_copy(out=ktok[:, 64:128], in_=E1[:, 384:448]), reads=[E1], writes=[ktok])
                    fw.op("pe", lambda: nc.tensor.matmul(E2[:, 0:128], lhsT=TT[:], rhs=ktok[:], start=True, stop=True), reads=[TT, ktok], writes=[E2])
                    fw.op("dve", lambda: nc.vector.tensor_copy(out=WU[:], in_=E2[:, 0:128]), reads=[E2], writes=[WU])
                    fw.op("dve", lambda: nc.vector.tensor_scalar(out=nWU[:], in0=E2[:, 0:128], scalar1=-1.0, scalar2=None, op0=ALU.mult), reads=[E2], writes=[nWU])
                    for c in range(2):
                        fw.op("dve", lambda: nc.vector.tensor_scalar(out=nWc[c][:], in0=E2[:, 0:64], scalar1=RM[:, 2 + c:3 + c], scalar2=None, op0=ALU.mult), reads=[E2, RM], writes=[nWc[c]])
                    yield
                    fw.op("pe", lambda: nc.tensor.matmul(X[0:64, 384:512], lhsT=ident[:, 0:64], rhs=rt[:, cs], start=True, stop=False), reads=[ident, rt], writes=[X])
                    fw.op("pe", lambda: nc.tensor.matmul(X[0:64, 384:512], lhsT=nWU[:, 0:64], rhs=SCa[:, 128:256], start=False, stop=True), reads=[nWU, SCa], writes=[X])
                    fw.op("act", lambda: nc.scalar.copy(out=Rp[0:64, :], in_=X[0:64, 384:512]), reads=[X], writes=[Rp])
                    yield
                    for c in range(2):
                        rs_ = slice(c * 64, (c + 1) * 64)
                        gi = tl * 2 + c
                        fw.op("dve", lambda: nc.vector.tensor_scalar(out=dg[0:64, :], in0=ident[0:64, 0:64], scalar1=gam[:, gi:gi + 1], scalar2=None, op0=ALU.mult), reads=[ident, gam], writes=[dg])
                        fw.op("pe", lambda: nc.tensor.matmul(E2[0:64, 128 + c * 128:128 + c * 128 + 64], lhsT=ident[:, 0:64], rhs=dg[:], start=True, stop=False), reads=[ident, dg], writes=[E2])
                        fw.op("pe", lambda: nc.tensor.matmul(E2[0:64, 128 + c * 128:128 + c * 128 + 64], lhsT=nWc[c][:], rhs=bhtok[:], start=False, stop=True), reads=[nWc[c], bhtok], writes=[E2])
                        fw.op("pe", lambda: nc.tensor.matmul(E2[0:64, 128 + c * 128 + 64:128 + c * 128 + 128], lhsT=khc[c][:], rhs=vtok[:], start=True, stop=False), reads=[khc[c], vtok], writes=[E2])
                        fw.op("pe", lambda: nc.tensor.matmul(E2[0:64, 128 + c * 128 + 64:128 + c * 128 + 128], lhsT=bhc[c][:], rhs=nWU[:, 64:128], start=False, stop=True), reads=[bhc[c], nWU], writes=[E2])
                        fw.op("dve", lambda: nc.vector.tensor_copy(out=PTm[c][:], in_=E2[0:64, 128 + c * 128:128 + c * 128 + 64]), reads=[E2], writes=[PTm[c]])
                        fw.op("dve", lambda: nc.vector.tensor_copy(out=Qm[c][:], in_=E2[0:64, 128 + c * 128 + 64:128 + c * 128 + 128]), reads=[E2], writes=[Qm[c]])
                    yield
                    for c in range(2):
                        ysl = slice(128 + c * 64, 128 + (c + 1) * 64)
                        fw.op("pe", lambda: nc.tensor.matmul(E2[0:64, 384 + c * 64:384 + (c + 1) * 64], lhsT=vtok[:], rhs=SCb[:, ysl], start=True, stop=False), reads=[vtok, SCb], writes=[E2])
                        fw.op("pe", lambda: nc.tensor.matmul(E2[0:64, 384 + c * 64:384 + (c + 1) * 64], lhsT=nWU[:, 64:128], rhs=SCa[:, ysl], start=False, stop=False), reads=[nWU, SCa], writes=[E2])
                        fw.op("pe", lambda: nc.tensor.matmul(E2[0:64, 384 + c * 64:384 + (c + 1) * 64], lhsT=Mh[:], rhs=Rp[:, c * 64:(c + 1) * 64], start=False, stop=True),
                              reads=[Mh, Rp], writes=[E2])
                        fw.op("pe", lambda: nc.tensor.matmul(E1[0:64, 448:512], lhsT=PTm[c][:], rhs=Mh[0:64, :], start=True, stop=True), reads=[PTm[c], Mh], writes=[E1])
                        fw.op("dve", lambda: nc.vector.tensor_tensor(out=Mh[0:64, :], in0=E1[0:64, 448:512], in1=Qm[c][:], op=ALU.add), reads=[E1, Qm[c]], writes=[Mh])
                    fw.op("dve", lambda: nc.vector.tensor_copy(out=yo[:, cs], in_=E2[0:64, 384:512]), reads=[E2], writes=[yo])
                yield
                fw.op("pe", lambda: nc.tensor.matmul(E0[0:64, :], lhsT=C["ones64"][:], rhs=yo[:], start=True, stop=True), reads=[C["ones64"], yo], writes=[E0])
                fw.op("dve", lambda: nc.vector.scalar_tensor_tensor(out=t1[:], in0=E0[0:64, :], scalar=-1.0 / 64, in1=yo[:], op0=ALU.mult, op1=ALU.add), reads=[E0, yo], writes=[t1])
                fw.op("act", lambda: nc.scalar.activation(out=t2[:], in_=t1[:], func=AF.Square), reads=[t1], writes=[t2])
                fw.op("pe", lambda: nc.tensor.matmul(E1[0:64, :], lhsT=C["ones64"][:], rhs=t2[:], start=True, stop=True), reads=[C["ones64"], t2], writes=[E1])
                fw.op("dve", lambda: nc.vector.tensor_scalar(out=t2[:], in0=E1[0:64, :], scalar1=1.0 / 64, scalar2=64e-5, op0=ALU.mult, op1=ALU.add), reads=[E1], writes=[t2])
                fw.op("act", lambda: nc.scalar.activation(out=t2[:], in_=t2[:], func=AF.Sqrt), reads=[t2], writes=[t2])
                fw.op("dve", lambda: nc.vector.reciprocal(out=t2[:], in_=t2[:]), reads=[t2], writes=[t2])
                fw.op("dve", lambda: nc.vector.scalar_tensor_tensor(out=t1[:], in0=t1[:], scalar=lnw[:, h:h + 1], in1=t2[:], op0=ALU.mult, op1=ALU.mult), reads=[t1, t2, lnw], writes=[t1])
                fw.op("dve", lambda: nc.vector.scalar_tensor_tensor(out=t2[:], in0=r_s[:], scalar=r_k[:, h:h + 1], in1=kp[:], op0=ALU.mult, op1=ALU.mult), reads=[r_s, kp, r_k], writes=[t2])
                fw.op("pe", lambda: nc.tensor.matmul(E2[0:64, :], lhsT=C["ones64"][:], rhs=t2[:], start=True, stop=True), reads=[C["ones64"], t2], writes=[E2])
                fw.op("dve", lambda: nc.vector.tensor_tensor(out=t2[:], in0=E2[0:64, :], in1=v_s[:], op=ALU.mult), reads=[E2, v_s], writes=[t2])
                fw.op("dve", lambda: nc.vector.scalar_tensor_tensor(out=t1[:], in0=t1[:], scalar=lnb[:, h:h + 1], in1=t2[:], op0=ALU.add, op1=ALU.add), reads=[t1, t2, lnb], writes=[t1])
                yb = ybuf
                fw.op("dve", lambda: nc.vector.tensor_tensor(out=yb[:], in0=t1[:], in1=g_[:], op=ALU.mult), reads=[t1, g_], writes=[yb])
                fw.dma("sp", YT.t.ap()[2, h * 64:(h + 1) * 64, tb * TB:(tb + 1) * TB], yb[:], reads=[yb], writes=[k.YTtok[2][h // 2][tb]])


            for hp in range(0, NHX, 2):
                gens = [unit(0, hp)] + ([unit(1, hp + 1)] if hp + 1 < NHX else [])
                live = list(gens)
                while live:
                    for g_i in list(live):
                        try:
                            next(g_i)
                        except StopIteration:
                            live.remove(g_i)

import numpy as np, os

DFF = 2816
NFF = DFF // 128


def setup_p5(nc, fw, k):
    def din(name, shape, dt=F32):
        return fw.dram(name, shape, dt, kind="ExternalInput")
    k.w_branch = din("w_branch", [4, 3, 512, D])
    k.w_out = din("w_out", [4, D, D])
    k.ffn_w1 = din("ffn_w1", [4, D, DFF]); k.ffn_w3 = din("ffn_w3", [4, D, DFF]); k.ffn_w2 = din("ffn_w2", [4, DFF, D])
    k.final_w = din("final_norm_w", [D])


def norm_mod_tok(nc, fw, k, XT, xtok, g, s, hT, out_dram=None, out_tok=None):
    for tb in range(NTB):
        xs = k.xs_bufs[tb % 2]
        fw.dma("sp", xs[:], XT.t.ap()[:, tb * TB:(tb + 1) * TB].rearrange("(k p) t -> p k t", p=128), reads=[xtok[tb]], writes=[xs])
        sq = k.sq_buf
        fw.op("act", lambda: nc.scalar.activation(out=sq[:], in_=xs[:], func=AF.Square), reads=[xs], writes=[sq])
        pst = k.ps[7]
        for kc in range(8):
            fw.op("pe", lambda: nc.tensor.matmul(pst[:], lhsT=k.ones[:], rhs=sq[:, kc, :], start=(kc == 0), stop=(kc == 7)), reads=[k.ones, sq], writes=[pst])
        rstd = k.rstd_buf
        fw.op("dve", lambda: nc.vector.tensor_scalar(out=rstd[:], in0=pst[:], scalar1=1.0 / D, scalar2=1e-6, op0=ALU.mult, op1=ALU.add), reads=[pst], writes=[rstd])
        fw.op("act", lambda: nc.scalar.activation(out=rstd[:], in_=rstd[:], func=AF.Sqrt), reads=[rstd], writes=[rstd])
        fw.op("dve", lambda: nc.vector.reciprocal(out=rstd[:], in_=rstd[:]), reads=[rstd], writes=[rstd])
        for kc in range(8):
            if out_dram is None:
                tmp = k.tmp_bufs[kc % 2]
                fw.op("dve", lambda: nc.vector.scalar_tensor_tensor(out=tmp[:], in0=xs[:, kc, :], scalar=g[:, kc:kc + 1], in1=rstd[:], op0=ALU.mult, op1=ALU.mult),
                      reads=[xs, rstd, k.gsrc], writes=[tmp])
                fw.op("act", lambda: nc.scalar.activation(out=hT[:, kc, tb * TB:(tb + 1) * TB], in_=tmp[:], func=AF.Identity, bias=s[:, kc:kc + 1], scale=1.0),
                      reads=[tmp, k.gsrc], writes=[hT])
            else:
                fw.op("dve", lambda: nc.vector.scalar_tensor_tensor(out=sq[:, kc, :], in0=xs[:, kc, :], scalar=g[:, kc:kc + 1], in1=rstd[:], op0=ALU.mult, op1=ALU.mult),
                      reads=[xs, rstd, k.gsrc], writes=[sq])
        if out_dram is not None:
            fw.dma("sp", out_dram.t.ap()[:, tb * TB:(tb + 1) * TB].rearrange("(k p) t -> p k t", p=128), sq[:], reads=[sq], writes=[out_tok[tb]])


def proj_out(nc, fw, k, l, PT, YT, XTin, xin_tok, XTout, xout_tok):
    with fw.scope():
        wbr = fw.sbuf("p5_wbr", [128, 12, D], BF16)
        wo = fw.sbuf("p5_wo", [128, 8, D], BF16)
        stg = [fw.sbuf(f"p5_stg{i}", [128, 4, D]) for i in range(2)]
        for br in range(3):
            st = stg[br % 2]
            fw.dma("sp", st[:], k.w_branch.t.ap()[l, br].rearrange("(k p) n -> p k n", p=128), writes=[st])
            e = "pool" if br % 2 == 0 else "dve"
            if e == "pool":
                fw.op("pool", lambda: nc.gpsimd.tensor_copy(out=wbr[:, br * 4:(br + 1) * 4, :], in_=st[:]), reads=[st], writes=[wbr])
            else:
                fw.op("dve", lambda: nc.vector.tensor_copy(out=wbr[:, br * 4:(br + 1) * 4, :], in_=st[:]), reads=[st], writes=[wbr])
        for hf in range(2):
            st = stg[(hf + 1) % 2]
            fw.dma("sp", st[:], k.w_out.t.ap()[l, hf * 512:(hf + 1) * 512, :].rearrange("(k p) n -> p k n", p=128), writes=[st])
            fw.op("pool", lambda: nc.gpsimd.tensor_copy(out=wo[:, hf * 4:(hf + 1) * 4, :], in_=st[:]), reads=[st], writes=[wo])
        yb = [[fw.sbuf(f"p5_y{i}_{m}", [128, 4, TB], BF16) for m in range(3)] for i in range(2)]
        gt = [fw.sbuf(f"p5_g{i}", [128, TB]) for i in range(3)]
        mg = fw.sbuf("p5_mg", [128, 8, TB], BF16)
        acc = fw.sbuf("p5_acc", [128, TB]); tt = [fw.sbuf(f"p5_tt{i}", [128, TB]) for i in range(2)]
        xs = [fw.sbuf(f"p5_xs{i}", [128, 8, TB]) for i in range(2)]
        gt1 = k.ada[:, l, 16:24]
        for tb in range(NTB):
            y = yb[tb % 2]
            for m in range(3):
                fw.dma("sp", y[m][:], YT.t.ap()[m, :, tb * TB:(tb + 1) * TB].rearrange("(k p) t -> p k t", p=128),
                       reads=[k.YTtok[m][c][tb] for c in range(4)], writes=[y[m]])
            x_ = xs[tb % 2]
            fw.dma("sp", x_[:], XTin.t.ap()[:, tb * TB:(tb + 1) * TB].rearrange("(k p) t -> p k t", p=128), reads=[xin_tok[tb]], writes=[x_])
            for n in range(8):
                for br in range(3):
                    ch = 41 + br * 8 + n
                    fw.dma("sp", gt[br][:], PT.t.ap()[ch * 128:(ch + 1) * 128, tb * TB:(tb + 1) * TB], reads=[k.PTtok[ch][tb]], writes=[gt[br]])
                    fw.op("act", lambda: nc.scalar.activation(out=gt[br][:], in_=gt[br][:], func=AF.Sigmoid), reads=[gt[br]], writes=[gt[br]])
                    pst = k.ps[3 + br]
                    for kc in range(4):
                        fw.op("pe", lambda: nc.tensor.matmul(pst[:], lhsT=wbr[:, br * 4 + kc, n * 128:(n + 1) * 128], rhs=y[br][:, kc, :], start=(kc == 0), stop=(kc == 3)),
                              reads=[wbr, y[br]], writes=[pst])
                fw.op("dve", lambda: nc.vector.tensor_tensor(out=acc[:], in0=k.ps[3][:], in1=gt[0][:], op=ALU.mult), reads=[k.ps[3], gt[0]], writes=[acc])
                fw.op("dve", lambda: nc.vector.tensor_tensor(out=tt[0][:], in0=k.ps[4][:], in1=gt[1][:], op=ALU.mult), reads=[k.ps[4], gt[1]], writes=[tt[0]])
                fw.op("dve", lambda: nc.vector.tensor_tensor(out=tt[1][:], in0=k.ps[5][:], in1=gt[2][:], op=ALU.mult), reads=[k.ps[5], gt[2]], writes=[tt[1]])
                fw.op("pool", lambda: nc.gpsimd.tensor_tensor(out=acc[:], in0=acc[:], in1=tt[0][:], op=ALU.add), reads=[acc, tt[0]], writes=[acc])
                fw.op("pool", lambda: nc.gpsimd.tensor_tensor(out=mg[:, n, :], in0=acc[:], in1=tt[1][:], op=ALU.add), reads=[acc, tt[1]], writes=[mg])
            for n in range(8):
                pst = k.ps[6 + n % 2]
                for kc in range(8):
                    fw.op("pe", lambda: nc.tensor.matmul(pst[:], lhsT=wo[:, kc, n * 128:(n + 1) * 128], rhs=mg[:, kc, :], start=(kc == 0), stop=(kc == 7)), reads=[wo, mg], writes=[pst])
                fw.op("dve", lambda: nc.vector.scalar_tensor_tensor(out=x_[:, n, :], in0=pst[:], scalar=gt1[:, n:n + 1], in1=x_[:, n, :], op0=ALU.mult, op1=ALU.add),
                      reads=[pst, x_, k.gsrc], writes=[x_])
            fw.dma("sp", XTout.t.ap()[:, tb * TB:(tb + 1) * TB].rearrange("(k p) t -> p k t", p=128), x_[:], reads=[x_], writes=[xout_tok[tb]])


def ffn(nc, fw, k, l, XT1, x1_tok, XT2, x2_tok, UT):
    with fw.scope():
        alloc_p1_small(fw, k)
        hT = fw.sbuf("f_hT", [128, 8, T], BF16)
        norm_mod_tok(nc, fw, k, XT1, x1_tok, k.g2[:, l, :], k.ada[:, l, 24:32], hT)
        stg = [fw.sbuf(f"f_stg{i}", [128, 8, 256]) for i in range(2)]
        w1b = [fw.sbuf(f"f_w1b{i}", [128, 8, 256], BF16) for i in range(2)]
        w3b = [fw.sbuf(f"f_w3b{i}", [128, 8, 256], BF16) for i in range(2)]
        sl = [fw.sbuf(f"f_sl{i}", [128, TB]) for i in range(2)]
        ub = [fw.sbuf(f"f_ub{i}", [128, TB], BF16) for i in range(2)]
        ev = 0
        for fg in range(NFF // 2):
            wa, wc = w1b[fg % 2], w3b[fg % 2]
            fw.dma("sp", stg[0][:], k.ffn_w1.t.ap()[l, :, fg * 256:(fg + 1) * 256].rearrange("(k p) n -> p k n", p=128), writes=[stg[0]])
            fw.op("pool", lambda: nc.gpsimd.tensor_copy(out=wa[:], in_=stg[0][:]), reads=[stg[0]], writes=[wa])
            fw.dma("sp", stg[1][:], k.ffn_w3.t.ap()[l, :, fg * 256:(fg + 1) * 256].rearrange("(k p) n -> p k n", p=128), writes=[stg[1]])
            fw.op("pool", lambda: nc.gpsimd.tensor_copy(out=wc[:], in_=stg[1][:]), reads=[stg[1]], writes=[wc])
            for tb in range(NTB):
                for c in range(2):
                    pa = k.ps[ev % 2]
                    pd = k.ps[3 + ev % 2]
                    for kc in range(8):
                        fw.op("pe", lambda: nc.tensor.matmul(pa[:], lhsT=wa[:, kc, c * 128:(c + 1) * 128], rhs=hT[:, kc, tb * TB:(tb + 1) * TB], start=(kc == 0), stop=(kc == 7)),
                              reads=[wa, hT], writes=[pa])
                    for kc in range(8):
                        fw.op("pe", lambda: nc.tensor.matmul(pd[:], lhsT=wc[:, kc, c * 128:(c + 1) * 128], rhs=hT[:, kc, tb * TB:(tb + 1) * TB], start=(kc == 0), stop=(kc == 7)),
                              reads=[wc, hT], writes=[pd])
                    s_ = sl[ev % 2]; u_ = ub[ev % 2]
                    fw.op("act", lambda: nc.scalar.activation(out=s_[:], in_=pa[:], func=AF.Silu), reads=[pa], writes=[s_])
                    fw.op("dve", lambda: nc.vector.tensor_tensor(out=u_[:], in0=pd[:], in1=s_[:], op=ALU.mult), reads=[pd, s_], writes=[u_])
                    ch = fg * 2 + c
                    fw.dma("sp", UT.t.ap()[ch * 128:(ch + 1) * 128, tb * TB:(tb + 1) * TB], u_[:], reads=[u_], writes=[k.UTtok[ch][tb]])
                    ev += 1
    with fw.scope():
        w2b = fw.sbuf("f_w2b", [128, NFF, D], BF16)
        stg = [fw.sbuf(f"f_stgb{i}", [128, 2, D]) for i in range(2)]
        for g in range(NFF // 2):
            st = stg[g % 2]
            fw.dma("sp", st[:], k.ffn_w2.t.ap()[l, g * 256:(g + 1) * 256, :].rearrange("(k p) n -> p k n", p=128), writes=[st])
            if g % 2 == 0:
                fw.op("pool", lambda: nc.gpsimd.tensor_copy(out=w2b[:, g * 2:(g + 1) * 2, :], in_=st[:]), reads=[st], writes=[w2b])
            else:
                fw.op("dve", lambda: nc.vector.tensor_copy(out=w2b[:, g * 2:(g + 1) * 2, :], in_=st[:]), reads=[st], writes=[w2b])
        ub = [fw.sbuf(f"f_ublk{i}", [128, NFF, TB], BF16) for i in range(2)]
        xs = [fw.sbuf(f"f_xs{i}", [128, 8, TB]) for i in range(2)]
        gt2 = k.ada[:, l, 40:48]
        for tb in range(NTB):
            u_ = ub[tb % 2]; x_ = xs[tb % 2]
            fw.dma("sp", u_[:], UT.t.ap()[:, tb * TB:(tb + 1) * TB].rearrange("(k p) t -> p k t", p=128), reads=[k.UTtok[c][tb] for c in range(NFF)], writes=[u_])
            fw.dma("sp", x_[:], XT1.t.ap()[:, tb * TB:(tb + 1) * TB].rearrange("(k p) t -> p k t", p=128), reads=[x1_tok[tb]], writes=[x_])
            for n in range(8):
                pst = k.ps[3 + n % 2]
                for kc in range(NFF):
                    fw.op("pe", lambda: nc.tensor.matmul(pst[:], lhsT=w2b[:, kc, n * 128:(n + 1) * 128], rhs=u_[:, kc, :], start=(kc == 0), stop=(kc == NFF - 1)),
                          reads=[w2b, u_], writes=[pst])
                fw.op("dve", lambda: nc.vector.scalar_tensor_tensor(out=x_[:, n, :], in0=pst[:], scalar=gt2[:, n:n + 1], in1=x_[:, n, :], op0=ALU.mult, op1=ALU.add),
                      reads=[pst, x_, k.gsrc], writes=[x_])
            fw.dma("sp", XT2.t.ap()[:, tb * TB:(tb + 1) * TB].rearrange("(k p) t -> p k t", p=128), x_[:], reads=[x_], writes=[x2_tok[tb]])


def alloc_p1_small(fw, k):
    _x = fw.sbuf("xs0", [128, 8, TB])
    k.xs_bufs = [_x, _x]
    k.sq_buf = fw.sbuf("sq", [128, 8, TB])
    k.rstd_buf = fw.sbuf("rstd", [128, TB])
    k.tmp_bufs = [fw.sbuf(f"tmp{i}", [128, TB]) for i in range(2)]


def in_proj_phase(nc, fw, k, l, XTin, xin_tok, PT):
    with fw.scope():
        alloc_p1_small(fw, k)
        hT = fw.sbuf("hT", [128, 8, T], BF16)
        _w = fw.sbuf("wst0", [128, 8, 640])
        k.w_st = [_w, _w]
        k.w_bf = [fw.sbuf(f"wbf{i}", [128, 8, 640], BF16) for i in range(2)]
        k.ev_bufs = [fw.sbuf(f"ev{i}", [128, TB]) for i in range(4)]
        norm_mod_tok(nc, fw, k, XTin, xin_tok, k.g1[:, l, :], k.ada[:, l, 0:8], hT)
        in_proj(nc, fw, k, l, hT, PT)


def final_norm(nc, fw, k, XT, xtok, OUT, otok):
    with fw.scope():
        alloc_p1_small(fw, k)
        fwt = fw.sbuf("fin_w", [128, 8])
        fw.dma("sp", fwt[:], k.final_w.t.ap().rearrange("(j p) -> p j", p=128), writes=[fwt], allow_slow_non_contiguous=True)
        old = k.gsrc
        k.gsrc = fwt
        norm_mod_tok(nc, fw, k, XT, xtok, fwt[:, :], None, None, out_dram=OUT, out_tok=otok)
        k.gsrc = old

import numpy as np, os

BIG = 1.0e30
NEGM = -240000.0
NQ = T // 128


def t5_bucket_np(dist):
    n = np.maximum(dist, 0)
    nf = np.maximum(n, 1).astype(np.float32)
    large = 16 + (np.log(nf / np.float32(16)) / np.float32(np.log(128 / 16)) * np.float32(16)).astype(np.int32)
    large = np.minimum(large, 31)
    return np.where(n < 16, n, large)


def nsa_consts(rel_bias):
    c = {}
    kq = np.arange(128)
    dD = kq[None, :] - kq[:, None]
    c["nsa_tabD"] = np.ascontiguousarray(np.transpose(rel_bias[t5_bucket_np(dD)], (2, 0, 1))).astype(np.float32)
    c["nsa_maskD"] = (dD >= 0).astype(np.float32)
    c["nsa_tabP"] = np.ascontiguousarray(np.transpose(rel_bias[t5_bucket_np(dD + 128)], (2, 0, 1))).astype(np.float32)
    c["nsa_maskW4"] = (dD < 0).astype(np.float32)
    m = np.arange(504)
    dC = kq[None, :] - 16 * (m[:, None] - 248) - 31
    c["nsa_tabC"] = np.ascontiguousarray(np.transpose(rel_bias[t5_bucket_np(dC)], (2, 0, 1))).astype(np.float32)
    c["nsa_maskC"] = (dC >= 0).astype(np.float32)
    c["nsa_b31"] = np.ascontiguousarray(rel_bias[31:32, :]).astype(np.float32)
    u = np.arange(126) - 62
    cur = (kq >= 64).astype(np.int64)
    A = np.zeros((128, 126), np.float32)
    A[(u[None, :] == cur[:, None]) | (u[None, :] == cur[:, None] - 1)] = BIG
    A[u[None, :] > cur[:, None]] = -BIG
    c["nsa_A"] = A
    n = np.arange(256)
    mm = np.arange(64)
    cov = ((16 * n[:, None] < 64 * mm[None, :] + 64) & (16 * n[:, None] + 32 > 64 * mm[None, :]) & (n[:, None] < 255)).astype(np.float32)
    c["nsa_cover"] = cov
    keys = np.arange(T)
    c["nsa_xexp"] = (keys[None, :] // 64 == mm[:, None]).astype(np.float32)
    sel = np.zeros((24, 24 * 64), np.float32)
    for r in range(24):
        sel[r, r * 64:(r + 1) * 64] = 1.0
    c["nsa_selall"] = sel
    return c


def setup_nsa(nc, fw, k):
    def din(name, shape, dt=F32):
        return fw.dram(name, shape, dt, kind="ExternalInput")
    k.nsa_in = {}
    for nm, shp in (("nsa_pe_k", [4, 32, 64]), ("nsa_cmp_w1_k", [4, 2048, 256]), ("nsa_cmp_w2_k", [4, 256, 64]),
                    ("nsa_pe_v", [4, 32, 64]), ("nsa_cmp_w1_v", [4, 2048, 256]), ("nsa_cmp_w2_v", [4, 256, 64])):
        k.nsa_in[nm] = din(nm, shp)
    tabD = din("nsa_tabD", [8, 128, 128]); maskD = din("nsa_maskD", [128, 128]); tabP = din("nsa_tabP", [8, 128, 128]); maskW4 = din("nsa_maskW4", [128, 128])
    tabC = din("nsa_tabC", [8, 504, 128]); maskC = din("nsa_maskC", [504, 128]); b31 = din("nsa_b31", [1, 8])
    A = din("nsa_A", [128, 126]); cover = din("nsa_cover", [256, 64]); xexp = din("nsa_xexp", [64, T]); selall = din("nsa_selall", [24, 24 * 64])
    k.GcT = fw.dram("nsa_GcT", [8, 504, 128], F32)
    k.GcTtok = Buf(None, "gct")
    n = k.nsa = {}
    n["Ed"] = fw.sbuf("n_Ed", [128, 8, 128]); n["Ep"] = fw.sbuf("n_Ep", [128, 8, 128]); n["W4"] = fw.sbuf("n_W4", [128, 128])
    n["A"] = fw.sbuf("n_A", [128, 126]); n["cover"] = fw.sbuf("n_cover", [128, 2, 64]); n["xexp"] = fw.sbuf("n_xexp", [64, T], BF16); n["selall"] = fw.sbuf("n_selall", [24, 24 * 64])
    n["nb31"] = fw.sbuf("n_nb31", [128, 8])
    fw.dma("sp", n["A"][:], A.t.ap()[:, :], writes=[n["A"]])
    fw.dma("sp", n["cover"][:], cover.t.ap().rearrange("(a p) m -> p a m", p=128), writes=[n["cover"]])
    fw.dma("sp", n["selall"][:], selall.t.ap()[:, :], writes=[n["selall"]])
    fw.dma("sp", n["W4"][:], maskW4.t.ap()[:, :], writes=[n["W4"]])
    fw.dma("sp", n["nb31"][:], b31.t.ap()[0:1, :].partition_broadcast(128), writes=[n["nb31"]])
    fw.op("dve", lambda: nc.vector.tensor_scalar(out=n["nb31"][:], in0=n["nb31"][:], scalar1=-1.0, scalar2=None, op0=ALU.mult), reads=[n["nb31"]], writes=[n["nb31"]])
    with fw.scope():
        st = fw.sbuf("n_st", [64, T])
        fw.dma("sp", st[:], xexp.t.ap()[:, :], writes=[st])
        fw.op("dve", lambda: nc.vector.tensor_copy(out=n["xexp"][:], in_=st[:]), reads=[st], writes=[n["xexp"]])
        mD = fw.sbuf("n_mD", [128, 128]); fw.dma("sp", mD[:], maskD.t.ap()[:, :], writes=[mD])
        raw = fw.sbuf("n_raw", [128, 8, 128])
        for (src, dst, msk) in ((tabD, n["Ed"], mD), (tabP, n["Ep"], None)):
            fw.dma("sp", raw[:], src.t.ap().rearrange("h k q -> k h q"), writes=[raw])
            for h in range(8):
                fw.op("act", lambda: nc.scalar.activation(out=dst[:, h, :], in_=raw[:, h, :], func=AF.Exp, bias=n["nb31"][:, h:h + 1], scale=1.0), reads=[raw, n["nb31"]], writes=[dst])
                if msk is not None:
                    fw.op("dve", lambda: nc.vector.tensor_tensor(out=dst[:, h, :], in0=dst[:, h, :], in1=msk[:], op=ALU.mult), reads=[dst, msk], writes=[dst])
        rawc = fw.sbuf("n_rawc", [126, 4, 128]); mC = fw.sbuf("n_mC", [126, 4, 128])
        fw.dma("sp", mC[:], maskC.t.ap().rearrange("(a p) q -> p a q", p=126), writes=[mC])
        for h in range(8):
            fw.dma("sp", rawc[:], tabC.t.ap()[h].rearrange("(a p) q -> p a q", p=126), writes=[rawc])
            fw.op("act", lambda: nc.scalar.activation(out=rawc[:], in_=rawc[:], func=AF.Exp, bias=n["nb31"][0:126, h:h + 1], scale=1.0), reads=[rawc, n["nb31"]], writes=[rawc])
            fw.op("dve", lambda: nc.vector.tensor_tensor(out=rawc[:], in0=rawc[:], in1=mC[:], op=ALU.mult), reads=[rawc, mC], writes=[rawc])
            fw.dma("sp", k.GcT.t.ap()[h].rearrange("(a p) q -> p a q", p=126), rawc[:], reads=[rawc], writes=[k.GcTtok])


def nsa(nc, fw, k, l, PT, YT, GX=2, IQ=None):
    n = k.nsa
    ps = k.ps
    ident = k.ident
    IQ = list(range(NQ)) if IQ is None else IQ
    with fw.scope():
        bf = lambda nm, shp: fw.sbuf(nm, shp, BF16)
        stg = [fw.sbuf(f"n_stg{i}", [64, TB]) for i in range(2)]
        kcmp = bf("n_kcmp", [64, T]); vcmp = bf("n_vcmp", [64, T]); ksel = bf("n_ksel", [64, T]); kwin = bf("n_kwin", [64, T])
        vseltok = bf("n_vseltok", [128, NQ, 64]); vwintok = bf("n_vwintok", [128, NQ, 64])
        qb = bf("n_qb", [64, 4, T])
        kcT = bf("n_kcT", [64, 256]); vctok = bf("n_vctok", [128, 2, 64])
        w1s = fw.sbuf("n_w1s", [64, 16, 256]); w1b = bf("n_w1b", [64, 32, 256]); w2s = fw.sbuf("n_w2s", [128, 2, 64]); w2b = bf("n_w2b", [128, 2, 64])
        peT = fw.sbuf("n_peT", [64, 32]); peTb = bf("n_peTb", [64, 32]); biasc = fw.sbuf("n_biasc", [128, 2])
        aT = bf("n_aT", [128, 2, 256])
        sgT = fw.sbuf("n_sgT", [24, T])
        onesb = k.onesb
        Ef = [fw.sbuf(f"n_Ef{i}", [128, 512]) for i in range(2)]; Pb = [bf(f"n_Pb{i}", [128, 512]) for i in range(2)]
        Gt = [fw.sbuf(f"n_Gt{i}", [128, 4, 128]) for i in range(2)]
        Pc = [fw.sbuf(f"n_Pc{i}", [128, 512]) for i in range(2)]; Pcb = [bf(f"n_Pcb{i}", [128, 512]) for i in range(2)]
        rd = fw.sbuf("n_rd", [128, 512])
        sc = fw.sbuf("n_sc", [128, 64]); sc2 = fw.sbuf("n_sc2", [128, 64]); m8 = fw.sbuf("n_m8", [128, 16]); negq = fw.sbuf("n_negq", [128, 64])
        negT4 = bf("n_negT4", [64, 4, 128])
        acc = fw.sbuf("n_acc", [64, 512]); wt = fw.sbuf("n_wt", [64, 512]); ot = fw.sbuf("n_ot", [64, 512]); yb = [bf(f"n_yb{i}", [64, 512]) for i in range(2)]
        fw.op("dve", lambda: nc.vector.memset(kcT[:], 0.0), writes=[kcT])
        for tb in range(NTB):
            fw.dma("sp", sgT[:, tb * TB:(tb + 1) * TB], PT.t.ap()[26 * 128:26 * 128 + 24, tb * TB:(tb + 1) * TB], reads=[k.PTtok[26][tb]], writes=[sgT])
        fw.op("act", lambda: nc.scalar.activation(out=sgT[:], in_=sgT[:], func=AF.Sigmoid), reads=[sgT], writes=[sgT])
        evi = 0
        for g in range(GX):
            def load_stream(dst_ap_fn, ch, row0):
                for tb in range(NTB):
                    s_ = stg[tb % 2]
                    r0 = ch * 128 + row0
                    fw.dma("sp", s_[:], PT.t.ap()[r0:r0 + 64, tb * TB:(tb + 1) * TB], reads=[k.PTtok[ch][tb]], writes=[s_])
                    dst, dbuf = dst_ap_fn(tb)
                    if tb % 2 == 0:
                        fw.op("dve", lambda: nc.vector.tensor_copy(out=dst, in_=s_[:]), reads=[s_], writes=[dbuf])
                    else:
                        fw.op("pool", lambda: nc.gpsimd.tensor_copy(out=dst, in_=s_[:]), reads=[s_], writes=[dbuf])
            load_stream(lambda tb: (kcmp[:, tb * TB:(tb + 1) * TB], kcmp), 20, g * 64)
            load_stream(lambda tb: (vcmp[:, tb * TB:(tb + 1) * TB], vcmp), 21, g * 64)
            load_stream(lambda tb: (ksel[:, tb * TB:(tb + 1) * TB], ksel), 22, g * 64)
            load_stream(lambda tb: (kwin[:, tb * TB:(tb + 1) * TB], kwin), 24, g * 64)
            for j in range(4):
                hd = g * 4 + j
                load_stream(lambda tb: (qb[:, j, tb * TB:(tb + 1) * TB], qb), 16 + hd // 2, (hd % 2) * 64)
            for (ch, vt) in ((23, vseltok), (25, vwintok)):
                for tb in range(NTB):
                    s_ = stg[tb % 2]
                    r0 = ch * 128 + g * 64
                    fw.dma("sp", s_[:], PT.t.ap()[r0:r0 + 64, tb * TB:(tb + 1) * TB], reads=[k.PTtok[ch][tb]], writes=[s_])
                    for q4 in range(4):
                        fw.op("pe", lambda: nc.tensor.matmul(ps[5][:, q4 * 64:(q4 + 1) * 64], lhsT=s_[:, q4 * 128:(q4 + 1) * 128], rhs=ident[0:64, 0:64], start=True, stop=True),
                              reads=[s_, ident], writes=[ps[5]])
                    fw.op("dve", lambda: nc.vector.tensor_copy(out=vt[:, tb * 4:(tb + 1) * 4, :], in_=ps[5][:, 0:256].rearrange("p (a d) -> p a d", d=64)), reads=[ps[5]], writes=[vt])
            for (kv, src, w1n, w2n, pen) in (("k", kcmp, "nsa_cmp_w1_k", "nsa_cmp_w2_k", "nsa_pe_k"), ("v", vcmp, "nsa_cmp_w1_v", "nsa_cmp_w2_v", "nsa_pe_v")):
                for hf in range(2):
                    fw.dma("sp", w1s[:], k.nsa_in[w1n].t.ap()[l, hf * 1024:(hf + 1) * 1024, :].rearrange("(l d) c -> d l c", d=64), writes=[w1s])
                    fw.op("pool", lambda: nc.gpsimd.tensor_copy(out=w1b[:, hf * 16:(hf + 1) * 16, :], in_=w1s[:]), reads=[w1s], writes=[w1b])
                fw.dma("sp", w2s[:], k.nsa_in[w2n].t.ap()[l].rearrange("(a p) d -> p a d", p=128), writes=[w2s])
                fw.op("dve", lambda: nc.vector.tensor_copy(out=w2b[:], in_=w2s[:]), reads=[w2s], writes=[w2b])
                fw.dma("sp", peT[:], k.nsa_in[pen].t.ap()[l].rearrange("l d -> d l"), writes=[peT], allow_slow_non_contiguous=True)
                fw.op("dve", lambda: nc.vector.tensor_copy(out=peTb[:], in_=peT[:]), reads=[peT], writes=[peTb])
                src3 = src.t[:].rearrange("p (n s) -> p n s", s=16)
                for cc in range(2):
                    for li in range(32):
                        fw.op("pe", lambda: nc.tensor.matmul(ps[2][:, cc:cc + 1], lhsT=w1b[:, li, cc * 128:(cc + 1) * 128], rhs=peTb[:, li:li + 1], start=(li == 0), stop=(li == 31)),
                              reads=[w1b, peTb], writes=[ps[2]])
                fw.op("dve", lambda: nc.vector.tensor_copy(out=biasc[:], in_=ps[2][:, 0:2]), reads=[ps[2]], writes=[biasc])
                for cc in range(2):
                    for li in range(32):
                        rhs = src3[:, li // 16:li // 16 + 255, li % 16]
                        fw.op("pe", lambda: nc.tensor.matmul(ps[cc][:, 0:255], lhsT=w1b[:, li, cc * 128:(cc + 1) * 128], rhs=rhs, start=(li == 0), stop=(li == 31)),
                              reads=[w1b, src], writes=[ps[cc]])
                    fw.op("act", lambda: nc.scalar.activation(out=aT[:, cc, 0:255], in_=ps[cc][:, 0:255], func=AF.Silu, bias=biasc[:, cc:cc + 1], scale=1.0), reads=[ps[cc], biasc], writes=[aT])
                if kv == "k":
                    for cc in range(2):
                        fw.op("pe", lambda: nc.tensor.matmul(ps[3][0:64, 0:255], lhsT=w2b[:, cc, :], rhs=aT[:, cc, 0:255], start=(cc == 0), stop=(cc == 1)), reads=[w2b, aT], writes=[ps[3]])
                    fw.op("dve", lambda: nc.vector.tensor_copy(out=kcT[:, 0:255], in_=ps[3][0:64, 0:255]), reads=[ps[3]], writes=[kcT])
                else:
                    fw.op("dve", lambda: nc.vector.memset(vctok[:], 0.0), writes=[vctok])
                    for nt in range(2):
                        rows = 128 if nt == 0 else 127
                        for cc in range(2):
                            fw.op("pe", lambda: nc.tensor.matmul(ps[4][0:rows, nt * 64:(nt + 1) * 64], lhsT=aT[:, cc, nt * 128:nt * 128 + rows], rhs=w2b[:, cc, :], start=(cc == 0), stop=(cc == 1)),
                                  reads=[aT, w2b], writes=[ps[4]])
                        fw.op("dve", lambda: nc.vector.tensor_copy(out=vctok[0:rows, nt, :], in_=ps[4][0:rows, nt * 64:(nt + 1) * 64]), reads=[ps[4]], writes=[vctok])
            for i in IQ:
                Q = qb[:, :, i * 128:(i + 1) * 128]
                nts = [0] if i < 16 else [0, 1]
                for nt in nts:
                    sb = ps[evi % 2]; e_ = Pc[nt]; g_ = Gt[nt]
                    fw.op("pe", lambda: nc.tensor.matmul(sb[:], lhsT=kcT[:, nt * 128:(nt + 1) * 128], rhs=Q, start=True, stop=True), reads=[kcT, qb], writes=[sb])
                    r0 = 248 - 8 * i + nt * 128
                    fw.dma("sp", g_[:], k.GcT.t.ap()[g * 4:(g + 1) * 4, r0:r0 + 128, :].rearrange("h n q -> n h q"), reads=[k.GcTtok], writes=[g_])
                    fw.op("act", lambda: nc.scalar.activation(out=e_[:], in_=sb[:], func=AF.Exp, scale=0.125), reads=[sb], writes=[e_])
                    fw.op("dve", lambda: nc.vector.tensor_tensor(out=e_[:], in0=e_[:], in1=g_[:].rearrange("p h q -> p (h q)"), op=ALU.mult), reads=[e_, g_], writes=[e_])
                    fw.op("pool", lambda: nc.gpsimd.tensor_copy(out=Pcb[nt][:], in_=e_[:]), reads=[e_], writes=[Pcb[nt]])
                    evi += 1
                for x, nt in enumerate(nts):
                    fw.op("pe", lambda: nc.tensor.matmul(ps[3][:], lhsT=k.ones[:], rhs=Pc[nt][:], start=(x == 0), stop=(x == len(nts) - 1)), reads=[k.ones, Pc[nt]], writes=[ps[3]])
                for x, nt in enumerate(nts):
                    fw.op("pe", lambda: nc.tensor.matmul(ps[5][0:64, :], lhsT=vctok[:, nt, :], rhs=Pcb[nt][:], start=(x == 0), stop=(x == len(nts) - 1)), reads=[vctok, Pcb[nt]], writes=[ps[5]])
                fw.op("dve", lambda: nc.vector.tensor_scalar(out=rd[:], in0=ps[3][:], scalar1=1e-30, scalar2=None, op0=ALU.max), reads=[ps[3]], writes=[rd])
                fw.op("dve", lambda: nc.vector.reciprocal(out=rd[:], in_=rd[:]), reads=[rd], writes=[rd])
                for nt in nts:
                    fw.op("dve", lambda: nc.vector.tensor_tensor(out=Pc[nt][:], in0=Pc[nt][:], in1=rd[:], op=ALU.mult), reads=[Pc[nt], rd], writes=[Pc[nt]])
                tot = 4 * len(nts); x = 0
                for nt in nts:
                    for j in range(4):
                        fw.op("pe", lambda: nc.tensor.matmul(ps[4][:, 0:64], lhsT=Pc[nt][:, j * 128:(j + 1) * 128], rhs=n["cover"][:, nt, :], start=(x == 0), stop=(x == tot - 1)),
                              reads=[Pc[nt], n["cover"]], writes=[ps[4]])
                        x += 1
                a0 = 62 - 2 * i
                fw.op("dve", lambda: nc.vector.tensor_tensor(out=sc[:], in0=ps[4][:, 0:64], in1=n["A"][:, a0:a0 + 64], op=ALU.add), reads=[ps[4], n["A"]], writes=[sc])
                fw.op("dve", lambda: nc.vector.memset(sc[:, 0:1], BIG), writes=[sc])
                fw.op("dve", lambda: nc.vector.max(out=m8[:, 0:8], in_=sc[:]), reads=[sc], writes=[m8])
                fw.op("dve", lambda: nc.vector.tensor_scalar(out=sc2[:], in0=sc[:], scalar1=m8[:, 7:8], scalar2=-3.0 * BIG, op0=ALU.is_ge, op1=ALU.mult), reads=[sc, m8], writes=[sc2])
                fw.op("dve", lambda: nc.vector.tensor_tensor(out=sc2[:], in0=sc2[:], in1=sc[:], op=ALU.add), reads=[sc2, sc], writes=[sc2])
                fw.op("dve", lambda: nc.vector.max(out=m8[:, 8:16], in_=sc2[:]), reads=[sc2], writes=[m8])
                fw.op("dve", lambda: nc.vector.tensor_scalar(out=negq[:], in0=sc[:], scalar1=m8[:, 15:16], scalar2=None, op0=ALU.is_lt), reads=[sc, m8], writes=[negq])
                fw.op("pe", lambda: nc.tensor.matmul(ps[4][0:64, 128:256], lhsT=negq[:], rhs=ident[:], start=True, stop=True), reads=[negq, ident], writes=[ps[4]])
                for j in range(4):
                    fw.op("dve", lambda: nc.vector.tensor_scalar(out=negT4[:, j, :], in0=ps[4][0:64, 128:256], scalar1=NEGM, scalar2=None, op0=ALU.mult), reads=[ps[4]], writes=[negT4])
                def attend(kT, vtok, kts, tables, with_sel, p_num, p_den):
                    nonlocal evi
                    for x, kt in enumerate(kts):
                        sb = ps[evi % 2]; e_ = Ef[evi % 2]; p_ = Pb[evi % 2]
                        first, last = (x == 0), (x == len(kts) - 1)
                        fw.op("pe", lambda: nc.tensor.matmul(sb[:], lhsT=kT[:, kt * 128:(kt + 1) * 128], rhs=Q, start=True, stop=not with_sel), reads=[kT, qb], writes=[sb])
                        if with_sel:
                            fw.op("pe", lambda: nc.tensor.matmul(sb[:], lhsT=n["xexp"][:, kt * 128:(kt + 1) * 128], rhs=negT4[:], start=False, stop=True), reads=[n["xexp"], negT4], writes=[sb])
                        tab = tables.get(kt)
                        if tab is None:
                            fw.op("act", lambda: nc.scalar.activation(out=p_[:], in_=sb[:], func=AF.Exp, scale=0.125), reads=[sb], writes=[p_])
                        else:
                            fw.op("act", lambda: nc.scalar.activation(out=e_[:], in_=sb[:], func=AF.Exp, scale=0.125), reads=[sb], writes=[e_])
                            tb_, tap = tab
                            fw.op("dve", lambda: nc.vector.tensor_tensor(out=p_[:].rearrange("p (h q) -> p h q", h=4), in0=e_[:].rearrange("p (h q) -> p h q", h=4), in1=tap, op=ALU.mult),
                                  reads=[e_, tb_], writes=[p_])
                        fw.op("pe", lambda: nc.tensor.matmul(p_num[0:64, :], lhsT=vtok[:, kt, :], rhs=p_[:], start=first, stop=last), reads=[vtok, p_], writes=[p_num])
                        fw.op("pe", lambda: nc.tensor.matmul(p_den[0:64, :], lhsT=onesb[:, 0:64], rhs=p_[:], start=first, stop=last), reads=[onesb, p_], writes=[p_den])
                        evi += 1
                Ed_g = n["Ed"][:, g * 4:(g + 1) * 4, :]; Ep_g = n["Ep"][:, g * 4:(g + 1) * 4, :]
                W4b = n["W4"][:].unsqueeze(1).to_broadcast([128, 4, 128])
                tabs = {i: (n["Ed"], Ed_g)}
                if i >= 1:
                    tabs[i - 1] = (n["Ep"], Ep_g)
                def combine(br, p_num, p_den, first):
                    for j in range(4):
                        r = (g * 4 + j) * 3 + br
                        fw.op("pe", lambda: nc.tensor.matmul(ps[2][0:64, j * 128:(j + 1) * 128], lhsT=n["selall"][:, r * 64:(r + 1) * 64], rhs=sgT[:, i * 128:(i + 1) * 128], start=True, stop=True),
                              reads=[n["selall"], sgT], writes=[ps[2]])
                    fw.op("dve", lambda: nc.vector.tensor_scalar(out=wt[:], in0=p_den, scalar1=1e-30, scalar2=None, op0=ALU.max), reads=[ps_of[id(p_den)]], writes=[wt])
                    fw.op("dve", lambda: nc.vector.reciprocal(out=wt[:], in_=wt[:]), reads=[wt], writes=[wt])
                    fw.op("dve", lambda: nc.vector.tensor_tensor(out=wt[:], in0=wt[:], in1=ps[2][0:64, :], op=ALU.mult), reads=[wt, ps[2]], writes=[wt])
                    if first:
                        fw.op("dve", lambda: nc.vector.tensor_tensor(out=acc[:], in0=p_num, in1=wt[:], op=ALU.mult), reads=[ps_of[id(p_num)], wt], writes=[acc])
                    else:
                        fw.op("dve", lambda: nc.vector.tensor_tensor(out=ot[:], in0=p_num, in1=wt[:], op=ALU.mult), reads=[ps_of[id(p_num)], wt], writes=[ot])
                        fw.op("pool", lambda: nc.gpsimd.tensor_tensor(out=acc[:], in0=acc[:], in1=ot[:], op=ALU.add), reads=[acc, ot], writes=[acc])
                ps_of = {}
                numc = ps[5][0:64, :]; denc = ps[3][0:64, :]
                ps_of[id(numc)] = ps[5]; ps_of[id(denc)] = ps[3]
                combine(0, numc, denc, True)
                attend(ksel, vseltok, list(range(i + 1)), tabs, True, ps[6], ps[3])
                nums = ps[6][0:64, :]; dens = ps[3][0:64, :]
                ps_of[id(nums)] = ps[6]; ps_of[id(dens)] = ps[3]
                combine(1, nums, dens, False)
                wk = [kt for kt in range(i - 4, i + 1) if kt >= 0]
                wtabs = dict(tabs)
                if i >= 4:
                    wtabs[i - 4] = (n["W4"], W4b)
                attend(kwin, vwintok, wk, wtabs, False, ps[7], ps[4])
                numw = ps[7][0:64, :]; denw = ps[4][0:64, :]
                ps_of[id(numw)] = ps[7]; ps_of[id(denw)] = ps[4]
                combine(2, numw, denw, False)
                y_ = yb[i % 2]
                fw.op("dve", lambda: nc.vector.tensor_copy(out=y_[:], in_=acc[:]), reads=[acc], writes=[y_])
                fw.dma("sp", YT.t.ap()[1].rearrange("(h d) t -> d h t", d=64)[:, g * 4:(g + 1) * 4, i * 128:(i + 1) * 128], y_[:].rearrange("p (h q) -> p h q", h=4),
                       reads=[y_], writes=[k.YTtok[1][g * 2][i // 4], k.YTtok[1][g * 2 + 1][i // 4]])


N_LAYERS = 4
N_CORES = 4


def build_all(nc, fw, n_layers=N_LAYERS):
    k = K()
    alloc_tokens(k)
    setup_common(nc, fw, k, n_layers)
    setup_hgrn(nc, fw, k, n_layers)
    setup_rwkv(nc, fw, k)
    setup_nsa(nc, fw, k)
    setup_p5(nc, fw, k)
    fw.barrier()
    k.gsrc = Buf(None, "gsrc")
    PT = fw.dram("PT", [NP, T], F32)
    YT = fw.dram("YT", [3, 512, T], BF16)
    UT = fw.dram("UT", [DFF, T], BF16)
    XT1 = fw.dram("XT1", [D, T], F32)
    XT2 = fw.dram("XT2", [D, T], F32)
    OUT = fw.dram("OUT", [D, T], F32, kind="ExternalOutput")
    x0_tok = [Buf(None, f"x0_{t}") for t in range(NTB)]
    x1_tok = [Buf(None, f"x1_{t}") for t in range(NTB)]
    x2_tok = [Buf(None, f"x2_{t}") for t in range(NTB)]
    o_tok = [Buf(None, f"o_{t}") for t in range(NTB)]
    XTin, xin_tok = k.xT_in, x0_tok
    for l in range(n_layers):
        in_proj_phase(nc, fw, k, l, XTin, xin_tok, PT)
        hgrn(nc, fw, k, l, PT, YT)
        nsa(nc, fw, k, l, PT, YT)
        rwkv(nc, fw, k, l, PT, YT)
        proj_out(nc, fw, k, l, PT, YT, XTin, xin_tok, XT1, x1_tok)
        ffn(nc, fw, k, l, XT1, x1_tok, XT2, x2_tok, UT)
        XTin, xin_tok = XT2, x2_tok
    final_norm(nc, fw, k, XTin, xin_tok, OUT, o_tok)
    fw.finish(o_tok)
    return k


def kernel(**inputs):
    inp = {k_: np.asarray(v) for k_, v in inputs.items()}
    consts = {**host_consts(), **hgrn_consts(), **rwkv_consts(), **nsa_consts(inp["rel_bias"].astype(np.float32))}
    shared = {
        "ada_w": inp["ada_w"], "ada_b": inp["ada_b"], "norm1_w": inp["norm1_w"], "norm2_w": inp["norm2_w"],
        "w_in_p": pad_w_in(inp["w_in"]),
        "hgrn_lb_logits": inp["hgrn_lb_logits"], "hgrn_norm_w": inp["hgrn_norm_w"],
        "w_branch": inp["w_branch"], "w_out": inp["w_out"], "ffn_w1": inp["ffn_w1"], "ffn_w3": inp["ffn_w3"], "ffn_w2": inp["ffn_w2"],
        "final_norm_w": inp["final_norm_w"],
    }
    for nm in ("rw_mu", "rw_w0", "rw_w2", "rw_a0", "rw_a2", "rw_g2", "rw_k_k", "rw_k_a", "rw_lnx_w", "rw_lnx_b"):
        shared[nm] = inp[nm]
    shared["rw_r_k"] = np.ascontiguousarray(inp["rw_r_k"].reshape(4, 512))
    for nm in ("nsa_pe_k", "nsa_cmp_w1_k", "nsa_cmp_w2_k", "nsa_pe_v", "nsa_cmp_w1_v", "nsa_cmp_w2_v"):
        shared[nm] = inp[nm]
    shared.update(consts)
    shared = {k_: np.ascontiguousarray(v, dtype=np.float32) for k_, v in shared.items()}
    B = inp["x"].shape[0]
    in_maps = []
    for b in range(B):
        m = dict(shared)
        m["xT"] = np.ascontiguousarray(inp["x"][b].T.astype(np.float32))
        m["c8"] = np.ascontiguousarray(inp["c"][b].reshape(8, 128).T.astype(np.float32))
        in_maps.append(m)
    nc = bass.Bass("TRN2", target_bir_lowering=False)
    with ExitStack() as es:
        fw = FW(nc, es)
        build_all(nc, fw)
    res = run_bass_kernel_spmd(nc, in_maps, core_ids=list(range(B)))
    out = np.stack([np.asarray(res.results[b]["OUT"]).T for b in range(B)], axis=0)
    return np.ascontiguousarray(out.astype(np.float32))
```

```python
import numpy as np
from contextlib import ExitStack, contextmanager
import concourse.bass as bass
import concourse.mybir as mybir
from concourse.bass_utils import run_bass_kernel_spmd

F32 = mybir.dt.float32
BF16 = mybir.dt.bfloat16
AF = mybir.ActivationFunctionType
ALU = mybir.AluOpType
AX = mybir.AxisListType

F32R = mybir.dt.float32r
USE_F32R = True


def MM(nc, out, lhsT, rhs, **kw):
    if USE_F32R and (lhsT.dtype == F32R or rhs.dtype == F32R):
        if lhsT.dtype == F32:
            lhsT = lhsT.bitcast(F32R)
        if rhs.dtype == F32:
            rhs = rhs.bitcast(F32R)
    return nc.tensor.matmul(out, lhsT=lhsT, rhs=rhs, **kw)


SAME_ENGINE_SYNC = True
N_DMA_SLOTS = 12
DMA_Q_MAP = {"pool": "sp", "act": "sp"}


class Buf:
    __slots__ = ("t", "w", "r", "name")

    def __init__(self, t=None, name=""):
        self.t = t
        self.w = None
        self.r = []
        self.name = name

    def __getitem__(self, k):
        return self.t[k]


class FW:
    def __init__(self, nc, es):
        self.nc = nc
        self.es = es
        self.eng = {"pe": nc.tensor, "act": nc.scalar, "dve": nc.vector, "pool": nc.gpsimd, "sp": nc.sync}
        self.sem = {k: es.enter_context(nc.semaphore("s_" + k)) for k in self.eng}
        self.cnt = {k: 0 for k in self.eng}
        self.seen = {k: {} for k in self.eng}
        self.dsem = {}
        self.dslot_use = {}
        self.dnext = {}
        for q in ("sp", "act", "pool"):
            self.dsem[q] = [es.enter_context(nc.semaphore(f"d_{q}{i}")) for i in range(N_DMA_SLOTS)]
            self.dslot_use[q] = [0] * N_DMA_SLOTS
            self.dnext[q] = 0
        self.n_inst = 0

    def sbuf(self, name, shape, dt=F32):
        self.n_alloc = getattr(self, "n_alloc", 0) + 1
        name = f"{name}_u{self.n_alloc}"
        return Buf(self.es.enter_context(self.nc.sbuf_tensor(name, list(shape), dt)), name)

    def psum(self, name, shape, dt=F32):
        return Buf(self.es.enter_context(self.nc.psum_tensor(name, list(shape), dt)), name)

    def dram(self, name, shape, dt=F32, kind="Internal"):
        return Buf(self.nc.dram_tensor(name, list(shape), dt, kind=kind), name)

    def _wait(self, e, ticket):
        if ticket is None:
            return
        kind = ticket[0]
        if kind == "e":
            _, src, n = ticket
            if src == e and (e == "pe" or not SAME_ENGINE_SYNC):
                return
            key = ("e", src)
            if self.seen[e].get(key, 0) >= n:
                return
            self.eng[e].wait_ge(self.sem[src], n)
            self.seen[e][key] = n
        else:
            _, q, slot, val = ticket
            key = ("d", q, slot)
            if self.seen[e].get(key, 0) >= val:
                return
            self.eng[e].wait_ge(self.dsem[q][slot], val)
            self.seen[e][key] = val

    def _deps(self, e, reads, writes):
        for b in reads:
            self._wait(e, b.w)
        for b in writes:
            self._wait(e, b.w)
            for t in b.r:
                self._wait(e, t)

    def _record(self, ticket, reads, writes):
        for b in reads:
            if ticket[0] == "e":
                b.r = [t for t in b.r if not (t[0] == "e" and t[1] == ticket[1])]
            b.r.append(ticket)
        for b in writes:
            b.w = ticket
            b.r = []

    def op(self, e, fn, reads=(), writes=()):
        self._deps(e, reads, writes)
        inst = fn()
        self.cnt[e] += 1
        inst.then_inc(self.sem[e], 1)
        self._record(("e", e, self.cnt[e]), reads, writes)
        self.n_inst += 1
        return inst

    def dma(self, q, out, in_, reads=(), writes=(), **kw):
        q = DMA_Q_MAP.get(q, q)
        self._deps(q, reads, writes)
        slot = self.dnext[q]
        self.dnext[q] = (slot + 1) % N_DMA_SLOTS
        uses = self.dslot_use[q][slot]
        if uses > 0:
            self._wait(q, ("d", q, slot, 16 * uses))
        inst = self.eng[q].dma_start(out=out, in_=in_, **kw)
        inst.then_inc(self.dsem[q][slot], 16)
        self.dslot_use[q][slot] = uses + 1
        self._record(("d", q, slot, 16 * (uses + 1)), reads, writes)
        self.n_inst += 1
        return inst

    @contextmanager
    def scope(self):
        old = self.es
        with ExitStack() as es2:
            self.es = es2
            try:
                yield
            finally:
                self.es = old
            self.barrier()

    def barrier(self):
        for e in self.eng:
            for k2 in self.eng:
                if k2 != e and self.cnt[k2] > 0:
                    self._wait(e, ("e", k2, self.cnt[k2]))
            for q in self.dsem:
                for slot in range(N_DMA_SLOTS):
                    u = self.dslot_use[q][slot]
                    if u:
                        self._wait(e, ("d", q, slot, 16 * u))

    def finish(self, bufs):
        for b in bufs:
            self._wait("sp", b.w)
        for k in self.eng:
            if k != "sp" and self.cnt[k] > 0:
                self._wait("sp", ("e", k, self.cnt[k]))
        for q in self.dsem:
            for slot in range(N_DMA_SLOTS):
                u = self.dslot_use[q][slot]
                if u:
                    self._wait("sp", ("d", q, slot, 16 * u))


import numpy as np

T = 4096
D = 1024
NCH = 65
NP = NCH * 128
TB = 512
NTB = T // TB


def host_consts():
    c = {}
    c["ident"] = np.eye(128, dtype=np.float32)
    c["ones"] = np.ones((128, 128), np.float32)
    return c


class K:
    pass


def setup_common(nc, fw, k, n_layers):
    def din(name, shape, dt=F32):
        return fw.dram(name, shape, dt, kind="ExternalInput")
    k.xT_in = din("xT", [D, T])
    k.c8 = din("c8", [128, 8])
    k.ada_w = din("ada_w", [4, D, 6 * D])
    k.ada_b = din("ada_b", [4, 6 * D])
    k.norm1_w = din("norm1_w", [4, D])
    k.norm2_w = din("norm2_w", [4, D])
    k.w_in = din("w_in_p", [4, D, NP])
    k.identD = din("ident", [128, 128])
    k.onesD = din("ones", [128, 128])

    k.ident = fw.sbuf("ident_s", [128, 128])
    k.ones = fw.sbuf("ones_s", [128, 128])
    fw.dma("sp", k.ident[:], k.identD.t.ap()[:, :], writes=[k.ident])
    fw.dma("sp", k.ones[:], k.onesD.t.ap()[:, :], writes=[k.ones])
    k.identb = fw.sbuf("ident_b", [128, 128], BF16)
    k.onesb = fw.sbuf("ones_b", [128, 128], BF16)
    fw.op("dve", lambda: nc.vector.tensor_copy(out=k.identb[:], in_=k.ident[:]), reads=[k.ident], writes=[k.identb])
    fw.op("dve", lambda: nc.vector.tensor_copy(out=k.onesb[:], in_=k.ones[:]), reads=[k.ones], writes=[k.onesb])

    k.ps = [fw.psum(f"ps{i}", [128, 512]) for i in range(8)]

    cs = fw.sbuf("c_s", [128, 8])
    fw.dma("sp", cs[:], k.c8.t.ap()[:, :], writes=[cs])
    cond = fw.sbuf("cond_s", [128, 8])
    fw.op("act", lambda: nc.scalar.activation(out=cond[:], in_=cs[:], func=AF.Silu), reads=[cs], writes=[cond])
    k.ada = fw.sbuf("ada_s", [128, 4, 48])
    adab = fw.sbuf("adab_s", [128, 4, 48])
    fw.dma("sp", adab[:], k.ada_b.t.ap().rearrange("l (j p) -> p l j", p=128), writes=[adab], allow_slow_non_contiguous=True)
    k.n1 = fw.sbuf("n1_s", [128, 4, 8])
    k.n2 = fw.sbuf("n2_s", [128, 4, 8])
    fw.dma("sp", k.n1[:], k.norm1_w.t.ap().rearrange("l (j p) -> p l j", p=128), writes=[k.n1], allow_slow_non_contiguous=True)
    fw.dma("sp", k.n2[:], k.norm2_w.t.ap().rearrange("l (j p) -> p l j", p=128), writes=[k.n2], allow_slow_non_contiguous=True)
    with fw.scope():
      wst = [fw.sbuf(f"adaw_st{i}", [128, 6 * D]) for i in range(2)]
      for l in range(n_layers):
        pst = k.ps[l % 2]
        for kc in range(8):
            w = wst[kc % 2]
            fw.dma("sp" if kc % 2 == 0 else "act", w[:], k.ada_w.t.ap()[l, kc * 128:(kc + 1) * 128, :], writes=[w])
            for j in range(48):
                col = kc * 48 + j
                fw.op("pe", lambda: nc.tensor.matmul(pst[:, col:col + 1], lhsT=w[:, j * 128:(j + 1) * 128], rhs=cond[:, kc:kc + 1],
                                                     start=True, stop=True), reads=[w, cond], writes=[pst])
        fw.op("dve", lambda: nc.vector.tensor_reduce(out=k.ada[:, l, :], in_=pst[:, 0:384].rearrange("p (k j) -> p j k", k=8),
                                                     axis=AX.X, op=ALU.add), reads=[pst], writes=[k.ada])
        fw.op("dve", lambda: nc.vector.tensor_tensor(out=k.ada[:, l, :], in0=k.ada[:, l, :], in1=adab[:, l, :], op=ALU.add),
              reads=[k.ada, adab], writes=[k.ada])
    k.g1 = fw.sbuf("g1_s", [128, 4, 8])
    k.g2 = fw.sbuf("g2_s", [128, 4, 8])
    for l in range(n_layers):
        fw.op("dve", lambda: nc.vector.scalar_tensor_tensor(out=k.g1[:, l, :], in0=k.ada[:, l, 8:16], scalar=1.0, in1=k.n1[:, l, :],
                                                            op0=ALU.add, op1=ALU.mult), reads=[k.ada, k.n1], writes=[k.g1])
        fw.op("dve", lambda: nc.vector.scalar_tensor_tensor(out=k.g2[:, l, :], in0=k.ada[:, l, 32:40], scalar=1.0, in1=k.n2[:, l, :],
                                                            op0=ALU.add, op1=ALU.mult), reads=[k.ada, k.n2], writes=[k.g2])


def norm_mod(nc, fw, k, XT, g, s, hT, tag):
    xs_b = k.xs_bufs
    for tb in range(NTB):
        xs = xs_b[tb % 2]
        fw.dma("sp" if tb % 2 == 0 else "act", xs[:], XT.t.ap()[:, tb * TB:(tb + 1) * TB].rearrange("(k p) t -> p k t", p=128),
               reads=[XT], writes=[xs])
        sq = k.sq_buf
        fw.op("act", lambda: nc.scalar.activation(out=sq[:], in_=xs[:], func=AF.Square), reads=[xs], writes=[sq])
        pst = k.ps[7]
        for kc in range(8):
            fw.op("pe", lambda: nc.tensor.matmul(pst[:], lhsT=k.ones[:], rhs=sq[:, kc, :], start=(kc == 0), stop=(kc == 7)),
                  reads=[k.ones, sq], writes=[pst])
        rstd = k.rstd_buf
        fw.op("dve", lambda: nc.vector.tensor_scalar(out=rstd[:], in0=pst[:], scalar1=1.0 / D, scalar2=1e-6, op0=ALU.mult, op1=ALU.add),
              reads=[pst], writes=[rstd])
        fw.op("act", lambda: nc.scalar.activation(out=rstd[:], in_=rstd[:], func=AF.Sqrt), reads=[rstd], writes=[rstd])
        fw.op("dve", lambda: nc.vector.reciprocal(out=rstd[:], in_=rstd[:]), reads=[rstd], writes=[rstd])
        for kc in range(8):
            tmp = k.tmp_bufs[kc % 2]
            fw.op("dve", lambda: nc.vector.scalar_tensor_tensor(out=tmp[:], in0=xs[:, kc, :], scalar=g[:, kc:kc + 1], in1=rstd[:],
                                                                op0=ALU.mult, op1=ALU.mult), reads=[xs, rstd], writes=[tmp])
            fw.op("act", lambda: nc.scalar.activation(out=hT[:, kc, tb * TB:(tb + 1) * TB], in_=tmp[:], func=AF.Identity,
                                                      bias=s[:, kc:kc + 1], scale=1.0), reads=[tmp], writes=[hT])


def in_proj(nc, fw, k, l, hT, PT):
    GC = 5
    NG = NCH // GC
    ev = 0
    for cg in range(NG):
        wst = k.w_st[cg % 2]
        wb = k.w_bf[cg % 2]
        fw.dma("sp" if cg % 2 == 0 else "act", wst[:], k.w_in.t.ap()[l, :, cg * 640:(cg + 1) * 640].rearrange("(k p) n -> p k n", p=128),
               reads=[], writes=[wst])
        if cg % 2 == 0:
            fw.op("pool", lambda: nc.gpsimd.tensor_copy(out=wb[:], in_=wst[:]), reads=[wst], writes=[wb])
        else:
            fw.op("dve", lambda: nc.vector.tensor_copy(out=wb[:], in_=wst[:]), reads=[wst], writes=[wb])
        for tb in range(NTB):
            for ch in range(GC):
                pst = k.ps[(0, 3, 1, 4)[ev % 4]]
                for kc in range(8):
                    fw.op("pe", lambda: nc.tensor.matmul(pst[:], lhsT=wb[:, kc, ch * 128:(ch + 1) * 128], rhs=hT[:, kc, tb * TB:(tb + 1) * TB],
                                                         start=(kc == 0), stop=(kc == 7)), reads=[wb, hT], writes=[pst])
                o = k.ev_bufs[ev % 4]
                if ev % 2 == 0:
                    fw.op("act", lambda: nc.scalar.copy(out=o[:], in_=pst[:]), reads=[pst], writes=[o])
                else:
                    fw.op("dve", lambda: nc.vector.tensor_copy(out=o[:], in_=pst[:]), reads=[pst], writes=[o])
                row = (cg * GC + ch) * 128
                fw.dma("pool" if ev % 2 == 0 else "sp", PT.t.ap()[row:row + 128, tb * TB:(tb + 1) * TB], o[:], reads=[o], writes=[k.PTtok[cg * GC + ch][tb]])
                ev += 1


def alloc_p1(fw, k):
    k.xs_bufs = [fw.sbuf(f"xs{i}", [128, 8, TB]) for i in range(2)]
    k.sq_buf = fw.sbuf("sq", [128, 8, TB])
    k.rstd_buf = fw.sbuf("rstd", [128, TB])
    k.tmp_bufs = [fw.sbuf(f"tmp{i}", [128, TB]) for i in range(2)]
    k.hT = fw.sbuf("hT", [128, 8, T], BF16)
    k.w_st = [fw.sbuf(f"wst{i}", [128, 8, 640]) for i in range(2)]
    k.w_bf = [fw.sbuf(f"wbf{i}", [128, 8, 640], BF16) for i in range(2)]
    k.ev_bufs = [fw.sbuf(f"ev{i}", [128, TB]) for i in range(4)]
    k.PTtok = [[Buf(None, f"pt{c}_{t}") for t in range(NTB)] for c in range(NCH)]


def pad_w_in(w_in):
    L = w_in.shape[0]
    out = np.zeros((L, D, NP), np.float32)
    out[:, :, :3352] = w_in[:, :, :3352]
    out[:, :, 3456:] = w_in[:, :, 3352:]
    return out


import numpy as np


def hgrn_consts():
    c = {}
    s = np.arange(128)
    c["mask_bd"] = ((s[:, None] // 32 == s[None, :] // 32) & (s[:, None] <= s[None, :])).astype(np.float32)
    c["rowmask"] = (s[:, None] // 32 == np.arange(4)[None, :]).astype(np.float32)
    m = np.ones((128, 512), np.float32)
    m[:, ::32] = 0.0
    c["scanmask32"] = m
    return c


def setup_hgrn(nc, fw, k, n_layers):
    def din(name, shape, dt=F32):
        return fw.dram(name, shape, dt, kind="ExternalInput")
    k.lb_logits = din("hgrn_lb_logits", [4, 512])
    k.hgrn_nw = din("hgrn_norm_w", [4, 512])
    md = din("mask_bd", [128, 128]); rm = din("rowmask", [128, 4]); sm = din("scanmask32", [128, 512])
    k.mask_bd = fw.sbuf("mask_bd_s", [128, 128]); k.rowmask = fw.sbuf("rowmask_s", [128, 4]); k.scanmask32 = fw.sbuf("scanmask32_s", [128, 512])
    fw.dma("sp", k.mask_bd[:], md.t.ap()[:, :], writes=[k.mask_bd])
    fw.dma("sp", k.rowmask[:], rm.t.ap()[:, :], writes=[k.rowmask])
    fw.dma("sp", k.scanmask32[:], sm.t.ap()[:, :], writes=[k.scanmask32])
    k.hnw = fw.sbuf("hnw_s", [128, 4, 4])
    fw.dma("sp", k.hnw[:], k.hgrn_nw.t.ap().rearrange("l (h p) -> p l h", p=128), writes=[k.hnw], allow_slow_non_contiguous=True)
    lbl = fw.sbuf("lbl_s", [128, 4, 4])
    fw.dma("sp", lbl[:], k.lb_logits.t.ap().rearrange("l (h p) -> p l h", p=128), writes=[lbl], allow_slow_non_contiguous=True)
    e = fw.sbuf("lbe_s", [128, 4, 4])
    fw.op("act", lambda: nc.scalar.activation(out=e[:], in_=lbl[:], func=AF.Exp), reads=[lbl], writes=[e])
    ssum = fw.sbuf("lbsum_s", [128, 4])
    fw.op("dve", lambda: nc.vector.tensor_tensor(out=ssum[:], in0=e[:, 0, :], in1=e[:, 1, :], op=ALU.add), reads=[e], writes=[ssum])
    fw.op("dve", lambda: nc.vector.tensor_tensor(out=ssum[:], in0=ssum[:], in1=e[:, 2, :], op=ALU.add), reads=[e, ssum], writes=[ssum])
    fw.op("dve", lambda: nc.vector.tensor_tensor(out=ssum[:], in0=ssum[:], in1=e[:, 3, :], op=ALU.add), reads=[e, ssum], writes=[ssum])
    fw.op("dve", lambda: nc.vector.reciprocal(out=ssum[:], in_=ssum[:]), reads=[ssum], writes=[ssum])
    k.lb = fw.sbuf("lb_s", [128, 4, 4])
    k.oml = fw.sbuf("oml_s", [128, 4, 4])
    k.noml = fw.sbuf("noml_s", [128, 4, 4])
    fw.op("dve", lambda: nc.vector.memset(k.lb[:], 0.0), writes=[k.lb])
    for l in range(1, 4):
        fw.op("dve", lambda: nc.vector.tensor_tensor(out=e[:, l, :], in0=e[:, l, :], in1=ssum[:], op=ALU.mult), reads=[e, ssum], writes=[e])
        fw.op("dve", lambda: nc.vector.tensor_tensor(out=k.lb[:, l, :], in0=k.lb[:, l - 1, :], in1=e[:, l, :], op=ALU.add), reads=[e, k.lb], writes=[k.lb])
    fw.op("dve", lambda: nc.vector.tensor_scalar(out=k.oml[:], in0=k.lb[:], scalar1=-1.0, scalar2=1.0, op0=ALU.mult, op1=ALU.add), reads=[k.lb], writes=[k.oml])
    fw.op("dve", lambda: nc.vector.tensor_scalar(out=k.noml[:], in0=k.oml[:], scalar1=-1.0, scalar2=None, op0=ALU.mult), reads=[k.oml], writes=[k.noml])


def load_pt(fw, k, PT, q, dst, ch, tb, rows=128, row0=0):
    r = ch * 128 + row0
    fw.dma(q, dst, PT.t.ap()[r:r + rows, tb * TB:(tb + 1) * TB], reads=[k.PTtok[ch][tb]], writes=[])


def hgrn(nc, fw, k, l, PT, YT):
    with fw.scope():
        f32t = lambda n: fw.sbuf(n, [128, TB])
        bft = lambda n: fw.sbuf(n, [128, TB], BF16)
        zq = [f32t(f"h_zq{i}") for i in range(2)]; zf = [f32t(f"h_zf{i}") for i in range(2)]
        zi = [f32t(f"h_zi{i}") for i in range(2)]; zg = [f32t(f"h_zg{i}") for i in range(2)]
        q = f32t("h_q"); sg = f32t("h_sg"); kk = f32t("h_k"); b = f32t("h_b"); t1 = f32t("h_t1"); t2 = f32t("h_t2")
        kh = f32t("h_kh"); gam = fw.sbuf("h_gam", [128, 16])
        qt = bft("h_qt"); kt = bft("h_kt"); khb = bft("h_khb"); vb = bft("h_vb")
        AT = [fw.sbuf(f"h_AT{i}", [128, 128], BF16) for i in range(2)]
        Vt = [fw.sbuf(f"h_Vt{i}", [128, 128], BF16) for i in range(2)]
        khz = [[fw.sbuf(f"h_khz{i}_{c}", [128, 128], BF16) for c in range(4)] for i in range(2)]
        S = fw.sbuf("h_S", [128, 128])
        Sb = [fw.sbuf(f"h_Sb{i}", [128, 128], BF16) for i in range(8)]
        ob = f32t("h_ob"); rs = f32t("h_rs"); yb = [bft(f"h_yb{i}") for i in range(2)]
        P_sc, P_vt, P_kt, P_o, P_u0, P_u1, P_n = (k.ps[i] for i in (3, 0, 4, 1, 5, 6, 7))
        it = 0
        sbi = 0
        import os
        SUB = int(os.environ.get('SUB', '9')); STAGE = int(os.environ.get('STAGE', '9')); NHX = int(os.environ.get('NHX', '4')); NTBX = int(os.environ.get('NTBX', '8'))
        for h in range(NHX):
            fw.op("dve", lambda: nc.vector.memset(S[:], 0.0), writes=[S])
            lbc = k.lb[:, l, h:h + 1]; omlc = k.oml[:, l, h:h + 1]; nomlc = k.noml[:, l, h:h + 1]
            for tb in range(NTBX):
                z_q, z_f, z_i, z_g = zq[it % 2], zf[it % 2], zi[it % 2], zg[it % 2]
                for (dst, ch, qq) in ((z_q, h, "sp"), (z_f, 4 + h, "act"), (z_i, 8 + h, "sp"), (z_g, 12 + h, "act")):
                    r = ch * 128
                    fw.dma(qq, dst[:], PT.t.ap()[r:r + 128, tb * TB:(tb + 1) * TB], reads=[k.PTtok[ch][tb]], writes=[dst])
                fw.op("act", lambda: nc.scalar.activation(out=q[:], in_=z_q[:], func=AF.Silu), reads=[z_q], writes=[q])
                fw.op("act", lambda: nc.scalar.activation(out=sg[:], in_=z_f[:], func=AF.Sigmoid), reads=[z_f], writes=[sg])
                fw.op("dve", lambda: nc.vector.tensor_scalar(out=t1[:], in0=sg[:], scalar1=omlc, scalar2=lbc, op0=ALU.mult, op1=ALU.add), reads=[sg], writes=[t1])
                fw.op("act", lambda: nc.scalar.activation(out=t1[:], in_=t1[:], func=AF.Ln), reads=[t1], writes=[t1])
                fw.op("dve", lambda: nc.vector.tensor_scalar(out=kk[:], in0=sg[:], scalar1=nomlc, scalar2=omlc, op0=ALU.mult, op1=ALU.add), reads=[sg], writes=[kk])
                fw.op("dve", lambda: nc.vector.tensor_tensor_scan(out=b[:], data0=k.scanmask32[:], data1=t1[:], initial=0.0, op0=ALU.mult, op1=ALU.add),
                      reads=[k.scanmask32, t1], writes=[b])
                fw.op("act", lambda: nc.scalar.activation(out=t2[:], in_=b[:], func=AF.Exp), reads=[b], writes=[t2])
                fw.op("dve", lambda: nc.vector.tensor_tensor(out=qt[:], in0=q[:], in1=t2[:], op=ALU.mult), reads=[q, t2], writes=[qt])
                fw.op("act", lambda: nc.scalar.activation(out=t2[:], in_=b[:], func=AF.Exp, scale=-1.0), reads=[b], writes=[t2])
                fw.op("dve", lambda: nc.vector.tensor_tensor(out=kt[:], in0=kk[:], in1=t2[:], op=ALU.mult), reads=[kk, t2], writes=[kt])
                b3 = b.t[:].rearrange("p (c t) -> p c t", t=32)
                fw.op("dve", lambda: nc.vector.tensor_tensor(out=t2.t[:].rearrange("p (c t) -> p c t", t=32), in0=b3[:, :, 31:32].to_broadcast([128, 16, 32]), in1=b3,
                                                             op=ALU.subtract), reads=[b], writes=[t2])
                fw.op("act", lambda: nc.scalar.activation(out=t2[:], in_=t2[:], func=AF.Exp), reads=[t2], writes=[t2])
                fw.op("dve", lambda: nc.vector.tensor_tensor(out=khb[:], in0=kk[:], in1=t2[:], op=ALU.mult), reads=[kk, t2], writes=[khb])
                fw.op("pool", lambda: nc.gpsimd.tensor_copy(out=vb[:], in_=z_i[:]), reads=[z_i], writes=[vb])
                fw.op("act", lambda: nc.scalar.activation(out=gam[:], in_=b3[:, :, 31], func=AF.Exp), reads=[b], writes=[gam])
                if os.environ.get('DBG') and it == 0:
                    for di, src in enumerate((q, kk, b, t2, t1, sg)):
                        fw.dma('sp', k.DBG.t.ap()[:, di, :], src[:], reads=[src], writes=[k.DBGtok])
                if STAGE < 2:
                    continue
                for tl in range(4):
                    cs = slice(tl * 128, (tl + 1) * 128)
                    a_t = AT[tl % 2]; v_t = Vt[tl % 2]; kz = khz[tl % 2]
                    fw.op("pe", lambda: nc.tensor.matmul(P_sc[:, cs], lhsT=kt[:, cs], rhs=qt[:, cs], start=True, stop=True), reads=[kt, qt], writes=[P_sc])
                    fw.op("dve", lambda: nc.vector.tensor_tensor(out=a_t[:], in0=P_sc[:, cs], in1=k.mask_bd[:], op=ALU.mult), reads=[P_sc, k.mask_bd], writes=[a_t])
                    if SUB < 2: continue
                    fw.op("pe", lambda: nc.tensor.matmul(P_vt[:, cs], lhsT=vb[:, cs], rhs=k.identb[:], start=True, stop=True), reads=[vb, k.identb], writes=[P_vt])
                    fw.op("act", lambda: nc.scalar.copy(out=v_t[:], in_=P_vt[:, cs]), reads=[P_vt], writes=[v_t])
                    if SUB < 3: continue
                    ksrc = {'khb': khb, 'vb': vb, 'qt': qt}[os.environ.get('KSRC', 'khb')]
                    fw.op("pe", lambda: nc.tensor.matmul(P_kt[:, cs], lhsT=ksrc[:, cs], rhs=k.identb[:], start=True, stop=True), reads=[ksrc, k.identb], writes=[P_kt])
                    for c in range(int(os.environ.get('NEV', '4'))):
                        if c % 2 == 0 or True:
                            fw.op("dve", lambda: nc.vector.tensor_scalar(out=kz[c][:], in0=P_kt[:, cs], scalar1=k.rowmask[:, c:c + 1], scalar2=None, op0=ALU.mult),
                                  reads=[P_kt, k.rowmask], writes=[kz[c]])
                        else:
                            fw.op("act", lambda: nc.scalar.activation(out=kz[c][:], in_=P_kt[:, cs], func=AF.Identity, scale=k.rowmask[:, c:c + 1]),
                                  reads=[P_kt, k.rowmask], writes=[kz[c]])
                    if SUB < 4: continue
                    if SUB < 5: continue
                    for c in range(4):
                        s_b = Sb[sbi % 8]; sbi += 1
                        fw.op("act", lambda: nc.scalar.copy(out=s_b[:], in_=S[:]), reads=[S], writes=[s_b])
                        c0 = tl * 128 + c * 32
                        fw.op("pe", lambda: nc.tensor.matmul(P_o[:, c0:c0 + 32], lhsT=v_t[:], rhs=a_t[:, c * 32:(c + 1) * 32], start=True, stop=False), reads=[v_t, a_t], writes=[P_o])
                        fw.op("pe", lambda: nc.tensor.matmul(P_o[:, c0:c0 + 32], lhsT=s_b[:], rhs=qt[:, c0:c0 + 32], start=False, stop=True), reads=[s_b, qt], writes=[P_o])
                        P_u = P_u0 if c % 2 == 0 else P_u1
                        fw.op("pe", lambda: nc.tensor.matmul(P_u[:, 0:128], lhsT=kz[c][:], rhs=v_t[:], start=True, stop=True), reads=[kz[c], v_t], writes=[P_u])
                        gi = tl * 4 + c
                        fw.op("dve", lambda: nc.vector.scalar_tensor_tensor(out=S[:], in0=S[:], scalar=gam[:, gi:gi + 1], in1=P_u[:, 0:128], op0=ALU.mult, op1=ALU.add),
                              reads=[S, gam, P_u], writes=[S])
                    fw.op("act", lambda: nc.scalar.copy(out=ob[:, cs], in_=P_o[:, cs]), reads=[P_o], writes=[ob])
                if STAGE < 3:
                    continue
                fw.op("act", lambda: nc.scalar.activation(out=t2[:], in_=ob[:], func=AF.Square), reads=[ob], writes=[t2])
                fw.op("pe", lambda: nc.tensor.matmul(P_n[:], lhsT=k.ones[:], rhs=t2[:], start=True, stop=True), reads=[k.ones, t2], writes=[P_n])
                S3 = int(os.environ.get('S3', '9'))
                if S3 < 2: continue
                fw.op("dve", lambda: nc.vector.tensor_scalar(out=rs[:], in0=P_n[:], scalar1=1.0 / 128, scalar2=1e-5, op0=ALU.mult, op1=ALU.add), reads=[P_n], writes=[rs])
                fw.op("act", lambda: nc.scalar.activation(out=rs[:], in_=rs[:], func=AF.Sqrt), reads=[rs], writes=[rs])
                fw.op("dve", lambda: nc.vector.reciprocal(out=rs[:], in_=rs[:]), reads=[rs], writes=[rs])
                if S3 < 3: continue
                fw.op("act", lambda: nc.scalar.activation(out=t1[:], in_=z_g[:], func=AF.Sigmoid), reads=[z_g], writes=[t1])
                fw.op("dve", lambda: nc.vector.scalar_tensor_tensor(out=rs[:], in0=ob[:], scalar=k.hnw[:, l, h:h + 1], in1=rs[:], op0=ALU.mult, op1=ALU.mult),
                      reads=[ob, rs, k.hnw], writes=[rs])
                y_b = yb[it % 2]
                fw.op("dve", lambda: nc.vector.tensor_tensor(out=y_b[:], in0=rs[:], in1=t1[:], op=ALU.mult), reads=[rs, t1], writes=[y_b])
                if S3 < 4: continue
                fw.dma("sp", YT.t.ap()[0, h * 128:(h + 1) * 128, tb * TB:(tb + 1) * TB], y_b[:], reads=[y_b], writes=[k.YTtok[0][h][tb]])
                it += 1


def alloc_tokens(k):
    k.PTtok = [[Buf(None, f"pt{c}_{t}") for t in range(NTB)] for c in range(NCH)]
    k.YTtok = [[[Buf(None, f"yt{m}_{c}_{t}") for t in range(NTB)] for c in range(4)] for m in range(3)]
    k.UTtok = [[Buf(None, f"ut{c}_{t}") for t in range(NTB)] for c in range(22)]


import numpy as np, os

CH_R, CH_K, CH_V, CH_WA, CH_G = 27, 31, 35, 39, 40


def rwkv_consts():
    c = {}
    i = np.arange(128)
    same = (i[:, None] // 64 == i[None, :] // 64)
    su = (same & (i[:, None] < i[None, :])).astype(np.float32)
    iu = (same & (i[:, None] <= i[None, :])).astype(np.float32)
    sl = (same & (i[:, None] > i[None, :])).astype(np.float32)
    c["rw_mask_su_iu"] = np.concatenate([su, iu], axis=1)
    c["rw_mask_negsu"] = -su
    c["rw_mask_negsl"] = -sl
    m = np.ones((64, 512), np.float32)
    m[:, ::64] = 0.0
    c["scanmask64"] = m
    c["ones64"] = np.ones((64, 64), np.float32)
    c["rowmask64"] = np.concatenate([(i[:, None] // 64 == np.arange(2)[None, :]).astype(np.float32), -(i[:, None] // 64 == np.arange(2)[None, :]).astype(np.float32)], axis=1)
    return c


def setup_rwkv(nc, fw, k):
    def din(name, shape, dt=F32):
        return fw.dram(name, shape, dt, kind="ExternalInput")
    k.rw = {}
    for nm, shp in (("rw_mu", [4, 1792]), ("rw_w0", [4, 512]), ("rw_w2", [4, 64, 512]), ("rw_a0", [4, 512]), ("rw_a2", [4, 64, 512]),
                    ("rw_g2", [4, 128, 512]), ("rw_k_k", [4, 512]), ("rw_k_a", [4, 512]), ("rw_r_k", [4, 512]), ("rw_lnx_w", [4, 512]), ("rw_lnx_b", [4, 512])):
        k.rw[nm] = din(nm, shp)
    cm = {}
    for nm, shp in (("rw_mask_su_iu", [128, 256]), ("rw_mask_negsu", [128, 128]), ("rw_mask_negsl", [128, 128]), ("scanmask64", [64, 512]), ("ones64", [64, 64]), ("rowmask64", [128, 4])):
        d = din(nm, shp)
        t = fw.sbuf(nm + "_s", shp)
        fw.dma("sp", t[:], d.t.ap()[:, :], writes=[t])
        cm[nm] = t
    k.rwc = cm


def rwkv(nc, fw, k, l, PT, YT, NHX=8, NTBX=NTB):
    with fw.scope():
        W = k.rw
        def pvec(nm):
            t = fw.sbuf("rp_" + nm, [64, 8])
            fw.dma("sp", t[:], W[nm].t.ap()[l, :].rearrange("(h p) -> p h", p=64), writes=[t], allow_slow_non_contiguous=True)
            return t
        w0 = pvec("rw_w0"); a0 = pvec("rw_a0"); k_k = pvec("rw_k_k"); k_a = pvec("rw_k_a"); r_k = pvec("rw_r_k"); lnw = pvec("rw_lnx_w"); lnb = pvec("rw_lnx_b")
        mu_rkv = fw.sbuf("rp_mu", [64, 24])
        fw.dma("sp", mu_rkv[:], W["rw_mu"].t.ap()[l, 0:1536].rearrange("(h p) -> p h", p=64), writes=[mu_rkv], allow_slow_non_contiguous=True)
        mu_wa = fw.sbuf("rp_muwa", [64, 2])
        fw.dma("sp", mu_wa[:], W["rw_mu"].t.ap()[l, 1536:1664].rearrange("(h p) -> p h", p=64), writes=[mu_wa], allow_slow_non_contiguous=True)
        mu_g = fw.sbuf("rp_mug", [128, 1])
        fw.dma("sp", mu_g[:], W["rw_mu"].t.ap()[l, 1664:1792].rearrange("(h p) -> p h", p=128), writes=[mu_g], allow_slow_non_contiguous=True)
        w2 = fw.sbuf("rp_w2", [64, 512], F32R); a2 = fw.sbuf("rp_a2", [64, 512], F32R); g2 = fw.sbuf("rp_g2", [128, 512], F32R)
        wstg = fw.sbuf("rp_wstg", [128, 512])
        for (dst_, nm_, rows_) in ((w2, "rw_w2", 64), (a2, "rw_a2", 64), (g2, "rw_g2", 128)):
            fw.dma("sp", wstg[0:rows_, :], W[nm_].t.ap()[l], writes=[wstg])
            fw.op("dve", lambda: nc.vector.tensor_copy(out=dst_[:], in_=wstg[0:rows_, :]), reads=[wstg], writes=[dst_])
        identR = fw.sbuf("rp_identR", [128, 128], F32R); ones64R = fw.sbuf("rp_ones64R", [64, 64], F32R)
        fw.op("dve", lambda: nc.vector.tensor_copy(out=identR[:], in_=k.ident[:]), reads=[k.ident], writes=[identR])
        fw.op("dve", lambda: nc.vector.tensor_copy(out=ones64R[:], in_=k.rwc["ones64"][:]), reads=[k.rwc["ones64"]], writes=[ones64R])
        C = k.rwc
        ident = identR
        t64 = lambda n, dt=F32: fw.sbuf(n, [64, TB], dt)
        xw_in = fw.sbuf("r_xwin", [64, TB + 1]); xa_in = fw.sbuf("r_xain", [64, TB + 1]); xg_in = fw.sbuf("r_xgin", [128, TB + 1])
        th = t64("r_th", F32R); xa = t64("r_xa", F32R); sgg = fw.sbuf("r_sgg", [128, TB], F32R); sq_r = t64("r_sqr", F32R); dtmp = fw.sbuf("r_dtmp", [128, TB])
        rin = fw.sbuf("r_rin", [64, TB + 1]); kin = fw.sbuf("r_kin", [64, TB + 1]); vin = fw.sbuf("r_vin", [64, TB + 1])
        k_s = t64("r_ks"); ld = t64("r_ld"); a_ = t64("r_a"); kap = t64("r_kap"); b_ = t64("r_b"); L = t64("r_L")
        e1 = t64("r_e1"); e2 = t64("r_e2"); e3 = t64("r_e3"); t1 = t64("r_t1"); t2 = t64("r_t2")
        M = [fw.sbuf(f"r_M{h}", [128, 64], F32R) for h in range(8)]
        S = []
        for s_ in range(2):
            B_ = {}
            for n_ in ("r_s", "v_s", "kp", "g_", "kapt", "kt", "bt", "kh", "bh", "yo"):
                B_[n_] = t64(f"r_{n_}{s_}", F32 if n_ in ("r_s", "kp", "g_") else F32R)
            B_["rt"] = fw.sbuf(f"r_rt{s_}", [128, TB], F32R); B_["gam"] = fw.sbuf(f"r_gam{s_}", [64, 8]); B_["dg"] = fw.sbuf(f"r_dg{s_}", [128, 64], F32R)
            B_["SCa"] = fw.sbuf(f"r_SCa{s_}", [128, 256], F32R); B_["SCb"] = fw.sbuf(f"r_SCb{s_}", [128, 256], F32R)
            B_["Y"] = [fw.sbuf(f"r_Y{s_}{i}", [128, 128], F32R) for i in range(2)]; B_["Z"] = [fw.sbuf(f"r_Z{s_}{i}", [128, 128], F32R) for i in range(2)]
            B_["P"] = [fw.sbuf(f"r_P{s_}{i}", [128, 128], F32R) for i in range(2)]
            B_["ktok"] = fw.sbuf(f"r_ktok{s_}", [128, 128], F32R); B_["vtok"] = fw.sbuf(f"r_vtok{s_}", [128, 64], F32R); B_["bhtok"] = fw.sbuf(f"r_bhtok{s_}", [128, 64], F32R)
            B_["khc"] = [fw.sbuf(f"r_khc{s_}{i}", [128, 64], F32R) for i in range(2)]; B_["bhc"] = [fw.sbuf(f"r_bhc{s_}{i}", [128, 64], F32R) for i in range(2)]
            B_["nWc"] = [fw.sbuf(f"r_nWc{s_}{i}", [128, 64], F32R) for i in range(2)]
            B_["WU"] = fw.sbuf(f"r_WU{s_}", [128, 128]); B_["nWU"] = fw.sbuf(f"r_nWU{s_}", [128, 128], F32R); B_["Rp"] = fw.sbuf(f"r_Rp{s_}", [128, 128], F32R)
            B_["PTm"] = [fw.sbuf(f"r_PTm{s_}{i}", [64, 64], F32R) for i in range(2)]; B_["Qm"] = [fw.sbuf(f"r_Qm{s_}{i}", [64, 64]) for i in range(2)]
            B_["ybuf"] = fw.sbuf(f"r_yb{s_}", [64, TB], BF16)
            bk = (0, 3, 4, 5) if s_ == 0 else (1, 6, 7, 2)
            B_["X"], B_["E0"], B_["E1"], B_["E2"] = (k.ps[i] for i in bk)
            S.append(B_)
        ps = k.ps
        for h in range(8):
            fw.op("dve", lambda: nc.vector.memset(M[h][:].bitcast(F32), 0.0), writes=[M[h]])
        for B_ in S:
            fw.op("dve", lambda: nc.vector.memset(B_["rt"][:].bitcast(F32), 0.0), writes=[B_["rt"]])
            fw.op("dve", lambda: nc.vector.memset(B_["dg"][:].bitcast(F32), 0.0), writes=[B_["dg"]])
            fw.op("dve", lambda: nc.vector.memset(B_["Rp"][:].bitcast(F32), 0.0), writes=[B_["Rp"]])
        RM = C["rowmask64"]

        def load_shift(dst, ch, row0, rows, tb):
            r0 = ch * 128 + row0
            if tb == 0:
                fw.op("dve", lambda: nc.vector.memset(dst[0:rows, 0:1], 0.0), writes=[dst])
                fw.dma("sp", dst[0:rows, 1:TB + 1], PT.t.ap()[r0:r0 + rows, 0:TB], reads=[k.PTtok[ch][0]], writes=[dst])
            else:
                fw.dma("sp", dst[0:rows, 0:TB + 1], PT.t.ap()[r0:r0 + rows, tb * TB - 1:(tb + 1) * TB], reads=[k.PTtok[ch][tb], k.PTtok[ch][tb - 1]], writes=[dst])

        def shift_mix(out, src, rows, mu_ap, tmp, mub):
            fw.op("dve", lambda: nc.vector.tensor_tensor(out=tmp[0:rows, :], in0=src[0:rows, 0:TB], in1=src[0:rows, 1:TB + 1], op=ALU.subtract), reads=[src], writes=[tmp])
            fw.op("dve", lambda: nc.vector.scalar_tensor_tensor(out=out[0:rows, :], in0=tmp[0:rows, :], scalar=mu_ap, in1=src[0:rows, 1:TB + 1], op0=ALU.mult, op1=ALU.add),
                  reads=[tmp, src, mub], writes=[out])

        it = 0
        for tb in range(NTBX):
            load_shift(xw_in, CH_WA, 0, 64, tb); load_shift(xa_in, CH_WA, 64, 64, tb); load_shift(xg_in, CH_G, 0, 128, tb)
            shift_mix(th, xw_in, 64, mu_wa[:, 0:1], dtmp, mu_wa)
            fw.op("act", lambda: nc.scalar.activation(out=th[:], in_=th[:], func=AF.Tanh), reads=[th], writes=[th])
            shift_mix(xa, xa_in, 64, mu_wa[:, 1:2], dtmp, mu_wa)
            shift_mix(sgg, xg_in, 128, mu_g[:, 0:1], dtmp, mu_g)
            fw.op("act", lambda: nc.scalar.activation(out=sgg[:], in_=sgg[:], func=AF.Sigmoid), reads=[sgg], writes=[sgg])
            def unit(s, h):
                B_ = S[s]
                X, E0, E1, E2 = B_["X"], B_["E0"], B_["E1"], B_["E2"]
                r_s, v_s, kp, g_, kapt, kt, bt, rt, kh, bh, gam, dg, yo = (B_[n_] for n_ in ("r_s", "v_s", "kp", "g_", "kapt", "kt", "bt", "rt", "kh", "bh", "gam", "dg", "yo"))
                SCa, SCb, Y, Z, P, ktok, vtok, bhtok, khc, bhc, nWc, WU, nWU, Rp, PTm, Qm, ybuf = (B_[n_] for n_ in ("SCa", "SCb", "Y", "Z", "P", "ktok", "vtok", "bhtok", "khc", "bhc", "nWc", "WU", "nWU", "Rp", "PTm", "Qm", "ybuf"))
                j, hh = h // 2, h % 2
                load_shift(rin, CH_R + j, hh * 64, 64, tb); load_shift(kin, CH_K + j, hh * 64, 64, tb); load_shift(vin, CH_V + j, hh * 64, 64, tb)
                shift_mix(r_s, rin, 64, mu_rkv[:, h:h + 1], dtmp, mu_rkv)
                shift_mix(k_s, kin, 64, mu_rkv[:, 8 + h:9 + h], dtmp, mu_rkv)
                shift_mix(v_s, vin, 64, mu_rkv[:, 16 + h:17 + h], dtmp, mu_rkv)
                hs = slice(h * 64, (h + 1) * 64)
                fw.op("pe", lambda: MM(nc, X[0:64, :], lhsT=w2[:, hs], rhs=th[:], start=True, stop=True), reads=[w2, th], writes=[X])
                fw.op("act", lambda: nc.scalar.activation(out=ld[:], in_=X[0:64, :], func=AF.Sigmoid, bias=w0[:, h:h + 1], scale=1.0), reads=[X, w0], writes=[ld])
                fw.op("dve", lambda: nc.vector.tensor_scalar(out=ld[:], in0=ld[:], scalar1=-0.6065306597126334, scalar2=None, op0=ALU.mult), reads=[ld], writes=[ld])
                fw.op("pe", lambda: MM(nc, X[0:64, :], lhsT=a2[:, hs], rhs=xa[:], start=True, stop=True), reads=[a2, xa], writes=[X])
                fw.op("act", lambda: nc.scalar.activation(out=a_[:], in_=X[0:64, :], func=AF.Sigmoid, bias=a0[:, h:h + 1], scale=1.0), reads=[X, a0], writes=[a_])
                fw.op("pe", lambda: MM(nc, X[0:64, :], lhsT=g2[:, hs], rhs=sgg[:], start=True, stop=True), reads=[g2, sgg], writes=[X])
                fw.op("act", lambda: nc.scalar.copy(out=g_[:], in_=X[0:64, :]), reads=[X], writes=[g_])
                fw.op("dve", lambda: nc.vector.tensor_scalar(out=kap[:], in0=k_s[:], scalar1=k_k[:, h:h + 1], scalar2=None, op0=ALU.mult), reads=[k_s, k_k], writes=[kap])
                fw.op("act", lambda: nc.scalar.activation(out=sq_r[:], in_=kap[:], func=AF.Square), reads=[kap], writes=[sq_r])
                fw.op("pe", lambda: MM(nc, E0[0:64, :], lhsT=ones64R[:], rhs=sq_r[:], start=True, stop=True), reads=[ones64R, sq_r], writes=[E0])
                fw.op("dve", lambda: nc.vector.tensor_scalar(out=t1[:], in0=E0[0:64, :], scalar1=1e-24, scalar2=None, op0=ALU.max), reads=[E0], writes=[t1])
                fw.op("act", lambda: nc.scalar.activation(out=t1[:], in_=t1[:], func=AF.Sqrt), reads=[t1], writes=[t1])
                fw.op("dve", lambda: nc.vector.reciprocal(out=t1[:], in_=t1[:]), reads=[t1], writes=[t1])
                fw.op("dve", lambda: nc.vector.tensor_tensor(out=kap[:], in0=kap[:], in1=t1[:], op=ALU.mult), reads=[kap, t1], writes=[kap])
                fw.op("dve", lambda: nc.vector.tensor_scalar(out=t1[:], in0=a_[:], scalar1=-1.0, scalar2=k_a[:, h:h + 1], op0=ALU.add, op1=ALU.mult), reads=[a_, k_a], writes=[t1])
                fw.op("dve", lambda: nc.vector.scalar_tensor_tensor(out=kp[:], in0=t1[:], scalar=1.0, in1=k_s[:], op0=ALU.add, op1=ALU.mult), reads=[t1, k_s], writes=[kp])
                fw.op("dve", lambda: nc.vector.tensor_tensor(out=b_[:], in0=a_[:], in1=kap[:], op=ALU.mult), reads=[a_, kap], writes=[b_])
                fw.op("dve", lambda: nc.vector.tensor_tensor_scan(out=L[:], data0=C["scanmask64"][:], data1=ld[:], initial=0.0, op0=ALU.mult, op1=ALU.add),
                      reads=[C["scanmask64"], ld], writes=[L])
                fw.op("act", lambda: nc.scalar.activation(out=e1[:], in_=L[:], func=AF.Exp), reads=[L], writes=[e1])
                fw.op("act", lambda: nc.scalar.activation(out=e2[:], in_=L[:], func=AF.Exp, scale=-1.0), reads=[L], writes=[e2])
                L3 = L.t[:].rearrange("p (c t) -> p c t", t=64)
                fw.op("dve", lambda: nc.vector.tensor_tensor(out=t2.t[:].rearrange("p (c t) -> p c t", t=64), in0=L3[:, :, 63:64].to_broadcast([64, 8, 64]), in1=L3, op=ALU.subtract),
                      reads=[L], writes=[t2])
                fw.op("act", lambda: nc.scalar.activation(out=e3[:], in_=t2[:], func=AF.Exp), reads=[t2], writes=[e3])
                fw.op("act", lambda: nc.scalar.activation(out=gam[:], in_=L3[:, :, 63], func=AF.Exp), reads=[L], writes=[gam])
                fw.op("dve", lambda: nc.vector.tensor_tensor(out=t1[:], in0=L[:], in1=ld[:], op=ALU.subtract), reads=[L, ld], writes=[t1])
                fw.op("act", lambda: nc.scalar.activation(out=t1[:], in_=t1[:], func=AF.Exp), reads=[t1], writes=[t1])
                fw.op("dve", lambda: nc.vector.tensor_tensor(out=kapt[:], in0=kap[:], in1=t1[:], op=ALU.mult), reads=[kap, t1], writes=[kapt])
                fw.op("dve", lambda: nc.vector.tensor_tensor(out=kt[:], in0=kp[:], in1=e2[:], op=ALU.mult), reads=[kp, e2], writes=[kt])
                fw.op("dve", lambda: nc.vector.tensor_tensor(out=bt[:], in0=b_[:], in1=e2[:], op=ALU.mult), reads=[b_, e2], writes=[bt])
                fw.op("dve", lambda: nc.vector.tensor_tensor(out=rt[0:64, :], in0=r_s[:], in1=e1[:], op=ALU.mult), reads=[r_s, e1], writes=[rt])
                fw.op("dve", lambda: nc.vector.tensor_tensor(out=kh[:], in0=kp[:], in1=e3[:], op=ALU.mult), reads=[kp, e3], writes=[kh])
                fw.op("dve", lambda: nc.vector.tensor_tensor(out=bh[:], in0=b_[:], in1=e3[:], op=ALU.mult), reads=[b_, e3], writes=[bh])
                Mh = M[h]
                yield
                for tl in range(4):
                    cs = slice(tl * 128, (tl + 1) * 128)
                    fw.op("pe", lambda: MM(nc, E0[:, 0:128], lhsT=bt[:, cs], rhs=kapt[:, cs], start=True, stop=True), reads=[bt, kapt], writes=[E0])
                    fw.op("pe", lambda: MM(nc, E0[:, 128:256], lhsT=bt[:, cs], rhs=rt[0:64, cs], start=True, stop=True), reads=[bt, rt], writes=[E0])
                    fw.op("pe", lambda: MM(nc, E0[:, 256:384], lhsT=kt[:, cs], rhs=kapt[:, cs], start=True, stop=True), reads=[kt, kapt], writes=[E0])
                    fw.op("pe", lambda: MM(nc, E0[:, 384:512], lhsT=kt[:, cs], rhs=rt[0:64, cs], start=True, stop=True), reads=[kt, rt], writes=[E0])
                    fw.op("pe", lambda: MM(nc, E1[:, 0:128], lhsT=kapt[:, cs], rhs=bt[:, cs], start=True, stop=True), reads=[kapt, bt], writes=[E1])
                    fw.op("dve", lambda: nc.vector.tensor_tensor(out=SCa[:], in0=E0[:, 0:256], in1=C["rw_mask_su_iu"][:], op=ALU.mult), reads=[E0, C["rw_mask_su_iu"]], writes=[SCa])
                    fw.op("dve", lambda: nc.vector.tensor_tensor(out=SCb[:], in0=E0[:, 256:512], in1=C["rw_mask_su_iu"][:], op=ALU.mult), reads=[E0, C["rw_mask_su_iu"]], writes=[SCb])
                    fw.op("dve", lambda: nc.vector.tensor_tensor(out=Y[0][:], in0=E0[:, 0:128], in1=C["rw_mask_negsu"][:], op=ALU.mult), reads=[E0, C["rw_mask_negsu"]], writes=[Y[0]])
                    fw.op("dve", lambda: nc.vector.tensor_tensor(out=Z[0][:], in0=E1[:, 0:128], in1=C["rw_mask_negsl"][:], op=ALU.mult), reads=[E1, C["rw_mask_negsl"]], writes=[Z[0]])
                    yield
                    fw.op("dve", lambda: nc.vector.tensor_tensor(out=P[0][:], in0=Y[0][:], in1=ident[:], op=ALU.add), reads=[Y[0], ident], writes=[P[0]])
                    cur = 0
                    for lev in range(1, 6):
                        nxt = 1 - cur
                        fw.op("pe", lambda: MM(nc, X[:, 0:128], lhsT=Y[cur][:], rhs=Z[cur][:], start=True, stop=True), reads=[Y[cur], Z[cur]], writes=[X])
                        if lev < 5:
                            fw.op("pe", lambda: MM(nc, E1[:, 128:256], lhsT=Z[cur][:], rhs=Y[cur][:], start=True, stop=True), reads=[Y[cur], Z[cur]], writes=[E1])
                        fw.op("act", lambda: nc.scalar.copy(out=Z[nxt][:], in_=X[:, 0:128]), reads=[X], writes=[Z[nxt]])
                        if lev < 5:
                            fw.op("dve", lambda: nc.vector.tensor_copy(out=Y[nxt][:], in_=E1[:, 128:256]), reads=[E1], writes=[Y[nxt]])
                        fw.op("pe", lambda: MM(nc, E1[:, 256:384], lhsT=Z[nxt][:], rhs=P[cur][:], start=True, stop=True), reads=[Z[nxt], P[cur]], writes=[E1])
                        fw.op("dve", lambda: nc.vector.tensor_tensor(out=P[nxt][:], in0=E1[:, 256:384], in1=P[cur][:], op=ALU.add), reads=[E1, P[cur]], writes=[P[nxt]])
                        cur = nxt
                        yield
                    TT = P[cur]
                    fw.op("pe", lambda: MM(nc, X[:, 128:192], lhsT=kapt[:, cs], rhs=ident[0:64, 0:64], start=True, stop=True), reads=[kapt, ident], writes=[X])
                    fw.op("pe", lambda: MM(nc, X[:, 192:256], lhsT=v_s[:, cs], rhs=ident[0:64, 0:64], start=True, stop=True), reads=[v_s, ident], writes=[X])
                    fw.op("pe", lambda: MM(nc, X[:, 256:320], lhsT=kh[:, cs], rhs=ident[0:64, 0:64], start=True, stop=True), reads=[kh, ident], writes=[X])
                    fw.op("pe", lambda: MM(nc, X[:, 320:384], lhsT=bh[:, cs], rhs=ident[0:64, 0:64], start=True, stop=True), reads=[bh, ident], writes=[X])
                    fw.op("act", lambda: nc.scalar.copy(out=ktok[:, 0:64], in_=X[:, 128:192]), reads=[X], writes=[ktok])
                    fw.op("act", lambda: nc.scalar.copy(out=vtok[:], in_=X[:, 192:256]), reads=[X], writes=[vtok])
                    fw.op("act", lambda: nc.scalar.copy(out=bhtok[:], in_=X[:, 320:384]), reads=[X], writes=[bhtok])
                    for c in range(2):
                        fw.op("act", lambda: nc.scalar.activation(out=khc[c][:], in_=X[:, 256:320], func=AF.Identity, scale=RM[:, c:c + 1]), reads=[X, RM], writes=[khc[c]])
                        fw.op("act", lambda: nc.scalar.activation(out=bhc[c][:], in_=X[:, 320:384], func=AF.Identity, scale=RM[:, c:c + 1]), reads=[X, RM], writes=[bhc[c]])
                    fw.op("pe", lambda: MM(nc, E1[:, 384:448], lhsT=SCb[:, 0:128], rhs=vtok[:], start=True, stop=True), reads=[SCb, vtok], writes=[E1])
                    fw.op("dve", lambda: nc.vector.tensor_copy(out=ktok[:, 64:128], in_=E1[:, 384:448]), reads=[E1], writes=[ktok])
                    fw.op("pe", lambda: MM(nc, E2[:, 0:128], lhsT=TT[:], rhs=ktok[:], start=True, stop=True), reads=[TT, ktok], writes=[E2])
                    fw.op("dve", lambda: nc.vector.tensor_copy(out=WU[:], in_=E2[:, 0:128]), reads=[E2], writes=[WU])
                    fw.op("dve", lambda: nc.vector.tensor_scalar(out=nWU[:], in0=E2[:, 0:128], scalar1=-1.0, scalar2=None, op0=ALU.mult), reads=[E2], writes=[nWU])
                    for c in range(2):
                        fw.op("dve", lambda: nc.vector.tensor_scalar(out=nWc[c][:], in0=E2[:, 0:64], scalar1=RM[:, 2 + c:3 + c], scalar2=None, op0=ALU.mult), reads=[E2, RM], writes=[nWc[c]])
                    yield
                    fw.op("pe", lambda: MM(nc, X[0:64, 384:512], lhsT=ident[:, 0:64], rhs=rt[:, cs], start=True, stop=False), reads=[ident, rt], writes=[X])
                    fw.op("pe", lambda: MM(nc, X[0:64, 384:512], lhsT=nWU[:, 0:64], rhs=SCa[:, 128:256], start=False, stop=True), reads=[nWU, SCa], writes=[X])
                    fw.op("act", lambda: nc.scalar.copy(out=Rp[0:64, :], in_=X[0:64, 384:512]), reads=[X], writes=[Rp])
                    yield
                    for c in range(2):
                        rs_ = slice(c * 64, (c + 1) * 64)
                        gi = tl * 2 + c
                        fw.op("dve", lambda: nc.vector.tensor_scalar(out=dg[0:64, :], in0=ident[0:64, 0:64], scalar1=gam[:, gi:gi + 1], scalar2=None, op0=ALU.mult), reads=[ident, gam], writes=[dg])
                        fw.op("pe", lambda: MM(nc, E2[0:64, 128 + c * 128:128 + c * 128 + 64], lhsT=ident[:, 0:64], rhs=dg[:], start=True, stop=False), reads=[ident, dg], writes=[E2])
                        fw.op("pe", lambda: MM(nc, E2[0:64, 128 + c * 128:128 + c * 128 + 64], lhsT=nWc[c][:], rhs=bhtok[:], start=False, stop=True), reads=[nWc[c], bhtok], writes=[E2])
                        fw.op("pe", lambda: MM(nc, E2[0:64, 128 + c * 128 + 64:128 + c * 128 + 128], lhsT=khc[c][:], rhs=vtok[:], start=True, stop=False), reads=[khc[c], vtok], writes=[E2])
                        fw.op("pe", lambda: MM(nc, E2[0:64, 128 + c * 128 + 64:128 + c * 128 + 128], lhsT=bhc[c][:], rhs=nWU[:, 64:128], start=False, stop=True), reads=[bhc[c], nWU], writes=[E2])
                        fw.op("dve", lambda: nc.vector.tensor_copy(out=PTm[c][:], in_=E2[0:64, 128 + c * 128:128 + c * 128 + 64]), reads=[E2], writes=[PTm[c]])
                        fw.op("dve", lambda: nc.vector.tensor_copy(out=Qm[c][:], in_=E2[0:64, 128 + c * 128 + 64:128 + c * 128 + 128]), reads=[E2], writes=[Qm[c]])
                    yield
                    for c in range(2):
                        ysl = slice(128 + c * 64, 128 + (c + 1) * 64)
                        fw.op("pe", lambda: MM(nc, E2[0:64, 384 + c * 64:384 + (c + 1) * 64], lhsT=vtok[:], rhs=SCb[:, ysl], start=True, stop=False), reads=[vtok, SCb], writes=[E2])
                        fw.op("pe", lambda: MM(nc, E2[0:64, 384 + c * 64:384 + (c + 1) * 64], lhsT=nWU[:, 64:128], rhs=SCa[:, ysl], start=False, stop=False), reads=[nWU, SCa], writes=[E2])
                        fw.op("pe", lambda: MM(nc, E2[0:64, 384 + c * 64:384 + (c + 1) * 64], lhsT=Mh[:], rhs=Rp[:, c * 64:(c + 1) * 64], start=False, stop=True),
                              reads=[Mh, Rp], writes=[E2])
                        fw.op("pe", lambda: MM(nc, E1[0:64, 448:512], lhsT=PTm[c][:], rhs=Mh[0:64, :], start=True, stop=True), reads=[PTm[c], Mh], writes=[E1])
                        fw.op("dve", lambda: nc.vector.tensor_tensor(out=Mh[0:64, :], in0=E1[0:64, 448:512], in1=Qm[c][:], op=ALU.add), reads=[E1, Qm[c]], writes=[Mh])
                    fw.op("dve", lambda: nc.vector.tensor_copy(out=yo[:, cs], in_=E2[0:64, 384:512]), reads=[E2], writes=[yo])
                yield
                fw.op("pe", lambda: MM(nc, E0[0:64, :], lhsT=ones64R[:], rhs=yo[:], start=True, stop=True), reads=[ones64R, yo], writes=[E0])
                fw.op("dve", lambda: nc.vector.scalar_tensor_tensor(out=t1[:], in0=E0[0:64, :], scalar=-1.0 / 64, in1=yo[:], op0=ALU.mult, op1=ALU.add), reads=[E0, yo], writes=[t1])
                fw.op("act", lambda: nc.scalar.activation(out=sq_r[:], in_=t1[:], func=AF.Square), reads=[t1], writes=[sq_r])
                fw.op("pe", lambda: MM(nc, E1[0:64, :], lhsT=ones64R[:], rhs=sq_r[:], start=True, stop=True), reads=[ones64R, sq_r], writes=[E1])
                fw.op("dve", lambda: nc.vector.tensor_scalar(out=t2[:], in0=E1[0:64, :], scalar1=1.0 / 64, scalar2=64e-5, op0=ALU.mult, op1=ALU.add), reads=[E1], writes=[t2])
                fw.op("act", lambda: nc.scalar.activation(out=t2[:], in_=t2[:], func=AF.Sqrt), reads=[t2], writes=[t2])
                fw.op("dve", lambda: nc.vector.reciprocal(out=t2[:], in_=t2[:]), reads=[t2], writes=[t2])
                fw.op("dve", lambda: nc.vector.scalar_tensor_tensor(out=t1[:], in0=t1[:], scalar=lnw[:, h:h + 1], in1=t2[:], op0=ALU.mult, op1=ALU.mult), reads=[t1, t2, lnw], writes=[t1])
                fw.op("dve", lambda: nc.vector.scalar_tensor_tensor(out=sq_r[:], in0=r_s[:], scalar=r_k[:, h:h + 1], in1=kp[:], op0=ALU.mult, op1=ALU.mult), reads=[r_s, kp, r_k], writes=[sq_r])
                fw.op("pe", lambda: MM(nc, E2[0:64, :], lhsT=ones64R[:], rhs=sq_r[:], start=True, stop=True), reads=[ones64R, sq_r], writes=[E2])
                fw.op("dve", lambda: nc.vector.tensor_tensor(out=t2[:], in0=E2[0:64, :], in1=v_s[:], op=ALU.mult), reads=[E2, v_s], writes=[t2])
                fw.op("dve", lambda: nc.vector.scalar_tensor_tensor(out=t1[:], in0=t1[:], scalar=lnb[:, h:h + 1], in1=t2[:], op0=ALU.add, op1=ALU.add), reads=[t1, t2, lnb], writes=[t1])
                yb = ybuf
                fw.op("dve", lambda: nc.vector.tensor_tensor(out=yb[:], in0=t1[:], in1=g_[:], op=ALU.mult), reads=[t1, g_], writes=[yb])
                fw.dma("sp", YT.t.ap()[2, h * 64:(h + 1) * 64, tb * TB:(tb + 1) * TB], yb[:], reads=[yb], writes=[k.YTtok[2][h // 2][tb]])


            for hp in range(0, NHX, 2):
                gens = [unit(0, hp)] + ([unit(1, hp + 1)] if hp + 1 < NHX else [])
                live = list(gens)
                while live:
                    for g_i in list(live):
                        try:
                            next(g_i)
                        except StopIteration:
                            live.remove(g_i)

import numpy as np, os

DFF = 2816
NFF = DFF // 128


def setup_p5(nc, fw, k):
    def din(name, shape, dt=F32):
        return fw.dram(name, shape, dt, kind="ExternalInput")
    k.w_branch = din("w_branch", [4, 3, 512, D])
    k.w_out = din("w_out", [4, D, D])
    k.ffn_w1 = din("ffn_w1", [4, D, DFF]); k.ffn_w3 = din("ffn_w3", [4, D, DFF]); k.ffn_w2 = din("ffn_w2", [4, DFF, D])
    k.final_w = din("final_norm_w", [D])


def norm_mod_tok(nc, fw, k, XT, xtok, g, s, hT, out_dram=None, out_tok=None):
    for tb in range(NTB):
        xs = k.xs_bufs[tb % 2]
        fw.dma("sp", xs[:], XT.t.ap()[:, tb * TB:(tb + 1) * TB].rearrange("(k p) t -> p k t", p=128), reads=[xtok[tb]], writes=[xs])
        sq = k.sq_buf
        fw.op("act", lambda: nc.scalar.activation(out=sq[:], in_=xs[:], func=AF.Square), reads=[xs], writes=[sq])
        pst = k.ps[7]
        for kc in range(8):
            fw.op("pe", lambda: nc.tensor.matmul(pst[:], lhsT=k.ones[:], rhs=sq[:, kc, :], start=(kc == 0), stop=(kc == 7)), reads=[k.ones, sq], writes=[pst])
        rstd = k.rstd_buf
        fw.op("dve", lambda: nc.vector.tensor_scalar(out=rstd[:], in0=pst[:], scalar1=1.0 / D, scalar2=1e-6, op0=ALU.mult, op1=ALU.add), reads=[pst], writes=[rstd])
        fw.op("act", lambda: nc.scalar.activation(out=rstd[:], in_=rstd[:], func=AF.Sqrt), reads=[rstd], writes=[rstd])
        fw.op("dve", lambda: nc.vector.reciprocal(out=rstd[:], in_=rstd[:]), reads=[rstd], writes=[rstd])
        for kc in range(8):
            if out_dram is None:
                tmp = k.tmp_bufs[kc % 2]
                fw.op("dve", lambda: nc.vector.scalar_tensor_tensor(out=tmp[:], in0=xs[:, kc, :], scalar=g[:, kc:kc + 1], in1=rstd[:], op0=ALU.mult, op1=ALU.mult),
                      reads=[xs, rstd, k.gsrc], writes=[tmp])
                fw.op("act", lambda: nc.scalar.activation(out=hT[:, kc, tb * TB:(tb + 1) * TB], in_=tmp[:], func=AF.Identity, bias=s[:, kc:kc + 1], scale=1.0),
                      reads=[tmp, k.gsrc], writes=[hT])
            else:
                fw.op("dve", lambda: nc.vector.scalar_tensor_tensor(out=sq[:, kc, :], in0=xs[:, kc, :], scalar=g[:, kc:kc + 1], in1=rstd[:], op0=ALU.mult, op1=ALU.mult),
                      reads=[xs, rstd, k.gsrc], writes=[sq])
        if out_dram is not None:
            fw.dma("sp", out_dram.t.ap()[:, tb * TB:(tb + 1) * TB].rearrange("(k p) t -> p k t", p=128), sq[:], reads=[sq], writes=[out_tok[tb]])


def proj_out(nc, fw, k, l, PT, YT, XTin, xin_tok, XTout, xout_tok):
    with fw.scope():
        wbr = fw.sbuf("p5_wbr", [128, 12, D], BF16)
        wo = fw.sbuf("p5_wo", [128, 8, D], BF16)
        stg = [fw.sbuf(f"p5_stg{i}", [128, 4, D]) for i in range(2)]
        for br in range(3):
            st = stg[br % 2]
            fw.dma("sp", st[:], k.w_branch.t.ap()[l, br].rearrange("(k p) n -> p k n", p=128), writes=[st])
            e = "pool" if br % 2 == 0 else "dve"
            if e == "pool":
                fw.op("pool", lambda: nc.gpsimd.tensor_copy(out=wbr[:, br * 4:(br + 1) * 4, :], in_=st[:]), reads=[st], writes=[wbr])
            else:
                fw.op("dve", lambda: nc.vector.tensor_copy(out=wbr[:, br * 4:(br + 1) * 4, :], in_=st[:]), reads=[st], writes=[wbr])
        for hf in range(2):
            st = stg[(hf + 1) % 2]
            fw.dma("sp", st[:], k.w_out.t.ap()[l, hf * 512:(hf + 1) * 512, :].rearrange("(k p) n -> p k n", p=128), writes=[st])
            fw.op("pool", lambda: nc.gpsimd.tensor_copy(out=wo[:, hf * 4:(hf + 1) * 4, :], in_=st[:]), reads=[st], writes=[wo])
        yb = [[fw.sbuf(f"p5_y{i}_{m}", [128, 4, TB], BF16) for m in range(3)] for i in range(2)]
        gt = [fw.sbuf(f"p5_g{i}", [128, TB]) for i in range(3)]
        mg = fw.sbuf("p5_mg", [128, 8, TB], BF16)
        acc = fw.sbuf("p5_acc", [128, TB]); tt = [fw.sbuf(f"p5_tt{i}", [128, TB]) for i in range(2)]
        xs = [fw.sbuf(f"p5_xs{i}", [128, 8, TB]) for i in range(2)]
        gt1 = k.ada[:, l, 16:24]
        for tb in range(NTB):
            y = yb[tb % 2]
            for m in range(3):
                fw.dma("sp", y[m][:], YT.t.ap()[m, :, tb * TB:(tb + 1) * TB].rearrange("(k p) t -> p k t", p=128),
                       reads=[k.YTtok[m][c][tb] for c in range(4)], writes=[y[m]])
            x_ = xs[tb % 2]
            fw.dma("sp", x_[:], XTin.t.ap()[:, tb * TB:(tb + 1) * TB].rearrange("(k p) t -> p k t", p=128), reads=[xin_tok[tb]], writes=[x_])
            for n in range(8):
                for br in range(3):
                    ch = 41 + br * 8 + n
                    fw.dma("sp", gt[br][:], PT.t.ap()[ch * 128:(ch + 1) * 128, tb * TB:(tb + 1) * TB], reads=[k.PTtok[ch][tb]], writes=[gt[br]])
                    fw.op("act", lambda: nc.scalar.activation(out=gt[br][:], in_=gt[br][:], func=AF.Sigmoid), reads=[gt[br]], writes=[gt[br]])
                    pst = k.ps[3 + br]
                    for kc in range(4):
                        fw.op("pe", lambda: nc.tensor.matmul(pst[:], lhsT=wbr[:, br * 4 + kc, n * 128:(n + 1) * 128], rhs=y[br][:, kc, :], start=(kc == 0), stop=(kc == 3)),
                              reads=[wbr, y[br]], writes=[pst])
                fw.op("dve", lambda: nc.vector.tensor_tensor(out=acc[:], in0=k.ps[3][:], in1=gt[0][:], op=ALU.mult), reads=[k.ps[3], gt[0]], writes=[acc])
                fw.op("dve", lambda: nc.vector.tensor_tensor(out=tt[0][:], in0=k.ps[4][:], in1=gt[1][:], op=ALU.mult), reads=[k.ps[4], gt[1]], writes=[tt[0]])
                fw.op("dve", lambda: nc.vector.tensor_tensor(out=tt[1][:], in0=k.ps[5][:], in1=gt[2][:], op=ALU.mult), reads=[k.ps[5], gt[2]], writes=[tt[1]])
                fw.op("pool", lambda: nc.gpsimd.tensor_tensor(out=acc[:], in0=acc[:], in1=tt[0][:], op=ALU.add), reads=[acc, tt[0]], writes=[acc])
                fw.op("pool", lambda: nc.gpsimd.tensor_tensor(out=mg[:, n, :], in0=acc[:], in1=tt[1][:], op=ALU.add), reads=[acc, tt[1]], writes=[mg])
            for n in range(8):
                pst = k.ps[6 + n % 2]
                for kc in range(8):
                    fw.op("pe", lambda: nc.tensor.matmul(pst[:], lhsT=wo[:, kc, n * 128:(n + 1) * 128], rhs=mg[:, kc, :], start=(kc == 0), stop=(kc == 7)), reads=[wo, mg], writes=[pst])
                fw.op("dve", lambda: nc.vector.scalar_tensor_tensor(out=x_[:, n, :], in0=pst[:], scalar=gt1[:, n:n + 1], in1=x_[:, n, :], op0=ALU.mult, op1=ALU.add),
                      reads=[pst, x_, k.gsrc], writes=[x_])
            fw.dma("sp", XTout.t.ap()[:, tb * TB:(tb + 1) * TB].rearrange("(k p) t -> p k t", p=128), x_[:], reads=[x_], writes=[xout_tok[tb]])


def ffn(nc, fw, k, l, XT1, x1_tok, XT2, x2_tok, UT):
    with fw.scope():
        alloc_p1_small(fw, k)
        hT = fw.sbuf("f_hT", [128, 8, T], BF16)
        norm_mod_tok(nc, fw, k, XT1, x1_tok, k.g2[:, l, :], k.ada[:, l, 24:32], hT)
        stg = [fw.sbuf(f"f_stg{i}", [128, 8, 256]) for i in range(2)]
        w1b = [fw.sbuf(f"f_w1b{i}", [128, 8, 256], BF16) for i in range(2)]
        w3b = [fw.sbuf(f"f_w3b{i}", [128, 8, 256], BF16) for i in range(2)]
        sl = [fw.sbuf(f"f_sl{i}", [128, TB]) for i in range(2)]
        ub = [fw.sbuf(f"f_ub{i}", [128, TB], BF16) for i in range(2)]
        ev = 0
        for fg in range(NFF // 2):
            wa, wc = w1b[fg % 2], w3b[fg % 2]
            fw.dma("sp", stg[0][:], k.ffn_w1.t.ap()[l, :, fg * 256:(fg + 1) * 256].rearrange("(k p) n -> p k n", p=128), writes=[stg[0]])
            fw.op("pool", lambda: nc.gpsimd.tensor_copy(out=wa[:], in_=stg[0][:]), reads=[stg[0]], writes=[wa])
            fw.dma("sp", stg[1][:], k.ffn_w3.t.ap()[l, :, fg * 256:(fg + 1) * 256].rearrange("(k p) n -> p k n", p=128), writes=[stg[1]])
            fw.op("pool", lambda: nc.gpsimd.tensor_copy(out=wc[:], in_=stg[1][:]), reads=[stg[1]], writes=[wc])
            for tb in range(NTB):
                for c in range(2):
                    pa = k.ps[ev % 2]
                    pd = k.ps[3 + ev % 2]
                    for kc in range(8):
                        fw.op("pe", lambda: nc.tensor.matmul(pa[:], lhsT=wa[:, kc, c * 128:(c + 1) * 128], rhs=hT[:, kc, tb * TB:(tb + 1) * TB], start=(kc == 0), stop=(kc == 7)),
                              reads=[wa, hT], writes=[pa])
                    for kc in range(8):
                        fw.op("pe", lambda: nc.tensor.matmul(pd[:], lhsT=wc[:, kc, c * 128:(c + 1) * 128], rhs=hT[:, kc, tb * TB:(tb + 1) * TB], start=(kc == 0), stop=(kc == 7)),
                              reads=[wc, hT], writes=[pd])
                    s_ = sl[ev % 2]; u_ = ub[ev % 2]
                    fw.op("act", lambda: nc.scalar.activation(out=s_[:], in_=pa[:], func=AF.Silu), reads=[pa], writes=[s_])
                    fw.op("dve", lambda: nc.vector.tensor_tensor(out=u_[:], in0=pd[:], in1=s_[:], op=ALU.mult), reads=[pd, s_], writes=[u_])
                    ch = fg * 2 + c
                    fw.dma("sp", UT.t.ap()[ch * 128:(ch + 1) * 128, tb * TB:(tb + 1) * TB], u_[:], reads=[u_], writes=[k.UTtok[ch][tb]])
                    ev += 1
    with fw.scope():
        w2b = fw.sbuf("f_w2b", [128, NFF, D], BF16)
        stg = [fw.sbuf(f"f_stgb{i}", [128, 2, D]) for i in range(2)]
        for g in range(NFF // 2):
            st = stg[g % 2]
            fw.dma("sp", st[:], k.ffn_w2.t.ap()[l, g * 256:(g + 1) * 256, :].rearrange("(k p) n -> p k n", p=128), writes=[st])
            if g % 2 == 0:
                fw.op("pool", lambda: nc.gpsimd.tensor_copy(out=w2b[:, g * 2:(g + 1) * 2, :], in_=st[:]), reads=[st], writes=[w2b])
            else:
                fw.op("dve", lambda: nc.vector.tensor_copy(out=w2b[:, g * 2:(g + 1) * 2, :], in_=st[:]), reads=[st], writes=[w2b])
        ub = [fw.sbuf(f"f_ublk{i}", [128, NFF, TB], BF16) for i in range(2)]
        xs = [fw.sbuf(f"f_xs{i}", [128, 8, TB]) for i in range(2)]
        gt2 = k.ada[:, l, 40:48]
        for tb in range(NTB):
            u_ = ub[tb % 2]; x_ = xs[tb % 2]
            fw.dma("sp", u_[:], UT.t.ap()[:, tb * TB:(tb + 1) * TB].rearrange("(k p) t -> p k t", p=128), reads=[k.UTtok[c][tb] for c in range(NFF)], writes=[u_])
            fw.dma("sp", x_[:], XT1.t.ap()[:, tb * TB:(tb + 1) * TB].rearrange("(k p) t -> p k t", p=128), reads=[x1_tok[tb]], writes=[x_])
            for n in range(8):
                pst = k.ps[3 + n % 2]
                for kc in range(NFF):
                    fw.op("pe", lambda: nc.tensor.matmul(pst[:], lhsT=w2b[:, kc, n * 128:(n + 1) * 128], rhs=u_[:, kc, :], start=(kc == 0), stop=(kc == NFF - 1)),
                          reads=[w2b, u_], writes=[pst])
                fw.op("dve", lambda: nc.vector.scalar_tensor_tensor(out=x_[:, n, :], in0=pst[:], scalar=gt2[:, n:n + 1], in1=x_[:, n, :], op0=ALU.mult, op1=ALU.add),
                      reads=[pst, x_, k.gsrc], writes=[x_])
            fw.dma("sp", XT2.t.ap()[:, tb * TB:(tb + 1) * TB].rearrange("(k p) t -> p k t", p=128), x_[:], reads=[x_], writes=[x2_tok[tb]])


def alloc_p1_small(fw, k):
    _x = fw.sbuf("xs0", [128, 8, TB])
    k.xs_bufs = [_x, _x]
    k.sq_buf = fw.sbuf("sq", [128, 8, TB])
    k.rstd_buf = fw.sbuf("rstd", [128, TB])
    k.tmp_bufs = [fw.sbuf(f"tmp{i}", [128, TB]) for i in range(2)]


def in_proj_phase(nc, fw, k, l, XTin, xin_tok, PT):
    with fw.scope():
        alloc_p1_small(fw, k)
        hT = fw.sbuf("hT", [128, 8, T], BF16)
        _w = fw.sbuf("wst0", [128, 8, 640])
        k.w_st = [_w, _w]
        k.w_bf = [fw.sbuf(f"wbf{i}", [128, 8, 640], BF16) for i in range(2)]
        k.ev_bufs = [fw.sbuf(f"ev{i}", [128, TB]) for i in range(4)]
        norm_mod_tok(nc, fw, k, XTin, xin_tok, k.g1[:, l, :], k.ada[:, l, 0:8], hT)
        in_proj(nc, fw, k, l, hT, PT)


def final_norm(nc, fw, k, XT, xtok, OUT, otok):
    with fw.scope():
        alloc_p1_small(fw, k)
        fwt = fw.sbuf("fin_w", [128, 8])
        fw.dma("sp", fwt[:], k.final_w.t.ap().rearrange("(j p) -> p j", p=128), writes=[fwt], allow_slow_non_contiguous=True)
        old = k.gsrc
        k.gsrc = fwt
        norm_mod_tok(nc, fw, k, XT, xtok, fwt[:, :], None, None, out_dram=OUT, out_tok=otok)
        k.gsrc = old

import numpy as np, os

BIG = 1.0e30
NEGM = -240000.0
NQ = T // 128


def t5_bucket_np(dist):
    n = np.maximum(dist, 0)
    nf = np.maximum(n, 1).astype(np.float32)
    large = 16 + (np.log(nf / np.float32(16)) / np.float32(np.log(128 / 16)) * np.float32(16)).astype(np.int32)
    large = np.minimum(large, 31)
    return np.where(n < 16, n, large)


def nsa_consts(rel_bias):
    c = {}
    kq = np.arange(128)
    dD = kq[None, :] - kq[:, None]
    c["nsa_tabD"] = np.ascontiguousarray(np.transpose(rel_bias[t5_bucket_np(dD)], (2, 0, 1))).astype(np.float32)
    c["nsa_maskD"] = (dD >= 0).astype(np.float32)
    c["nsa_tabP"] = np.ascontiguousarray(np.transpose(rel_bias[t5_bucket_np(dD + 128)], (2, 0, 1))).astype(np.float32)
    c["nsa_maskW4"] = (dD < 0).astype(np.float32)
    m = np.arange(504)
    dC = kq[None, :] - 16 * (m[:, None] - 248) - 31
    c["nsa_tabC"] = np.ascontiguousarray(np.transpose(rel_bias[t5_bucket_np(dC)], (2, 0, 1))).astype(np.float32)
    c["nsa_maskC"] = (dC >= 0).astype(np.float32)
    c["nsa_b31"] = np.ascontiguousarray(rel_bias[31:32, :]).astype(np.float32)
    u = np.arange(126) - 62
    cur = (kq >= 64).astype(np.int64)
    A = np.zeros((128, 126), np.float32)
    A[(u[None, :] == cur[:, None]) | (u[None, :] == cur[:, None] - 1)] = BIG
    A[u[None, :] > cur[:, None]] = -BIG
    c["nsa_A"] = A
    n = np.arange(256)
    mm = np.arange(64)
    cov = ((16 * n[:, None] < 64 * mm[None, :] + 64) & (16 * n[:, None] + 32 > 64 * mm[None, :]) & (n[:, None] < 255)).astype(np.float32)
    c["nsa_cover"] = cov
    keys = np.arange(T)
    c["nsa_xexp"] = (keys[None, :] // 64 == mm[:, None]).astype(np.float32)
    sel = np.zeros((24, 24 * 64), np.float32)
    for r in range(24):
        sel[r, r * 64:(r + 1) * 64] = 1.0
    c["nsa_selall"] = sel
    return c


def setup_nsa(nc, fw, k):
    def din(name, shape, dt=F32):
        return fw.dram(name, shape, dt, kind="ExternalInput")
    k.nsa_in = {}
    for nm, shp in (("nsa_pe_k", [4, 32, 64]), ("nsa_cmp_w1_k", [4, 2048, 256]), ("nsa_cmp_w2_k", [4, 256, 64]),
                    ("nsa_pe_v", [4, 32, 64]), ("nsa_cmp_w1_v", [4, 2048, 256]), ("nsa_cmp_w2_v", [4, 256, 64])):
        k.nsa_in[nm] = din(nm, shp)
    tabD = din("nsa_tabD", [8, 128, 128]); maskD = din("nsa_maskD", [128, 128]); tabP = din("nsa_tabP", [8, 128, 128]); maskW4 = din("nsa_maskW4", [128, 128])
    tabC = din("nsa_tabC", [8, 504, 128]); maskC = din("nsa_maskC", [504, 128]); b31 = din("nsa_b31", [1, 8])
    A = din("nsa_A", [128, 126]); cover = din("nsa_cover", [256, 64]); xexp = din("nsa_xexp", [64, T]); selall = din("nsa_selall", [24, 24 * 64])
    k.GcT = fw.dram("nsa_GcT", [8, 504, 128], F32)
    k.GcTtok = Buf(None, "gct")
    n = k.nsa = {}
    n["Ed"] = fw.sbuf("n_Ed", [128, 8, 128]); n["Ep"] = fw.sbuf("n_Ep", [128, 8, 128]); n["W4"] = fw.sbuf("n_W4", [128, 128])
    n["A"] = fw.sbuf("n_A", [128, 126]); n["cover"] = fw.sbuf("n_cover", [128, 2, 64]); n["xexp"] = fw.sbuf("n_xexp", [64, T], BF16); n["selall"] = fw.sbuf("n_selall", [24, 24 * 64])
    n["nb31"] = fw.sbuf("n_nb31", [128, 8])
    fw.dma("sp", n["A"][:], A.t.ap()[:, :], writes=[n["A"]])
    fw.dma("sp", n["cover"][:], cover.t.ap().rearrange("(a p) m -> p a m", p=128), writes=[n["cover"]])
    fw.dma("sp", n["selall"][:], selall.t.ap()[:, :], writes=[n["selall"]])
    fw.dma("sp", n["W4"][:], maskW4.t.ap()[:, :], writes=[n["W4"]])
    fw.dma("sp", n["nb31"][:], b31.t.ap()[0:1, :].partition_broadcast(128), writes=[n["nb31"]])
    fw.op("dve", lambda: nc.vector.tensor_scalar(out=n["nb31"][:], in0=n["nb31"][:], scalar1=-1.0, scalar2=None, op0=ALU.mult), reads=[n["nb31"]], writes=[n["nb31"]])
    with fw.scope():
        st = fw.sbuf("n_st", [64, T])
        fw.dma("sp", st[:], xexp.t.ap()[:, :], writes=[st])
        fw.op("dve", lambda: nc.vector.tensor_copy(out=n["xexp"][:], in_=st[:]), reads=[st], writes=[n["xexp"]])
        mD = fw.sbuf("n_mD", [128, 128]); fw.dma("sp", mD[:], maskD.t.ap()[:, :], writes=[mD])
        raw = fw.sbuf("n_raw", [128, 8, 128])
        for (src, dst, msk) in ((tabD, n["Ed"], mD), (tabP, n["Ep"], None)):
            fw.dma("sp", raw[:], src.t.ap().rearrange("h k q -> k h q"), writes=[raw])
            for h in range(8):
                fw.op("act", lambda: nc.scalar.activation(out=dst[:, h, :], in_=raw[:, h, :], func=AF.Exp, bias=n["nb31"][:, h:h + 1], scale=1.0), reads=[raw, n["nb31"]], writes=[dst])
                if msk is not None:
                    fw.op("dve", lambda: nc.vector.tensor_tensor(out=dst[:, h, :], in0=dst[:, h, :], in1=msk[:], op=ALU.mult), reads=[dst, msk], writes=[dst])
        rawc = fw.sbuf("n_rawc", [126, 4, 128]); mC = fw.sbuf("n_mC", [126, 4, 128])
        fw.dma("sp", mC[:], maskC.t.ap().rearrange("(a p) q -> p a q", p=126), writes=[mC])
        for h in range(8):
            fw.dma("sp", rawc[:], tabC.t.ap()[h].rearrange("(a p) q -> p a q", p=126), writes=[rawc])
            fw.op("act", lambda: nc.scalar.activation(out=rawc[:], in_=rawc[:], func=AF.Exp, bias=n["nb31"][0:126, h:h + 1], scale=1.0), reads=[rawc, n["nb31"]], writes=[rawc])
            fw.op("dve", lambda: nc.vector.tensor_tensor(out=rawc[:], in0=rawc[:], in1=mC[:], op=ALU.mult), reads=[rawc, mC], writes=[rawc])
            fw.dma("sp", k.GcT.t.ap()[h].rearrange("(a p) q -> p a q", p=126), rawc[:], reads=[rawc], writes=[k.GcTtok])


def nsa(nc, fw, k, l, PT, YT, GX=2, IQ=None):
    n = k.nsa
    ps = k.ps
    ident = k.ident
    IQ = list(range(NQ)) if IQ is None else IQ
    with fw.scope():
        bf = lambda nm, shp: fw.sbuf(nm, shp, BF16)
        stg = [fw.sbuf(f"n_stg{i}", [64, TB]) for i in range(2)]
        kcmp = bf("n_kcmp", [64, T]); vcmp = bf("n_vcmp", [64, T]); ksel = bf("n_ksel", [64, T]); kwin = bf("n_kwin", [64, T])
        vseltok = bf("n_vseltok", [128, NQ, 64]); vwintok = bf("n_vwintok", [128, NQ, 64])
        qb = bf("n_qb", [64, 4, T])
        kcT = bf("n_kcT", [64, 256]); vctok = bf("n_vctok", [128, 2, 64])
        w1s = fw.sbuf("n_w1s", [64, 16, 256]); w1b = bf("n_w1b", [64, 32, 256]); w2s = fw.sbuf("n_w2s", [128, 2, 64]); w2b = bf("n_w2b", [128, 2, 64])
        peT = fw.sbuf("n_peT", [64, 32]); peTb = bf("n_peTb", [64, 32]); biasc = fw.sbuf("n_biasc", [128, 2])
        aT = bf("n_aT", [128, 2, 256])
        sgT = fw.sbuf("n_sgT", [24, T])
        onesb = k.onesb
        Ef = [fw.sbuf(f"n_Ef{i}", [128, 512]) for i in range(2)]; Pb = [bf(f"n_Pb{i}", [128, 512]) for i in range(2)]
        Gt = [fw.sbuf(f"n_Gt{i}", [128, 4, 128]) for i in range(2)]
        Pc = [fw.sbuf(f"n_Pc{i}", [128, 512]) for i in range(2)]; Pcb = [bf(f"n_Pcb{i}", [128, 512]) for i in range(2)]
        rd = fw.sbuf("n_rd", [128, 512])
        sc = fw.sbuf("n_sc", [128, 64]); sc2 = fw.sbuf("n_sc2", [128, 64]); m8 = fw.sbuf("n_m8", [128, 16]); negq = fw.sbuf("n_negq", [128, 64])
        negT4 = bf("n_negT4", [64, 4, 128])
        acc = fw.sbuf("n_acc", [64, 512]); wt = fw.sbuf("n_wt", [64, 512]); ot = fw.sbuf("n_ot", [64, 512]); yb = [bf(f"n_yb{i}", [64, 512]) for i in range(2)]
        fw.op("dve", lambda: nc.vector.memset(kcT[:], 0.0), writes=[kcT])
        for tb in range(NTB):
            fw.dma("sp", sgT[:, tb * TB:(tb + 1) * TB], PT.t.ap()[26 * 128:26 * 128 + 24, tb * TB:(tb + 1) * TB], reads=[k.PTtok[26][tb]], writes=[sgT])
        fw.op("act", lambda: nc.scalar.activation(out=sgT[:], in_=sgT[:], func=AF.Sigmoid), reads=[sgT], writes=[sgT])
        evi = 0
        for g in range(GX):
            def load_stream(dst_ap_fn, ch, row0):
                for tb in range(NTB):
                    s_ = stg[tb % 2]
                    r0 = ch * 128 + row0
                    fw.dma("sp", s_[:], PT.t.ap()[r0:r0 + 64, tb * TB:(tb + 1) * TB], reads=[k.PTtok[ch][tb]], writes=[s_])
                    dst, dbuf = dst_ap_fn(tb)
                    if tb % 2 == 0:
                        fw.op("dve", lambda: nc.vector.tensor_copy(out=dst, in_=s_[:]), reads=[s_], writes=[dbuf])
                    else:
                        fw.op("pool", lambda: nc.gpsimd.tensor_copy(out=dst, in_=s_[:]), reads=[s_], writes=[dbuf])
            load_stream(lambda tb: (kcmp[:, tb * TB:(tb + 1) * TB], kcmp), 20, g * 64)
            load_stream(lambda tb: (vcmp[:, tb * TB:(tb + 1) * TB], vcmp), 21, g * 64)
            load_stream(lambda tb: (ksel[:, tb * TB:(tb + 1) * TB], ksel), 22, g * 64)
            load_stream(lambda tb: (kwin[:, tb * TB:(tb + 1) * TB], kwin), 24, g * 64)
            for j in range(4):
                hd = g * 4 + j
                load_stream(lambda tb: (qb[:, j, tb * TB:(tb + 1) * TB], qb), 16 + hd // 2, (hd % 2) * 64)
            for (ch, vt) in ((23, vseltok), (25, vwintok)):
                for tb in range(NTB):
                    s_ = stg[tb % 2]
                    r0 = ch * 128 + g * 64
                    fw.dma("sp", s_[:], PT.t.ap()[r0:r0 + 64, tb * TB:(tb + 1) * TB], reads=[k.PTtok[ch][tb]], writes=[s_])
                    for q4 in range(4):
                        fw.op("pe", lambda: nc.tensor.matmul(ps[5][:, q4 * 64:(q4 + 1) * 64], lhsT=s_[:, q4 * 128:(q4 + 1) * 128], rhs=ident[0:64, 0:64], start=True, stop=True),
                              reads=[s_, ident], writes=[ps[5]])
                    fw.op("dve", lambda: nc.vector.tensor_copy(out=vt[:, tb * 4:(tb + 1) * 4, :], in_=ps[5][:, 0:256].rearrange("p (a d) -> p a d", d=64)), reads=[ps[5]], writes=[vt])
            for (kv, src, w1n, w2n, pen) in (("k", kcmp, "nsa_cmp_w1_k", "nsa_cmp_w2_k", "nsa_pe_k"), ("v", vcmp, "nsa_cmp_w1_v", "nsa_cmp_w2_v", "nsa_pe_v")):
                for hf in range(2):
                    fw.dma("sp", w1s[:], k.nsa_in[w1n].t.ap()[l, hf * 1024:(hf + 1) * 1024, :].rearrange("(l d) c -> d l c", d=64), writes=[w1s])
                    fw.op("pool", lambda: nc.gpsimd.tensor_copy(out=w1b[:, hf * 16:(hf + 1) * 16, :], in_=w1s[:]), reads=[w1s], writes=[w1b])
                fw.dma("sp", w2s[:], k.nsa_in[w2n].t.ap()[l].rearrange("(a p) d -> p a d", p=128), writes=[w2s])
                fw.op("dve", lambda: nc.vector.tensor_copy(out=w2b[:], in_=w2s[:]), reads=[w2s], writes=[w2b])
                fw.dma("sp", peT[:], k.nsa_in[pen].t.ap()[l].rearrange("l d -> d l"), writes=[peT], allow_slow_non_contiguous=True)
                fw.op("dve", lambda: nc.vector.tensor_copy(out=peTb[:], in_=peT[:]), reads=[peT], writes=[peTb])
                src3 = src.t[:].rearrange("p (n s) -> p n s", s=16)
                for cc in range(2):
                    for li in range(32):
                        fw.op("pe", lambda: nc.tensor.matmul(ps[2][:, cc:cc + 1], lhsT=w1b[:, li, cc * 128:(cc + 1) * 128], rhs=peTb[:, li:li + 1], start=(li == 0), stop=(li == 31)),
                              reads=[w1b, peTb], writes=[ps[2]])
                fw.op("dve", lambda: nc.vector.tensor_copy(out=biasc[:], in_=ps[2][:, 0:2]), reads=[ps[2]], writes=[biasc])
                for cc in range(2):
                    for li in range(32):
                        rhs = src3[:, li // 16:li // 16 + 255, li % 16]
                        fw.op("pe", lambda: nc.tensor.matmul(ps[cc][:, 0:255], lhsT=w1b[:, li, cc * 128:(cc + 1) * 128], rhs=rhs, start=(li == 0), stop=(li == 31)),
                              reads=[w1b, src], writes=[ps[cc]])
                    fw.op("act", lambda: nc.scalar.activation(out=aT[:, cc, 0:255], in_=ps[cc][:, 0:255], func=AF.Silu, bias=biasc[:, cc:cc + 1], scale=1.0), reads=[ps[cc], biasc], writes=[aT])
                if kv == "k":
                    for cc in range(2):
                        fw.op("pe", lambda: nc.tensor.matmul(ps[3][0:64, 0:255], lhsT=w2b[:, cc, :], rhs=aT[:, cc, 0:255], start=(cc == 0), stop=(cc == 1)), reads=[w2b, aT], writes=[ps[3]])
                    fw.op("dve", lambda: nc.vector.tensor_copy(out=kcT[:, 0:255], in_=ps[3][0:64, 0:255]), reads=[ps[3]], writes=[kcT])
                else:
                    fw.op("dve", lambda: nc.vector.memset(vctok[:], 0.0), writes=[vctok])
                    for nt in range(2):
                        rows = 128 if nt == 0 else 127
                        for cc in range(2):
                            fw.op("pe", lambda: nc.tensor.matmul(ps[4][0:rows, nt * 64:(nt + 1) * 64], lhsT=aT[:, cc, nt * 128:nt * 128 + rows], rhs=w2b[:, cc, :], start=(cc == 0), stop=(cc == 1)),
                                  reads=[aT, w2b], writes=[ps[4]])
                        fw.op("dve", lambda: nc.vector.tensor_copy(out=vctok[0:rows, nt, :], in_=ps[4][0:rows, nt * 64:(nt + 1) * 64]), reads=[ps[4]], writes=[vctok])
            for i in IQ:
                Q = qb[:, :, i * 128:(i + 1) * 128]
                nts = [0] if i < 16 else [0, 1]
                for nt in nts:
                    sb = ps[evi % 2]; e_ = Pc[nt]; g_ = Gt[nt]
                    fw.op("pe", lambda: nc.tensor.matmul(sb[:], lhsT=kcT[:, nt * 128:(nt + 1) * 128], rhs=Q, start=True, stop=True), reads=[kcT, qb], writes=[sb])
                    r0 = 248 - 8 * i + nt * 128
                    fw.dma("sp", g_[:], k.GcT.t.ap()[g * 4:(g + 1) * 4, r0:r0 + 128, :].rearrange("h n q -> n h q"), reads=[k.GcTtok], writes=[g_])
                    fw.op("act", lambda: nc.scalar.activation(out=e_[:], in_=sb[:], func=AF.Exp, scale=0.125), reads=[sb], writes=[e_])
                    fw.op("dve", lambda: nc.vector.tensor_tensor(out=e_[:], in0=e_[:], in1=g_[:].rearrange("p h q -> p (h q)"), op=ALU.mult), reads=[e_, g_], writes=[e_])
                    fw.op("pool", lambda: nc.gpsimd.tensor_copy(out=Pcb[nt][:], in_=e_[:]), reads=[e_], writes=[Pcb[nt]])
                    evi += 1
                for x, nt in enumerate(nts):
                    fw.op("pe", lambda: nc.tensor.matmul(ps[3][:], lhsT=k.ones[:], rhs=Pc[nt][:], start=(x == 0), stop=(x == len(nts) - 1)), reads=[k.ones, Pc[nt]], writes=[ps[3]])
                for x, nt in enumerate(nts):
                    fw.op("pe", lambda: nc.tensor.matmul(ps[5][0:64, :], lhsT=vctok[:, nt, :], rhs=Pcb[nt][:], start=(x == 0), stop=(x == len(nts) - 1)), reads=[vctok, Pcb[nt]], writes=[ps[5]])
                fw.op("dve", lambda: nc.vector.tensor_scalar(out=rd[:], in0=ps[3][:], scalar1=1e-30, scalar2=None, op0=ALU.max), reads=[ps[3]], writes=[rd])
                fw.op("dve", lambda: nc.vector.reciprocal(out=rd[:], in_=rd[:]), reads=[rd], writes=[rd])
                for nt in nts:
                    fw.op("dve", lambda: nc.vector.tensor_tensor(out=Pc[nt][:], in0=Pc[nt][:], in1=rd[:], op=ALU.mult), reads=[Pc[nt], rd], writes=[Pc[nt]])
                tot = 4 * len(nts); x = 0
                for nt in nts:
                    for j in range(4):
                        fw.op("pe", lambda: nc.tensor.matmul(ps[4][:, 0:64], lhsT=Pc[nt][:, j * 128:(j + 1) * 128], rhs=n["cover"][:, nt, :], start=(x == 0), stop=(x == tot - 1)),
                              reads=[Pc[nt], n["cover"]], writes=[ps[4]])
                        x += 1
                a0 = 62 - 2 * i
                fw.op("dve", lambda: nc.vector.tensor_tensor(out=sc[:], in0=ps[4][:, 0:64], in1=n["A"][:, a0:a0 + 64], op=ALU.add), reads=[ps[4], n["A"]], writes=[sc])
                fw.op("dve", lambda: nc.vector.memset(sc[:, 0:1], BIG), writes=[sc])
                fw.op("dve", lambda: nc.vector.max(out=m8[:, 0:8], in_=sc[:]), reads=[sc], writes=[m8])
                fw.op("dve", lambda: nc.vector.tensor_scalar(out=sc2[:], in0=sc[:], scalar1=m8[:, 7:8], scalar2=-3.0 * BIG, op0=ALU.is_ge, op1=ALU.mult), reads=[sc, m8], writes=[sc2])
                fw.op("dve", lambda: nc.vector.tensor_tensor(out=sc2[:], in0=sc2[:], in1=sc[:], op=ALU.add), reads=[sc2, sc], writes=[sc2])
                fw.op("dve", lambda: nc.vector.max(out=m8[:, 8:16], in_=sc2[:]), reads=[sc2], writes=[m8])
                fw.op("dve", lambda: nc.vector.tensor_scalar(out=negq[:], in0=sc[:], scalar1=m8[:, 15:16], scalar2=None, op0=ALU.is_lt), reads=[sc, m8], writes=[negq])
                fw.op("pe", lambda: nc.tensor.matmul(ps[4][0:64, 128:256], lhsT=negq[:], rhs=ident[:], start=True, stop=True), reads=[negq, ident], writes=[ps[4]])
                for j in range(4):
                    fw.op("dve", lambda: nc.vector.tensor_scalar(out=negT4[:, j, :], in0=ps[4][0:64, 128:256], scalar1=NEGM, scalar2=None, op0=ALU.mult), reads=[ps[4]], writes=[negT4])
                def attend(kT, vtok, kts, tables, with_sel, p_num, p_den):
                    nonlocal evi
                    for x, kt in enumerate(kts):
                        sb = ps[evi % 2]; e_ = Ef[evi % 2]; p_ = Pb[evi % 2]
                        first, last = (x == 0), (x == len(kts) - 1)
                        fw.op("pe", lambda: nc.tensor.matmul(sb[:], lhsT=kT[:, kt * 128:(kt + 1) * 128], rhs=Q, start=True, stop=not with_sel), reads=[kT, qb], writes=[sb])
                        if with_sel:
                            fw.op("pe", lambda: nc.tensor.matmul(sb[:], lhsT=n["xexp"][:, kt * 128:(kt + 1) * 128], rhs=negT4[:], start=False, stop=True), reads=[n["xexp"], negT4], writes=[sb])
                        tab = tables.get(kt)
                        if tab is None:
                            fw.op("act", lambda: nc.scalar.activation(out=p_[:], in_=sb[:], func=AF.Exp, scale=0.125), reads=[sb], writes=[p_])
                        else:
                            fw.op("act", lambda: nc.scalar.activation(out=e_[:], in_=sb[:], func=AF.Exp, scale=0.125), reads=[sb], writes=[e_])
                            tb_, tap = tab
                            fw.op("dve", lambda: nc.vector.tensor_tensor(out=p_[:].rearrange("p (h q) -> p h q", h=4), in0=e_[:].rearrange("p (h q) -> p h q", h=4), in1=tap, op=ALU.mult),
                                  reads=[e_, tb_], writes=[p_])
                        fw.op("pe", lambda: nc.tensor.matmul(p_num[0:64, :], lhsT=vtok[:, kt, :], rhs=p_[:], start=first, stop=last), reads=[vtok, p_], writes=[p_num])
                        fw.op("pe", lambda: nc.tensor.matmul(p_den[0:64, :], lhsT=onesb[:, 0:64], rhs=p_[:], start=first, stop=last), reads=[onesb, p_], writes=[p_den])
                        evi += 1
                Ed_g = n["Ed"][:, g * 4:(g + 1) * 4, :]; Ep_g = n["Ep"][:, g * 4:(g + 1) * 4, :]
                W4b = n["W4"][:].unsqueeze(1).to_broadcast([128, 4, 128])
                tabs = {i: (n["Ed"], Ed_g)}
                if i >= 1:
                    tabs[i - 1] = (n["Ep"], Ep_g)
                def combine(br, p_num, p_den, first):
                    for j in range(4):
                        r = (g * 4 + j) * 3 + br
                        fw.op("pe", lambda: nc.tensor.matmul(ps[2][0:64, j * 128:(j + 1) * 128], lhsT=n["selall"][:, r * 64:(r + 1) * 64], rhs=sgT[:, i * 128:(i + 1) * 128], start=True, stop=True),
                              reads=[n["selall"], sgT], writes=[ps[2]])
                    fw.op("dve", lambda: nc.vector.tensor_scalar(out=wt[:], in0=p_den, scalar1=1e-30, scalar2=None, op0=ALU.max), reads=[ps_of[id(p_den)]], writes=[wt])
                    fw.op("dve", lambda: nc.vector.reciprocal(out=wt[:], in_=wt[:]), reads=[wt], writes=[wt])
                    fw.op("dve", lambda: nc.vector.tensor_tensor(out=wt[:], in0=wt[:], in1=ps[2][0:64, :], op=ALU.mult), reads=[wt, ps[2]], writes=[wt])
                    if first:
                        fw.op("dve", lambda: nc.vector.tensor_tensor(out=acc[:], in0=p_num, in1=wt[:], op=ALU.mult), reads=[ps_of[id(p_num)], wt], writes=[acc])
                    else:
                        fw.op("dve", lambda: nc.vector.tensor_tensor(out=ot[:], in0=p_num, in1=wt[:], op=ALU.mult), reads=[ps_of[id(p_num)], wt], writes=[ot])
                        fw.op("pool", lambda: nc.gpsimd.tensor_tensor(out=acc[:], in0=acc[:], in1=ot[:], op=ALU.add), reads=[acc, ot], writes=[acc])
                ps_of = {}
                numc = ps[5][0:64, :]; denc = ps[3][0:64, :]
                ps_of[id(numc)] = ps[5]; ps_of[id(denc)] = ps[3]
                combine(0, numc, denc, True)
                attend(ksel, vseltok, list(range(i + 1)), tabs, True, ps[6], ps[3])
                nums = ps[6][0:64, :]; dens = ps[3][0:64, :]
                ps_of[id(nums)] = ps[6]; ps_of[id(dens)] = ps[3]
                combine(1, nums, dens, False)
                wk = [kt for kt in range(i - 4, i + 1) if kt >= 0]
                wtabs = dict(tabs)
                if i >= 4:
                    wtabs[i - 4] = (n["W4"], W4b)
                attend(kwin, vwintok, wk, wtabs, False, ps[7], ps[4])
                numw = ps[7][0:64, :]; denw = ps[4][0:64, :]
                ps_of[id(numw)] = ps[7]; ps_of[id(denw)] = ps[4]
                combine(2, numw, denw, False)
                y_ = yb[i % 2]
                fw.op("dve", lambda: nc.vector.tensor_copy(out=y_[:], in_=acc[:]), reads=[acc], writes=[y_])
                fw.dma("sp", YT.t.ap()[1].rearrange("(h d) t -> d h t", d=64)[:, g * 4:(g + 1) * 4, i * 128:(i + 1) * 128], y_[:].rearrange("p (h q) -> p h q", h=4),
                       reads=[y_], writes=[k.YTtok[1][g * 2][i // 4], k.YTtok[1][g * 2 + 1][i // 4]])


N_LAYERS = 4
N_CORES = 4


def build_all(nc, fw, n_layers=N_LAYERS):
    k = K()
    alloc_tokens(k)
    setup_common(nc, fw, k, n_layers)
    setup_hgrn(nc, fw, k, n_layers)
    setup_rwkv(nc, fw, k)
    setup_nsa(nc, fw, k)
    setup_p5(nc, fw, k)
    fw.barrier()
    k.gsrc = Buf(None, "gsrc")
    PT = fw.dram("PT", [NP, T], F32)
    YT = fw.dram("YT", [3, 512, T], BF16)
    UT = fw.dram("UT", [DFF, T], BF16)
    XT1 = fw.dram("XT1", [D, T], F32)
    XT2 = fw.dram("XT2", [D, T], F32)
    OUT = fw.dram("OUT", [D, T], F32, kind="ExternalOutput")
    x0_tok = [Buf(None, f"x0_{t}") for t in range(NTB)]
    x1_tok = [Buf(None, f"x1_{t}") for t in range(NTB)]
    x2_tok = [Buf(None, f"x2_{t}") for t in range(NTB)]
    o_tok = [Buf(None, f"o_{t}") for t in range(NTB)]
    XTin, xin_tok = k.xT_in, x0_tok
    for l in range(n_layers):
        in_proj_phase(nc, fw, k, l, XTin, xin_tok, PT)
        hgrn(nc, fw, k, l, PT, YT)
        nsa(nc, fw, k, l, PT, YT)
        rwkv(nc, fw, k, l, PT, YT)
        proj_out(nc, fw, k, l, PT, YT, XTin, xin_tok, XT1, x1_tok)
        ffn(nc, fw, k, l, XT1, x1_tok, XT2, x2_tok, UT)
        XTin, xin_tok = XT2, x2_tok
    final_norm(nc, fw, k, XTin, xin_tok, OUT, o_tok)
    fw.finish(o_tok)
    return k


def kernel(**inputs):
    inp = {k_: np.asarray(v) for k_, v in inputs.items()}
    consts = {**host_consts(), **hgrn_consts(), **rwkv_consts(), **nsa_consts(inp["rel_bias"].astype(np.float32))}
    shared = {
        "ada_w": inp["ada_w"], "ada_b": inp["ada_b"], "norm1_w": inp["norm1_w"], "norm2_w": inp["norm2_w"],
        "w_in_p": pad_w_in(inp["w_in"]),
        "hgrn_lb_logits": inp["hgrn_lb_logits"], "hgrn_norm_w": inp["hgrn_norm_w"],
        "w_branch": inp["w_branch"], "w_out": inp["w_out"], "ffn_w1": inp["ffn_w1"], "ffn_w3": inp["ffn_w3"], "ffn_w2": inp["ffn_w2"],
        "final_norm_w": inp["final_norm_w"],
    }
    for nm in ("rw_mu", "rw_w0", "rw_w2", "rw_a0", "rw_a2", "rw_g2", "rw_k_k", "rw_k_a", "rw_lnx_w", "rw_lnx_b"):
        shared[nm] = inp[nm]
    shared["rw_r_k"] = np.ascontiguousarray(inp["rw_r_k"].reshape(4, 512))
    for nm in ("nsa_pe_k", "nsa_cmp_w1_k", "nsa_cmp_w2_k", "nsa_pe_v", "nsa_cmp_w1_v", "nsa_cmp_w2_v"):
        shared[nm] = inp[nm]
    shared.update(consts)
    shared = {k_: np.ascontiguousarray(v, dtype=np.float32) for k_, v in shared.items()}
    B = inp["x"].shape[0]
    in_maps = []
    for b in range(B):
        m = dict(shared)
        m["xT"] = np.ascontiguousarray(inp["x"][b].T.astype(np.float32))
        m["c8"] = np.ascontiguousarray(inp["c"][b].reshape(8, 128).T.astype(np.float32))
        in_maps.append(m)
    nc = bass.Bass("TRN2", target_bir_lowering=False)
    with ExitStack() as es:
        fw = FW(nc, es)
        build_all(nc, fw)
    res = run_bass_kernel_spmd(nc, in_maps, core_ids=list(range(B)))
    out = np.stack([np.asarray(res.results[b]["OUT"]).T for b in range(B)], axis=0)
    return np.ascontiguousarray(out.astype(np.float32))
```

```python
import numpy as np
from contextlib import ExitStack, contextmanager
import concourse.bass as bass
import concourse.mybir as mybir
from concourse.bass_utils import run_bass_kernel_spmd

F32 = mybir.dt.float32
BF16 = mybir.dt.bfloat16
AF = mybir.ActivationFunctionType
ALU = mybir.AluOpType
AX = mybir.AxisListType

F32R = mybir.dt.float32r
USE_F32R = True


def MM(nc, out, lhsT, rhs, **kw):
    if USE_F32R and (lhsT.dtype == F32R or rhs.dtype == F32R):
        if lhsT.dtype == F32:
            lhsT = lhsT.bitcast(F32R)
        if rhs.dtype == F32:
            rhs = rhs.bitcast(F32R)
    return nc.tensor.matmul(out, lhsT=lhsT, rhs=rhs, **kw)


SAME_ENGINE_SYNC = True
N_DMA_SLOTS = 12
DMA_Q_MAP = {"pool": "sp", "act": "sp"}


class Buf:
    __slots__ = ("t", "w", "r", "name")

    def __init__(self, t=None, name=""):
        self.t = t
        self.w = None
        self.r = []
        self.name = name

    def __getitem__(self, k):
        return self.t[k]


class FW:
    def __init__(self, nc, es):
        self.nc = nc
        self.es = es
        self.eng = {"pe": nc.tensor, "act": nc.scalar, "dve": nc.vector, "pool": nc.gpsimd, "sp": nc.sync}
        self.sem = {k: es.enter_context(nc.semaphore("s_" + k)) for k in self.eng}
        self.cnt = {k: 0 for k in self.eng}
        self.seen = {k: {} for k in self.eng}
        self.dsem = {}
        self.dslot_use = {}
        self.dnext = {}
        for q in ("sp", "act", "pool"):
            self.dsem[q] = [es.enter_context(nc.semaphore(f"d_{q}{i}")) for i in range(N_DMA_SLOTS)]
            self.dslot_use[q] = [0] * N_DMA_SLOTS
            self.dnext[q] = 0
        self.n_inst = 0

    def sbuf(self, name, shape, dt=F32):
        self.n_alloc = getattr(self, "n_alloc", 0) + 1
        name = f"{name}_u{self.n_alloc}"
        return Buf(self.es.enter_context(self.nc.sbuf_tensor(name, list(shape), dt)), name)

    def psum(self, name, shape, dt=F32):
        return Buf(self.es.enter_context(self.nc.psum_tensor(name, list(shape), dt)), name)

    def dram(self, name, shape, dt=F32, kind="Internal"):
        return Buf(self.nc.dram_tensor(name, list(shape), dt, kind=kind), name)

    def _wait(self, e, ticket):
        if ticket is None:
            return
        kind = ticket[0]
        if kind == "e":
            _, src, n = ticket
            if src == e and (e == "pe" or not SAME_ENGINE_SYNC):
                return
            key = ("e", src)
            if self.seen[e].get(key, 0) >= n:
                return
            self.eng[e].wait_ge(self.sem[src], n)
            self.seen[e][key] = n
        else:
            _, q, slot, val = ticket
            key = ("d", q, slot)
            if self.seen[e].get(key, 0) >= val:
                return
            self.eng[e].wait_ge(self.dsem[q][slot], val)
            self.seen[e][key] = val

    def _deps(self, e, reads, writes):
        for b in reads:
            self._wait(e, b.w)
        for b in writes:
            self._wait(e, b.w)
            for t in b.r:
                self._wait(e, t)

    def _record(self, ticket, reads, writes):
        for b in reads:
            if ticket[0] == "e":
                b.r = [t for t in b.r if not (t[0] == "e" and t[1] == ticket[1])]
            b.r.append(ticket)
        for b in writes:
            b.w = ticket
            b.r = []

    def op(self, e, fn, reads=(), writes=()):
        self._deps(e, reads, writes)
        inst = fn()
        self.cnt[e] += 1
        inst.then_inc(self.sem[e], 1)
        self._record(("e", e, self.cnt[e]), reads, writes)
        self.n_inst += 1
        return inst

    def dma(self, q, out, in_, reads=(), writes=(), **kw):
        q = DMA_Q_MAP.get(q, q)
        self._deps(q, reads, writes)
        slot = self.dnext[q]
        self.dnext[q] = (slot + 1) % N_DMA_SLOTS
        uses = self.dslot_use[q][slot]
        if uses > 0:
            self._wait(q, ("d", q, slot, 16 * uses))
        inst = self.eng[q].dma_start(out=out, in_=in_, **kw)
        inst.then_inc(self.dsem[q][slot], 16)
        self.dslot_use[q][slot] = uses + 1
        self._record(("d", q, slot, 16 * (uses + 1)), reads, writes)
        self.n_inst += 1
        return inst

    @contextmanager
    def scope(self):
        old = self.es
        with ExitStack() as es2:
            self.es = es2
            try:
                yield
            finally:
                self.es = old
            self.barrier()

    def barrier(self):
        for e in self.eng:
            for k2 in self.eng:
                if k2 != e and self.cnt[k2] > 0:
                    self._wait(e, ("e", k2, self.cnt[k2]))
            for q in self.dsem:
                for slot in range(N_DMA_SLOTS):
                    u = self.dslot_use[q][slot]
                    if u:
                        self._wait(e, ("d", q, slot, 16 * u))

    def finish(self, bufs):
        for b in bufs:
            self._wait("sp", b.w)
        for k in self.eng:
            if k != "sp" and self.cnt[k] > 0:
                self._wait("sp", ("e", k, self.cnt[k]))
        for q in self.dsem:
            for slot in range(N_DMA_SLOTS):
                u = self.dslot_use[q][slot]
                if u:
                    self._wait("sp", ("d", q, slot, 16 * u))


import numpy as np

T = 4096
D = 1024
NCH = 65
NP = NCH * 128
TB = 512
NTB = T // TB


def host_consts():
    c = {}
    c["ident"] = np.eye(128, dtype=np.float32)
    c["ones"] = np.ones((128, 128), np.float32)
    return c


class K:
    pass


def setup_common(nc, fw, k, n_layers):
    def din(name, shape, dt=F32):
        return fw.dram(name, shape, dt, kind="ExternalInput")
    k.xT_in = din("xT", [D, T])
    k.c8 = din("c8", [128, 8])
    k.ada_w = din("ada_w", [4, D, 6 * D])
    k.ada_b = din("ada_b", [4, 6 * D])
    k.norm1_w = din("norm1_w", [4, D])
    k.norm2_w = din("norm2_w", [4, D])
    k.w_in = din("w_in_p", [4, D, NP])
    k.identD = din("ident", [128, 128])
    k.onesD = din("ones", [128, 128])

    k.ident = fw.sbuf("ident_s", [128, 128])
    k.ones = fw.sbuf("ones_s", [128, 128])
    fw.dma("sp", k.ident[:], k.identD.t.ap()[:, :], writes=[k.ident])
    fw.dma("sp", k.ones[:], k.onesD.t.ap()[:, :], writes=[k.ones])
    k.identb = fw.sbuf("ident_b", [128, 128], BF16)
    k.onesb = fw.sbuf("ones_b", [128, 128], BF16)
    fw.op("dve", lambda: nc.vector.tensor_copy(out=k.identb[:], in_=k.ident[:]), reads=[k.ident], writes=[k.identb])
    fw.op("dve", lambda: nc.vector.tensor_copy(out=k.onesb[:], in_=k.ones[:]), reads=[k.ones], writes=[k.onesb])

    k.ps = [fw.psum(f"ps{i}", [128, 512]) for i in range(8)]

    cs = fw.sbuf("c_s", [128, 8])
    fw.dma("sp", cs[:], k.c8.t.ap()[:, :], writes=[cs])
    cond = fw.sbuf("cond_s", [128, 8])
    fw.op("act", lambda: nc.scalar.activation(out=cond[:], in_=cs[:], func=AF.Silu), reads=[cs], writes=[cond])
    k.ada = fw.sbuf("ada_s", [128, 4, 48])
    adab = fw.sbuf("adab_s", [128, 4, 48])
    fw.dma("sp", adab[:], k.ada_b.t.ap().rearrange("l (j p) -> p l j", p=128), writes=[adab], allow_slow_non_contiguous=True)
    k.n1 = fw.sbuf("n1_s", [128, 4, 8])
    k.n2 = fw.sbuf("n2_s", [128, 4, 8])
    fw.dma("sp", k.n1[:], k.norm1_w.t.ap().rearrange("l (j p) -> p l j", p=128), writes=[k.n1], allow_slow_non_contiguous=True)
    fw.dma("sp", k.n2[:], k.norm2_w.t.ap().rearrange("l (j p) -> p l j", p=128), writes=[k.n2], allow_slow_non_contiguous=True)
    with fw.scope():
      wst = [fw.sbuf(f"adaw_st{i}", [128, 6 * D]) for i in range(2)]
      for l in range(n_layers):
        pst = k.ps[l % 2]
        for kc in range(8):
            w = wst[kc % 2]
            fw.dma("sp" if kc % 2 == 0 else "act", w[:], k.ada_w.t.ap()[l, kc * 128:(kc + 1) * 128, :], writes=[w])
            for j in range(48):
                col = kc * 48 + j
                fw.op("pe", lambda: nc.tensor.matmul(pst[:, col:col + 1], lhsT=w[:, j * 128:(j + 1) * 128], rhs=cond[:, kc:kc + 1],
                                                     start=True, stop=True), reads=[w, cond], writes=[pst])
        fw.op("dve", lambda: nc.vector.tensor_reduce(out=k.ada[:, l, :], in_=pst[:, 0:384].rearrange("p (k j) -> p j k", k=8),
                                                     axis=AX.X, op=ALU.add), reads=[pst], writes=[k.ada])
        fw.op("dve", lambda: nc.vector.tensor_tensor(out=k.ada[:, l, :], in0=k.ada[:, l, :], in1=adab[:, l, :], op=ALU.add),
              reads=[k.ada, adab], writes=[k.ada])
    k.g1 = fw.sbuf("g1_s", [128, 4, 8])
    k.g2 = fw.sbuf("g2_s", [128, 4, 8])
    for l in range(n_layers):
        fw.op("dve", lambda: nc.vector.scalar_tensor_tensor(out=k.g1[:, l, :], in0=k.ada[:, l, 8:16], scalar=1.0, in1=k.n1[:, l, :],
                                                            op0=ALU.add, op1=ALU.mult), reads=[k.ada, k.n1], writes=[k.g1])
        fw.op("dve", lambda: nc.vector.scalar_tensor_tensor(out=k.g2[:, l, :], in0=k.ada[:, l, 32:40], scalar=1.0, in1=k.n2[:, l, :],
                                                            op0=ALU.add, op1=ALU.mult), reads=[k.ada, k.n2], writes=[k.g2])


def norm_mod(nc, fw, k, XT, g, s, hT, tag):
    xs_b = k.xs_bufs
    for tb in range(NTB):
        xs = xs_b[tb % 2]
        fw.dma("sp" if tb % 2 == 0 else "act", xs[:], XT.t.ap()[:, tb * TB:(tb + 1) * TB].rearrange("(k p) t -> p k t", p=128),
               reads=[XT], writes=[xs])
        sq = k.sq_buf
        fw.op("act", lambda: nc.scalar.activation(out=sq[:], in_=xs[:], func=AF.Square), reads=[xs], writes=[sq])
        pst = k.ps[7]
        for kc in range(8):
            fw.op("pe", lambda: nc.tensor.matmul(pst[:], lhsT=k.ones[:], rhs=sq[:, kc, :], start=(kc == 0), stop=(kc == 7)),
                  reads=[k.ones, sq], writes=[pst])
        rstd = k.rstd_buf
        fw.op("dve", lambda: nc.vector.tensor_scalar(out=rstd[:], in0=pst[:], scalar1=1.0 / D, scalar2=1e-6, op0=ALU.mult, op1=ALU.add),
              reads=[pst], writes=[rstd])
        fw.op("act", lambda: nc.scalar.activation(out=rstd[:], in_=rstd[:], func=AF.Sqrt), reads=[rstd], writes=[rstd])
        fw.op("dve", lambda: nc.vector.reciprocal(out=rstd[:], in_=rstd[:]), reads=[rstd], writes=[rstd])
        for kc in range(8):
            tmp = k.tmp_bufs[kc % 2]
            fw.op("dve", lambda: nc.vector.scalar_tensor_tensor(out=tmp[:], in0=xs[:, kc, :], scalar=g[:, kc:kc + 1], in1=rstd[:],
                                                                op0=ALU.mult, op1=ALU.mult), reads=[xs, rstd], writes=[tmp])
            fw.op("act", lambda: nc.scalar.activation(out=hT[:, kc, tb * TB:(tb + 1) * TB], in_=tmp[:], func=AF.Identity,
                                                      bias=s[:, kc:kc + 1], scale=1.0), reads=[tmp], writes=[hT])


def in_proj(nc, fw, k, l, hT, PT):
    GC = 5
    NG = NCH // GC
    ev = 0
    for cg in range(NG):
        wst = k.w_st[cg % 2]
        wb = k.w_bf[cg % 2]
        fw.dma("sp" if cg % 2 == 0 else "act", wst[:], k.w_in.t.ap()[l, :, cg * 640:(cg + 1) * 640].rearrange("(k p) n -> p k n", p=128),
               reads=[], writes=[wst])
        if cg % 2 == 0:
            fw.op("pool", lambda: nc.gpsimd.tensor_copy(out=wb[:], in_=wst[:]), reads=[wst], writes=[wb])
        else:
            fw.op("dve", lambda: nc.vector.tensor_copy(out=wb[:], in_=wst[:]), reads=[wst], writes=[wb])
        for tb in range(NTB):
            for ch in range(GC):
                pst = k.ps[(0, 3, 1, 4)[ev % 4]]
                for kc in range(8):
                    fw.op("pe", lambda: nc.tensor.matmul(pst[:], lhsT=wb[:, kc, ch * 128:(ch + 1) * 128], rhs=hT[:, kc, tb * TB:(tb + 1) * TB],
                                                         start=(kc == 0), stop=(kc == 7)), reads=[wb, hT], writes=[pst])
                o = k.ev_bufs[ev % 4]
                if ev % 2 == 0:
                    fw.op("act", lambda: nc.scalar.copy(out=o[:], in_=pst[:]), reads=[pst], writes=[o])
                else:
                    fw.op("dve", lambda: nc.vector.tensor_copy(out=o[:], in_=pst[:]), reads=[pst], writes=[o])
                row = (cg * GC + ch) * 128
                fw.dma("pool" if ev % 2 == 0 else "sp", PT.t.ap()[row:row + 128, tb * TB:(tb + 1) * TB], o[:], reads=[o], writes=[k.PTtok[cg * GC + ch][tb]])
                ev += 1


def alloc_p1(fw, k):
    k.xs_bufs = [fw.sbuf(f"xs{i}", [128, 8, TB]) for i in range(2)]
    k.sq_buf = fw.sbuf("sq", [128, 8, TB])
    k.rstd_buf = fw.sbuf("rstd", [128, TB])
    k.tmp_bufs = [fw.sbuf(f"tmp{i}", [128, TB]) for i in range(2)]
    k.hT = fw.sbuf("hT", [128, 8, T], BF16)
    k.w_st = [fw.sbuf(f"wst{i}", [128, 8, 640]) for i in range(2)]
    k.w_bf = [fw.sbuf(f"wbf{i}", [128, 8, 640], BF16) for i in range(2)]
    k.ev_bufs = [fw.sbuf(f"ev{i}", [128, TB]) for i in range(4)]
    k.PTtok = [[Buf(None, f"pt{c}_{t}") for t in range(NTB)] for c in range(NCH)]


def pad_w_in(w_in):
    L = w_in.shape[0]
    out = np.zeros((L, D, NP), np.float32)
    out[:, :, :3352] = w_in[:, :, :3352]
    out[:, :, 3456:] = w_in[:, :, 3352:]
    return out


import numpy as np


def hgrn_consts():
    c = {}
    s = np.arange(128)
    c["mask_bd"] = ((s[:, None] // 32 == s[None, :] // 32) & (s[:, None] <= s[None, :])).astype(np.float32)
    c["rowmask"] = (s[:, None] // 32 == np.arange(4)[None, :]).astype(np.float32)
    m = np.ones((128, 512), np.float32)
    m[:, ::32] = 0.0
    c["scanmask32"] = m
    return c


def setup_hgrn(nc, fw, k, n_layers):
    def din(name, shape, dt=F32):
        return fw.dram(name, shape, dt, kind="ExternalInput")
    k.lb_logits = din("hgrn_lb_logits", [4, 512])
    k.hgrn_nw = din("hgrn_norm_w", [4, 512])
    md = din("mask_bd", [128, 128]); rm = din("rowmask", [128, 4]); sm = din("scanmask32", [128, 512])
    k.mask_bd = fw.sbuf("mask_bd_s", [128, 128]); k.rowmask = fw.sbuf("rowmask_s", [128, 4]); k.scanmask32 = fw.sbuf("scanmask32_s", [128, 512])
    fw.dma("sp", k.mask_bd[:], md.t.ap()[:, :], writes=[k.mask_bd])
    fw.dma("sp", k.rowmask[:], rm.t.ap()[:, :], writes=[k.rowmask])
    fw.dma("sp", k.scanmask32[:], sm.t.ap()[:, :], writes=[k.scanmask32])
    k.hnw = fw.sbuf("hnw_s", [128, 4, 4])
    fw.dma("sp", k.hnw[:], k.hgrn_nw.t.ap().rearrange("l (h p) -> p l h", p=128), writes=[k.hnw], allow_slow_non_contiguous=True)
    lbl = fw.sbuf("lbl_s", [128, 4, 4])
    fw.dma("sp", lbl[:], k.lb_logits.t.ap().rearrange("l (h p) -> p l h", p=128), writes=[lbl], allow_slow_non_contiguous=True)
    e = fw.sbuf("lbe_s", [128, 4, 4])
    fw.op("act", lambda: nc.scalar.activation(out=e[:], in_=lbl[:], func=AF.Exp), reads=[lbl], writes=[e])
    ssum = fw.sbuf("lbsum_s", [128, 4])
    fw.op("dve", lambda: nc.vector.tensor_tensor(out=ssum[:], in0=e[:, 0, :], in1=e[:, 1, :], op=ALU.add), reads=[e], writes=[ssum])
    fw.op("dve", lambda: nc.vector.tensor_tensor(out=ssum[:], in0=ssum[:], in1=e[:, 2, :], op=ALU.add), reads=[e, ssum], writes=[ssum])
    fw.op("dve", lambda: nc.vector.tensor_tensor(out=ssum[:], in0=ssum[:], in1=e[:, 3, :], op=ALU.add), reads=[e, ssum], writes=[ssum])
    fw.op("dve", lambda: nc.vector.reciprocal(out=ssum[:], in_=ssum[:]), reads=[ssum], writes=[ssum])
    k.lb = fw.sbuf("lb_s", [128, 4, 4])
    k.oml = fw.sbuf("oml_s", [128, 4, 4])
    k.noml = fw.sbuf("noml_s", [128, 4, 4])
    fw.op("dve", lambda: nc.vector.memset(k.lb[:], 0.0), writes=[k.lb])
    for l in range(1, 4):
        fw.op("dve", lambda: nc.vector.tensor_tensor(out=e[:, l, :], in0=e[:, l, :], in1=ssum[:], op=ALU.mult), reads=[e, ssum], writes=[e])
        fw.op("dve", lambda: nc.vector.tensor_tensor(out=k.lb[:, l, :], in0=k.lb[:, l - 1, :], in1=e[:, l, :], op=ALU.add), reads=[e, k.lb], writes=[k.lb])
    fw.op("dve", lambda: nc.vector.tensor_scalar(out=k.oml[:], in0=k.lb[:], scalar1=-1.0, scalar2=1.0, op0=ALU.mult, op1=ALU.add), reads=[k.lb], writes=[k.oml])
    fw.op("dve", lambda: nc.vector.tensor_scalar(out=k.noml[:], in0=k.oml[:], scalar1=-1.0, scalar2=None, op0=ALU.mult), reads=[k.oml], writes=[k.noml])


def load_pt(fw, k, PT, q, dst, ch, tb, rows=128, row0=0):
    r = ch * 128 + row0
    fw.dma(q, dst, PT.t.ap()[r:r + rows, tb * TB:(tb + 1) * TB], reads=[k.PTtok[ch][tb]], writes=[])


def hgrn(nc, fw, k, l, PT, YT):
    with fw.scope():
        f32t = lambda n: fw.sbuf(n, [128, TB])
        bft = lambda n: fw.sbuf(n, [128, TB], BF16)
        zq = [f32t(f"h_zq{i}") for i in range(2)]; zf = [f32t(f"h_zf{i}") for i in range(2)]
        zi = [f32t(f"h_zi{i}") for i in range(2)]; zg = [f32t(f"h_zg{i}") for i in range(2)]
        q = f32t("h_q"); sg = f32t("h_sg"); kk = f32t("h_k"); b = f32t("h_b"); t1 = f32t("h_t1"); t2 = f32t("h_t2")
        kh = f32t("h_kh"); gam = fw.sbuf("h_gam", [128, 16])
        qt = bft("h_qt"); kt = bft("h_kt"); khb = bft("h_khb"); vb = bft("h_vb")
        AT = [fw.sbuf(f"h_AT{i}", [128, 128], BF16) for i in range(2)]
        Vt = [fw.sbuf(f"h_Vt{i}", [128, 128], BF16) for i in range(2)]
        khz = [[fw.sbuf(f"h_khz{i}_{c}", [128, 128], BF16) for c in range(4)] for i in range(2)]
        S = fw.sbuf("h_S", [128, 128])
        Sb = [fw.sbuf(f"h_Sb{i}", [128, 128], BF16) for i in range(8)]
        ob = f32t("h_ob"); rs = f32t("h_rs"); yb = [bft(f"h_yb{i}") for i in range(2)]
        P_sc, P_vt, P_kt, P_o, P_u0, P_u1, P_n = (k.ps[i] for i in (3, 0, 4, 1, 5, 6, 7))
        it = 0
        sbi = 0
        import os
        SUB = int(os.environ.get('SUB', '9')); STAGE = int(os.environ.get('STAGE', '9')); NHX = int(os.environ.get('NHX', '4')); NTBX = int(os.environ.get('NTBX', '8'))
        for h in range(NHX):
            fw.op("dve", lambda: nc.vector.memset(S[:], 0.0), writes=[S])
            lbc = k.lb[:, l, h:h + 1]; omlc = k.oml[:, l, h:h + 1]; nomlc = k.noml[:, l, h:h + 1]
            for tb in range(NTBX):
                z_q, z_f, z_i, z_g = zq[it % 2], zf[it % 2], zi[it % 2], zg[it % 2]
                for (dst, ch, qq) in ((z_q, h, "sp"), (z_f, 4 + h, "act"), (z_i, 8 + h, "sp"), (z_g, 12 + h, "act")):
                    r = ch * 128
                    fw.dma(qq, dst[:], PT.t.ap()[r:r + 128, tb * TB:(tb + 1) * TB], reads=[k.PTtok[ch][tb]], writes=[dst])
                fw.op("act", lambda: nc.scalar.activation(out=q[:], in_=z_q[:], func=AF.Silu), reads=[z_q], writes=[q])
                fw.op("act", lambda: nc.scalar.activation(out=sg[:], in_=z_f[:], func=AF.Sigmoid), reads=[z_f], writes=[sg])
                fw.op("dve", lambda: nc.vector.tensor_scalar(out=t1[:], in0=sg[:], scalar1=omlc, scalar2=lbc, op0=ALU.mult, op1=ALU.add), reads=[sg], writes=[t1])
                fw.op("act", lambda: nc.scalar.activation(out=t1[:], in_=t1[:], func=AF.Ln), reads=[t1], writes=[t1])
                fw.op("dve", lambda: nc.vector.tensor_scalar(out=kk[:], in0=sg[:], scalar1=nomlc, scalar2=omlc, op0=ALU.mult, op1=ALU.add), reads=[sg], writes=[kk])
                fw.op("dve", lambda: nc.vector.tensor_tensor_scan(out=b[:], data0=k.scanmask32[:], data1=t1[:], initial=0.0, op0=ALU.mult, op1=ALU.add),
                      reads=[k.scanmask32, t1], writes=[b])
                fw.op("act", lambda: nc.scalar.activation(out=t2[:], in_=b[:], func=AF.Exp), reads=[b], writes=[t2])
                fw.op("dve", lambda: nc.vector.tensor_tensor(out=qt[:], in0=q[:], in1=t2[:], op=ALU.mult), reads=[q, t2], writes=[qt])
                fw.op("act", lambda: nc.scalar.activation(out=t2[:], in_=b[:], func=AF.Exp, scale=-1.0), reads=[b], writes=[t2])
                fw.op("dve", lambda: nc.vector.tensor_tensor(out=kt[:], in0=kk[:], in1=t2[:], op=ALU.mult), reads=[kk, t2], writes=[kt])
                b3 = b.t[:].rearrange("p (c t) -> p c t", t=32)
                fw.op("dve", lambda: nc.vector.tensor_tensor(out=t2.t[:].rearrange("p (c t) -> p c t", t=32), in0=b3[:, :, 31:32].to_broadcast([128, 16, 32]), in1=b3,
                                                             op=ALU.subtract), reads=[b], writes=[t2])
                fw.op("act", lambda: nc.scalar.activation(out=t2[:], in_=t2[:], func=AF.Exp), reads=[t2], writes=[t2])
                fw.op("dve", lambda: nc.vector.tensor_tensor(out=khb[:], in0=kk[:], in1=t2[:], op=ALU.mult), reads=[kk, t2], writes=[khb])
                fw.op("pool", lambda: nc.gpsimd.tensor_copy(out=vb[:], in_=z_i[:]), reads=[z_i], writes=[vb])
                fw.op("act", lambda: nc.scalar.activation(out=gam[:], in_=b3[:, :, 31], func=AF.Exp), reads=[b], writes=[gam])
                if os.environ.get('DBG') and it == 0:
                    for di, src in enumerate((q, kk, b, t2, t1, sg)):
                        fw.dma('sp', k.DBG.t.ap()[:, di, :], src[:], reads=[src], writes=[k.DBGtok])
                if STAGE < 2:
                    continue
                for tl in range(4):
                    cs = slice(tl * 128, (tl + 1) * 128)
                    a_t = AT[tl % 2]; v_t = Vt[tl % 2]; kz = khz[tl % 2]
                    fw.op("pe", lambda: nc.tensor.matmul(P_sc[:, cs], lhsT=kt[:, cs], rhs=qt[:, cs], start=True, stop=True), reads=[kt, qt], writes=[P_sc])
                    fw.op("dve", lambda: nc.vector.tensor_tensor(out=a_t[:], in0=P_sc[:, cs], in1=k.mask_bd[:], op=ALU.mult), reads=[P_sc, k.mask_bd], writes=[a_t])
                    if SUB < 2: continue
                    fw.op("pe", lambda: nc.tensor.matmul(P_vt[:, cs], lhsT=vb[:, cs], rhs=k.identb[:], start=True, stop=True), reads=[vb, k.identb], writes=[P_vt])
                    fw.op("act", lambda: nc.scalar.copy(out=v_t[:], in_=P_vt[:, cs]), reads=[P_vt], writes=[v_t])
                    if SUB < 3: continue
                    ksrc = {'khb': khb, 'vb': vb, 'qt': qt}[os.environ.get('KSRC', 'khb')]
                    fw.op("pe", lambda: nc.tensor.matmul(P_kt[:, cs], lhsT=ksrc[:, cs], rhs=k.identb[:], start=True, stop=True), reads=[ksrc, k.identb], writes=[P_kt])
                    for c in range(int(os.environ.get('NEV', '4'))):
                        if c % 2 == 0 or True:
                            fw.op("dve", lambda: nc.vector.tensor_scalar(out=kz[c][:], in0=P_kt[:, cs], scalar1=k.rowmask[:, c:c + 1], scalar2=None, op0=ALU.mult),
                                  reads=[P_kt, k.rowmask], writes=[kz[c]])
                        else:
                            fw.op("act", lambda: nc.scalar.activation(out=kz[c][:], in_=P_kt[:, cs], func=AF.Identity, scale=k.rowmask[:, c:c + 1]),
                                  reads=[P_kt, k.rowmask], writes=[kz[c]])
                    if SUB < 4: continue
                    if SUB < 5: continue
                    for c in range(4):
                        s_b = Sb[sbi % 8]; sbi += 1
                        fw.op("act", lambda: nc.scalar.copy(out=s_b[:], in_=S[:]), reads=[S], writes=[s_b])
                        c0 = tl * 128 + c * 32
                        fw.op("pe", lambda: nc.tensor.matmul(P_o[:, c0:c0 + 32], lhsT=v_t[:], rhs=a_t[:, c * 32:(c + 1) * 32], start=True, stop=False), reads=[v_t, a_t], writes=[P_o])
                        fw.op("pe", lambda: nc.tensor.matmul(P_o[:, c0:c0 + 32], lhsT=s_b[:], rhs=qt[:, c0:c0 + 32], start=False, stop=True), reads=[s_b, qt], writes=[P_o])
                        P_u = P_u0 if c % 2 == 0 else P_u1
                        fw.op("pe", lambda: nc.tensor.matmul(P_u[:, 0:128], lhsT=kz[c][:], rhs=v_t[:], start=True, stop=True), reads=[kz[c], v_t], writes=[P_u])
                        gi = tl * 4 + c
                        fw.op("dve", lambda: nc.vector.scalar_tensor_tensor(out=S[:], in0=S[:], scalar=gam[:, gi:gi + 1], in1=P_u[:, 0:128], op0=ALU.mult, op1=ALU.add),
                              reads=[S, gam, P_u], writes=[S])
                    fw.op("act", lambda: nc.scalar.copy(out=ob[:, cs], in_=P_o[:, cs]), reads=[P_o], writes=[ob])
                if STAGE < 3:
                    continue
                fw.op("act", lambda: nc.scalar.activation(out=t2[:], in_=ob[:], func=AF.Square), reads=[ob], writes=[t2])
                fw.op("pe", lambda: nc.tensor.matmul(P_n[:], lhsT=k.ones[:], rhs=t2[:], start=True, stop=True), reads=[k.ones, t2], writes=[P_n])
                S3 = int(os.environ.get('S3', '9'))
                if S3 < 2: continue
                fw.op("dve", lambda: nc.vector.tensor_scalar(out=rs[:], in0=P_n[:], scalar1=1.0 / 128, scalar2=1e-5, op0=ALU.mult, op1=ALU.add), reads=[P_n], writes=[rs])
                fw.op("act", lambda: nc.scalar.activation(out=rs[:], in_=rs[:], func=AF.Sqrt), reads=[rs], writes=[rs])
                fw.op("dve", lambda: nc.vector.reciprocal(out=rs[:], in_=rs[:]), reads=[rs], writes=[rs])
                if S3 < 3: continue
                fw.op("act", lambda: nc.scalar.activation(out=t1[:], in_=z_g[:], func=AF.Sigmoid), reads=[z_g], writes=[t1])
                fw.op("dve", lambda: nc.vector.scalar_tensor_tensor(out=rs[:], in0=ob[:], scalar=k.hnw[:, l, h:h + 1], in1=rs[:], op0=ALU.mult, op1=ALU.mult),
                      reads=[ob, rs, k.hnw], writes=[rs])
                y_b = yb[it % 2]
                fw.op("dve", lambda: nc.vector.tensor_tensor(out=y_b[:], in0=rs[:], in1=t1[:], op=ALU.mult), reads=[rs, t1], writes=[y_b])
                if S3 < 4: continue
                fw.dma("sp", YT.t.ap()[0, h * 128:(h + 1) * 128, tb * TB:(tb + 1) * TB], y_b[:], reads=[y_b], writes=[k.YTtok[0][h][tb]])
                it += 1


def alloc_tokens(k):
    k.PTtok = [[Buf(None, f"pt{c}_{t}") for t in range(NTB)] for c in range(NCH)]
    k.YTtok = [[[Buf(None, f"yt{m}_{c}_{t}") for t in range(NTB)] for c in range(4)] for m in range(3)]
    k.UTtok = [[Buf(None, f"ut{c}_{t}") for t in range(NTB)] for c in range(22)]


import numpy as np, os

CH_R, CH_K, CH_V, CH_WA, CH_G = 27, 31, 35, 39, 40


def rwkv_consts():
    c = {}
    i = np.arange(128)
    same = (i[:, None] // 64 == i[None, :] // 64)
    su = (same & (i[:, None] < i[None, :])).astype(np.float32)
    iu = (same & (i[:, None] <= i[None, :])).astype(np.float32)
    sl = (same & (i[:, None] > i[None, :])).astype(np.float32)
    c["rw_mask_su_iu"] = np.concatenate([su, iu], axis=1)
    c["rw_mask_negsu"] = -su
    c["rw_mask_negsl"] = -sl
    m = np.ones((64, 512), np.float32)
    m[:, ::64] = 0.0
    c["scanmask64"] = m
    c["ones64"] = np.ones((64, 64), np.float32)
    c["rowmask64"] = np.concatenate([(i[:, None] // 64 == np.arange(2)[None, :]).astype(np.float32), -(i[:, None] // 64 == np.arange(2)[None, :]).astype(np.float32)], axis=1)
    return c


def setup_rwkv(nc, fw, k):
    def din(name, shape, dt=F32):
        return fw.dram(name, shape, dt, kind="ExternalInput")
    k.rw = {}
    for nm, shp in (("rw_mu", [4, 1792]), ("rw_w0", [4, 512]), ("rw_w2", [4, 64, 512]), ("rw_a0", [4, 512]), ("rw_a2", [4, 64, 512]),
                    ("rw_g2", [4, 128, 512]), ("rw_k_k", [4, 512]), ("rw_k_a", [4, 512]), ("rw_r_k", [4, 512]), ("rw_lnx_w", [4, 512]), ("rw_lnx_b", [4, 512])):
        k.rw[nm] = din(nm, shp)
    cm = {}
    for nm, shp in (("rw_mask_su_iu", [128, 256]), ("rw_mask_negsu", [128, 128]), ("rw_mask_negsl", [128, 128]), ("scanmask64", [64, 512]), ("ones64", [64, 64]), ("rowmask64", [128, 4])):
        d = din(nm, shp)
        t = fw.sbuf(nm + "_s", shp)
        fw.dma("sp", t[:], d.t.ap()[:, :], writes=[t])
        cm[nm] = t
    k.rwc = cm


def rwkv(nc, fw, k, l, PT, YT, NHX=8, NTBX=NTB):
    with fw.scope():
        W = k.rw
        def pvec(nm):
            t = fw.sbuf("rp_" + nm, [64, 8])
            fw.dma("sp", t[:], W[nm].t.ap()[l, :].rearrange("(h p) -> p h", p=64), writes=[t], allow_slow_non_contiguous=True)
            return t
        w0 = pvec("rw_w0"); a0 = pvec("rw_a0"); k_k = pvec("rw_k_k"); k_a = pvec("rw_k_a"); r_k = pvec("rw_r_k"); lnw = pvec("rw_lnx_w"); lnb = pvec("rw_lnx_b")
        mu_rkv = fw.sbuf("rp_mu", [64, 24])
        fw.dma("sp", mu_rkv[:], W["rw_mu"].t.ap()[l, 0:1536].rearrange("(h p) -> p h", p=64), writes=[mu_rkv], allow_slow_non_contiguous=True)
        mu_wa = fw.sbuf("rp_muwa", [64, 2])
        fw.dma("sp", mu_wa[:], W["rw_mu"].t.ap()[l, 1536:1664].rearrange("(h p) -> p h", p=64), writes=[mu_wa], allow_slow_non_contiguous=True)
        mu_g = fw.sbuf("rp_mug", [128, 1])
        fw.dma("sp", mu_g[:], W["rw_mu"].t.ap()[l, 1664:1792].rearrange("(h p) -> p h", p=128), writes=[mu_g], allow_slow_non_contiguous=True)
        w2 = fw.sbuf("rp_w2", [64, 512], F32R); a2 = fw.sbuf("rp_a2", [64, 512], F32R); g2 = fw.sbuf("rp_g2", [128, 512], F32R)
        wstg = fw.sbuf("rp_wstg", [128, 512])
        for (dst_, nm_, rows_) in ((w2, "rw_w2", 64), (a2, "rw_a2", 64), (g2, "rw_g2", 128)):
            fw.dma("sp", wstg[0:rows_, :], W[nm_].t.ap()[l], writes=[wstg])
            fw.op("dve", lambda: nc.vector.tensor_copy(out=dst_[:], in_=wstg[0:rows_, :]), reads=[wstg], writes=[dst_])
        identR = fw.sbuf("rp_identR", [128, 128], F32R); ones64R = fw.sbuf("rp_ones64R", [64, 64], F32R)
        fw.op("dve", lambda: nc.vector.tensor_copy(out=identR[:], in_=k.ident[:]), reads=[k.ident], writes=[identR])
        fw.op("dve", lambda: nc.vector.tensor_copy(out=ones64R[:], in_=k.rwc["ones64"][:]), reads=[k.rwc["ones64"]], writes=[ones64R])
        C = k.rwc
        ident = identR
        t64 = lambda n, dt=F32: fw.sbuf(n, [64, TB], dt)
        xw_in = fw.sbuf("r_xwin", [64, TB + 1]); xa_in = fw.sbuf("r_xain", [64, TB + 1]); xg_in = fw.sbuf("r_xgin", [128, TB + 1])
        th = t64("r_th", F32R); xa = t64("r_xa", F32R); sgg = fw.sbuf("r_sgg", [128, TB], F32R); sq_r = t64("r_sqr", F32R); dtmp = fw.sbuf("r_dtmp", [128, TB])
        rin = fw.sbuf("r_rin", [64, TB + 1]); kin = fw.sbuf("r_kin", [64, TB + 1]); vin = fw.sbuf("r_vin", [64, TB + 1])
        k_s = t64("r_ks"); ld = t64("r_ld"); a_ = t64("r_a"); kap = t64("r_kap"); b_ = t64("r_b"); L = t64("r_L")
        e1 = t64("r_e1"); e2 = t64("r_e2"); e3 = t64("r_e3"); t1 = t64("r_t1"); t2 = t64("r_t2")
        M = [fw.sbuf(f"r_M{h}", [128, 64], F32R) for h in range(8)]
        S = []
        for s_ in range(2):
            B_ = {}
            for n_ in ("r_s", "v_s", "kp", "g_", "kapt", "kt", "bt", "kh", "bh", "yo"):
                B_[n_] = t64(f"r_{n_}{s_}", F32 if n_ in ("r_s", "kp", "g_") else F32R)
            B_["rt"] = fw.sbuf(f"r_rt{s_}", [128, TB], F32R); B_["gam"] = fw.sbuf(f"r_gam{s_}", [64, 8]); B_["dg"] = fw.sbuf(f"r_dg{s_}", [128, 64], F32R)
            B_["SCa"] = fw.sbuf(f"r_SCa{s_}", [128, 256], F32R); B_["SCb"] = fw.sbuf(f"r_SCb{s_}", [128, 256], F32R)
            B_["Y"] = [fw.sbuf(f"r_Y{s_}{i}", [128, 128], F32R) for i in range(2)]; B_["Z"] = [fw.sbuf(f"r_Z{s_}{i}", [128, 128], F32R) for i in range(2)]
            B_["P"] = [fw.sbuf(f"r_P{s_}{i}", [128, 128], F32R) for i in range(2)]
            B_["ktok"] = fw.sbuf(f"r_ktok{s_}", [128, 128], F32R); B_["vtok"] = fw.sbuf(f"r_vtok{s_}", [128, 64], F32R); B_["bhtok"] = fw.sbuf(f"r_bhtok{s_}", [128, 64], F32R)
            B_["khc"] = [fw.sbuf(f"r_khc{s_}{i}", [128, 64], F32R) for i in range(2)]; B_["bhc"] = [fw.sbuf(f"r_bhc{s_}{i}", [128, 64], F32R) for i in range(2)]
            B_["nWc"] = [fw.sbuf(f"r_nWc{s_}{i}", [128, 64], F32R) for i in range(2)]
            B_["WU"] = fw.sbuf(f"r_WU{s_}", [128, 128]); B_["nWU"] = fw.sbuf(f"r_nWU{s_}", [128, 128], F32R); B_["Rp"] = fw.sbuf(f"r_Rp{s_}", [128, 128], F32R)
            B_["PTm"] = [fw.sbuf(f"r_PTm{s_}{i}", [64, 64], F32R) for i in range(2)]; B_["Qm"] = [fw.sbuf(f"r_Qm{s_}{i}", [64, 64]) for i in range(2)]
            B_["ybuf"] = fw.sbuf(f"r_yb{s_}", [64, TB], BF16)
            bk = (0, 3, 4, 5) if s_ == 0 else (1, 6, 7, 2)
            B_["X"], B_["E0"], B_["E1"], B_["E2"] = (k.ps[i] for i in bk)
            S.append(B_)
        ps = k.ps
        for h in range(8):
            fw.op("dve", lambda: nc.vector.memset(M[h][:].bitcast(F32), 0.0), writes=[M[h]])
        for B_ in S:
            fw.op("dve", lambda: nc.vector.memset(B_["rt"][:].bitcast(F32), 0.0), writes=[B_["rt"]])
            fw.op("dve", lambda: nc.vector.memset(B_["dg"][:].bitcast(F32), 0.0), writes=[B_["dg"]])
            fw.op("dve", lambda: nc.vector.memset(B_["Rp"][:].bitcast(F32), 0.0), writes=[B_["Rp"]])
        RM = C["rowmask64"]

        def load_shift(dst, ch, row0, rows, tb):
            r0 = ch * 128 + row0
            if tb == 0:
                fw.op("dve", lambda: nc.vector.memset(dst[0:rows, 0:1], 0.0), writes=[dst])
                fw.dma("sp", dst[0:rows, 1:TB + 1], PT.t.ap()[r0:r0 + rows, 0:TB], reads=[k.PTtok[ch][0]], writes=[dst])
            else:
                fw.dma("sp", dst[0:rows, 0:TB + 1], PT.t.ap()[r0:r0 + rows, tb * TB - 1:(tb + 1) * TB], reads=[k.PTtok[ch][tb], k.PTtok[ch][tb - 1]], writes=[dst])

        def shift_mix(out, src, rows, mu_ap, tmp, mub):
            fw.op("dve", lambda: nc.vector.tensor_tensor(out=tmp[0:rows, :], in0=src[0:rows, 0:TB], in1=src[0:rows, 1:TB + 1], op=ALU.subtract), reads=[src], writes=[tmp])
            fw.op("dve", lambda: nc.vector.scalar_tensor_tensor(out=out[0:rows, :], in0=tmp[0:rows, :], scalar=mu_ap, in1=src[0:rows, 1:TB + 1], op0=ALU.mult, op1=ALU.add),
                  reads=[tmp, src, mub], writes=[out])

        it = 0
        for tb in range(NTBX):
            load_shift(xw_in, CH_WA, 0, 64, tb); load_shift(xa_in, CH_WA, 64, 64, tb); load_shift(xg_in, CH_G, 0, 128, tb)
            shift_mix(th, xw_in, 64, mu_wa[:, 0:1], dtmp, mu_wa)
            fw.op("act", lambda: nc.scalar.activation(out=th[:], in_=th[:], func=AF.Tanh), reads=[th], writes=[th])
            shift_mix(xa, xa_in, 64, mu_wa[:, 1:2], dtmp, mu_wa)
            shift_mix(sgg, xg_in, 128, mu_g[:, 0:1], dtmp, mu_g)
            fw.op("act", lambda: nc.scalar.activation(out=sgg[:], in_=sgg[:], func=AF.Sigmoid), reads=[sgg], writes=[sgg])
            def unit(s, h):
                B_ = S[s]
                X, E0, E1, E2 = B_["X"], B_["E0"], B_["E1"], B_["E2"]
                r_s, v_s, kp, g_, kapt, kt, bt, rt, kh, bh, gam, dg, yo = (B_[n_] for n_ in ("r_s", "v_s", "kp", "g_", "kapt", "kt", "bt", "rt", "kh", "bh", "gam", "dg", "yo"))
                SCa, SCb, Y, Z, P, ktok, vtok, bhtok, khc, bhc, nWc, WU, nWU, Rp, PTm, Qm, ybuf = (B_[n_] for n_ in ("SCa", "SCb", "Y", "Z", "P", "ktok", "vtok", "bhtok", "khc", "bhc", "nWc", "WU", "nWU", "Rp", "PTm", "Qm", "ybuf"))
                j, hh = h // 2, h % 2
                load_shift(rin, CH_R + j, hh * 64, 64, tb); load_shift(kin, CH_K + j, hh * 64, 64, tb); load_shift(vin, CH_V + j, hh * 64, 64, tb)
                shift_mix(r_s, rin, 64, mu_rkv[:, h:h + 1], dtmp, mu_rkv)
                shift_mix(k_s, kin, 64, mu_rkv[:, 8 + h:9 + h], dtmp, mu_rkv)
                shift_mix(v_s, vin, 64, mu_rkv[:, 16 + h:17 + h], dtmp, mu_rkv)
                hs = slice(h * 64, (h + 1) * 64)
                fw.op("pe", lambda: MM(nc, X[0:64, :], lhsT=w2[:, hs], rhs=th[:], start=True, stop=True), reads=[w2, th], writes=[X])
                fw.op("act", lambda: nc.scalar.activation(out=ld[:], in_=X[0:64, :], func=AF.Sigmoid, bias=w0[:, h:h + 1], scale=1.0), reads=[X, w0], writes=[ld])
                fw.op("dve", lambda: nc.vector.tensor_scalar(out=ld[:], in0=ld[:], scalar1=-0.6065306597126334, scalar2=None, op0=ALU.mult), reads=[ld], writes=[ld])
                fw.op("pe", lambda: MM(nc, X[0:64, :], lhsT=a2[:, hs], rhs=xa[:], start=True, stop=True), reads=[a2, xa], writes=[X])
                fw.op("act", lambda: nc.scalar.activation(out=a_[:], in_=X[0:64, :], func=AF.Sigmoid, bias=a0[:, h:h + 1], scale=1.0), reads=[X, a0], writes=[a_])
                fw.op("pe", lambda: MM(nc, X[0:64, :], lhsT=g2[:, hs], rhs=sgg[:], start=True, stop=True), reads=[g2, sgg], writes=[X])
                fw.op("act", lambda: nc.scalar.copy(out=g_[:], in_=X[0:64, :]), reads=[X], writes=[g_])
                fw.op("dve", lambda: nc.vector.tensor_scalar(out=kap[:], in0=k_s[:], scalar1=k_k[:, h:h + 1], scalar2=None, op0=ALU.mult), reads=[k_s, k_k], writes=[kap])
                fw.op("act", lambda: nc.scalar.activation(out=sq_r[:], in_=kap[:], func=AF.Square), reads=[kap], writes=[sq_r])
                fw.op("pe", lambda: MM(nc, E0[0:64, :], lhsT=ones64R[:], rhs=sq_r[:], start=True, stop=True), reads=[ones64R, sq_r], writes=[E0])
                fw.op("dve", lambda: nc.vector.tensor_scalar(out=t1[:], in0=E0[0:64, :], scalar1=1e-24, scalar2=None, op0=ALU.max), reads=[E0], writes=[t1])
                fw.op("act", lambda: nc.scalar.activation(out=t1[:], in_=t1[:], func=AF.Sqrt), reads=[t1], writes=[t1])
                fw.op("dve", lambda: nc.vector.reciprocal(out=t1[:], in_=t1[:]), reads=[t1], writes=[t1])
                fw.op("dve", lambda: nc.vector.tensor_tensor(out=kap[:], in0=kap[:], in1=t1[:], op=ALU.mult), reads=[kap, t1], writes=[kap])
                fw.op("dve", lambda: nc.vector.tensor_scalar(out=t1[:], in0=a_[:], scalar1=-1.0, scalar2=k_a[:, h:h + 1], op0=ALU.add, op1=ALU.mult), reads=[a_, k_a], writes=[t1])
                fw.op("dve", lambda: nc.vector.scalar_tensor_tensor(out=kp[:], in0=t1[:], scalar=1.0, in1=k_s[:], op0=ALU.add, op1=ALU.mult), reads=[t1, k_s], writes=[kp])
                fw.op("dve", lambda: nc.vector.tensor_tensor(out=b_[:], in0=a_[:], in1=kap[:], op=ALU.mult), reads=[a_, kap], writes=[b_])
                fw.op("dve", lambda: nc.vector.tensor_tensor_scan(out=L[:], data0=C["scanmask64"][:], data1=ld[:], initial=0.0, op0=ALU.mult, op1=ALU.add),
                      reads=[C["scanmask64"], ld], writes=[L])
                fw.op("act", lambda: nc.scalar.activation(out=e1[:], in_=L[:], func=AF.Exp), reads=[L], writes=[e1])
                fw.op("act", lambda: nc.scalar.activation(out=e2[:], in_=L[:], func=AF.Exp, scale=-1.0), reads=[L], writes=[e2])
                L3 = L.t[:].rearrange("p (c t) -> p c t", t=64)
                fw.op("dve", lambda: nc.vector.tensor_tensor(out=t2.t[:].rearrange("p (c t) -> p c t", t=64), in0=L3[:, :, 63:64].to_broadcast([64, 8, 64]), in1=L3, op=ALU.subtract),
                      reads=[L], writes=[t2])
                fw.op("act", lambda: nc.scalar.activation(out=e3[:], in_=t2[:], func=AF.Exp), reads=[t2], writes=[e3])
                fw.op("act", lambda: nc.scalar.activation(out=gam[:], in_=L3[:, :, 63], func=AF.Exp), reads=[L], writes=[gam])
                fw.op("dve", lambda: nc.vector.tensor_tensor(out=t1[:], in0=L[:], in1=ld[:], op=ALU.subtract), reads=[L, ld], writes=[t1])
                fw.op("act", lambda: nc.scalar.activation(out=t1[:], in_=t1[:], func=AF.Exp), reads=[t1], writes=[t1])
                fw.op("dve", lambda: nc.vector.tensor_tensor(out=kapt[:], in0=kap[:], in1=t1[:], op=ALU.mult), reads=[kap, t1], writes=[kapt])
                fw.op("dve", lambda: nc.vector.tensor_tensor(out=kt[:], in0=kp[:], in1=e2[:], op=ALU.mult), reads=[kp, e2], writes=[kt])
                fw.op("dve", lambda: nc.vector.tensor_tensor(out=bt[:], in0=b_[:], in1=e2[:], op=ALU.mult), reads=[b_, e2], writes=[bt])
                fw.op("dve", lambda: nc.vector.tensor_tensor(out=rt[0:64, :], in0=r_s[:], in1=e1[:], op=ALU.mult), reads=[r_s, e1], writes=[rt])
                fw.op("dve", lambda: nc.vector.tensor_tensor(out=kh[:], in0=kp[:], in1=e3[:], op=ALU.mult), reads=[kp, e3], writes=[kh])
                fw.op("dve", lambda: nc.vector.tensor_tensor(out=bh[:], in0=b_[:], in1=e3[:], op=ALU.mult), reads=[b_, e3], writes=[bh])
                Mh = M[h]
                yield
                for tl in range(4):
                    cs = slice(tl * 128, (tl + 1) * 128)
                    fw.op("pe", lambda: MM(nc, E0[:, 0:128], lhsT=bt[:, cs], rhs=kapt[:, cs], start=True, stop=True), reads=[bt, kapt], writes=[E0])
                    fw.op("pe", lambda: MM(nc, E0[:, 128:256], lhsT=bt[:, cs], rhs=rt[0:64, cs], start=True, stop=True), reads=[bt, rt], writes=[E0])
                    fw.op("pe", lambda: MM(nc, E0[:, 256:384], lhsT=kt[:, cs], rhs=kapt[:, cs], start=True, stop=True), reads=[kt, kapt], writes=[E0])
                    fw.op("pe", lambda: MM(nc, E0[:, 384:512], lhsT=kt[:, cs], rhs=rt[0:64, cs], start=True, stop=True), reads=[kt, rt], writes=[E0])
                    fw.op("pe", lambda: MM(nc, E1[:, 0:128], lhsT=kapt[:, cs], rhs=bt[:, cs], start=True, stop=True), reads=[kapt, bt], writes=[E1])
                    fw.op("dve", lambda: nc.vector.tensor_tensor(out=SCa[:], in0=E0[:, 0:256], in1=C["rw_mask_su_iu"][:], op=ALU.mult), reads=[E0, C["rw_mask_su_iu"]], writes=[SCa])
                    fw.op("dve", lambda: nc.vector.tensor_tensor(out=SCb[:], in0=E0[:, 256:512], in1=C["rw_mask_su_iu"][:], op=ALU.mult), reads=[E0, C["rw_mask_su_iu"]], writes=[SCb])
                    fw.op("dve", lambda: nc.vector.tensor_tensor(out=Y[0][:], in0=E0[:, 0:128], in1=C["rw_mask_negsu"][:], op=ALU.mult), reads=[E0, C["rw_mask_negsu"]], writes=[Y[0]])
                    fw.op("dve", lambda: nc.vector.tensor_tensor(out=Z[0][:], in0=E1[:, 0:128], in1=C["rw_mask_negsl"][:], op=ALU.mult), reads=[E1, C["rw_mask_negsl"]], writes=[Z[0]])
                    yield
                    fw.op("dve", lambda: nc.vector.tensor_tensor(out=P[0][:], in0=Y[0][:], in1=ident[:], op=ALU.add), reads=[Y[0], ident], writes=[P[0]])
                    cur = 0
                    for lev in range(1, 6):
                        nxt = 1 - cur
                        fw.op("pe", lambda: MM(nc, X[:, 0:128], lhsT=Y[cur][:], rhs=Z[cur][:], start=True, stop=True), reads=[Y[cur], Z[cur]], writes=[X])
                        if lev < 5:
                            fw.op("pe", lambda: MM(nc, E1[:, 128:256], lhsT=Z[cur][:], rhs=Y[cur][:], start=True, stop=True), reads=[Y[cur], Z[cur]], writes=[E1])
                        fw.op("act", lambda: nc.scalar.copy(out=Z[nxt][:], in_=X[:, 0:128]), reads=[X], writes=[Z[nxt]])
                        if lev < 5:
                            fw.op("dve", lambda: nc.vector.tensor_copy(out=Y[nxt][:], in_=E1[:, 128:256]), reads=[E1], writes=[Y[nxt]])
                        fw.op("pe", lambda: MM(nc, E1[:, 256:384], lhsT=Z[nxt][:], rhs=P[cur][:], start=True, stop=True), reads=[Z[nxt], P[cur]], writes=[E1])
                        fw.op("dve", lambda: nc.vector.tensor_tensor(out=P[nxt][:], in0=E1[:, 256:384], in1=P[cur][:], op=ALU.add), reads=[E1, P[cur]], writes=[P[nxt]])
                        cur = nxt
                        yield
                    TT = P[cur]
                    fw.op("pe", lambda: MM(nc, X[:, 128:192], lhsT=kapt[:, cs], rhs=ident[0:64, 0:64], start=True, stop=True), reads=[kapt, ident], writes=[X])
                    fw.op("pe", lambda: MM(nc, X[:, 192:256], lhsT=v_s[:, cs], rhs=ident[0:64, 0:64], start=True, stop=True), reads=[v_s, ident], writes=[X])
                    fw.op("pe", lambda: MM(nc, X[:, 256:320], lhsT=kh[:, cs], rhs=ident[0:64, 0:64], start=True, stop=True), reads=[kh, ident], writes=[X])
                    fw.op("pe", lambda: MM(nc, X[:, 320:384], lhsT=bh[:, cs], rhs=ident[0:64, 0:64], start=True, stop=True), reads=[bh, ident], writes=[X])
                    fw.op("act", lambda: nc.scalar.copy(out=ktok[:, 0:64], in_=X[:, 128:192]), reads=[X], writes=[ktok])
                    fw.op("act", lambda: nc.scalar.copy(out=vtok[:], in_=X[:, 192:256]), reads=[X], writes=[vtok])
                    fw.op("act", lambda: nc.scalar.copy(out=bhtok[:], in_=X[:, 320:384]), reads=[X], writes=[bhtok])
                    for c in range(2):
                        fw.op("act", lambda: nc.scalar.activation(out=khc[c][:], in_=X[:, 256:320], func=AF.Identity, scale=RM[:, c:c + 1]), reads=[X, RM], writes=[khc[c]])
                        fw.op("act", lambda: nc.scalar.activation(out=bhc[c][:], in_=X[:, 320:384], func=AF.Identity, scale=RM[:, c:c + 1]), reads=[X, RM], writes=[bhc[c]])
                    fw.op("pe", lambda: MM(nc, E1[:, 384:448], lhsT=SCb[:, 0:128], rhs=vtok[:], start=True, stop=True), reads=[SCb, vtok], writes=[E1])
                    fw.op("dve", lambda: nc.vector.tensor_copy(out=ktok[:, 64:128], in_=E1[:, 384:448]), reads=[E1], writes=[ktok])
                    fw.op("pe", lambda: MM(nc, E2[:, 0:128], lhsT=TT[:], rhs=ktok[:], start=True, stop=True), reads=[TT, ktok], writes=[E2])
                    fw.op("dve", lambda: nc.vector.tensor_scalar(out=nWU[:], in0=E2[:, 0:128], scalar1=-1.0, scalar2=None, op0=ALU.mult), reads=[E2], writes=[nWU])
                    for c in range(2):
                        fw.op("dve", lambda: nc.vector.tensor_scalar(out=nWc[c][:], in0=E2[:, 0:64], scalar1=RM[:, 2 + c:3 + c], scalar2=None, op0=ALU.mult), reads=[E2, RM], writes=[nWc[c]])
                    yield
                    fw.op("pe", lambda: MM(nc, X[0:64, 384:512], lhsT=ident[:, 0:64], rhs=rt[:, cs], start=True, stop=False), reads=[ident, rt], writes=[X])
                    fw.op("pe", lambda: MM(nc, X[0:64, 384:512], lhsT=nWU[:, 0:64], rhs=SCa[:, 128:256], start=False, stop=True), reads=[nWU, SCa], writes=[X])
                    fw.op("act", lambda: nc.scalar.copy(out=Rp[0:64, :], in_=X[0:64, 384:512]), reads=[X], writes=[Rp])
                    yield
                    for c in range(2):
                        rs_ = slice(c * 64, (c + 1) * 64)
                        gi = tl * 2 + c
                        fw.op("dve", lambda: nc.vector.tensor_scalar(out=dg[0:64, :], in0=ident[0:64, 0:64], scalar1=gam[:, gi:gi + 1], scalar2=None, op0=ALU.mult), reads=[ident, gam], writes=[dg])
                        fw.op("pe", lambda: MM(nc, E2[0:64, 128 + c * 128:128 + c * 128 + 64], lhsT=ident[:, 0:64], rhs=dg[:], start=True, stop=False), reads=[ident, dg], writes=[E2])
                        fw.op("pe", lambda: MM(nc, E2[0:64, 128 + c * 128:128 + c * 128 + 64], lhsT=nWc[c][:], rhs=bhtok[:], start=False, stop=True), reads=[nWc[c], bhtok], writes=[E2])
                        fw.op("pe", lambda: MM(nc, E2[0:64, 128 + c * 128 + 64:128 + c * 128 + 128], lhsT=khc[c][:], rhs=vtok[:], start=True, stop=False), reads=[khc[c], vtok], writes=[E2])
                        fw.op("pe", lambda: MM(nc, E2[0:64, 128 + c * 128 + 64:128 + c * 128 + 128], lhsT=bhc[c][:], rhs=nWU[:, 64:128], start=False, stop=True), reads=[bhc[c], nWU], writes=[E2])
                        fw.op("dve", lambda: nc.vector.tensor_copy(out=PTm[c][:], in_=E2[0:64, 128 + c * 128:128 + c * 128 + 64]), reads=[E2], writes=[PTm[c]])
                        fw.op("dve", lambda: nc.vector.tensor_copy(out=Qm[c][:], in_=E2[0:64, 128 + c * 128 + 64:128 + c * 128 + 128]), reads=[E2], writes=[Qm[c]])
                    yield
                    for c in range(2):
                        ysl = slice(128 + c * 64, 128 + (c + 1) * 64)
                        fw.op("pe", lambda: MM(nc, E2[0:64, 384 + c * 64:384 + (c + 1) * 64], lhsT=vtok[:], rhs=SCb[:, ysl], start=True, stop=False), reads=[vtok, SCb], writes=[E2])
                        fw.op("pe", lambda: MM(nc, E2[0:64, 384 + c * 64:384 + (c + 1) * 64], lhsT=nWU[:, 64:128], rhs=SCa[:, ysl], start=False, stop=False), reads=[nWU, SCa], writes=[E2])
                        fw.op("pe", lambda: MM(nc, E2[0:64, 384 + c * 64:384 + (c + 1) * 64], lhsT=Mh[:], rhs=Rp[:, c * 64:(c + 1) * 64], start=False, stop=True),
                              reads=[Mh, Rp], writes=[E2])
                        fw.op("pe", lambda: MM(nc, E1[0:64, 448:512], lhsT=PTm[c][:], rhs=Mh[0:64, :], start=True, stop=True), reads=[PTm[c], Mh], writes=[E1])
                        fw.op("dve", lambda: nc.vector.tensor_tensor(out=Mh[0:64, :], in0=E1[0:64, 448:512], in1=Qm[c][:], op=ALU.add), reads=[E1, Qm[c]], writes=[Mh])
                    fw.op("dve", lambda: nc.vector.tensor_copy(out=yo[:, cs], in_=E2[0:64, 384:512]), reads=[E2], writes=[yo])
                yield
                fw.op("pe", lambda: MM(nc, E0[0:64, :], lhsT=ones64R[:], rhs=yo[:], start=True, stop=True), reads=[ones64R, yo], writes=[E0])
                fw.op("dve", lambda: nc.vector.scalar_tensor_tensor(out=t1[:], in0=E0[0:64, :], scalar=-1.0 / 64, in1=yo[:], op0=ALU.mult, op1=ALU.add), reads=[E0, yo], writes=[t1])
                fw.op("act", lambda: nc.scalar.activation(out=sq_r[:], in_=t1[:], func=AF.Square), reads=[t1], writes=[sq_r])
                fw.op("pe", lambda: MM(nc, E1[0:64, :], lhsT=ones64R[:], rhs=sq_r[:], start=True, stop=True), reads=[ones64R, sq_r], writes=[E1])
                fw.op("dve", lambda: nc.vector.tensor_scalar(out=t2[:], in0=E1[0:64, :], scalar1=1.0 / 64, scalar2=64e-5, op0=ALU.mult, op1=ALU.add), reads=[E1], writes=[t2])
                fw.op("act", lambda: nc.scalar.activation(out=t2[:], in_=t2[:], func=AF.Sqrt), reads=[t2], writes=[t2])
                fw.op("dve", lambda: nc.vector.reciprocal(out=t2[:], in_=t2[:]), reads=[t2], writes=[t2])
                fw.op("dve", lambda: nc.vector.scalar_tensor_tensor(out=t1[:], in0=t1[:], scalar=lnw[:, h:h + 1], in1=t2[:], op0=ALU.mult, op1=ALU.mult), reads=[t1, t2, lnw], writes=[t1])
                fw.op("dve", lambda: nc.vector.scalar_tensor_tensor(out=sq_r[:], in0=r_s[:], scalar=r_k[:, h:h + 1], in1=kp[:], op0=ALU.mult, op1=ALU.mult), reads=[r_s, kp, r_k], writes=[sq_r])
                fw.op("pe", lambda: MM(nc, E2[0:64, :], lhsT=ones64R[:], rhs=sq_r[:], start=True, stop=True), reads=[ones64R, sq_r], writes=[E2])
                fw.op("dve", lambda: nc.vector.tensor_tensor(out=t2[:], in0=E2[0:64, :], in1=v_s[:], op=ALU.mult), reads=[E2, v_s], writes=[t2])
                fw.op("dve", lambda: nc.vector.scalar_tensor_tensor(out=t1[:], in0=t1[:], scalar=lnb[:, h:h + 1], in1=t2[:], op0=ALU.add, op1=ALU.add), reads=[t1, t2, lnb], writes=[t1])
                yb = ybuf
                fw.op("dve", lambda: nc.vector.tensor_tensor(out=yb[:], in0=t1[:], in1=g_[:], op=ALU.mult), reads=[t1, g_], writes=[yb])
                fw.dma("sp", YT.t.ap()[2, h * 64:(h + 1) * 64, tb * TB:(tb + 1) * TB], yb[:], reads=[yb], writes=[k.YTtok[2][h // 2][tb]])


            for hp in range(0, NHX, 2):
                gens = [unit(0, hp)] + ([unit(1, hp + 1)] if hp + 1 < NHX else [])
                live = list(gens)
                while live:
                    for g_i in list(live):
                        try:
                            next(g_i)
                        except StopIteration:
                            live.remove(g_i)

import numpy as np, os

DFF = 2816
NFF = DFF // 128


def setup_p5(nc, fw, k):
    def din(name, shape, dt=F32):
        return fw.dram(name, shape, dt, kind="ExternalInput")
    k.w_branch = din("w_branch", [4, 3, 512, D])
    k.w_out = din("w_out", [4, D, D])
    k.ffn_w1 = din("ffn_w1", [4, D, DFF]); k.ffn_w3 = din("ffn_w3", [4, D, DFF]); k.ffn_w2 = din("ffn_w2", [4, DFF, D])
    k.final_w = din("final_norm_w", [D])


def norm_mod_tok(nc, fw, k, XT, xtok, g, s, hT, out_dram=None, out_tok=None):
    for tb in range(NTB):
        xs = k.xs_bufs[tb % 2]
        fw.dma("sp", xs[:], XT.t.ap()[:, tb * TB:(tb + 1) * TB].rearrange("(k p) t -> p k t", p=128), reads=[xtok[tb]], writes=[xs])
        sq = k.sq_buf
        fw.op("act", lambda: nc.scalar.activation(out=sq[:], in_=xs[:], func=AF.Square), reads=[xs], writes=[sq])
        pst = k.ps[7]
        for kc in range(8):
            fw.op("pe", lambda: nc.tensor.matmul(pst[:], lhsT=k.ones[:], rhs=sq[:, kc, :], start=(kc == 0), stop=(kc == 7)), reads=[k.ones, sq], writes=[pst])
        rstd = k.rstd_buf
        fw.op("dve", lambda: nc.vector.tensor_scalar(out=rstd[:], in0=pst[:], scalar1=1.0 / D, scalar2=1e-6, op0=ALU.mult, op1=ALU.add), reads=[pst], writes=[rstd])
        fw.op("act", lambda: nc.scalar.activation(out=rstd[:], in_=rstd[:], func=AF.Sqrt), reads=[rstd], writes=[rstd])
        fw.op("dve", lambda: nc.vector.reciprocal(out=rstd[:], in_=rstd[:]), reads=[rstd], writes=[rstd])
        for kc in range(8):
            if out_dram is None:
                tmp = k.tmp_bufs[kc % 2]
                fw.op("dve", lambda: nc.vector.scalar_tensor_tensor(out=tmp[:], in0=xs[:, kc, :], scalar=g[:, kc:kc + 1], in1=rstd[:], op0=ALU.mult, op1=ALU.mult),
                      reads=[xs, rstd, k.gsrc], writes=[tmp])
                fw.op("act", lambda: nc.scalar.activation(out=hT[:, kc, tb * TB:(tb + 1) * TB], in_=tmp[:], func=AF.Identity, bias=s[:, kc:kc + 1], scale=1.0),
                      reads=[tmp, k.gsrc], writes=[hT])
            else:
                fw.op("dve", lambda: nc.vector.scalar_tensor_tensor(out=sq[:, kc, :], in0=xs[:, kc, :], scalar=g[:, kc:kc + 1], in1=rstd[:], op0=ALU.mult, op1=ALU.mult),
                      reads=[xs, rstd, k.gsrc], writes=[sq])
        if out_dram is not None:
            fw.dma("sp", out_dram.t.ap()[:, tb * TB:(tb + 1) * TB].rearrange("(k p) t -> p k t", p=128), sq[:], reads=[sq], writes=[out_tok[tb]])


def proj_out(nc, fw, k, l, PT, YT, XTin, xin_tok, XTout, xout_tok):
    with fw.scope():
        wbr = fw.sbuf("p5_wbr", [128, 12, D], BF16)
        wo = fw.sbuf("p5_wo", [128, 8, D], BF16)
        stg = [fw.sbuf(f"p5_stg{i}", [128, 4, D]) for i in range(2)]
        for br in range(3):
            st = stg[br % 2]
            fw.dma("sp", st[:], k.w_branch.t.ap()[l, br].rearrange("(k p) n -> p k n", p=128), writes=[st])
            e = "pool" if br % 2 == 0 else "dve"
            if e == "pool":
                fw.op("pool", lambda: nc.gpsimd.tensor_copy(out=wbr[:, br * 4:(br + 1) * 4, :], in_=st[:]), reads=[st], writes=[wbr])
            else:
                fw.op("dve", lambda: nc.vector.tensor_copy(out=wbr[:, br * 4:(br + 1) * 4, :], in_=st[:]), reads=[st], writes=[wbr])
        for hf in range(2):
            st = stg[(hf + 1) % 2]
            fw.dma("sp", st[:], k.w_out.t.ap()[l, hf * 512:(hf + 1) * 512, :].rearrange("(k p) n -> p k n", p=128), writes=[st])
            fw.op("pool", lambda: nc.gpsimd.tensor_copy(out=wo[:, hf * 4:(hf + 1) * 4, :], in_=st[:]), reads=[st], writes=[wo])
        yb = [[fw.sbuf(f"p5_y{i}_{m}", [128, 4, TB], BF16) for m in range(3)] for i in range(2)]
        gt = [fw.sbuf(f"p5_g{i}", [128, TB]) for i in range(3)]
        mg = fw.sbuf("p5_mg", [128, 8, TB], BF16)
        acc = fw.sbuf("p5_acc", [128, TB]); tt = [fw.sbuf(f"p5_tt{i}", [128, TB]) for i in range(2)]
        xs = [fw.sbuf(f"p5_xs{i}", [128, 8, TB]) for i in range(2)]
        gt1 = k.ada[:, l, 16:24]
        for tb in range(NTB):
            y = yb[tb % 2]
            for m in range(3):
                fw.dma("sp", y[m][:], YT.t.ap()[m, :, tb * TB:(tb + 1) * TB].rearrange("(k p) t -> p k t", p=128),
                       reads=[k.YTtok[m][c][tb] for c in range(4)], writes=[y[m]])
            x_ = xs[tb % 2]
            fw.dma("sp", x_[:], XTin.t.ap()[:, tb * TB:(tb + 1) * TB].rearrange("(k p) t -> p k t", p=128), reads=[xin_tok[tb]], writes=[x_])
            for n in range(8):
                for br in range(3):
                    ch = 41 + br * 8 + n
                    fw.dma("sp", gt[br][:], PT.t.ap()[ch * 128:(ch + 1) * 128, tb * TB:(tb + 1) * TB], reads=[k.PTtok[ch][tb]], writes=[gt[br]])
                    fw.op("act", lambda: nc.scalar.activation(out=gt[br][:], in_=gt[br][:], func=AF.Sigmoid), reads=[gt[br]], writes=[gt[br]])
                    pst = k.ps[3 + br]
                    for kc in range(4):
                        fw.op("pe", lambda: nc.tensor.matmul(pst[:], lhsT=wbr[:, br * 4 + kc, n * 128:(n + 1) * 128], rhs=y[br][:, kc, :], start=(kc == 0), stop=(kc == 3)),
                              reads=[wbr, y[br]], writes=[pst])
                fw.op("dve", lambda: nc.vector.tensor_tensor(out=acc[:], in0=k.ps[3][:], in1=gt[0][:], op=ALU.mult), reads=[k.ps[3], gt[0]], writes=[acc])
                fw.op("dve", lambda: nc.vector.tensor_tensor(out=tt[0][:], in0=k.ps[4][:], in1=gt[1][:], op=ALU.mult), reads=[k.ps[4], gt[1]], writes=[tt[0]])
                fw.op("dve", lambda: nc.vector.tensor_tensor(out=tt[1][:], in0=k.ps[5][:], in1=gt[2][:], op=ALU.mult), reads=[k.ps[5], gt[2]], writes=[tt[1]])
                fw.op("pool", lambda: nc.gpsimd.tensor_tensor(out=acc[:], in0=acc[:], in1=tt[0][:], op=ALU.add), reads=[acc, tt[0]], writes=[acc])
                fw.op("pool", lambda: nc.gpsimd.tensor_tensor(out=mg[:, n, :], in0=acc[:], in1=tt[1][:], op=ALU.add), reads=[acc, tt[1]], writes=[mg])
            for n in range(8):
                pst = k.ps[6 + n % 2]
                for kc in range(8):
                    fw.op("pe", lambda: nc.tensor.matmul(pst[:], lhsT=wo[:, kc, n * 128:(n + 1) * 128], rhs=mg[:, kc, :], start=(kc == 0), stop=(kc == 7)), reads=[wo, mg], writes=[pst])
                fw.op("dve", lambda: nc.vector.scalar_tensor_tensor(out=x_[:, n, :], in0=pst[:], scalar=gt1[:, n:n + 1], in1=x_[:, n, :], op0=ALU.mult, op1=ALU.add),
                      reads=[pst, x_, k.gsrc], writes=[x_])
            fw.dma("sp", XTout.t.ap()[:, tb * TB:(tb + 1) * TB].rearrange("(k p) t -> p k t", p=128), x_[:], reads=[x_], writes=[xout_tok[tb]])


def ffn(nc, fw, k, l, XT1, x1_tok, XT2, x2_tok, UT):
    with fw.scope():
        alloc_p1_small(fw, k)
        hT = fw.sbuf("f_hT", [128, 8, T], BF16)
        norm_mod_tok(nc, fw, k, XT1, x1_tok, k.g2[:, l, :], k.ada[:, l, 24:32], hT)
        stg = [fw.sbuf(f"f_stg{i}", [128, 8, 256]) for i in range(2)]
        w1b = [fw.sbuf(f"f_w1b{i}", [128, 8, 256], BF16) for i in range(2)]
        w3b = [fw.sbuf(f"f_w3b{i}", [128, 8, 256], BF16) for i in range(2)]
        sl = [fw.sbuf(f"f_sl{i}", [128, TB]) for i in range(2)]
        ub = [fw.sbuf(f"f_ub{i}", [128, TB], BF16) for i in range(2)]
        ev = 0
        for fg in range(NFF // 2):
            wa, wc = w1b[fg % 2], w3b[fg % 2]
            fw.dma("sp", stg[0][:], k.ffn_w1.t.ap()[l, :, fg * 256:(fg + 1) * 256].rearrange("(k p) n -> p k n", p=128), writes=[stg[0]])
            fw.op("pool", lambda: nc.gpsimd.tensor_copy(out=wa[:], in_=stg[0][:]), reads=[stg[0]], writes=[wa])
            fw.dma("sp", stg[1][:], k.ffn_w3.t.ap()[l, :, fg * 256:(fg + 1) * 256].rearrange("(k p) n -> p k n", p=128), writes=[stg[1]])
            fw.op("pool", lambda: nc.gpsimd.tensor_copy(out=wc[:], in_=stg[1][:]), reads=[stg[1]], writes=[wc])
            for tb in range(NTB):
                for c in range(2):
                    pa = k.ps[ev % 2]
                    pd = k.ps[3 + ev % 2]
                    for kc in range(8):
                        fw.op("pe", lambda: nc.tensor.matmul(pa[:], lhsT=wa[:, kc, c * 128:(c + 1) * 128], rhs=hT[:, kc, tb * TB:(tb + 1) * TB], start=(kc == 0), stop=(kc == 7)),
                              reads=[wa, hT], writes=[pa])
                    for kc in range(8):
                        fw.op("pe", lambda: nc.tensor.matmul(pd[:], lhsT=wc[:, kc, c * 128:(c + 1) * 128], rhs=hT[:, kc, tb * TB:(tb + 1) * TB], start=(kc == 0), stop=(kc == 7)),
                              reads=[wc, hT], writes=[pd])
                    s_ = sl[ev % 2]; u_ = ub[ev % 2]
                    fw.op("act", lambda: nc.scalar.activation(out=s_[:], in_=pa[:], func=AF.Silu), reads=[pa], writes=[s_])
                    fw.op("dve", lambda: nc.vector.tensor_tensor(out=u_[:], in0=pd[:], in1=s_[:], op=ALU.mult), reads=[pd, s_], writes=[u_])
                    ch = fg * 2 + c
                    fw.dma("sp", UT.t.ap()[ch * 128:(ch + 1) * 128, tb * TB:(tb + 1) * TB], u_[:], reads=[u_], writes=[k.UTtok[ch][tb]])
                    ev += 1
    with fw.scope():
        w2b = fw.sbuf("f_w2b", [128, NFF, D], BF16)
        stg = [fw.sbuf(f"f_stgb{i}", [128, 2, D]) for i in range(2)]
        for g in range(NFF // 2):
            st = stg[g % 2]
            fw.dma("sp", st[:], k.ffn_w2.t.ap()[l, g * 256:(g + 1) * 256, :].rearrange("(k p) n -> p k n", p=128), writes=[st])
            if g % 2 == 0:
                fw.op("pool", lambda: nc.gpsimd.tensor_copy(out=w2b[:, g * 2:(g + 1) * 2, :], in_=st[:]), reads=[st], writes=[w2b])
            else:
                fw.op("dve", lambda: nc.vector.tensor_copy(out=w2b[:, g * 2:(g + 1) * 2, :], in_=st[:]), reads=[st], writes=[w2b])
        ub = [fw.sbuf(f"f_ublk{i}", [128, NFF, TB], BF16) for i in range(2)]
        xs = [fw.sbuf(f"f_xs{i}", [128, 8, TB]) for i in range(2)]
        gt2 = k.ada[:, l, 40:48]
        for tb in range(NTB):
            u_ = ub[tb % 2]; x_ = xs[tb % 2]
            fw.dma("sp", u_[:], UT.t.ap()[:, tb * TB:(tb + 1) * TB].rearrange("(k p) t -> p k t", p=128), reads=[k.UTtok[c][tb] for c in range(NFF)], writes=[u_])
            fw.dma("sp", x_[:], XT1.t.ap()[:, tb * TB:(tb + 1) * TB].rearrange("(k p) t -> p k t", p=128), reads=[x1_tok[tb]], writes=[x_])
            for n in range(8):
                pst = k.ps[3 + n % 2]
                for kc in range(NFF):
                    fw.op("pe", lambda: nc.tensor.matmul(pst[:], lhsT=w2b[:, kc, n * 128:(n + 1) * 128], rhs=u_[:, kc, :], start=(kc == 0), stop=(kc == NFF - 1)),
                          reads=[w2b, u_], writes=[pst])
                fw.op("dve", lambda: nc.vector.scalar_tensor_tensor(out=x_[:, n, :], in0=pst[:], scalar=gt2[:, n:n + 1], in1=x_[:, n, :], op0=ALU.mult, op1=ALU.add),
                      reads=[pst, x_, k.gsrc], writes=[x_])
            fw.dma("sp", XT2.t.ap()[:, tb * TB:(tb + 1) * TB].rearrange("(k p) t -> p k t", p=128), x_[:], reads=[x_], writes=[x2_tok[tb]])


def alloc_p1_small(fw, k):
    _x = fw.sbuf("xs0", [128, 8, TB])
    k.xs_bufs = [_x, _x]
    k.sq_buf = fw.sbuf("sq", [128, 8, TB])
    k.rstd_buf = fw.sbuf("rstd", [128, TB])
    k.tmp_bufs = [fw.sbuf(f"tmp{i}", [128, TB]) for i in range(2)]


def in_proj_phase(nc, fw, k, l, XTin, xin_tok, PT):
    with fw.scope():
        alloc_p1_small(fw, k)
        hT = fw.sbuf("hT", [128, 8, T], BF16)
        _w = fw.sbuf("wst0", [128, 8, 640])
        k.w_st = [_w, _w]
        k.w_bf = [fw.sbuf(f"wbf{i}", [128, 8, 640], BF16) for i in range(2)]
        k.ev_bufs = [fw.sbuf(f"ev{i}", [128, TB]) for i in range(4)]
        norm_mod_tok(nc, fw, k, XTin, xin_tok, k.g1[:, l, :], k.ada[:, l, 0:8], hT)
        in_proj(nc, fw, k, l, hT, PT)


def final_norm(nc, fw, k, XT, xtok, OUT, otok):
    with fw.scope():
        alloc_p1_small(fw, k)
        fwt = fw.sbuf("fin_w", [128, 8])
        fw.dma("sp", fwt[:], k.final_w.t.ap().rearrange("(j p) -> p j", p=128), writes=[fwt], allow_slow_non_contiguous=True)
        old = k.gsrc
        k.gsrc = fwt
        norm_mod_tok(nc, fw, k, XT, xtok, fwt[:, :], None, None, out_dram=OUT, out_tok=otok)
        k.gsrc = old

import numpy as np, os

BIG = 1.0e30
NEGM = -240000.0
NQ = T // 128


def t5_bucket_np(dist):
    n = np.maximum(dist, 0)
    nf = np.maximum(n, 1).astype(np.float32)
    large = 16 + (np.log(nf / np.float32(16)) / np.float32(np.log(128 / 16)) * np.float32(16)).astype(np.int32)
    large = np.minimum(large, 31)
    return np.where(n < 16, n, large)


def nsa_consts(rel_bias):
    c = {}
    kq = np.arange(128)
    dD = kq[None, :] - kq[:, None]
    c["nsa_tabD"] = np.ascontiguousarray(np.transpose(rel_bias[t5_bucket_np(dD)], (2, 0, 1))).astype(np.float32)
    c["nsa_maskD"] = (dD >= 0).astype(np.float32)
    c["nsa_tabP"] = np.ascontiguousarray(np.transpose(rel_bias[t5_bucket_np(dD + 128)], (2, 0, 1))).astype(np.float32)
    c["nsa_maskW4"] = (dD < 0).astype(np.float32)
    m = np.arange(504)
    dC = kq[None, :] - 16 * (m[:, None] - 248) - 31
    c["nsa_tabC"] = np.ascontiguousarray(np.transpose(rel_bias[t5_bucket_np(dC)], (2, 0, 1))).astype(np.float32)
    c["nsa_maskC"] = (dC >= 0).astype(np.float32)
    c["nsa_b31"] = np.ascontiguousarray(rel_bias[31:32, :]).astype(np.float32)
    u = np.arange(126) - 62
    cur = (kq >= 64).astype(np.int64)
    A = np.zeros((128, 126), np.float32)
    A[(u[None, :] == cur[:, None]) | (u[None, :] == cur[:, None] - 1)] = BIG
    A[u[None, :] > cur[:, None]] = -BIG
    c["nsa_A"] = A
    n = np.arange(256)
    mm = np.arange(64)
    cov = ((16 * n[:, None] < 64 * mm[None, :] + 64) & (16 * n[:, None] + 32 > 64 * mm[None, :]) & (n[:, None] < 255)).astype(np.float32)
    c["nsa_cover"] = cov
    keys = np.arange(T)
    c["nsa_xexp"] = (keys[None, :] // 64 == mm[:, None]).astype(np.float32)
    sel = np.zeros((24, 24 * 64), np.float32)
    for r in range(24):
        sel[r, r * 64:(r + 1) * 64] = 1.0
    c["nsa_selall"] = sel
    return c


def setup_nsa(nc, fw, k):
    def din(name, shape, dt=F32):
        return fw.dram(name, shape, dt, kind="ExternalInput")
    k.nsa_in = {}
    for nm, shp in (("nsa_pe_k", [4, 32, 64]), ("nsa_cmp_w1_k", [4, 2048, 256]), ("nsa_cmp_w2_k", [4, 256, 64]),
                    ("nsa_pe_v", [4, 32, 64]), ("nsa_cmp_w1_v", [4, 2048, 256]), ("nsa_cmp_w2_v", [4, 256, 64])):
        k.nsa_in[nm] = din(nm, shp)
    tabD = din("nsa_tabD", [8, 128, 128]); maskD = din("nsa_maskD", [128, 128]); tabP = din("nsa_tabP", [8, 128, 128]); maskW4 = din("nsa_maskW4", [128, 128])
    tabC = din("nsa_tabC", [8, 504, 128]); maskC = din("nsa_maskC", [504, 128]); b31 = din("nsa_b31", [1, 8])
    A = din("nsa_A", [128, 126]); cover = din("nsa_cover", [256, 64]); xexp = din("nsa_xexp", [64, T]); selall = din("nsa_selall", [24, 24 * 64])
    k.GcT = fw.dram("nsa_GcT", [8, 504, 128], F32)
    k.GcTtok = Buf(None, "gct")
    n = k.nsa = {}
    n["Ed"] = fw.sbuf("n_Ed", [128, 8, 128]); n["Ep"] = fw.sbuf("n_Ep", [128, 8, 128]); n["W4"] = fw.sbuf("n_W4", [128, 128])
    n["A"] = fw.sbuf("n_A", [128, 126]); n["cover"] = fw.sbuf("n_cover", [128, 2, 64]); n["xexp"] = fw.sbuf("n_xexp", [128, T], BF16); n["r64"] = fw.sbuf("n_r64", [65, 64]); n["selall"] = fw.sbuf("n_selall", [24, 24 * 64])
    n["nb31"] = fw.sbuf("n_nb31", [128, 8])
    fw.dma("sp", n["A"][:], A.t.ap()[:, :], writes=[n["A"]])
    fw.dma("sp", n["cover"][:], cover.t.ap().rearrange("(a p) m -> p a m", p=128), writes=[n["cover"]])
    fw.dma("sp", n["selall"][:], selall.t.ap()[:, :], writes=[n["selall"]])
    fw.dma("sp", n["W4"][:], maskW4.t.ap()[:, :], writes=[n["W4"]])
    fw.dma("sp", n["nb31"][:], b31.t.ap()[0:1, :].partition_broadcast(128), writes=[n["nb31"]])
    fw.op("dve", lambda: nc.vector.tensor_scalar(out=n["nb31"][:], in0=n["nb31"][:], scalar1=-1.0, scalar2=None, op0=ALU.mult), reads=[n["nb31"]], writes=[n["nb31"]])
    with fw.scope():
        st = fw.sbuf("n_st", [128, T])
        fw.dma("sp", st[64:128, :], xexp.t.ap()[:, :], writes=[st])
        fw.op("dve", lambda: nc.vector.tensor_copy(out=n["xexp"][64:128, :], in_=st[64:128, :]), reads=[st], writes=[n["xexp"]])
        fw.op("dve", lambda: nc.vector.memset(n["r64"][:], 0.0), writes=[n["r64"]])
        fw.op("dve", lambda: nc.vector.memset(n["r64"][64:65, :], 1.0), writes=[n["r64"]])
        mD = fw.sbuf("n_mD", [128, 128]); fw.dma("sp", mD[:], maskD.t.ap()[:, :], writes=[mD])
        raw = fw.sbuf("n_raw", [128, 8, 128])
        for (src, dst, msk) in ((tabD, n["Ed"], mD), (tabP, n["Ep"], None)):
            fw.dma("sp", raw[:], src.t.ap().rearrange("h k q -> k h q"), writes=[raw])
            for h in range(8):
                fw.op("act", lambda: nc.scalar.activation(out=dst[:, h, :], in_=raw[:, h, :], func=AF.Exp, bias=n["nb31"][:, h:h + 1], scale=1.0), reads=[raw, n["nb31"]], writes=[dst])
                if msk is not None:
                    fw.op("dve", lambda: nc.vector.tensor_tensor(out=dst[:, h, :], in0=dst[:, h, :], in1=msk[:], op=ALU.mult), reads=[dst, msk], writes=[dst])
        rawc = fw.sbuf("n_rawc", [126, 4, 128]); mC = fw.sbuf("n_mC", [126, 4, 128])
        fw.dma("sp", mC[:], maskC.t.ap().rearrange("(a p) q -> p a q", p=126), writes=[mC])
        for h in range(8):
            fw.dma("sp", rawc[:], tabC.t.ap()[h].rearrange("(a p) q -> p a q", p=126), writes=[rawc])
            fw.op("act", lambda: nc.scalar.activation(out=rawc[:], in_=rawc[:], func=AF.Exp, bias=n["nb31"][0:126, h:h + 1], scale=1.0), reads=[rawc, n["nb31"]], writes=[rawc])
            fw.op("dve", lambda: nc.vector.tensor_tensor(out=rawc[:], in0=rawc[:], in1=mC[:], op=ALU.mult), reads=[rawc, mC], writes=[rawc])
            fw.dma("sp", k.GcT.t.ap()[h].rearrange("(a p) q -> p a q", p=126), rawc[:], reads=[rawc], writes=[k.GcTtok])


def nsa(nc, fw, k, l, PT, YT, GX=2, IQ=None):
    n = k.nsa
    ps = k.ps
    ident = k.ident
    IQ = list(range(NQ)) if IQ is None else IQ
    with fw.scope():
        bf = lambda nm, shp: fw.sbuf(nm, shp, BF16)
        stg = [fw.sbuf(f"n_stg{i}", [64, TB]) for i in range(2)]
        kcmp = bf("n_kcmp", [64, T]); vcmp = bf("n_vcmp", [64, T]); ksel = n["xexp"]; kwin = bf("n_kwin", [64, T])
        vseltok = bf("n_vseltok", [128, NQ, 65]); vwintok = bf("n_vwintok", [128, NQ, 65])
        qb = bf("n_qb", [128, 4, T]); negqp = fw.sbuf("n_negqp", [128, 128]); nd = fw.sbuf("n_nd", [65, 512])
        kcT = bf("n_kcT", [64, 256]); vctok = bf("n_vctok", [128, 2, 64])
        w1s = fw.sbuf("n_w1s", [64, 16, 256]); w1b = bf("n_w1b", [64, 32, 256]); w2s = fw.sbuf("n_w2s", [128, 2, 64]); w2b = bf("n_w2b", [128, 2, 64])
        peT = fw.sbuf("n_peT", [64, 32]); peTb = bf("n_peTb", [64, 32]); biasc = fw.sbuf("n_biasc", [128, 2])
        aT = bf("n_aT", [128, 2, 256])
        sgT = fw.sbuf("n_sgT", [24, T])
        onesb = k.onesb
        Ef = [fw.sbuf(f"n_Ef{i}", [128, 512]) for i in range(2)]; Pb = [bf(f"n_Pb{i}", [128, 512]) for i in range(2)]
        Gt = [fw.sbuf(f"n_Gt{i}", [128, 4, 128]) for i in range(2)]
        Pc = [fw.sbuf(f"n_Pc{i}", [128, 512]) for i in range(2)]; Pcb = [bf(f"n_Pcb{i}", [128, 512]) for i in range(2)]
        rd = fw.sbuf("n_rd", [128, 512])
        sc = fw.sbuf("n_sc", [128, 64]); sc2 = fw.sbuf("n_sc2", [128, 64]); m8 = fw.sbuf("n_m8", [128, 16]); negq = fw.sbuf("n_negq", [128, 64])
        negT4 = bf("n_negT4", [64, 4, 128])
        accs = [fw.sbuf(f"n_acc{i}", [64, 512]) for i in range(2)]; wt = fw.sbuf("n_wt", [64, 512]); ot = fw.sbuf("n_ot", [64, 512]); yb = [bf(f"n_yb{i}", [64, 512]) for i in range(2)]
        fw.op("dve", lambda: nc.vector.memset(kcT[:], 0.0), writes=[kcT])
        fw.op("dve", lambda: nc.vector.memset(negqp[:], 0.0), writes=[negqp])
        fw.op("dve", lambda: nc.vector.memset(vseltok[:, :, 64:65], 1.0), writes=[vseltok])
        fw.op("dve", lambda: nc.vector.memset(vwintok[:, :, 64:65], 1.0), writes=[vwintok])
        for tb in range(NTB):
            fw.dma("sp", sgT[:, tb * TB:(tb + 1) * TB], PT.t.ap()[26 * 128:26 * 128 + 24, tb * TB:(tb + 1) * TB], reads=[k.PTtok[26][tb]], writes=[sgT])
        fw.op("act", lambda: nc.scalar.activation(out=sgT[:], in_=sgT[:], func=AF.Sigmoid), reads=[sgT], writes=[sgT])
        evi = 0
        for g in range(GX):
            def load_stream(dst_ap_fn, ch, row0):
                for tb in range(NTB):
                    s_ = stg[tb % 2]
                    r0 = ch * 128 + row0
                    fw.dma("sp", s_[:], PT.t.ap()[r0:r0 + 64, tb * TB:(tb + 1) * TB], reads=[k.PTtok[ch][tb]], writes=[s_])
                    dst, dbuf = dst_ap_fn(tb)
                    if tb % 2 == 0:
                        fw.op("dve", lambda: nc.vector.tensor_copy(out=dst, in_=s_[:]), reads=[s_], writes=[dbuf])
                    else:
                        fw.op("pool", lambda: nc.gpsimd.tensor_copy(out=dst, in_=s_[:]), reads=[s_], writes=[dbuf])
            load_stream(lambda tb: (kcmp[:, tb * TB:(tb + 1) * TB], kcmp), 20, g * 64)
            load_stream(lambda tb: (vcmp[:, tb * TB:(tb + 1) * TB], vcmp), 21, g * 64)
            load_stream(lambda tb: (ksel[0:64, tb * TB:(tb + 1) * TB], ksel), 22, g * 64)
            load_stream(lambda tb: (kwin[:, tb * TB:(tb + 1) * TB], kwin), 24, g * 64)
            for j in range(4):
                hd = g * 4 + j
                load_stream(lambda tb: (qb[0:64, j, tb * TB:(tb + 1) * TB], qb), 16 + hd // 2, (hd % 2) * 64)
            for (ch, vt) in ((23, vseltok), (25, vwintok)):
                for tb in range(NTB):
                    s_ = stg[tb % 2]
                    r0 = ch * 128 + g * 64
                    fw.dma("sp", s_[:], PT.t.ap()[r0:r0 + 64, tb * TB:(tb + 1) * TB], reads=[k.PTtok[ch][tb]], writes=[s_])
                    for q4 in range(4):
                        fw.op("pe", lambda: nc.tensor.matmul(ps[5][:, q4 * 64:(q4 + 1) * 64], lhsT=s_[:, q4 * 128:(q4 + 1) * 128], rhs=ident[0:64, 0:64], start=True, stop=True),
                              reads=[s_, ident], writes=[ps[5]])
                    fw.op("dve", lambda: nc.vector.tensor_copy(out=vt[:, tb * 4:(tb + 1) * 4, 0:64], in_=ps[5][:, 0:256].rearrange("p (a d) -> p a d", d=64)), reads=[ps[5]], writes=[vt])
            for (kv, src, w1n, w2n, pen) in (("k", kcmp, "nsa_cmp_w1_k", "nsa_cmp_w2_k", "nsa_pe_k"), ("v", vcmp, "nsa_cmp_w1_v", "nsa_cmp_w2_v", "nsa_pe_v")):
                for hf in range(2):
                    fw.dma("sp", w1s[:], k.nsa_in[w1n].t.ap()[l, hf * 1024:(hf + 1) * 1024, :].rearrange("(l d) c -> d l c", d=64), writes=[w1s])
                    fw.op("pool", lambda: nc.gpsimd.tensor_copy(out=w1b[:, hf * 16:(hf + 1) * 16, :], in_=w1s[:]), reads=[w1s], writes=[w1b])
                fw.dma("sp", w2s[:], k.nsa_in[w2n].t.ap()[l].rearrange("(a p) d -> p a d", p=128), writes=[w2s])
                fw.op("dve", lambda: nc.vector.tensor_copy(out=w2b[:], in_=w2s[:]), reads=[w2s], writes=[w2b])
                fw.dma("sp", peT[:], k.nsa_in[pen].t.ap()[l].rearrange("l d -> d l"), writes=[peT], allow_slow_non_contiguous=True)
                fw.op("dve", lambda: nc.vector.tensor_copy(out=peTb[:], in_=peT[:]), reads=[peT], writes=[peTb])
                src3 = src.t[:].rearrange("p (n s) -> p n s", s=16)
                for cc in range(2):
                    for li in range(32):
                        fw.op("pe", lambda: nc.tensor.matmul(ps[2][:, cc:cc + 1], lhsT=w1b[:, li, cc * 128:(cc + 1) * 128], rhs=peTb[:, li:li + 1], start=(li == 0), stop=(li == 31)),
                              reads=[w1b, peTb], writes=[ps[2]])
                fw.op("dve", lambda: nc.vector.tensor_copy(out=biasc[:], in_=ps[2][:, 0:2]), reads=[ps[2]], writes=[biasc])
                for cc in range(2):
                    for li in range(32):
                        rhs = src3[:, li // 16:li // 16 + 255, li % 16]
                        fw.op("pe", lambda: nc.tensor.matmul(ps[cc][:, 0:255], lhsT=w1b[:, li, cc * 128:(cc + 1) * 128], rhs=rhs, start=(li == 0), stop=(li == 31)),
                              reads=[w1b, src], writes=[ps[cc]])
                    fw.op("act", lambda: nc.scalar.activation(out=aT[:, cc, 0:255], in_=ps[cc][:, 0:255], func=AF.Silu, bias=biasc[:, cc:cc + 1], scale=1.0), reads=[ps[cc], biasc], writes=[aT])
                if kv == "k":
                    for cc in range(2):
                        fw.op("pe", lambda: nc.tensor.matmul(ps[3][0:64, 0:255], lhsT=w2b[:, cc, :], rhs=aT[:, cc, 0:255], start=(cc == 0), stop=(cc == 1)), reads=[w2b, aT], writes=[ps[3]])
                    fw.op("dve", lambda: nc.vector.tensor_copy(out=kcT[:, 0:255], in_=ps[3][0:64, 0:255]), reads=[ps[3]], writes=[kcT])
                else:
                    fw.op("dve", lambda: nc.vector.memset(vctok[:], 0.0), writes=[vctok])
                    for nt in range(2):
                        rows = 128 if nt == 0 else 127
                        for cc in range(2):
                            fw.op("pe", lambda: nc.tensor.matmul(ps[4][0:rows, nt * 64:(nt + 1) * 64], lhsT=aT[:, cc, nt * 128:nt * 128 + rows], rhs=w2b[:, cc, :], start=(cc == 0), stop=(cc == 1)),
                                  reads=[aT, w2b], writes=[ps[4]])
                        fw.op("dve", lambda: nc.vector.tensor_copy(out=vctok[0:rows, nt, :], in_=ps[4][0:rows, nt * 64:(nt + 1) * 64]), reads=[ps[4]], writes=[vctok])
            def tile_gen(i):
                nonlocal evi
                acc = accs[IQ.index(i) % 2]
                Q = qb[0:64, :, i * 128:(i + 1) * 128]; Q128 = qb[:, :, i * 128:(i + 1) * 128]
                nts = [0] if i < 16 else [0, 1]
                for nt in nts:
                    sb = ps[evi % 2]; e_ = Pc[nt]; g_ = Gt[nt]
                    fw.op("pe", lambda: nc.tensor.matmul(sb[:], lhsT=kcT[:, nt * 128:(nt + 1) * 128], rhs=Q, start=True, stop=True), reads=[kcT, qb], writes=[sb])
                    r0 = 248 - 8 * i + nt * 128
                    fw.dma("sp", g_[:], k.GcT.t.ap()[g * 4:(g + 1) * 4, r0:r0 + 128, :].rearrange("h n q -> n h q"), reads=[k.GcTtok], writes=[g_])
                    fw.op("act", lambda: nc.scalar.activation(out=e_[:], in_=sb[:], func=AF.Exp, scale=0.125), reads=[sb], writes=[e_])
                    fw.op("dve", lambda: nc.vector.tensor_tensor(out=e_[:], in0=e_[:], in1=g_[:].rearrange("p h q -> p (h q)"), op=ALU.mult), reads=[e_, g_], writes=[e_])
                    fw.op("pool", lambda: nc.gpsimd.tensor_copy(out=Pcb[nt][:], in_=e_[:]), reads=[e_], writes=[Pcb[nt]])
                    evi += 1
                for x, nt in enumerate(nts):
                    fw.op("pe", lambda: nc.tensor.matmul(ps[3][:], lhsT=k.ones[:], rhs=Pc[nt][:], start=(x == 0), stop=(x == len(nts) - 1)), reads=[k.ones, Pc[nt]], writes=[ps[3]])
                for x, nt in enumerate(nts):
                    fw.op("pe", lambda: nc.tensor.matmul(ps[5][0:64, :], lhsT=vctok[:, nt, :], rhs=Pcb[nt][:], start=(x == 0), stop=(x == len(nts) - 1)), reads=[vctok, Pcb[nt]], writes=[ps[5]])
                fw.op("dve", lambda: nc.vector.tensor_scalar(out=rd[:], in0=ps[3][:], scalar1=1e-30, scalar2=None, op0=ALU.max), reads=[ps[3]], writes=[rd])
                fw.op("dve", lambda: nc.vector.reciprocal(out=rd[:], in_=rd[:]), reads=[rd], writes=[rd])
                for nt in nts:
                    fw.op("dve", lambda: nc.vector.tensor_tensor(out=Pc[nt][:], in0=Pc[nt][:], in1=rd[:], op=ALU.mult), reads=[Pc[nt], rd], writes=[Pc[nt]])
                tot = 4 * len(nts); x = 0
                for nt in nts:
                    for j in range(4):
                        fw.op("pe", lambda: nc.tensor.matmul(ps[4][:, 0:64], lhsT=Pc[nt][:, j * 128:(j + 1) * 128], rhs=n["cover"][:, nt, :], start=(x == 0), stop=(x == tot - 1)),
                              reads=[Pc[nt], n["cover"]], writes=[ps[4]])
                        x += 1
                a0 = 62 - 2 * i
                fw.op("dve", lambda: nc.vector.tensor_tensor(out=sc[:], in0=ps[4][:, 0:64], in1=n["A"][:, a0:a0 + 64], op=ALU.add), reads=[ps[4], n["A"]], writes=[sc])
                fw.op("dve", lambda: nc.vector.memset(sc[:, 0:1], BIG), writes=[sc])
                fw.op("dve", lambda: nc.vector.max(out=m8[:, 0:8], in_=sc[:]), reads=[sc], writes=[m8])
                fw.op("dve", lambda: nc.vector.tensor_scalar(out=sc2[:], in0=sc[:], scalar1=m8[:, 7:8], scalar2=-3.0 * BIG, op0=ALU.is_ge, op1=ALU.mult), reads=[sc, m8], writes=[sc2])
                fw.op("dve", lambda: nc.vector.tensor_tensor(out=sc2[:], in0=sc2[:], in1=sc[:], op=ALU.add), reads=[sc2, sc], writes=[sc2])
                fw.op("dve", lambda: nc.vector.max(out=m8[:, 8:16], in_=sc2[:]), reads=[sc2], writes=[m8])
                fw.op("dve", lambda: nc.vector.tensor_scalar(out=negqp[:, 64:128], in0=sc[:], scalar1=m8[:, 15:16], scalar2=None, op0=ALU.is_lt), reads=[sc, m8], writes=[negqp])
                fw.op("pe", lambda: nc.tensor.matmul(ps[4][:, 128:256], lhsT=negqp[:], rhs=ident[:], start=True, stop=True), reads=[negqp, ident], writes=[ps[4]])
                for j in range(4):
                    fw.op("dve", lambda: nc.vector.tensor_scalar(out=qb[64:128, j, i * 128:(i + 1) * 128], in0=ps[4][64:128, 128:256], scalar1=NEGM, scalar2=None, op0=ALU.mult), reads=[ps[4]], writes=[qb])
                def attend(kT, vtok, kts, tables, with_sel, p_num, p_den):
                    nonlocal evi
                    for x, kt in enumerate(kts):
                        sb = ps[evi % 2]; e_ = Ef[evi % 2]; p_ = Pb[evi % 2]
                        first, last = (x == 0), (x == len(kts) - 1)
                        if with_sel:
                            fw.op("pe", lambda: nc.tensor.matmul(sb[:], lhsT=kT[:, kt * 128:(kt + 1) * 128], rhs=Q128, start=True, stop=True), reads=[kT, qb], writes=[sb])
                        else:
                            fw.op("pe", lambda: nc.tensor.matmul(sb[:], lhsT=kT[0:64, kt * 128:(kt + 1) * 128], rhs=Q, start=True, stop=True), reads=[kT, qb], writes=[sb])
                        tab = tables.get(kt)
                        if tab is None:
                            fw.op("act", lambda: nc.scalar.activation(out=p_[:], in_=sb[:], func=AF.Exp, scale=0.125), reads=[sb], writes=[p_])
                        else:
                            fw.op("act", lambda: nc.scalar.activation(out=e_[:], in_=sb[:], func=AF.Exp, scale=0.125), reads=[sb], writes=[e_])
                            tb_, tap = tab
                            fw.op("dve", lambda: nc.vector.tensor_tensor(out=p_[:].rearrange("p (h q) -> p h q", h=4), in0=e_[:].rearrange("p (h q) -> p h q", h=4), in1=tap, op=ALU.mult),
                                  reads=[e_, tb_], writes=[p_])
                        fw.op("pe", lambda: nc.tensor.matmul(p_num[0:65, :], lhsT=vtok[:, kt, :], rhs=p_[:], start=first, stop=last), reads=[vtok, p_], writes=[p_num])
                        evi += 1
                    fw.op("dve", lambda: nc.vector.tensor_copy(out=nd[:], in_=p_num[0:65, :]), reads=[p_num], writes=[nd])
                    fw.op("pe", lambda: nc.tensor.matmul(p_den[0:64, :], lhsT=n["r64"][:], rhs=nd[:], start=True, stop=True), reads=[n["r64"], nd], writes=[p_den])
                Ed_g = n["Ed"][:, g * 4:(g + 1) * 4, :]; Ep_g = n["Ep"][:, g * 4:(g + 1) * 4, :]
                W4b = n["W4"][:].unsqueeze(1).to_broadcast([128, 4, 128])
                tabs = {i: (n["Ed"], Ed_g)}
                if i >= 1:
                    tabs[i - 1] = (n["Ep"], Ep_g)
                def combine(br, p_num, p_den, first):
                    for j in range(4):
                        r = (g * 4 + j) * 3 + br
                        fw.op("pe", lambda: nc.tensor.matmul(ps[2][0:64, j * 128:(j + 1) * 128], lhsT=n["selall"][:, r * 64:(r + 1) * 64], rhs=sgT[:, i * 128:(i + 1) * 128], start=True, stop=True),
                              reads=[n["selall"], sgT], writes=[ps[2]])
                    fw.op("dve", lambda: nc.vector.tensor_scalar(out=wt[:], in0=p_den, scalar1=1e-30, scalar2=None, op0=ALU.max), reads=[ps_of[id(p_den)]], writes=[wt])
                    fw.op("dve", lambda: nc.vector.reciprocal(out=wt[:], in_=wt[:]), reads=[wt], writes=[wt])
                    fw.op("dve", lambda: nc.vector.tensor_tensor(out=wt[:], in0=wt[:], in1=ps[2][0:64, :], op=ALU.mult), reads=[wt, ps[2]], writes=[wt])
                    if first:
                        fw.op("dve", lambda: nc.vector.tensor_tensor(out=acc[:], in0=p_num, in1=wt[:], op=ALU.mult), reads=[ps_of[id(p_num)], wt], writes=[acc])
                    else:
                        fw.op("dve", lambda: nc.vector.tensor_tensor(out=ot[:], in0=p_num, in1=wt[:], op=ALU.mult), reads=[ps_of[id(p_num)], wt], writes=[ot])
                        fw.op("pool", lambda: nc.gpsimd.tensor_tensor(out=acc[:], in0=acc[:], in1=ot[:], op=ALU.add), reads=[acc, ot], writes=[acc])
                ps_of = {}
                numc = ps[5][0:64, :]; denc = ps[3][0:64, :]
                ps_of[id(numc)] = ps[5]; ps_of[id(denc)] = ps[3]
                combine(0, numc, denc, True)
                yield
                attend(ksel, vseltok, list(range(i + 1)), tabs, True, ps[6], ps[3])
                nums = ps[6][0:64, :]; dens = ps[3][0:64, :]
                ps_of[id(nums)] = ps[6]; ps_of[id(dens)] = ps[3]
                combine(1, nums, dens, False)
                wk = [kt for kt in range(i - 4, i + 1) if kt >= 0]
                wtabs = dict(tabs)
                if i >= 4:
                    wtabs[i - 4] = (n["W4"], W4b)
                attend(kwin, vwintok, wk, wtabs, False, ps[7], ps[4])
                numw = ps[7][0:64, :]; denw = ps[4][0:64, :]
                ps_of[id(numw)] = ps[7]; ps_of[id(denw)] = ps[4]
                combine(2, numw, denw, False)
                y_ = yb[i % 2]
                fw.op("dve", lambda: nc.vector.tensor_copy(out=y_[:], in_=acc[:]), reads=[acc], writes=[y_])
                fw.dma("sp", YT.t.ap()[1].rearrange("(h d) t -> d h t", d=64)[:, g * 4:(g + 1) * 4, i * 128:(i + 1) * 128], y_[:].rearrange("p (h q) -> p h q", h=4),
                       reads=[y_], writes=[k.YTtok[1][g * 2][i // 4], k.YTtok[1][g * 2 + 1][i // 4]])
            prev_g = None
            for i in IQ:
                cur_g = tile_gen(i)
                next(cur_g)
                if prev_g is not None:
                    for _ in prev_g:
                        pass
                prev_g = cur_g
            if prev_g is not None:
                for _ in prev_g:
                    pass


N_LAYERS = 4
N_CORES = 4


def build_all(nc, fw, n_layers=N_LAYERS):
    k = K()
    alloc_tokens(k)
    setup_common(nc, fw, k, n_layers)
    setup_hgrn(nc, fw, k, n_layers)
    setup_rwkv(nc, fw, k)
    setup_nsa(nc, fw, k)
    setup_p5(nc, fw, k)
    fw.barrier()
    k.gsrc = Buf(None, "gsrc")
    PT = fw.dram("PT", [NP, T], F32)
    YT = fw.dram("YT", [3, 512, T], BF16)
    UT = fw.dram("UT", [DFF, T], BF16)
    XT1 = fw.dram("XT1", [D, T], F32)
    XT2 = fw.dram("XT2", [D, T], F32)
    OUT = fw.dram("OUT", [D, T], F32, kind="ExternalOutput")
    x0_tok = [Buf(None, f"x0_{t}") for t in range(NTB)]
    x1_tok = [Buf(None, f"x1_{t}") for t in range(NTB)]
    x2_tok = [Buf(None, f"x2_{t}") for t in range(NTB)]
    o_tok = [Buf(None, f"o_{t}") for t in range(NTB)]
    XTin, xin_tok = k.xT_in, x0_tok
    for l in range(n_layers):
        in_proj_phase(nc, fw, k, l, XTin, xin_tok, PT)
        hgrn(nc, fw, k, l, PT, YT)
        nsa(nc, fw, k, l, PT, YT)
        rwkv(nc, fw, k, l, PT, YT)
        proj_out(nc, fw, k, l, PT, YT, XTin, xin_tok, XT1, x1_tok)
        ffn(nc, fw, k, l, XT1, x1_tok, XT2, x2_tok, UT)
        XTin, xin_tok = XT2, x2_tok
    final_norm(nc, fw, k, XTin, xin_tok, OUT, o_tok)
    fw.finish(o_tok)
    return k


def kernel(**inputs):
    inp = {k_: np.asarray(v) for k_, v in inputs.items()}
    consts = {**host_consts(), **hgrn_consts(), **rwkv_consts(), **nsa_consts(inp["rel_bias"].astype(np.float32))}
    shared = {
        "ada_w": inp["ada_w"], "ada_b": inp["ada_b"], "norm1_w": inp["norm1_w"], "norm2_w": inp["norm2_w"],
        "w_in_p": pad_w_in(inp["w_in"]),
        "hgrn_lb_logits": inp["hgrn_lb_logits"], "hgrn_norm_w": inp["hgrn_norm_w"],
        "w_branch": inp["w_branch"], "w_out": inp["w_out"], "ffn_w1": inp["ffn_w1"], "ffn_w3": inp["ffn_w3"], "ffn_w2": inp["ffn_w2"],
        "final_norm_w": inp["final_norm_w"],
    }
    for nm in ("rw_mu", "rw_w0", "rw_w2", "rw_a0", "rw_a2", "rw_g2", "rw_k_k", "rw_k_a", "rw_lnx_w", "rw_lnx_b"):
        shared[nm] = inp[nm]
    shared["rw_r_k"] = np.ascontiguousarray(inp["rw_r_k"].reshape(4, 512))
    for nm in ("nsa_pe_k", "nsa_cmp_w1_k", "nsa_cmp_w2_k", "nsa_pe_v", "nsa_cmp_w1_v", "nsa_cmp_w2_v"):
        shared[nm] = inp[nm]
    shared.update(consts)
    shared = {k_: np.ascontiguousarray(v, dtype=np.float32) for k_, v in shared.items()}
    B = inp["x"].shape[0]
    in_maps = []
    for b in range(B):
        m = dict(shared)
        m["xT"] = np.ascontiguousarray(inp["x"][b].T.astype(np.float32))
        m["c8"] = np.ascontiguousarray(inp["c"][b].reshape(8, 128).T.astype(np.float32))
        in_maps.append(m)
    nc = bass.Bass("TRN2", target_bir_lowering=False)
    with ExitStack() as es:
        fw = FW(nc, es)
        build_all(nc, fw)
    res = run_bass_kernel_spmd(nc, in_maps, core_ids=list(range(B)))
    out = np.stack([np.asarray(res.results[b]["OUT"]).T for b in range(B)], axis=0)
    return np.ascontiguousarray(out.astype(np.float32))
```

```python
import numpy as np
from contextlib import ExitStack, contextmanager
import concourse.bass as bass
import concourse.mybir as mybir
from concourse.bass_utils import run_bass_kernel_spmd

F32 = mybir.dt.float32
BF16 = mybir.dt.bfloat16
AF = mybir.ActivationFunctionType
ALU = mybir.AluOpType
AX = mybir.AxisListType

F32R = mybir.dt.float32r
USE_F32R = True


def MM(nc, out, lhsT, rhs, **kw):
    if USE_F32R and (lhsT.dtype == F32R or rhs.dtype == F32R):
        if lhsT.dtype == F32:
            lhsT = lhsT.bitcast(F32R)
        if rhs.dtype == F32:
            rhs = rhs.bitcast(F32R)
    return nc.tensor.matmul(out, lhsT=lhsT, rhs=rhs, **kw)


SAME_ENGINE_SYNC = True
N_DMA_SLOTS = 12
DMA_Q_MAP = {"pool": "sp", "act": "sp"}


class Buf:
    __slots__ = ("t", "w", "r", "name")

    def __init__(self, t=None, name=""):
        self.t = t
        self.w = None
        self.r = []
        self.name = name

    def __getitem__(self, k):
        return self.t[k]


class FW:
    def __init__(self, nc, es):
        self.nc = nc
        self.es = es
        self.eng = {"pe": nc.tensor, "act": nc.scalar, "dve": nc.vector, "pool": nc.gpsimd, "sp": nc.sync}
        self.sem = {k: es.enter_context(nc.semaphore("s_" + k)) for k in self.eng}
        self.cnt = {k: 0 for k in self.eng}
        self.seen = {k: {} for k in self.eng}
        self.dsem = {}
        self.dslot_use = {}
        self.dnext = {}
        for q in ("sp", "act", "pool"):
            self.dsem[q] = [es.enter_context(nc.semaphore(f"d_{q}{i}")) for i in range(N_DMA_SLOTS)]
            self.dslot_use[q] = [0] * N_DMA_SLOTS
            self.dnext[q] = 0
        self.n_inst = 0

    def sbuf(self, name, shape, dt=F32):
        self.n_alloc = getattr(self, "n_alloc", 0) + 1
        name = f"{name}_u{self.n_alloc}"
        return Buf(self.es.enter_context(self.nc.sbuf_tensor(name, list(shape), dt)), name)

    def psum(self, name, shape, dt=F32):
        return Buf(self.es.enter_context(self.nc.psum_tensor(name, list(shape), dt)), name)

    def dram(self, name, shape, dt=F32, kind="Internal"):
        return Buf(self.nc.dram_tensor(name, list(shape), dt, kind=kind), name)

    def _wait(self, e, ticket):
        if ticket is None:
            return
        kind = ticket[0]
        if kind == "e":
            _, src, n = ticket
            if src == e and (e == "pe" or not SAME_ENGINE_SYNC):
                return
            key = ("e", src)
            if self.seen[e].get(key, 0) >= n:
                return
            self.eng[e].wait_ge(self.sem[src], n)
            self.seen[e][key] = n
        else:
            _, q, slot, val = ticket
            key = ("d", q, slot)
            if self.seen[e].get(key, 0) >= val:
                return
            self.eng[e].wait_ge(self.dsem[q][slot], val)
            self.seen[e][key] = val

    def _deps(self, e, reads, writes):
        for b in reads:
            self._wait(e, b.w)
        for b in writes:
            self._wait(e, b.w)
            for t in b.r:
                self._wait(e, t)

    def _record(self, ticket, reads, writes):
        for b in reads:
            if ticket[0] == "e":
                b.r = [t for t in b.r if not (t[0] == "e" and t[1] == ticket[1])]
            b.r.append(ticket)
        for b in writes:
            b.w = ticket
            b.r = []

    def op(self, e, fn, reads=(), writes=()):
        self._deps(e, reads, writes)
        inst = fn()
        self.cnt[e] += 1
        inst.then_inc(self.sem[e], 1)
        self._record(("e", e, self.cnt[e]), reads, writes)
        self.n_inst += 1
        return inst

    def dma(self, q, out, in_, reads=(), writes=(), **kw):
        q = DMA_Q_MAP.get(q, q)
        self._deps(q, reads, writes)
        slot = self.dnext[q]
        self.dnext[q] = (slot + 1) % N_DMA_SLOTS
        uses = self.dslot_use[q][slot]
        if uses > 0:
            self._wait(q, ("d", q, slot, 16 * uses))
        inst = self.eng[q].dma_start(out=out, in_=in_, **kw)
        inst.then_inc(self.dsem[q][slot], 16)
        self.dslot_use[q][slot] = uses + 1
        self._record(("d", q, slot, 16 * (uses + 1)), reads, writes)
        self.n_inst += 1
        return inst

    @contextmanager
    def scope(self):
        old = self.es
        with ExitStack() as es2:
            self.es = es2
            try:
                yield
            finally:
                self.es = old
            self.barrier()

    def barrier(self):
        for e in self.eng:
            for k2 in self.eng:
                if k2 != e and self.cnt[k2] > 0:
                    self._wait(e, ("e", k2, self.cnt[k2]))
            for q in self.dsem:
                for slot in range(N_DMA_SLOTS):
                    u = self.dslot_use[q][slot]
                    if u:
                        self._wait(e, ("d", q, slot, 16 * u))

    def finish(self, bufs):
        for b in bufs:
            self._wait("sp", b.w)
        for k in self.eng:
            if k != "sp" and self.cnt[k] > 0:
                self._wait("sp", ("e", k, self.cnt[k]))
        for q in self.dsem:
            for slot in range(N_DMA_SLOTS):
                u = self.dslot_use[q][slot]
                if u:
                    self._wait("sp", ("d", q, slot, 16 * u))


import numpy as np

T = 4096
D = 1024
NCH = 65
NP = NCH * 128
TB = 512
NTB = T // TB


def host_consts():
    c = {}
    c["ident"] = np.eye(128, dtype=np.float32)
    c["ones"] = np.ones((128, 128), np.float32)
    return c


class K:
    pass


def setup_common(nc, fw, k, n_layers):
    def din(name, shape, dt=F32):
        return fw.dram(name, shape, dt, kind="ExternalInput")
    k.xT_in = din("xT", [D, T])
    k.c8 = din("c8", [128, 8])
    k.ada_w = din("ada_w", [4, D, 6 * D])
    k.ada_b = din("ada_b", [4, 6 * D])
    k.norm1_w = din("norm1_w", [4, D])
    k.norm2_w = din("norm2_w", [4, D])
    k.w_in = din("w_in_p", [4, D, NP])
    k.identD = din("ident", [128, 128])
    k.onesD = din("ones", [128, 128])

    k.ident = fw.sbuf("ident_s", [128, 128])
    k.ones = fw.sbuf("ones_s", [128, 128])
    fw.dma("sp", k.ident[:], k.identD.t.ap()[:, :], writes=[k.ident])
    fw.dma("sp", k.ones[:], k.onesD.t.ap()[:, :], writes=[k.ones])
    k.identb = fw.sbuf("ident_b", [128, 128], BF16)
    k.onesb = fw.sbuf("ones_b", [128, 128], BF16)
    fw.op("dve", lambda: nc.vector.tensor_copy(out=k.identb[:], in_=k.ident[:]), reads=[k.ident], writes=[k.identb])
    fw.op("dve", lambda: nc.vector.tensor_copy(out=k.onesb[:], in_=k.ones[:]), reads=[k.ones], writes=[k.onesb])

    k.ps = [fw.psum(f"ps{i}", [128, 512]) for i in range(8)]

    cs = fw.sbuf("c_s", [128, 8])
    fw.dma("sp", cs[:], k.c8.t.ap()[:, :], writes=[cs])
    cond = fw.sbuf("cond_s", [128, 8])
    fw.op("act", lambda: nc.scalar.activation(out=cond[:], in_=cs[:], func=AF.Silu), reads=[cs], writes=[cond])
    k.ada = fw.sbuf("ada_s", [128, 4, 48])
    adab = fw.sbuf("adab_s", [128, 4, 48])
    fw.dma("sp", adab[:], k.ada_b.t.ap().rearrange("l (j p) -> p l j", p=128), writes=[adab], allow_slow_non_contiguous=True)
    k.n1 = fw.sbuf("n1_s", [128, 4, 8])
    k.n2 = fw.sbuf("n2_s", [128, 4, 8])
    fw.dma("sp", k.n1[:], k.norm1_w.t.ap().rearrange("l (j p) -> p l j", p=128), writes=[k.n1], allow_slow_non_contiguous=True)
    fw.dma("sp", k.n2[:], k.norm2_w.t.ap().rearrange("l (j p) -> p l j", p=128), writes=[k.n2], allow_slow_non_contiguous=True)
    with fw.scope():
      wst = [fw.sbuf(f"adaw_st{i}", [128, 6 * D]) for i in range(2)]
      for l in range(n_layers):
        pst = k.ps[l % 2]
        for kc in range(8):
            w = wst[kc % 2]
            fw.dma("sp" if kc % 2 == 0 else "act", w[:], k.ada_w.t.ap()[l, kc * 128:(kc + 1) * 128, :], writes=[w])
            for j in range(48):
                col = kc * 48 + j
                fw.op("pe", lambda: nc.tensor.matmul(pst[:, col:col + 1], lhsT=w[:, j * 128:(j + 1) * 128], rhs=cond[:, kc:kc + 1],
                                                     start=True, stop=True), reads=[w, cond], writes=[pst])
        fw.op("dve", lambda: nc.vector.tensor_reduce(out=k.ada[:, l, :], in_=pst[:, 0:384].rearrange("p (k j) -> p j k", k=8),
                                                     axis=AX.X, op=ALU.add), reads=[pst], writes=[k.ada])
        fw.op("dve", lambda: nc.vector.tensor_tensor(out=k.ada[:, l, :], in0=k.ada[:, l, :], in1=adab[:, l, :], op=ALU.add),
              reads=[k.ada, adab], writes=[k.ada])
    k.g1 = fw.sbuf("g1_s", [128, 4, 8])
    k.g2 = fw.sbuf("g2_s", [128, 4, 8])
    for l in range(n_layers):
        fw.op("dve", lambda: nc.vector.scalar_tensor_tensor(out=k.g1[:, l, :], in0=k.ada[:, l, 8:16], scalar=1.0, in1=k.n1[:, l, :],
                                                            op0=ALU.add, op1=ALU.mult), reads=[k.ada, k.n1], writes=[k.g1])
        fw.op("dve", lambda: nc.vector.scalar_tensor_tensor(out=k.g2[:, l, :], in0=k.ada[:, l, 32:40], scalar=1.0, in1=k.n2[:, l, :],
                                                            op0=ALU.add, op1=ALU.mult), reads=[k.ada, k.n2], writes=[k.g2])


def norm_mod(nc, fw, k, XT, g, s, hT, tag):
    xs_b = k.xs_bufs
    for tb in range(NTB):
        xs = xs_b[tb % 2]
        fw.dma("sp" if tb % 2 == 0 else "act", xs[:], XT.t.ap()[:, tb * TB:(tb + 1) * TB].rearrange("(k p) t -> p k t", p=128),
               reads=[XT], writes=[xs])
        sq = k.sq_buf
        fw.op("act", lambda: nc.scalar.activation(out=sq[:], in_=xs[:], func=AF.Square), reads=[xs], writes=[sq])
        pst = k.ps[7]
        for kc in range(8):
            fw.op("pe", lambda: nc.tensor.matmul(pst[:], lhsT=k.ones[:], rhs=sq[:, kc, :], start=(kc == 0), stop=(kc == 7)),
                  reads=[k.ones, sq], writes=[pst])
        rstd = k.rstd_buf
        fw.op("dve", lambda: nc.vector.tensor_scalar(out=rstd[:], in0=pst[:], scalar1=1.0 / D, scalar2=1e-6, op0=ALU.mult, op1=ALU.add),
              reads=[pst], writes=[rstd])
        fw.op("act", lambda: nc.scalar.activation(out=rstd[:], in_=rstd[:], func=AF.Sqrt), reads=[rstd], writes=[rstd])
        fw.op("dve", lambda: nc.vector.reciprocal(out=rstd[:], in_=rstd[:]), reads=[rstd], writes=[rstd])
        for kc in range(8):
            tmp = k.tmp_bufs[kc % 2]
            fw.op("dve", lambda: nc.vector.scalar_tensor_tensor(out=tmp[:], in0=xs[:, kc, :], scalar=g[:, kc:kc + 1], in1=rstd[:],
                                                                op0=ALU.mult, op1=ALU.mult), reads=[xs, rstd], writes=[tmp])
            fw.op("act", lambda: nc.scalar.activation(out=hT[:, kc, tb * TB:(tb + 1) * TB], in_=tmp[:], func=AF.Identity,
                                                      bias=s[:, kc:kc + 1], scale=1.0), reads=[tmp], writes=[hT])


def in_proj_wload(nc, fw, k, l, cg):
    wst = k.w_st[cg % 2]
    wb = k.w_bf[cg % 2]
    fw.dma("sp", wst[:], k.w_in.t.ap()[l, :, cg * 640:(cg + 1) * 640].rearrange("(k p) n -> p k n", p=128),
           reads=[], writes=[wst])
    fw.op("pool", lambda: nc.gpsimd.tensor_copy(out=wb[:], in_=wst[:]), reads=[wst], writes=[wb])


def in_proj(nc, fw, k, l, hT, PT, preloaded=False):
    GC = 5
    NG = NCH // GC
    ev = 0
    if not preloaded:
        in_proj_wload(nc, fw, k, l, 0)
    for cg in range(NG):
        wb = k.w_bf[cg % 2]
        if cg + 1 < NG:
            in_proj_wload(nc, fw, k, l, cg + 1)
        for tb in range(NTB):
            for ch in range(GC):
                pst = k.ps[(0, 3, 1, 4)[ev % 4]]
                for kc in range(8):
                    fw.op("pe", lambda: nc.tensor.matmul(pst[:], lhsT=wb[:, kc, ch * 128:(ch + 1) * 128], rhs=hT[:, kc, tb * TB:(tb + 1) * TB],
                                                         start=(kc == 0), stop=(kc == 7)), reads=[wb, hT], writes=[pst])
                o = k.ev_bufs[ev % 4]
                if ev % 2 == 0:
                    fw.op("act", lambda: nc.scalar.copy(out=o[:], in_=pst[:]), reads=[pst], writes=[o])
                else:
                    fw.op("dve", lambda: nc.vector.tensor_copy(out=o[:], in_=pst[:]), reads=[pst], writes=[o])
                row = (cg * GC + ch) * 128
                fw.dma("pool" if ev % 2 == 0 else "sp", PT.t.ap()[row:row + 128, tb * TB:(tb + 1) * TB], o[:], reads=[o], writes=[k.PTtok[cg * GC + ch][tb]])
                ev += 1


def alloc_p1(fw, k):
    k.xs_bufs = [fw.sbuf(f"xs{i}", [128, 8, TB]) for i in range(2)]
    k.sq_buf = fw.sbuf("sq", [128, 8, TB])
    k.rstd_buf = fw.sbuf("rstd", [128, TB])
    k.tmp_bufs = [fw.sbuf(f"tmp{i}", [128, TB]) for i in range(2)]
    k.hT = fw.sbuf("hT", [128, 8, T], BF16)
    k.w_st = [fw.sbuf(f"wst{i}", [128, 8, 640]) for i in range(2)]
    k.w_bf = [fw.sbuf(f"wbf{i}", [128, 8, 640], BF16) for i in range(2)]
    k.ev_bufs = [fw.sbuf(f"ev{i}", [128, TB]) for i in range(4)]
    k.PTtok = [[Buf(None, f"pt{c}_{t}") for t in range(NTB)] for c in range(NCH)]


def pad_w_in(w_in):
    L = w_in.shape[0]
    out = np.zeros((L, D, NP), np.float32)
    out[:, :, :3352] = w_in[:, :, :3352]
    out[:, :, 3456:] = w_in[:, :, 3352:]
    return out


import numpy as np


def hgrn_consts():
    c = {}
    s = np.arange(128)
    c["mask_bd"] = ((s[:, None] // 32 == s[None, :] // 32) & (s[:, None] <= s[None, :])).astype(np.float32)
    c["rowmask"] = (s[:, None] // 32 == np.arange(4)[None, :]).astype(np.float32)
    m = np.ones((128, 512), np.float32)
    m[:, ::32] = 0.0
    c["scanmask32"] = m
    return c


def setup_hgrn(nc, fw, k, n_layers):
    def din(name, shape, dt=F32):
        return fw.dram(name, shape, dt, kind="ExternalInput")
    k.lb_logits = din("hgrn_lb_logits", [4, 512])
    k.hgrn_nw = din("hgrn_norm_w", [4, 512])
    md = din("mask_bd", [128, 128]); rm = din("rowmask", [128, 4]); sm = din("scanmask32", [128, 512])
    k.mask_bd = fw.sbuf("mask_bd_s", [128, 128]); k.rowmask = fw.sbuf("rowmask_s", [128, 4]); k.scanmask32 = fw.sbuf("scanmask32_s", [128, 512])
    fw.dma("sp", k.mask_bd[:], md.t.ap()[:, :], writes=[k.mask_bd])
    fw.dma("sp", k.rowmask[:], rm.t.ap()[:, :], writes=[k.rowmask])
    fw.dma("sp", k.scanmask32[:], sm.t.ap()[:, :], writes=[k.scanmask32])
    k.hnw = fw.sbuf("hnw_s", [128, 4, 4])
    fw.dma("sp", k.hnw[:], k.hgrn_nw.t.ap().rearrange("l (h p) -> p l h", p=128), writes=[k.hnw], allow_slow_non_contiguous=True)
    lbl = fw.sbuf("lbl_s", [128, 4, 4])
    fw.dma("sp", lbl[:], k.lb_logits.t.ap().rearrange("l (h p) -> p l h", p=128), writes=[lbl], allow_slow_non_contiguous=True)
    e = fw.sbuf("lbe_s", [128, 4, 4])
    fw.op("act", lambda: nc.scalar.activation(out=e[:], in_=lbl[:], func=AF.Exp), reads=[lbl], writes=[e])
    ssum = fw.sbuf("lbsum_s", [128, 4])
    fw.op("dve", lambda: nc.vector.tensor_tensor(out=ssum[:], in0=e[:, 0, :], in1=e[:, 1, :], op=ALU.add), reads=[e], writes=[ssum])
    fw.op("dve", lambda: nc.vector.tensor_tensor(out=ssum[:], in0=ssum[:], in1=e[:, 2, :], op=ALU.add), reads=[e, ssum], writes=[ssum])
    fw.op("dve", lambda: nc.vector.tensor_tensor(out=ssum[:], in0=ssum[:], in1=e[:, 3, :], op=ALU.add), reads=[e, ssum], writes=[ssum])
    fw.op("dve", lambda: nc.vector.reciprocal(out=ssum[:], in_=ssum[:]), reads=[ssum], writes=[ssum])
    k.lb = fw.sbuf("lb_s", [128, 4, 4])
    k.oml = fw.sbuf("oml_s", [128, 4, 4])
    k.noml = fw.sbuf("noml_s", [128, 4, 4])
    fw.op("dve", lambda: nc.vector.memset(k.lb[:], 0.0), writes=[k.lb])
    for l in range(1, 4):
        fw.op("dve", lambda: nc.vector.tensor_tensor(out=e[:, l, :], in0=e[:, l, :], in1=ssum[:], op=ALU.mult), reads=[e, ssum], writes=[e])
        fw.op("dve", lambda: nc.vector.tensor_tensor(out=k.lb[:, l, :], in0=k.lb[:, l - 1, :], in1=e[:, l, :], op=ALU.add), reads=[e, k.lb], writes=[k.lb])
    fw.op("dve", lambda: nc.vector.tensor_scalar(out=k.oml[:], in0=k.lb[:], scalar1=-1.0, scalar2=1.0, op0=ALU.mult, op1=ALU.add), reads=[k.lb], writes=[k.oml])
    fw.op("dve", lambda: nc.vector.tensor_scalar(out=k.noml[:], in0=k.oml[:], scalar1=-1.0, scalar2=None, op0=ALU.mult), reads=[k.oml], writes=[k.noml])


def load_pt(fw, k, PT, q, dst, ch, tb, rows=128, row0=0):
    r = ch * 128 + row0
    fw.dma(q, dst, PT.t.ap()[r:r + rows, tb * TB:(tb + 1) * TB], reads=[k.PTtok[ch][tb]], writes=[])


def hgrn(nc, fw, k, l, PT, YT):
    with fw.scope():
        f32t = lambda n: fw.sbuf(n, [128, TB])
        bft = lambda n: fw.sbuf(n, [128, TB], BF16)
        zq = [f32t(f"h_zq{i}") for i in range(2)]; zf = [f32t(f"h_zf{i}") for i in range(2)]
        zi = [f32t(f"h_zi{i}") for i in range(2)]; zg = [f32t(f"h_zg{i}") for i in range(2)]
        q = f32t("h_q"); sg = f32t("h_sg"); kk = f32t("h_k"); b = f32t("h_b"); t1 = f32t("h_t1"); t2 = f32t("h_t2")
        kh = f32t("h_kh"); gam = fw.sbuf("h_gam", [128, 16])
        qt = bft("h_qt"); kt = bft("h_kt"); khb = bft("h_khb"); vb = bft("h_vb")
        AT = [fw.sbuf(f"h_AT{i}", [128, 128], BF16) for i in range(2)]
        Vt = [fw.sbuf(f"h_Vt{i}", [128, 128], BF16) for i in range(2)]
        khz = [[fw.sbuf(f"h_khz{i}_{c}", [128, 128], BF16) for c in range(4)] for i in range(2)]
        S = fw.sbuf("h_S", [128, 128])
        Sb = [fw.sbuf(f"h_Sb{i}", [128, 128], BF16) for i in range(8)]
        ob = f32t("h_ob"); rs = f32t("h_rs"); yb = [bft(f"h_yb{i}") for i in range(2)]
        P_sc, P_vt, P_kt, P_o, P_u0, P_u1, P_n = (k.ps[i] for i in (3, 0, 4, 1, 5, 6, 7))
        it = 0
        sbi = 0
        import os
        SUB = int(os.environ.get('SUB', '9')); STAGE = int(os.environ.get('STAGE', '9')); NHX = int(os.environ.get('NHX', '4')); NTBX = int(os.environ.get('NTBX', '8'))
        for h in range(NHX):
            fw.op("dve", lambda: nc.vector.memset(S[:], 0.0), writes=[S])
            lbc = k.lb[:, l, h:h + 1]; omlc = k.oml[:, l, h:h + 1]; nomlc = k.noml[:, l, h:h + 1]
            for tb in range(NTBX):
                z_q, z_f, z_i, z_g = zq[it % 2], zf[it % 2], zi[it % 2], zg[it % 2]
                for (dst, ch, qq) in ((z_q, h, "sp"), (z_f, 4 + h, "act"), (z_i, 8 + h, "sp"), (z_g, 12 + h, "act")):
                    r = ch * 128
                    fw.dma(qq, dst[:], PT.t.ap()[r:r + 128, tb * TB:(tb + 1) * TB], reads=[k.PTtok[ch][tb]], writes=[dst])
                fw.op("act", lambda: nc.scalar.activation(out=q[:], in_=z_q[:], func=AF.Silu), reads=[z_q], writes=[q])
                fw.op("act", lambda: nc.scalar.activation(out=sg[:], in_=z_f[:], func=AF.Sigmoid), reads=[z_f], writes=[sg])
                fw.op("dve", lambda: nc.vector.tensor_scalar(out=t1[:], in0=sg[:], scalar1=omlc, scalar2=lbc, op0=ALU.mult, op1=ALU.add), reads=[sg], writes=[t1])
                fw.op("act", lambda: nc.scalar.activation(out=t1[:], in_=t1[:], func=AF.Ln), reads=[t1], writes=[t1])
                fw.op("dve", lambda: nc.vector.tensor_scalar(out=kk[:], in0=sg[:], scalar1=nomlc, scalar2=omlc, op0=ALU.mult, op1=ALU.add), reads=[sg], writes=[kk])
                fw.op("dve", lambda: nc.vector.tensor_tensor_scan(out=b[:], data0=k.scanmask32[:], data1=t1[:], initial=0.0, op0=ALU.mult, op1=ALU.add),
                      reads=[k.scanmask32, t1], writes=[b])
                fw.op("act", lambda: nc.scalar.activation(out=t2[:], in_=b[:], func=AF.Exp), reads=[b], writes=[t2])
                fw.op("dve", lambda: nc.vector.tensor_tensor(out=qt[:], in0=q[:], in1=t2[:], op=ALU.mult), reads=[q, t2], writes=[qt])
                fw.op("act", lambda: nc.scalar.activation(out=t2[:], in_=b[:], func=AF.Exp, scale=-1.0), reads=[b], writes=[t2])
                fw.op("dve", lambda: nc.vector.tensor_tensor(out=kt[:], in0=kk[:], in1=t2[:], op=ALU.mult), reads=[kk, t2], writes=[kt])
                b3 = b.t[:].rearrange("p (c t) -> p c t", t=32)
                fw.op("dve", lambda: nc.vector.tensor_tensor(out=t2.t[:].rearrange("p (c t) -> p c t", t=32), in0=b3[:, :, 31:32].to_broadcast([128, 16, 32]), in1=b3,
                                                             op=ALU.subtract), reads=[b], writes=[t2])
                fw.op("act", lambda: nc.scalar.activation(out=t2[:], in_=t2[:], func=AF.Exp), reads=[t2], writes=[t2])
                fw.op("dve", lambda: nc.vector.tensor_tensor(out=khb[:], in0=kk[:], in1=t2[:], op=ALU.mult), reads=[kk, t2], writes=[khb])
                fw.op("pool", lambda: nc.gpsimd.tensor_copy(out=vb[:], in_=z_i[:]), reads=[z_i], writes=[vb])
                fw.op("act", lambda: nc.scalar.activation(out=gam[:], in_=b3[:, :, 31], func=AF.Exp), reads=[b], writes=[gam])
                if os.environ.get('DBG') and it == 0:
                    for di, src in enumerate((q, kk, b, t2, t1, sg)):
                        fw.dma('sp', k.DBG.t.ap()[:, di, :], src[:], reads=[src], writes=[k.DBGtok])
                if STAGE < 2:
                    continue
                for tl in range(4):
                    cs = slice(tl * 128, (tl + 1) * 128)
                    a_t = AT[tl % 2]; v_t = Vt[tl % 2]; kz = khz[tl % 2]
                    fw.op("pe", lambda: nc.tensor.matmul(P_sc[:, cs], lhsT=kt[:, cs], rhs=qt[:, cs], start=True, stop=True), reads=[kt, qt], writes=[P_sc])
                    fw.op("dve", lambda: nc.vector.tensor_tensor(out=a_t[:], in0=P_sc[:, cs], in1=k.mask_bd[:], op=ALU.mult), reads=[P_sc, k.mask_bd], writes=[a_t])
                    if SUB < 2: continue
                    fw.op("pe", lambda: nc.tensor.matmul(P_vt[:, cs], lhsT=vb[:, cs], rhs=k.identb[:], start=True, stop=True), reads=[vb, k.identb], writes=[P_vt])
                    fw.op("act", lambda: nc.scalar.copy(out=v_t[:], in_=P_vt[:, cs]), reads=[P_vt], writes=[v_t])
                    if SUB < 3: continue
                    ksrc = {'khb': khb, 'vb': vb, 'qt': qt}[os.environ.get('KSRC', 'khb')]
                    fw.op("pe", lambda: nc.tensor.matmul(P_kt[:, cs], lhsT=ksrc[:, cs], rhs=k.identb[:], start=True, stop=True), reads=[ksrc, k.identb], writes=[P_kt])
                    for c in range(int(os.environ.get('NEV', '4'))):
                        if c % 2 == 0 or True:
                            fw.op("dve", lambda: nc.vector.tensor_scalar(out=kz[c][:], in0=P_kt[:, cs], scalar1=k.rowmask[:, c:c + 1], scalar2=None, op0=ALU.mult),
                                  reads=[P_kt, k.rowmask], writes=[kz[c]])
                        else:
                            fw.op("act", lambda: nc.scalar.activation(out=kz[c][:], in_=P_kt[:, cs], func=AF.Identity, scale=k.rowmask[:, c:c + 1]),
                                  reads=[P_kt, k.rowmask], writes=[kz[c]])
                    if SUB < 4: continue
                    if SUB < 5: continue
                    for c in range(4):
                        s_b = Sb[sbi % 8]; sbi += 1
                        fw.op("act", lambda: nc.scalar.copy(out=s_b[:], in_=S[:]), reads=[S], writes=[s_b])
                        c0 = tl * 128 + c * 32
                        fw.op("pe", lambda: nc.tensor.matmul(P_o[:, c0:c0 + 32], lhsT=v_t[:], rhs=a_t[:, c * 32:(c + 1) * 32], start=True, stop=False), reads=[v_t, a_t], writes=[P_o])
                        fw.op("pe", lambda: nc.tensor.matmul(P_o[:, c0:c0 + 32], lhsT=s_b[:], rhs=qt[:, c0:c0 + 32], start=False, stop=True), reads=[s_b, qt], writes=[P_o])
                        P_u = P_u0 if c % 2 == 0 else P_u1
                        fw.op("pe", lambda: nc.tensor.matmul(P_u[:, 0:128], lhsT=kz[c][:], rhs=v_t[:], start=True, stop=True), reads=[kz[c], v_t], writes=[P_u])
                        gi = tl * 4 + c
                        fw.op("dve", lambda: nc.vector.scalar_tensor_tensor(out=S[:], in0=S[:], scalar=gam[:, gi:gi + 1], in1=P_u[:, 0:128], op0=ALU.mult, op1=ALU.add),
                              reads=[S, gam, P_u], writes=[S])
                    fw.op("act", lambda: nc.scalar.copy(out=ob[:, cs], in_=P_o[:, cs]), reads=[P_o], writes=[ob])
                if STAGE < 3:
                    continue
                fw.op("act", lambda: nc.scalar.activation(out=t2[:], in_=ob[:], func=AF.Square), reads=[ob], writes=[t2])
                fw.op("pe", lambda: nc.tensor.matmul(P_n[:], lhsT=k.ones[:], rhs=t2[:], start=True, stop=True), reads=[k.ones, t2], writes=[P_n])
                S3 = int(os.environ.get('S3', '9'))
                if S3 < 2: continue
                fw.op("dve", lambda: nc.vector.tensor_scalar(out=rs[:], in0=P_n[:], scalar1=1.0 / 128, scalar2=1e-5, op0=ALU.mult, op1=ALU.add), reads=[P_n], writes=[rs])
                fw.op("act", lambda: nc.scalar.activation(out=rs[:], in_=rs[:], func=AF.Sqrt), reads=[rs], writes=[rs])
                fw.op("dve", lambda: nc.vector.reciprocal(out=rs[:], in_=rs[:]), reads=[rs], writes=[rs])
                if S3 < 3: continue
                fw.op("act", lambda: nc.scalar.activation(out=t1[:], in_=z_g[:], func=AF.Sigmoid), reads=[z_g], writes=[t1])
                fw.op("dve", lambda: nc.vector.scalar_tensor_tensor(out=rs[:], in0=ob[:], scalar=k.hnw[:, l, h:h + 1], in1=rs[:], op0=ALU.mult, op1=ALU.mult),
                      reads=[ob, rs, k.hnw], writes=[rs])
                y_b = yb[it % 2]
                fw.op("dve", lambda: nc.vector.tensor_tensor(out=y_b[:], in0=rs[:], in1=t1[:], op=ALU.mult), reads=[rs, t1], writes=[y_b])
                if S3 < 4: continue
                fw.dma("sp", YT.t.ap()[0, h * 128:(h + 1) * 128, tb * TB:(tb + 1) * TB], y_b[:], reads=[y_b], writes=[k.YTtok[0][h][tb]])
                it += 1


def alloc_tokens(k):
    k.PTtok = [[Buf(None, f"pt{c}_{t}") for t in range(NTB)] for c in range(NCH)]
    k.YTtok = [[[Buf(None, f"yt{m}_{c}_{t}") for t in range(NTB)] for c in range(4)] for m in range(3)]
    k.UTtok = [[Buf(None, f"ut{c}_{t}") for t in range(NTB)] for c in range(22)]


import numpy as np, os

CH_R, CH_K, CH_V, CH_WA, CH_G = 27, 31, 35, 39, 40


def rwkv_consts():
    c = {}
    i = np.arange(128)
    same = (i[:, None] // 64 == i[None, :] // 64)
    su = (same & (i[:, None] < i[None, :])).astype(np.float32)
    iu = (same & (i[:, None] <= i[None, :])).astype(np.float32)
    sl = (same & (i[:, None] > i[None, :])).astype(np.float32)
    c["rw_mask_su_iu"] = np.concatenate([su, iu], axis=1)
    c["rw_mask_negsu"] = -su
    c["rw_mask_negsl"] = -sl
    m = np.ones((64, 512), np.float32)
    m[:, ::64] = 0.0
    c["scanmask64"] = m
    c["ones64"] = np.ones((64, 64), np.float32)
    c["rowmask64"] = np.concatenate([(i[:, None] // 64 == np.arange(2)[None, :]).astype(np.float32), -(i[:, None] // 64 == np.arange(2)[None, :]).astype(np.float32)], axis=1)
    return c


def setup_rwkv(nc, fw, k):
    def din(name, shape, dt=F32):
        return fw.dram(name, shape, dt, kind="ExternalInput")
    k.rw = {}
    for nm, shp in (("rw_mu", [4, 1792]), ("rw_w0", [4, 512]), ("rw_w2", [4, 64, 512]), ("rw_a0", [4, 512]), ("rw_a2", [4, 64, 512]),
                    ("rw_g2", [4, 128, 512]), ("rw_k_k", [4, 512]), ("rw_k_a", [4, 512]), ("rw_r_k", [4, 512]), ("rw_lnx_w", [4, 512]), ("rw_lnx_b", [4, 512])):
        k.rw[nm] = din(nm, shp)
    cm = {}
    for nm, shp in (("rw_mask_su_iu", [128, 256]), ("rw_mask_negsu", [128, 128]), ("rw_mask_negsl", [128, 128]), ("scanmask64", [64, 512]), ("ones64", [64, 64]), ("rowmask64", [128, 4])):
        d = din(nm, shp)
        t = fw.sbuf(nm + "_s", shp)
        fw.dma("sp", t[:], d.t.ap()[:, :], writes=[t])
        cm[nm] = t
    k.rwc = cm


def rwkv(nc, fw, k, l, PT, YT, NHX=8, NTBX=NTB):
    with fw.scope():
        W = k.rw
        def pvec(nm):
            t = fw.sbuf("rp_" + nm, [64, 8])
            fw.dma("sp", t[:], W[nm].t.ap()[l, :].rearrange("(h p) -> p h", p=64), writes=[t], allow_slow_non_contiguous=True)
            return t
        w0 = pvec("rw_w0"); a0 = pvec("rw_a0"); k_k = pvec("rw_k_k"); k_a = pvec("rw_k_a"); r_k = pvec("rw_r_k"); lnw = pvec("rw_lnx_w"); lnb = pvec("rw_lnx_b")
        mu_rkv = fw.sbuf("rp_mu", [64, 24])
        fw.dma("sp", mu_rkv[:], W["rw_mu"].t.ap()[l, 0:1536].rearrange("(h p) -> p h", p=64), writes=[mu_rkv], allow_slow_non_contiguous=True)
        mu_wa = fw.sbuf("rp_muwa", [64, 2])
        fw.dma("sp", mu_wa[:], W["rw_mu"].t.ap()[l, 1536:1664].rearrange("(h p) -> p h", p=64), writes=[mu_wa], allow_slow_non_contiguous=True)
        mu_g = fw.sbuf("rp_mug", [128, 1])
        fw.dma("sp", mu_g[:], W["rw_mu"].t.ap()[l, 1664:1792].rearrange("(h p) -> p h", p=128), writes=[mu_g], allow_slow_non_contiguous=True)
        w2 = fw.sbuf("rp_w2", [64, 512], F32R); a2 = fw.sbuf("rp_a2", [64, 512], F32R); g2 = fw.sbuf("rp_g2", [128, 512], F32R)
        wstg = fw.sbuf("rp_wstg", [128, 512])
        for (dst_, nm_, rows_) in ((w2, "rw_w2", 64), (a2, "rw_a2", 64), (g2, "rw_g2", 128)):
            fw.dma("sp", wstg[0:rows_, :], W[nm_].t.ap()[l], writes=[wstg])
            fw.op("dve", lambda: nc.vector.tensor_copy(out=dst_[:], in_=wstg[0:rows_, :]), reads=[wstg], writes=[dst_])
        identR = fw.sbuf("rp_identR", [128, 128], F32R); ones64R = fw.sbuf("rp_ones64R", [64, 64], F32R)
        fw.op("dve", lambda: nc.vector.tensor_copy(out=identR[:], in_=k.ident[:]), reads=[k.ident], writes=[identR])
        fw.op("dve", lambda: nc.vector.tensor_copy(out=ones64R[:], in_=k.rwc["ones64"][:]), reads=[k.rwc["ones64"]], writes=[ones64R])
        C = k.rwc
        ident = identR
        t64 = lambda n, dt=F32: fw.sbuf(n, [64, TB], dt)
        xw_in = fw.sbuf("r_xwin", [64, TB + 1]); xa_in = fw.sbuf("r_xain", [64, TB + 1]); xg_in = fw.sbuf("r_xgin", [128, TB + 1])
        th = t64("r_th", F32R); xa = t64("r_xa", F32R); sgg = fw.sbuf("r_sgg", [128, TB], F32R); sq_r = t64("r_sqr", F32R); dtmp = fw.sbuf("r_dtmp", [128, TB])
        rin = fw.sbuf("r_rin", [64, TB + 1]); kin = fw.sbuf("r_kin", [64, TB + 1]); vin = fw.sbuf("r_vin", [64, TB + 1])
        k_s = t64("r_ks"); ld = t64("r_ld"); a_ = t64("r_a"); kap = t64("r_kap"); b_ = t64("r_b"); L = t64("r_L")
        e1 = t64("r_e1"); e2 = t64("r_e2"); e3 = t64("r_e3"); t1 = t64("r_t1"); t2 = t64("r_t2")
        M = [fw.sbuf(f"r_M{h}", [128, 64], F32R) for h in range(8)]
        S = []
        for s_ in range(2):
            B_ = {}
            for n_ in ("r_s", "v_s", "kp", "g_", "kapt", "kt", "bt", "kh", "bh", "yo"):
                B_[n_] = t64(f"r_{n_}{s_}", F32 if n_ in ("r_s", "kp", "g_") else F32R)
            B_["rt"] = fw.sbuf(f"r_rt{s_}", [128, TB], F32R); B_["gam"] = fw.sbuf(f"r_gam{s_}", [64, 8]); B_["dg"] = fw.sbuf(f"r_dg{s_}", [128, 64], F32R)
            B_["SCa"] = fw.sbuf(f"r_SCa{s_}", [128, 256], F32R); B_["SCb"] = fw.sbuf(f"r_SCb{s_}", [128, 256], F32R)
            B_["Y"] = [fw.sbuf(f"r_Y{s_}{i}", [128, 128], F32R) for i in range(2)]; B_["Z"] = [fw.sbuf(f"r_Z{s_}{i}", [128, 128], F32R) for i in range(2)]
            B_["P"] = [fw.sbuf(f"r_P{s_}{i}", [128, 128], F32R) for i in range(2)]
            B_["ktok"] = fw.sbuf(f"r_ktok{s_}", [128, 128], F32R); B_["vtok"] = fw.sbuf(f"r_vtok{s_}", [128, 64], F32R); B_["bhtok"] = fw.sbuf(f"r_bhtok{s_}", [128, 64], F32R)
            B_["khc"] = [fw.sbuf(f"r_khc{s_}{i}", [128, 64], F32R) for i in range(2)]; B_["bhc"] = [fw.sbuf(f"r_bhc{s_}{i}", [128, 64], F32R) for i in range(2)]
            B_["nWc"] = [fw.sbuf(f"r_nWc{s_}{i}", [128, 64], F32R) for i in range(2)]
            B_["WU"] = fw.sbuf(f"r_WU{s_}", [128, 128]); B_["nWU"] = fw.sbuf(f"r_nWU{s_}", [128, 128], F32R); B_["Rp"] = fw.sbuf(f"r_Rp{s_}", [128, 128], F32R)
            B_["PTm"] = [fw.sbuf(f"r_PTm{s_}{i}", [64, 64], F32R) for i in range(2)]; B_["Qm"] = [fw.sbuf(f"r_Qm{s_}{i}", [64, 64]) for i in range(2)]
            B_["ybuf"] = fw.sbuf(f"r_yb{s_}", [64, TB], BF16)
            bk = (0, 3, 4, 5) if s_ == 0 else (1, 6, 7, 2)
            B_["X"], B_["E0"], B_["E1"], B_["E2"] = (k.ps[i] for i in bk)
            S.append(B_)
        ps = k.ps
        for h in range(8):
            fw.op("dve", lambda: nc.vector.memset(M[h][:].bitcast(F32), 0.0), writes=[M[h]])
        for B_ in S:
            fw.op("dve", lambda: nc.vector.memset(B_["rt"][:].bitcast(F32), 0.0), writes=[B_["rt"]])
            fw.op("dve", lambda: nc.vector.memset(B_["dg"][:].bitcast(F32), 0.0), writes=[B_["dg"]])
            fw.op("dve", lambda: nc.vector.memset(B_["Rp"][:].bitcast(F32), 0.0), writes=[B_["Rp"]])
        RM = C["rowmask64"]

        def load_shift(dst, ch, row0, rows, tb):
            r0 = ch * 128 + row0
            if tb == 0:
                fw.op("dve", lambda: nc.vector.memset(dst[0:rows, 0:1], 0.0), writes=[dst])
                fw.dma("sp", dst[0:rows, 1:TB + 1], PT.t.ap()[r0:r0 + rows, 0:TB], reads=[k.PTtok[ch][0]], writes=[dst])
            else:
                fw.dma("sp", dst[0:rows, 0:TB + 1], PT.t.ap()[r0:r0 + rows, tb * TB - 1:(tb + 1) * TB], reads=[k.PTtok[ch][tb], k.PTtok[ch][tb - 1]], writes=[dst])

        def shift_mix(out, src, rows, mu_ap, tmp, mub):
            fw.op("dve", lambda: nc.vector.tensor_tensor(out=tmp[0:rows, :], in0=src[0:rows, 0:TB], in1=src[0:rows, 1:TB + 1], op=ALU.subtract), reads=[src], writes=[tmp])
            fw.op("dve", lambda: nc.vector.scalar_tensor_tensor(out=out[0:rows, :], in0=tmp[0:rows, :], scalar=mu_ap, in1=src[0:rows, 1:TB + 1], op0=ALU.mult, op1=ALU.add),
                  reads=[tmp, src, mub], writes=[out])

        it = 0
        for tb in range(NTBX):
            load_shift(xw_in, CH_WA, 0, 64, tb); load_shift(xa_in, CH_WA, 64, 64, tb); load_shift(xg_in, CH_G, 0, 128, tb)
            shift_mix(th, xw_in, 64, mu_wa[:, 0:1], dtmp, mu_wa)
            fw.op("act", lambda: nc.scalar.activation(out=th[:], in_=th[:], func=AF.Tanh), reads=[th], writes=[th])
            shift_mix(xa, xa_in, 64, mu_wa[:, 1:2], dtmp, mu_wa)
            shift_mix(sgg, xg_in, 128, mu_g[:, 0:1], dtmp, mu_g)
            fw.op("act", lambda: nc.scalar.activation(out=sgg[:], in_=sgg[:], func=AF.Sigmoid), reads=[sgg], writes=[sgg])
            def unit(s, h):
                B_ = S[s]
                X, E0, E1, E2 = B_["X"], B_["E0"], B_["E1"], B_["E2"]
                r_s, v_s, kp, g_, kapt, kt, bt, rt, kh, bh, gam, dg, yo = (B_[n_] for n_ in ("r_s", "v_s", "kp", "g_", "kapt", "kt", "bt", "rt", "kh", "bh", "gam", "dg", "yo"))
                SCa, SCb, Y, Z, P, ktok, vtok, bhtok, khc, bhc, nWc, WU, nWU, Rp, PTm, Qm, ybuf = (B_[n_] for n_ in ("SCa", "SCb", "Y", "Z", "P", "ktok", "vtok", "bhtok", "khc", "bhc", "nWc", "WU", "nWU", "Rp", "PTm", "Qm", "ybuf"))
                j, hh = h // 2, h % 2
                load_shift(rin, CH_R + j, hh * 64, 64, tb); load_shift(kin, CH_K + j, hh * 64, 64, tb); load_shift(vin, CH_V + j, hh * 64, 64, tb)
                shift_mix(r_s, rin, 64, mu_rkv[:, h:h + 1], dtmp, mu_rkv)
                shift_mix(k_s, kin, 64, mu_rkv[:, 8 + h:9 + h], dtmp, mu_rkv)
                shift_mix(v_s, vin, 64, mu_rkv[:, 16 + h:17 + h], dtmp, mu_rkv)
                hs = slice(h * 64, (h + 1) * 64)
                fw.op("pe", lambda: MM(nc, X[0:64, :], lhsT=w2[:, hs], rhs=th[:], start=True, stop=True), reads=[w2, th], writes=[X])
                fw.op("act", lambda: nc.scalar.activation(out=ld[:], in_=X[0:64, :], func=AF.Sigmoid, bias=w0[:, h:h + 1], scale=1.0), reads=[X, w0], writes=[ld])
                fw.op("dve", lambda: nc.vector.tensor_scalar(out=ld[:], in0=ld[:], scalar1=-0.6065306597126334, scalar2=None, op0=ALU.mult), reads=[ld], writes=[ld])
                fw.op("pe", lambda: MM(nc, X[0:64, :], lhsT=a2[:, hs], rhs=xa[:], start=True, stop=True), reads=[a2, xa], writes=[X])
                fw.op("act", lambda: nc.scalar.activation(out=a_[:], in_=X[0:64, :], func=AF.Sigmoid, bias=a0[:, h:h + 1], scale=1.0), reads=[X, a0], writes=[a_])
                fw.op("pe", lambda: MM(nc, X[0:64, :], lhsT=g2[:, hs], rhs=sgg[:], start=True, stop=True), reads=[g2, sgg], writes=[X])
                fw.op("act", lambda: nc.scalar.copy(out=g_[:], in_=X[0:64, :]), reads=[X], writes=[g_])
                fw.op("dve", lambda: nc.vector.tensor_scalar(out=kap[:], in0=k_s[:], scalar1=k_k[:, h:h + 1], scalar2=None, op0=ALU.mult), reads=[k_s, k_k], writes=[kap])
                fw.op("act", lambda: nc.scalar.activation(out=sq_r[:], in_=kap[:], func=AF.Square), reads=[kap], writes=[sq_r])
                fw.op("pe", lambda: MM(nc, E0[0:64, :], lhsT=ones64R[:], rhs=sq_r[:], start=True, stop=True), reads=[ones64R, sq_r], writes=[E0])
                fw.op("dve", lambda: nc.vector.tensor_scalar(out=t1[:], in0=E0[0:64, :], scalar1=1e-24, scalar2=None, op0=ALU.max), reads=[E0], writes=[t1])
                fw.op("act", lambda: nc.scalar.activation(out=t1[:], in_=t1[:], func=AF.Sqrt), reads=[t1], writes=[t1])
                fw.op("dve", lambda: nc.vector.reciprocal(out=t1[:], in_=t1[:]), reads=[t1], writes=[t1])
                fw.op("dve", lambda: nc.vector.tensor_tensor(out=kap[:], in0=kap[:], in1=t1[:], op=ALU.mult), reads=[kap, t1], writes=[kap])
                fw.op("dve", lambda: nc.vector.tensor_scalar(out=t1[:], in0=a_[:], scalar1=-1.0, scalar2=k_a[:, h:h + 1], op0=ALU.add, op1=ALU.mult), reads=[a_, k_a], writes=[t1])
                fw.op("dve", lambda: nc.vector.scalar_tensor_tensor(out=kp[:], in0=t1[:], scalar=1.0, in1=k_s[:], op0=ALU.add, op1=ALU.mult), reads=[t1, k_s], writes=[kp])
                fw.op("dve", lambda: nc.vector.tensor_tensor(out=b_[:], in0=a_[:], in1=kap[:], op=ALU.mult), reads=[a_, kap], writes=[b_])
                fw.op("dve", lambda: nc.vector.tensor_tensor_scan(out=L[:], data0=C["scanmask64"][:], data1=ld[:], initial=0.0, op0=ALU.mult, op1=ALU.add),
                      reads=[C["scanmask64"], ld], writes=[L])
                fw.op("act", lambda: nc.scalar.activation(out=e1[:], in_=L[:], func=AF.Exp), reads=[L], writes=[e1])
                fw.op("act", lambda: nc.scalar.activation(out=e2[:], in_=L[:], func=AF.Exp, scale=-1.0), reads=[L], writes=[e2])
                L3 = L.t[:].rearrange("p (c t) -> p c t", t=64)
                fw.op("dve", lambda: nc.vector.tensor_tensor(out=t2.t[:].rearrange("p (c t) -> p c t", t=64), in0=L3[:, :, 63:64].to_broadcast([64, 8, 64]), in1=L3, op=ALU.subtract),
                      reads=[L], writes=[t2])
                fw.op("act", lambda: nc.scalar.activation(out=e3[:], in_=t2[:], func=AF.Exp), reads=[t2], writes=[e3])
                fw.op("act", lambda: nc.scalar.activation(out=gam[:], in_=L3[:, :, 63], func=AF.Exp), reads=[L], writes=[gam])
                fw.op("dve", lambda: nc.vector.tensor_tensor(out=t1[:], in0=L[:], in1=ld[:], op=ALU.subtract), reads=[L, ld], writes=[t1])
                fw.op("act", lambda: nc.scalar.activation(out=t1[:], in_=t1[:], func=AF.Exp), reads=[t1], writes=[t1])
                fw.op("dve", lambda: nc.vector.tensor_tensor(out=kapt[:], in0=kap[:], in1=t1[:], op=ALU.mult), reads=[kap, t1], writes=[kapt])
                fw.op("dve", lambda: nc.vector.tensor_tensor(out=kt[:], in0=kp[:], in1=e2[:], op=ALU.mult), reads=[kp, e2], writes=[kt])
                fw.op("dve", lambda: nc.vector.tensor_tensor(out=bt[:], in0=b_[:], in1=e2[:], op=ALU.mult), reads=[b_, e2], writes=[bt])
                fw.op("dve", lambda: nc.vector.tensor_tensor(out=rt[0:64, :], in0=r_s[:], in1=e1[:], op=ALU.mult), reads=[r_s, e1], writes=[rt])
                fw.op("dve", lambda: nc.vector.tensor_tensor(out=kh[:], in0=kp[:], in1=e3[:], op=ALU.mult), reads=[kp, e3], writes=[kh])
                fw.op("dve", lambda: nc.vector.tensor_tensor(out=bh[:], in0=b_[:], in1=e3[:], op=ALU.mult), reads=[b_, e3], writes=[bh])
                Mh = M[h]
                yield
                for tl in range(4):
                    cs = slice(tl * 128, (tl + 1) * 128)
                    fw.op("pe", lambda: MM(nc, E0[:, 0:128], lhsT=bt[:, cs], rhs=kapt[:, cs], start=True, stop=True), reads=[bt, kapt], writes=[E0])
                    fw.op("pe", lambda: MM(nc, E0[:, 128:256], lhsT=bt[:, cs], rhs=rt[0:64, cs], start=True, stop=True), reads=[bt, rt], writes=[E0])
                    fw.op("pe", lambda: MM(nc, E0[:, 256:384], lhsT=kt[:, cs], rhs=kapt[:, cs], start=True, stop=True), reads=[kt, kapt], writes=[E0])
                    fw.op("pe", lambda: MM(nc, E0[:, 384:512], lhsT=kt[:, cs], rhs=rt[0:64, cs], start=True, stop=True), reads=[kt, rt], writes=[E0])
                    fw.op("pe", lambda: MM(nc, E1[:, 0:128], lhsT=kapt[:, cs], rhs=bt[:, cs], start=True, stop=True), reads=[kapt, bt], writes=[E1])
                    fw.op("dve", lambda: nc.vector.tensor_tensor(out=SCa[:], in0=E0[:, 0:256], in1=C["rw_mask_su_iu"][:], op=ALU.mult), reads=[E0, C["rw_mask_su_iu"]], writes=[SCa])
                    fw.op("dve", lambda: nc.vector.tensor_tensor(out=SCb[:], in0=E0[:, 256:512], in1=C["rw_mask_su_iu"][:], op=ALU.mult), reads=[E0, C["rw_mask_su_iu"]], writes=[SCb])
                    fw.op("dve", lambda: nc.vector.tensor_tensor(out=Y[0][:], in0=E0[:, 0:128], in1=C["rw_mask_negsu"][:], op=ALU.mult), reads=[E0, C["rw_mask_negsu"]], writes=[Y[0]])
                    fw.op("dve", lambda: nc.vector.tensor_tensor(out=Z[0][:], in0=E1[:, 0:128], in1=C["rw_mask_negsl"][:], op=ALU.mult), reads=[E1, C["rw_mask_negsl"]], writes=[Z[0]])
                    yield
                    fw.op("dve", lambda: nc.vector.tensor_tensor(out=P[0][:], in0=Y[0][:], in1=ident[:], op=ALU.add), reads=[Y[0], ident], writes=[P[0]])
                    cur = 0
                    for lev in range(1, 6):
                        nxt = 1 - cur
                        fw.op("pe", lambda: MM(nc, X[:, 0:128], lhsT=Y[cur][:], rhs=Z[cur][:], start=True, stop=True), reads=[Y[cur], Z[cur]], writes=[X])
                        if lev < 5:
                            fw.op("pe", lambda: MM(nc, E1[:, 128:256], lhsT=Z[cur][:], rhs=Y[cur][:], start=True, stop=True), reads=[Y[cur], Z[cur]], writes=[E1])
                        fw.op("act", lambda: nc.scalar.copy(out=Z[nxt][:], in_=X[:, 0:128]), reads=[X], writes=[Z[nxt]])
                        if lev < 5:
                            fw.op("dve", lambda: nc.vector.tensor_copy(out=Y[nxt][:], in_=E1[:, 128:256]), reads=[E1], writes=[Y[nxt]])
                        fw.op("pe", lambda: MM(nc, E1[:, 256:384], lhsT=Z[nxt][:], rhs=P[cur][:], start=True, stop=True), reads=[Z[nxt], P[cur]], writes=[E1])
                        fw.op("dve", lambda: nc.vector.tensor_tensor(out=P[nxt][:], in0=E1[:, 256:384], in1=P[cur][:], op=ALU.add), reads=[E1, P[cur]], writes=[P[nxt]])
                        cur = nxt
                        yield
                    TT = P[cur]
                    fw.op("pe", lambda: MM(nc, X[:, 128:192], lhsT=kapt[:, cs], rhs=ident[0:64, 0:64], start=True, stop=True), reads=[kapt, ident], writes=[X])
                    fw.op("pe", lambda: MM(nc, X[:, 192:256], lhsT=v_s[:, cs], rhs=ident[0:64, 0:64], start=True, stop=True), reads=[v_s, ident], writes=[X])
                    fw.op("pe", lambda: MM(nc, X[:, 256:320], lhsT=kh[:, cs], rhs=ident[0:64, 0:64], start=True, stop=True), reads=[kh, ident], writes=[X])
                    fw.op("pe", lambda: MM(nc, X[:, 320:384], lhsT=bh[:, cs], rhs=ident[0:64, 0:64], start=True, stop=True), reads=[bh, ident], writes=[X])
                    fw.op("act", lambda: nc.scalar.copy(out=ktok[:, 0:64], in_=X[:, 128:192]), reads=[X], writes=[ktok])
                    fw.op("act", lambda: nc.scalar.copy(out=vtok[:], in_=X[:, 192:256]), reads=[X], writes=[vtok])
                    fw.op("act", lambda: nc.scalar.copy(out=bhtok[:], in_=X[:, 320:384]), reads=[X], writes=[bhtok])
                    for c in range(2):
                        fw.op("act", lambda: nc.scalar.activation(out=khc[c][:], in_=X[:, 256:320], func=AF.Identity, scale=RM[:, c:c + 1]), reads=[X, RM], writes=[khc[c]])
                        fw.op("act", lambda: nc.scalar.activation(out=bhc[c][:], in_=X[:, 320:384], func=AF.Identity, scale=RM[:, c:c + 1]), reads=[X, RM], writes=[bhc[c]])
                    fw.op("pe", lambda: MM(nc, E1[:, 384:448], lhsT=SCb[:, 0:128], rhs=vtok[:], start=True, stop=True), reads=[SCb, vtok], writes=[E1])
                    fw.op("dve", lambda: nc.vector.tensor_copy(out=ktok[:, 64:128], in_=E1[:, 384:448]), reads=[E1], writes=[ktok])
                    fw.op("pe", lambda: MM(nc, E2[:, 0:128], lhsT=TT[:], rhs=ktok[:], start=True, stop=True), reads=[TT, ktok], writes=[E2])
                    fw.op("dve", lambda: nc.vector.tensor_scalar(out=nWU[:], in0=E2[:, 0:128], scalar1=-1.0, scalar2=None, op0=ALU.mult), reads=[E2], writes=[nWU])
                    for c in range(2):
                        fw.op("dve", lambda: nc.vector.tensor_scalar(out=nWc[c][:], in0=E2[:, 0:64], scalar1=RM[:, 2 + c:3 + c], scalar2=None, op0=ALU.mult), reads=[E2, RM], writes=[nWc[c]])
                    yield
                    fw.op("pe", lambda: MM(nc, X[0:64, 384:512], lhsT=ident[:, 0:64], rhs=rt[:, cs], start=True, stop=False), reads=[ident, rt], writes=[X])
                    fw.op("pe", lambda: MM(nc, X[0:64, 384:512], lhsT=nWU[:, 0:64], rhs=SCa[:, 128:256], start=False, stop=True), reads=[nWU, SCa], writes=[X])
                    fw.op("act", lambda: nc.scalar.copy(out=Rp[0:64, :], in_=X[0:64, 384:512]), reads=[X], writes=[Rp])
                    yield
                    for c in range(2):
                        rs_ = slice(c * 64, (c + 1) * 64)
                        gi = tl * 2 + c
                        fw.op("dve", lambda: nc.vector.tensor_scalar(out=dg[0:64, :], in0=ident[0:64, 0:64], scalar1=gam[:, gi:gi + 1], scalar2=None, op0=ALU.mult), reads=[ident, gam], writes=[dg])
                        fw.op("pe", lambda: MM(nc, E2[0:64, 128 + c * 128:128 + c * 128 + 64], lhsT=ident[:, 0:64], rhs=dg[:], start=True, stop=False), reads=[ident, dg], writes=[E2])
                        fw.op("pe", lambda: MM(nc, E2[0:64, 128 + c * 128:128 + c * 128 + 64], lhsT=nWc[c][:], rhs=bhtok[:], start=False, stop=True), reads=[nWc[c], bhtok], writes=[E2])
                        fw.op("pe", lambda: MM(nc, E2[0:64, 128 + c * 128 + 64:128 + c * 128 + 128], lhsT=khc[c][:], rhs=vtok[:], start=True, stop=False), reads=[khc[c], vtok], writes=[E2])
                        fw.op("pe", lambda: MM(nc, E2[0:64, 128 + c * 128 + 64:128 + c * 128 + 128], lhsT=bhc[c][:], rhs=nWU[:, 64:128], start=False, stop=True), reads=[bhc[c], nWU], writes=[E2])
                        fw.op("dve", lambda: nc.vector.tensor_copy(out=PTm[c][:], in_=E2[0:64, 128 + c * 128:128 + c * 128 + 64]), reads=[E2], writes=[PTm[c]])
                        fw.op("dve", lambda: nc.vector.tensor_copy(out=Qm[c][:], in_=E2[0:64, 128 + c * 128 + 64:128 + c * 128 + 128]), reads=[E2], writes=[Qm[c]])
                    yield
                    for c in range(2):
                        ysl = slice(128 + c * 64, 128 + (c + 1) * 64)
                        fw.op("pe", lambda: MM(nc, E2[0:64, 384 + c * 64:384 + (c + 1) * 64], lhsT=vtok[:], rhs=SCb[:, ysl], start=True, stop=False), reads=[vtok, SCb], writes=[E2])
                        fw.op("pe", lambda: MM(nc, E2[0:64, 384 + c * 64:384 + (c + 1) * 64], lhsT=nWU[:, 64:128], rhs=SCa[:, ysl], start=False, stop=False), reads=[nWU, SCa], writes=[E2])
                        fw.op("pe", lambda: MM(nc, E2[0:64, 384 + c * 64:384 + (c + 1) * 64], lhsT=Mh[:], rhs=Rp[:, c * 64:(c + 1) * 64], start=False, stop=True),
                              reads=[Mh, Rp], writes=[E2])
                        fw.op("pe", lambda: MM(nc, E1[0:64, 448:512], lhsT=PTm[c][:], rhs=Mh[0:64, :], start=True, stop=True), reads=[PTm[c], Mh], writes=[E1])
                        fw.op("dve", lambda: nc.vector.tensor_tensor(out=Mh[0:64, :], in0=E1[0:64, 448:512], in1=Qm[c][:], op=ALU.add), reads=[E1, Qm[c]], writes=[Mh])
                    fw.op("dve", lambda: nc.vector.tensor_copy(out=yo[:, cs], in_=E2[0:64, 384:512]), reads=[E2], writes=[yo])
                yield
                fw.op("pe", lambda: MM(nc, E0[0:64, :], lhsT=ones64R[:], rhs=yo[:], start=True, stop=True), reads=[ones64R, yo], writes=[E0])
                fw.op("dve", lambda: nc.vector.scalar_tensor_tensor(out=t1[:], in0=E0[0:64, :], scalar=-1.0 / 64, in1=yo[:], op0=ALU.mult, op1=ALU.add), reads=[E0, yo], writes=[t1])
                fw.op("act", lambda: nc.scalar.activation(out=sq_r[:], in_=t1[:], func=AF.Square), reads=[t1], writes=[sq_r])
                fw.op("pe", lambda: MM(nc, E1[0:64, :], lhsT=ones64R[:], rhs=sq_r[:], start=True, stop=True), reads=[ones64R, sq_r], writes=[E1])
                fw.op("dve", lambda: nc.vector.tensor_scalar(out=t2[:], in0=E1[0:64, :], scalar1=1.0 / 64, scalar2=64e-5, op0=ALU.mult, op1=ALU.add), reads=[E1], writes=[t2])
                fw.op("act", lambda: nc.scalar.activation(out=t2[:], in_=t2[:], func=AF.Sqrt), reads=[t2], writes=[t2])
                fw.op("dve", lambda: nc.vector.reciprocal(out=t2[:], in_=t2[:]), reads=[t2], writes=[t2])
                fw.op("dve", lambda: nc.vector.scalar_tensor_tensor(out=t1[:], in0=t1[:], scalar=lnw[:, h:h + 1], in1=t2[:], op0=ALU.mult, op1=ALU.mult), reads=[t1, t2, lnw], writes=[t1])
                fw.op("dve", lambda: nc.vector.scalar_tensor_tensor(out=sq_r[:], in0=r_s[:], scalar=r_k[:, h:h + 1], in1=kp[:], op0=ALU.mult, op1=ALU.mult), reads=[r_s, kp, r_k], writes=[sq_r])
                fw.op("pe", lambda: MM(nc, E2[0:64, :], lhsT=ones64R[:], rhs=sq_r[:], start=True, stop=True), reads=[ones64R, sq_r], writes=[E2])
                fw.op("dve", lambda: nc.vector.tensor_tensor(out=t2[:], in0=E2[0:64, :], in1=v_s[:], op=ALU.mult), reads=[E2, v_s], writes=[t2])
                fw.op("dve", lambda: nc.vector.scalar_tensor_tensor(out=t1[:], in0=t1[:], scalar=lnb[:, h:h + 1], in1=t2[:], op0=ALU.add, op1=ALU.add), reads=[t1, t2, lnb], writes=[t1])
                yb = ybuf
                fw.op("dve", lambda: nc.vector.tensor_tensor(out=yb[:], in0=t1[:], in1=g_[:], op=ALU.mult), reads=[t1, g_], writes=[yb])
                fw.dma("sp", YT.t.ap()[2, h * 64:(h + 1) * 64, tb * TB:(tb + 1) * TB], yb[:], reads=[yb], writes=[k.YTtok[2][h // 2][tb]])


            for hp in range(0, NHX, 2):
                gens = [unit(0, hp)] + ([unit(1, hp + 1)] if hp + 1 < NHX else [])
                live = list(gens)
                while live:
                    for g_i in list(live):
                        try:
                            next(g_i)
                        except StopIteration:
                            live.remove(g_i)

import numpy as np, os

DFF = 2816
NFF = DFF // 128


def setup_p5(nc, fw, k):
    def din(name, shape, dt=F32):
        return fw.dram(name, shape, dt, kind="ExternalInput")
    k.w_branch = din("w_branch", [4, 3, 512, D])
    k.w_out = din("w_out", [4, D, D])
    k.ffn_w1 = din("ffn_w1", [4, D, DFF]); k.ffn_w3 = din("ffn_w3", [4, D, DFF]); k.ffn_w2 = din("ffn_w2", [4, DFF, D])
    k.final_w = din("final_norm_w", [D])


def norm_mod_tok(nc, fw, k, XT, xtok, g, s, hT, out_dram=None, out_tok=None):
    for tb in range(NTB):
        xs = k.xs_bufs[tb % 2]
        fw.dma("sp", xs[:], XT.t.ap()[:, tb * TB:(tb + 1) * TB].rearrange("(k p) t -> p k t", p=128), reads=[xtok[tb]], writes=[xs])
        sq = k.sq_buf
        fw.op("act", lambda: nc.scalar.activation(out=sq[:], in_=xs[:], func=AF.Square), reads=[xs], writes=[sq])
        pst = k.ps[7]
        for kc in range(8):
            fw.op("pe", lambda: nc.tensor.matmul(pst[:], lhsT=k.ones[:], rhs=sq[:, kc, :], start=(kc == 0), stop=(kc == 7)), reads=[k.ones, sq], writes=[pst])
        rstd = k.rstd_buf
        fw.op("dve", lambda: nc.vector.tensor_scalar(out=rstd[:], in0=pst[:], scalar1=1.0 / D, scalar2=1e-6, op0=ALU.mult, op1=ALU.add), reads=[pst], writes=[rstd])
        fw.op("act", lambda: nc.scalar.activation(out=rstd[:], in_=rstd[:], func=AF.Sqrt), reads=[rstd], writes=[rstd])
        fw.op("dve", lambda: nc.vector.reciprocal(out=rstd[:], in_=rstd[:]), reads=[rstd], writes=[rstd])
        for kc in range(8):
            if out_dram is None:
                tmp = k.tmp_bufs[kc % 2]
                fw.op("dve", lambda: nc.vector.scalar_tensor_tensor(out=tmp[:], in0=xs[:, kc, :], scalar=g[:, kc:kc + 1], in1=rstd[:], op0=ALU.mult, op1=ALU.mult),
                      reads=[xs, rstd, k.gsrc], writes=[tmp])
                fw.op("act", lambda: nc.scalar.activation(out=hT[:, kc, tb * TB:(tb + 1) * TB], in_=tmp[:], func=AF.Identity, bias=s[:, kc:kc + 1], scale=1.0),
                      reads=[tmp, k.gsrc], writes=[hT])
            else:
                fw.op("dve", lambda: nc.vector.scalar_tensor_tensor(out=sq[:, kc, :], in0=xs[:, kc, :], scalar=g[:, kc:kc + 1], in1=rstd[:], op0=ALU.mult, op1=ALU.mult),
                      reads=[xs, rstd, k.gsrc], writes=[sq])
        if out_dram is not None:
            fw.dma("sp", out_dram.t.ap()[:, tb * TB:(tb + 1) * TB].rearrange("(k p) t -> p k t", p=128), sq[:], reads=[sq], writes=[out_tok[tb]])


def proj_out(nc, fw, k, l, PT, YT, XTin, xin_tok, XTout, xout_tok):
    with fw.scope():
        wbr = fw.sbuf("p5_wbr", [128, 12, D], BF16)
        wo = fw.sbuf("p5_wo", [128, 8, D], BF16)
        stg = [fw.sbuf(f"p5_stg{i}", [128, 4, D]) for i in range(2)]
        for br in range(3):
            st = stg[br % 2]
            fw.dma("sp", st[:], k.w_branch.t.ap()[l, br].rearrange("(k p) n -> p k n", p=128), writes=[st])
            e = "pool" if br % 2 == 0 else "dve"
            if e == "pool":
                fw.op("pool", lambda: nc.gpsimd.tensor_copy(out=wbr[:, br * 4:(br + 1) * 4, :], in_=st[:]), reads=[st], writes=[wbr])
            else:
                fw.op("dve", lambda: nc.vector.tensor_copy(out=wbr[:, br * 4:(br + 1) * 4, :], in_=st[:]), reads=[st], writes=[wbr])
        for hf in range(2):
            st = stg[(hf + 1) % 2]
            fw.dma("sp", st[:], k.w_out.t.ap()[l, hf * 512:(hf + 1) * 512, :].rearrange("(k p) n -> p k n", p=128), writes=[st])
            fw.op("pool", lambda: nc.gpsimd.tensor_copy(out=wo[:, hf * 4:(hf + 1) * 4, :], in_=st[:]), reads=[st], writes=[wo])
        yb = [[fw.sbuf(f"p5_y{i}_{m}", [128, 4, TB], BF16) for m in range(3)] for i in range(2)]
        gt2b = [[fw.sbuf(f"p5_g{j}_{i}", [128, TB]) for i in range(3)] for j in range(2)]
        mg = fw.sbuf("p5_mg", [128, 8, TB], BF16)
        acc2b = [fw.sbuf(f"p5_acc{j}", [128, TB]) for j in range(2)]
        tt2b = [[fw.sbuf(f"p5_tt{j}_{i}", [128, TB]) for i in range(2)] for j in range(2)]
        xs = [fw.sbuf(f"p5_xs{i}", [128, 8, TB]) for i in range(2)]
        gt1 = k.ada[:, l, 16:24]
        for tb in range(NTB):
            y = yb[tb % 2]
            for m in range(3):
                fw.dma("sp", y[m][:], YT.t.ap()[m, :, tb * TB:(tb + 1) * TB].rearrange("(k p) t -> p k t", p=128),
                       reads=[k.YTtok[m][c][tb] for c in range(4)], writes=[y[m]])
            x_ = xs[tb % 2]
            fw.dma("sp", x_[:], XTin.t.ap()[:, tb * TB:(tb + 1) * TB].rearrange("(k p) t -> p k t", p=128), reads=[xin_tok[tb]], writes=[x_])
            for n in range(8):
                gt = gt2b[n % 2]; acc = acc2b[n % 2]; tt = tt2b[n % 2]
                for br in range(3):
                    ch = 41 + br * 8 + n
                    fw.dma("sp", gt[br][:], PT.t.ap()[ch * 128:(ch + 1) * 128, tb * TB:(tb + 1) * TB], reads=[k.PTtok[ch][tb]], writes=[gt[br]])
                    fw.op("act", lambda: nc.scalar.activation(out=gt[br][:], in_=gt[br][:], func=AF.Sigmoid), reads=[gt[br]], writes=[gt[br]])
                    pst = k.ps[3 + br]
                    for kc in range(4):
                        fw.op("pe", lambda: nc.tensor.matmul(pst[:], lhsT=wbr[:, br * 4 + kc, n * 128:(n + 1) * 128], rhs=y[br][:, kc, :], start=(kc == 0), stop=(kc == 3)),
                              reads=[wbr, y[br]], writes=[pst])
                fw.op("dve", lambda: nc.vector.tensor_tensor(out=acc[:], in0=k.ps[3][:], in1=gt[0][:], op=ALU.mult), reads=[k.ps[3], gt[0]], writes=[acc])
                fw.op("dve", lambda: nc.vector.tensor_tensor(out=tt[0][:], in0=k.ps[4][:], in1=gt[1][:], op=ALU.mult), reads=[k.ps[4], gt[1]], writes=[tt[0]])
                fw.op("dve", lambda: nc.vector.tensor_tensor(out=tt[1][:], in0=k.ps[5][:], in1=gt[2][:], op=ALU.mult), reads=[k.ps[5], gt[2]], writes=[tt[1]])
                fw.op("pool", lambda: nc.gpsimd.tensor_tensor(out=acc[:], in0=acc[:], in1=tt[0][:], op=ALU.add), reads=[acc, tt[0]], writes=[acc])
                fw.op("pool", lambda: nc.gpsimd.tensor_tensor(out=mg[:, n, :], in0=acc[:], in1=tt[1][:], op=ALU.add), reads=[acc, tt[1]], writes=[mg])
            for n in range(8):
                pst = k.ps[6 + n % 2]
                for kc in range(8):
                    fw.op("pe", lambda: nc.tensor.matmul(pst[:], lhsT=wo[:, kc, n * 128:(n + 1) * 128], rhs=mg[:, kc, :], start=(kc == 0), stop=(kc == 7)), reads=[wo, mg], writes=[pst])
                fw.op("dve", lambda: nc.vector.scalar_tensor_tensor(out=x_[:, n, :], in0=pst[:], scalar=gt1[:, n:n + 1], in1=x_[:, n, :], op0=ALU.mult, op1=ALU.add),
                      reads=[pst, x_, k.gsrc], writes=[x_])
            fw.dma("sp", XTout.t.ap()[:, tb * TB:(tb + 1) * TB].rearrange("(k p) t -> p k t", p=128), x_[:], reads=[x_], writes=[xout_tok[tb]])


def ffn(nc, fw, k, l, XT1, x1_tok, XT2, x2_tok, UT):
    with fw.scope():
        alloc_p1_small(fw, k)
        hT = fw.sbuf("f_hT", [128, 8, T], BF16)
        stg = [fw.sbuf(f"f_stg{i}", [128, 8, 256]) for i in range(2)]
        w1b = [fw.sbuf(f"f_w1b{i}", [128, 8, 256], BF16) for i in range(2)]
        w3b = [fw.sbuf(f"f_w3b{i}", [128, 8, 256], BF16) for i in range(2)]
        sl = [fw.sbuf(f"f_sl{i}", [128, TB]) for i in range(2)]
        ub = [fw.sbuf(f"f_ub{i}", [128, TB], BF16) for i in range(2)]

        def wload(fg):
            wa, wc = w1b[fg % 2], w3b[fg % 2]
            fw.dma("sp", stg[0][:], k.ffn_w1.t.ap()[l, :, fg * 256:(fg + 1) * 256].rearrange("(k p) n -> p k n", p=128), writes=[stg[0]])
            fw.op("pool", lambda: nc.gpsimd.tensor_copy(out=wa[:], in_=stg[0][:]), reads=[stg[0]], writes=[wa])
            fw.dma("sp", stg[1][:], k.ffn_w3.t.ap()[l, :, fg * 256:(fg + 1) * 256].rearrange("(k p) n -> p k n", p=128), writes=[stg[1]])
            fw.op("pool", lambda: nc.gpsimd.tensor_copy(out=wc[:], in_=stg[1][:]), reads=[stg[1]], writes=[wc])

        wload(0)
        norm_mod_tok(nc, fw, k, XT1, x1_tok, k.g2[:, l, :], k.ada[:, l, 24:32], hT)
        ev = 0
        for fg in range(NFF // 2):
            wa, wc = w1b[fg % 2], w3b[fg % 2]
            if fg + 1 < NFF // 2:
                wload(fg + 1)
            for tb in range(NTB):
                for c in range(2):
                    pa = k.ps[ev % 2]
                    pd = k.ps[3 + ev % 2]
                    for kc in range(8):
                        fw.op("pe", lambda: nc.tensor.matmul(pa[:], lhsT=wa[:, kc, c * 128:(c + 1) * 128], rhs=hT[:, kc, tb * TB:(tb + 1) * TB], start=(kc == 0), stop=(kc == 7)),
                              reads=[wa, hT], writes=[pa])
                    for kc in range(8):
                        fw.op("pe", lambda: nc.tensor.matmul(pd[:], lhsT=wc[:, kc, c * 128:(c + 1) * 128], rhs=hT[:, kc, tb * TB:(tb + 1) * TB], start=(kc == 0), stop=(kc == 7)),
                              reads=[wc, hT], writes=[pd])
                    s_ = sl[ev % 2]; u_ = ub[ev % 2]
                    fw.op("act", lambda: nc.scalar.activation(out=s_[:], in_=pa[:], func=AF.Silu), reads=[pa], writes=[s_])
                    fw.op("dve", lambda: nc.vector.tensor_tensor(out=u_[:], in0=pd[:], in1=s_[:], op=ALU.mult), reads=[pd, s_], writes=[u_])
                    ch = fg * 2 + c
                    fw.dma("sp", UT.t.ap()[ch * 128:(ch + 1) * 128, tb * TB:(tb + 1) * TB], u_[:], reads=[u_], writes=[k.UTtok[ch][tb]])
                    ev += 1
    with fw.scope():
        w2b = fw.sbuf("f_w2b", [128, NFF, D], BF16)
        stg = [fw.sbuf(f"f_stgb{i}", [128, 2, D]) for i in range(2)]
        for g in range(NFF // 2):
            st = stg[g % 2]
            fw.dma("sp", st[:], k.ffn_w2.t.ap()[l, g * 256:(g + 1) * 256, :].rearrange("(k p) n -> p k n", p=128), writes=[st])
            if g % 2 == 0:
                fw.op("pool", lambda: nc.gpsimd.tensor_copy(out=w2b[:, g * 2:(g + 1) * 2, :], in_=st[:]), reads=[st], writes=[w2b])
            else:
                fw.op("dve", lambda: nc.vector.tensor_copy(out=w2b[:, g * 2:(g + 1) * 2, :], in_=st[:]), reads=[st], writes=[w2b])
        ub = [fw.sbuf(f"f_ublk{i}", [128, NFF, TB], BF16) for i in range(2)]
        xs = [fw.sbuf(f"f_xs{i}", [128, 8, TB]) for i in range(2)]
        gt2 = k.ada[:, l, 40:48]
        for tb in range(NTB):
            u_ = ub[tb % 2]; x_ = xs[tb % 2]
            fw.dma("sp", u_[:], UT.t.ap()[:, tb * TB:(tb + 1) * TB].rearrange("(k p) t -> p k t", p=128), reads=[k.UTtok[c][tb] for c in range(NFF)], writes=[u_])
            fw.dma("sp", x_[:], XT1.t.ap()[:, tb * TB:(tb + 1) * TB].rearrange("(k p) t -> p k t", p=128), reads=[x1_tok[tb]], writes=[x_])
            for n in range(8):
                pst = k.ps[3 + n % 2]
                for kc in range(NFF):
                    fw.op("pe", lambda: nc.tensor.matmul(pst[:], lhsT=w2b[:, kc, n * 128:(n + 1) * 128], rhs=u_[:, kc, :], start=(kc == 0), stop=(kc == NFF - 1)),
                          reads=[w2b, u_], writes=[pst])
                fw.op("dve", lambda: nc.vector.scalar_tensor_tensor(out=x_[:, n, :], in0=pst[:], scalar=gt2[:, n:n + 1], in1=x_[:, n, :], op0=ALU.mult, op1=ALU.add),
                      reads=[pst, x_, k.gsrc], writes=[x_])
            fw.dma("sp", XT2.t.ap()[:, tb * TB:(tb + 1) * TB].rearrange("(k p) t -> p k t", p=128), x_[:], reads=[x_], writes=[x2_tok[tb]])


def alloc_p1_small(fw, k):
    _x = fw.sbuf("xs0", [128, 8, TB])
    k.xs_bufs = [_x, _x]
    k.sq_buf = fw.sbuf("sq", [128, 8, TB])
    k.rstd_buf = fw.sbuf("rstd", [128, TB])
    k.tmp_bufs = [fw.sbuf(f"tmp{i}", [128, TB]) for i in range(2)]


def in_proj_phase(nc, fw, k, l, XTin, xin_tok, PT):
    with fw.scope():
        alloc_p1_small(fw, k)
        hT = fw.sbuf("hT", [128, 8, T], BF16)
        _w = fw.sbuf("wst0", [128, 8, 640])
        k.w_st = [_w, _w]
        k.w_bf = [fw.sbuf(f"wbf{i}", [128, 8, 640], BF16) for i in range(2)]
        k.ev_bufs = [fw.sbuf(f"ev{i}", [128, TB]) for i in range(4)]
        in_proj_wload(nc, fw, k, l, 0)
        norm_mod_tok(nc, fw, k, XTin, xin_tok, k.g1[:, l, :], k.ada[:, l, 0:8], hT)
        in_proj(nc, fw, k, l, hT, PT, preloaded=True)


def final_norm(nc, fw, k, XT, xtok, OUT, otok):
    with fw.scope():
        alloc_p1_small(fw, k)
        fwt = fw.sbuf("fin_w", [128, 8])
        fw.dma("sp", fwt[:], k.final_w.t.ap().rearrange("(j p) -> p j", p=128), writes=[fwt], allow_slow_non_contiguous=True)
        old = k.gsrc
        k.gsrc = fwt
        norm_mod_tok(nc, fw, k, XT, xtok, fwt[:, :], None, None, out_dram=OUT, out_tok=otok)
        k.gsrc = old

import numpy as np, os

BIG = 1.0e30
NEGM = -240000.0
NQ = T // 128


def t5_bucket_np(dist):
    n = np.maximum(dist, 0)
    nf = np.maximum(n, 1).astype(np.float32)
    large = 16 + (np.log(nf / np.float32(16)) / np.float32(np.log(128 / 16)) * np.float32(16)).astype(np.int32)
    large = np.minimum(large, 31)
    return np.where(n < 16, n, large)


def nsa_consts(rel_bias):
    c = {}
    kq = np.arange(128)
    dD = kq[None, :] - kq[:, None]
    c["nsa_tabD"] = np.ascontiguousarray(np.transpose(rel_bias[t5_bucket_np(dD)], (2, 0, 1))).astype(np.float32)
    c["nsa_maskD"] = (dD >= 0).astype(np.float32)
    c["nsa_tabP"] = np.ascontiguousarray(np.transpose(rel_bias[t5_bucket_np(dD + 128)], (2, 0, 1))).astype(np.float32)
    c["nsa_maskW4"] = (dD < 0).astype(np.float32)
    m = np.arange(504)
    dC = kq[None, :] - 16 * (m[:, None] - 248) - 31
    c["nsa_tabC"] = np.ascontiguousarray(np.transpose(rel_bias[t5_bucket_np(dC)], (2, 0, 1))).astype(np.float32)
    c["nsa_maskC"] = (dC >= 0).astype(np.float32)
    c["nsa_b31"] = np.ascontiguousarray(rel_bias[31:32, :]).astype(np.float32)
    u = np.arange(126) - 62
    cur = (kq >= 64).astype(np.int64)
    A = np.zeros((128, 126), np.float32)
    A[(u[None, :] == cur[:, None]) | (u[None, :] == cur[:, None] - 1)] = BIG
    A[u[None, :] > cur[:, None]] = -BIG
    c["nsa_A"] = A
    n = np.arange(256)
    mm = np.arange(64)
    cov = ((16 * n[:, None] < 64 * mm[None, :] + 64) & (16 * n[:, None] + 32 > 64 * mm[None, :]) & (n[:, None] < 255)).astype(np.float32)
    c["nsa_cover"] = cov
    keys = np.arange(T)
    c["nsa_xexp"] = (keys[None, :] // 64 == mm[:, None]).astype(np.float32)
    sel = np.zeros((24, 24 * 64), np.float32)
    for r in range(24):
        sel[r, r * 64:(r + 1) * 64] = 1.0
    c["nsa_selall"] = sel
    return c


def setup_nsa(nc, fw, k):
    def din(name, shape, dt=F32):
        return fw.dram(name, shape, dt, kind="ExternalInput")
    k.nsa_in = {}
    for nm, shp in (("nsa_pe_k", [4, 32, 64]), ("nsa_cmp_w1_k", [4, 2048, 256]), ("nsa_cmp_w2_k", [4, 256, 64]),
                    ("nsa_pe_v", [4, 32, 64]), ("nsa_cmp_w1_v", [4, 2048, 256]), ("nsa_cmp_w2_v", [4, 256, 64])):
        k.nsa_in[nm] = din(nm, shp)
    tabD = din("nsa_tabD", [8, 128, 128]); maskD = din("nsa_maskD", [128, 128]); tabP = din("nsa_tabP", [8, 128, 128]); maskW4 = din("nsa_maskW4", [128, 128])
    tabC = din("nsa_tabC", [8, 504, 128]); maskC = din("nsa_maskC", [504, 128]); b31 = din("nsa_b31", [1, 8])
    A = din("nsa_A", [128, 126]); cover = din("nsa_cover", [256, 64]); xexp = din("nsa_xexp", [64, T]); selall = din("nsa_selall", [24, 24 * 64])
    k.GcT = fw.dram("nsa_GcT", [8, 504, 128], F32)
    k.GcTtok = Buf(None, "gct")
    n = k.nsa = {}
    n["Ed"] = fw.sbuf("n_Ed", [128, 8, 128]); n["Ep"] = fw.sbuf("n_Ep", [128, 8, 128]); n["W4"] = fw.sbuf("n_W4", [128, 128])
    n["A"] = fw.sbuf("n_A", [128, 126]); n["cover"] = fw.sbuf("n_cover", [128, 2, 64]); n["xexp"] = fw.sbuf("n_xexp", [128, T], BF16); n["r64"] = fw.sbuf("n_r64", [65, 64]); n["selall"] = fw.sbuf("n_selall", [24, 24 * 64])
    n["nb31"] = fw.sbuf("n_nb31", [128, 8])
    fw.dma("sp", n["A"][:], A.t.ap()[:, :], writes=[n["A"]])
    fw.dma("sp", n["cover"][:], cover.t.ap().rearrange("(a p) m -> p a m", p=128), writes=[n["cover"]])
    fw.dma("sp", n["selall"][:], selall.t.ap()[:, :], writes=[n["selall"]])
    fw.dma("sp", n["W4"][:], maskW4.t.ap()[:, :], writes=[n["W4"]])
    fw.dma("sp", n["nb31"][:], b31.t.ap()[0:1, :].partition_broadcast(128), writes=[n["nb31"]])
    fw.op("dve", lambda: nc.vector.tensor_scalar(out=n["nb31"][:], in0=n["nb31"][:], scalar1=-1.0, scalar2=None, op0=ALU.mult), reads=[n["nb31"]], writes=[n["nb31"]])
    with fw.scope():
        st = fw.sbuf("n_st", [128, T])
        fw.dma("sp", st[64:128, :], xexp.t.ap()[:, :], writes=[st])
        fw.op("dve", lambda: nc.vector.tensor_copy(out=n["xexp"][64:128, :], in_=st[64:128, :]), reads=[st], writes=[n["xexp"]])
        fw.op("dve", lambda: nc.vector.memset(n["r64"][:], 0.0), writes=[n["r64"]])
        fw.op("dve", lambda: nc.vector.memset(n["r64"][64:65, :], 1.0), writes=[n["r64"]])
        mD = fw.sbuf("n_mD", [128, 128]); fw.dma("sp", mD[:], maskD.t.ap()[:, :], writes=[mD])
        raw = fw.sbuf("n_raw", [128, 8, 128])
        for (src, dst, msk) in ((tabD, n["Ed"], mD), (tabP, n["Ep"], None)):
            fw.dma("sp", raw[:], src.t.ap().rearrange("h k q -> k h q"), writes=[raw])
            for h in range(8):
                fw.op("act", lambda: nc.scalar.activation(out=dst[:, h, :], in_=raw[:, h, :], func=AF.Exp, bias=n["nb31"][:, h:h + 1], scale=1.0), reads=[raw, n["nb31"]], writes=[dst])
                if msk is not None:
                    fw.op("dve", lambda: nc.vector.tensor_tensor(out=dst[:, h, :], in0=dst[:, h, :], in1=msk[:], op=ALU.mult), reads=[dst, msk], writes=[dst])
        rawc = fw.sbuf("n_rawc", [126, 4, 128]); mC = fw.sbuf("n_mC", [126, 4, 128])
        fw.dma("sp", mC[:], maskC.t.ap().rearrange("(a p) q -> p a q", p=126), writes=[mC])
        for h in range(8):
            fw.dma("sp", rawc[:], tabC.t.ap()[h].rearrange("(a p) q -> p a q", p=126), writes=[rawc])
            fw.op("act", lambda: nc.scalar.activation(out=rawc[:], in_=rawc[:], func=AF.Exp, bias=n["nb31"][0:126, h:h + 1], scale=1.0), reads=[rawc, n["nb31"]], writes=[rawc])
            fw.op("dve", lambda: nc.vector.tensor_tensor(out=rawc[:], in0=rawc[:], in1=mC[:], op=ALU.mult), reads=[rawc, mC], writes=[rawc])
            fw.dma("sp", k.GcT.t.ap()[h].rearrange("(a p) q -> p a q", p=126), rawc[:], reads=[rawc], writes=[k.GcTtok])


def nsa(nc, fw, k, l, PT, YT, GX=2, IQ=None):
    n = k.nsa
    ps = k.ps
    ident = k.ident
    IQ = list(range(NQ)) if IQ is None else IQ
    with fw.scope():
        bf = lambda nm, shp: fw.sbuf(nm, shp, BF16)
        stg = [fw.sbuf(f"n_stg{i}", [64, TB]) for i in range(2)]
        kcmp = bf("n_kcmp", [64, T]); vcmp = bf("n_vcmp", [64, T]); ksel = n["xexp"]; kwin = bf("n_kwin", [64, T])
        vseltok = bf("n_vseltok", [128, NQ, 65]); vwintok = bf("n_vwintok", [128, NQ, 65])
        qb = bf("n_qb", [128, 4, T]); negqp = fw.sbuf("n_negqp", [128, 128]); nd = fw.sbuf("n_nd", [65, 512])
        kcT = bf("n_kcT", [64, 256]); vctok = bf("n_vctok", [128, 2, 64])
        w1s = fw.sbuf("n_w1s", [64, 16, 256]); w1b = bf("n_w1b", [64, 32, 256]); w2s = fw.sbuf("n_w2s", [128, 2, 64]); w2b = bf("n_w2b", [128, 2, 64])
        peT = fw.sbuf("n_peT", [64, 32]); peTb = bf("n_peTb", [64, 32]); biasc = fw.sbuf("n_biasc", [128, 2])
        aT = bf("n_aT", [128, 2, 256])
        sgT = fw.sbuf("n_sgT", [24, T])
        onesb = k.onesb
        Ef = [fw.sbuf(f"n_Ef{i}", [128, 512]) for i in range(2)]; Pb = [bf(f"n_Pb{i}", [128, 512]) for i in range(2)]
        Gt = [fw.sbuf(f"n_Gt{i}", [128, 4, 128]) for i in range(2)]
        Pc = [fw.sbuf(f"n_Pc{i}", [128, 512]) for i in range(2)]; Pcb = [bf(f"n_Pcb{i}", [128, 512]) for i in range(2)]
        rd = fw.sbuf("n_rd", [128, 512])
        sc = fw.sbuf("n_sc", [128, 64]); sc2 = fw.sbuf("n_sc2", [128, 64]); m8 = fw.sbuf("n_m8", [128, 16]); negq = fw.sbuf("n_negq", [128, 64])
        negT4 = bf("n_negT4", [64, 4, 128])
        accs = [fw.sbuf(f"n_acc{i}", [64, 512]) for i in range(2)]; wt = fw.sbuf("n_wt", [64, 512]); ot = fw.sbuf("n_ot", [64, 512]); yb = [bf(f"n_yb{i}", [64, 512]) for i in range(2)]
        fw.op("dve", lambda: nc.vector.memset(kcT[:], 0.0), writes=[kcT])
        fw.op("dve", lambda: nc.vector.memset(negqp[:], 0.0), writes=[negqp])
        fw.op("dve", lambda: nc.vector.memset(vseltok[:, :, 64:65], 1.0), writes=[vseltok])
        fw.op("dve", lambda: nc.vector.memset(vwintok[:, :, 64:65], 1.0), writes=[vwintok])
        for tb in range(NTB):
            fw.dma("sp", sgT[:, tb * TB:(tb + 1) * TB], PT.t.ap()[26 * 128:26 * 128 + 24, tb * TB:(tb + 1) * TB], reads=[k.PTtok[26][tb]], writes=[sgT])
        fw.op("act", lambda: nc.scalar.activation(out=sgT[:], in_=sgT[:], func=AF.Sigmoid), reads=[sgT], writes=[sgT])
        evi = 0
        for g in range(GX):
            def load_stream(dst_ap_fn, ch, row0):
                for tb in range(NTB):
                    s_ = stg[tb % 2]
                    r0 = ch * 128 + row0
                    fw.dma("sp", s_[:], PT.t.ap()[r0:r0 + 64, tb * TB:(tb + 1) * TB], reads=[k.PTtok[ch][tb]], writes=[s_])
                    dst, dbuf = dst_ap_fn(tb)
                    if tb % 2 == 0:
                        fw.op("dve", lambda: nc.vector.tensor_copy(out=dst, in_=s_[:]), reads=[s_], writes=[dbuf])
                    else:
                        fw.op("pool", lambda: nc.gpsimd.tensor_copy(out=dst, in_=s_[:]), reads=[s_], writes=[dbuf])
            load_stream(lambda tb: (kcmp[:, tb * TB:(tb + 1) * TB], kcmp), 20, g * 64)
            load_stream(lambda tb: (vcmp[:, tb * TB:(tb + 1) * TB], vcmp), 21, g * 64)
            load_stream(lambda tb: (ksel[0:64, tb * TB:(tb + 1) * TB], ksel), 22, g * 64)
            load_stream(lambda tb: (kwin[:, tb * TB:(tb + 1) * TB], kwin), 24, g * 64)
            for j in range(4):
                hd = g * 4 + j
                load_stream(lambda tb: (qb[0:64, j, tb * TB:(tb + 1) * TB], qb), 16 + hd // 2, (hd % 2) * 64)
            for (ch, vt) in ((23, vseltok), (25, vwintok)):
                for tb in range(NTB):
                    s_ = stg[tb % 2]
                    r0 = ch * 128 + g * 64
                    fw.dma("sp", s_[:], PT.t.ap()[r0:r0 + 64, tb * TB:(tb + 1) * TB], reads=[k.PTtok[ch][tb]], writes=[s_])
                    for q4 in range(4):
                        fw.op("pe", lambda: nc.tensor.matmul(ps[5][:, q4 * 64:(q4 + 1) * 64], lhsT=s_[:, q4 * 128:(q4 + 1) * 128], rhs=ident[0:64, 0:64], start=True, stop=True),
                              reads=[s_, ident], writes=[ps[5]])
                    fw.op("dve", lambda: nc.vector.tensor_copy(out=vt[:, tb * 4:(tb + 1) * 4, 0:64], in_=ps[5][:, 0:256].rearrange("p (a d) -> p a d", d=64)), reads=[ps[5]], writes=[vt])
            for (kv, src, w1n, w2n, pen) in (("k", kcmp, "nsa_cmp_w1_k", "nsa_cmp_w2_k", "nsa_pe_k"), ("v", vcmp, "nsa_cmp_w1_v", "nsa_cmp_w2_v", "nsa_pe_v")):
                for hf in range(2):
                    fw.dma("sp", w1s[:], k.nsa_in[w1n].t.ap()[l, hf * 1024:(hf + 1) * 1024, :].rearrange("(l d) c -> d l c", d=64), writes=[w1s])
                    fw.op("pool", lambda: nc.gpsimd.tensor_copy(out=w1b[:, hf * 16:(hf + 1) * 16, :], in_=w1s[:]), reads=[w1s], writes=[w1b])
                fw.dma("sp", w2s[:], k.nsa_in[w2n].t.ap()[l].rearrange("(a p) d -> p a d", p=128), writes=[w2s])
                fw.op("dve", lambda: nc.vector.tensor_copy(out=w2b[:], in_=w2s[:]), reads=[w2s], writes=[w2b])
                fw.dma("sp", peT[:], k.nsa_in[pen].t.ap()[l].rearrange("l d -> d l"), writes=[peT], allow_slow_non_contiguous=True)
                fw.op("dve", lambda: nc.vector.tensor_copy(out=peTb[:], in_=peT[:]), reads=[peT], writes=[peTb])
                src3 = src.t[:].rearrange("p (n s) -> p n s", s=16)
                for cc in range(2):
                    for li in range(32):
                        fw.op("pe", lambda: nc.tensor.matmul(ps[2][:, cc:cc + 1], lhsT=w1b[:, li, cc * 128:(cc + 1) * 128], rhs=peTb[:, li:li + 1], start=(li == 0), stop=(li == 31)),
                              reads=[w1b, peTb], writes=[ps[2]])
                fw.op("dve", lambda: nc.vector.tensor_copy(out=biasc[:], in_=ps[2][:, 0:2]), reads=[ps[2]], writes=[biasc])
                for cc in range(2):
                    for li in range(32):
                        rhs = src3[:, li // 16:li // 16 + 255, li % 16]
                        fw.op("pe", lambda: nc.tensor.matmul(ps[cc][:, 0:255], lhsT=w1b[:, li, cc * 128:(cc + 1) * 128], rhs=rhs, start=(li == 0), stop=(li == 31)),
                              reads=[w1b, src], writes=[ps[cc]])
                    fw.op("act", lambda: nc.scalar.activation(out=aT[:, cc, 0:255], in_=ps[cc][:, 0:255], func=AF.Silu, bias=biasc[:, cc:cc + 1], scale=1.0), reads=[ps[cc], biasc], writes=[aT])
                if kv == "k":
                    for cc in range(2):
                        fw.op("pe", lambda: nc.tensor.matmul(ps[3][0:64, 0:255], lhsT=w2b[:, cc, :], rhs=aT[:, cc, 0:255], start=(cc == 0), stop=(cc == 1)), reads=[w2b, aT], writes=[ps[3]])
                    fw.op("dve", lambda: nc.vector.tensor_copy(out=kcT[:, 0:255], in_=ps[3][0:64, 0:255]), reads=[ps[3]], writes=[kcT])
                else:
                    fw.op("dve", lambda: nc.vector.memset(vctok[:], 0.0), writes=[vctok])
                    for nt in range(2):
                        rows = 128 if nt == 0 else 127
                        for cc in range(2):
                            fw.op("pe", lambda: nc.tensor.matmul(ps[4][0:rows, nt * 64:(nt + 1) * 64], lhsT=aT[:, cc, nt * 128:nt * 128 + rows], rhs=w2b[:, cc, :], start=(cc == 0), stop=(cc == 1)),
                                  reads=[aT, w2b], writes=[ps[4]])
                        fw.op("dve", lambda: nc.vector.tensor_copy(out=vctok[0:rows, nt, :], in_=ps[4][0:rows, nt * 64:(nt + 1) * 64]), reads=[ps[4]], writes=[vctok])
            def tile_gen(i):
                nonlocal evi
                acc = accs[IQ.index(i) % 2]
                Q = qb[0:64, :, i * 128:(i + 1) * 128]; Q128 = qb[:, :, i * 128:(i + 1) * 128]
                nts = [0] if i < 16 else [0, 1]
                for nt in nts:
                    sb = ps[evi % 2]; e_ = Pc[nt]; g_ = Gt[nt]
                    fw.op("pe", lambda: nc.tensor.matmul(sb[:], lhsT=kcT[:, nt * 128:(nt + 1) * 128], rhs=Q, start=True, stop=True), reads=[kcT, qb], writes=[sb])
                    r0 = 248 - 8 * i + nt * 128
                    fw.dma("sp", g_[:], k.GcT.t.ap()[g * 4:(g + 1) * 4, r0:r0 + 128, :].rearrange("h n q -> n h q"), reads=[k.GcTtok], writes=[g_])
                    fw.op("act", lambda: nc.scalar.activation(out=e_[:], in_=sb[:], func=AF.Exp, scale=0.125), reads=[sb], writes=[e_])
                    fw.op("dve", lambda: nc.vector.tensor_tensor(out=e_[:], in0=e_[:], in1=g_[:].rearrange("p h q -> p (h q)"), op=ALU.mult), reads=[e_, g_], writes=[e_])
                    fw.op("pool", lambda: nc.gpsimd.tensor_copy(out=Pcb[nt][:], in_=e_[:]), reads=[e_], writes=[Pcb[nt]])
                    evi += 1
                for x, nt in enumerate(nts):
                    fw.op("pe", lambda: nc.tensor.matmul(ps[3][:], lhsT=k.ones[:], rhs=Pc[nt][:], start=(x == 0), stop=(x == len(nts) - 1)), reads=[k.ones, Pc[nt]], writes=[ps[3]])
                for x, nt in enumerate(nts):
                    fw.op("pe", lambda: nc.tensor.matmul(ps[5][0:64, :], lhsT=vctok[:, nt, :], rhs=Pcb[nt][:], start=(x == 0), stop=(x == len(nts) - 1)), reads=[vctok, Pcb[nt]], writes=[ps[5]])
                fw.op("dve", lambda: nc.vector.tensor_scalar(out=rd[:], in0=ps[3][:], scalar1=1e-30, scalar2=None, op0=ALU.max), reads=[ps[3]], writes=[rd])
                fw.op("dve", lambda: nc.vector.reciprocal(out=rd[:], in_=rd[:]), reads=[rd], writes=[rd])
                for nt in nts:
                    fw.op("dve", lambda: nc.vector.tensor_tensor(out=Pc[nt][:], in0=Pc[nt][:], in1=rd[:], op=ALU.mult), reads=[Pc[nt], rd], writes=[Pc[nt]])
                tot = 4 * len(nts); x = 0
                for nt in nts:
                    for j in range(4):
                        fw.op("pe", lambda: nc.tensor.matmul(ps[4][:, 0:64], lhsT=Pc[nt][:, j * 128:(j + 1) * 128], rhs=n["cover"][:, nt, :], start=(x == 0), stop=(x == tot - 1)),
                              reads=[Pc[nt], n["cover"]], writes=[ps[4]])
                        x += 1
                a0 = 62 - 2 * i
                fw.op("dve", lambda: nc.vector.tensor_tensor(out=sc[:], in0=ps[4][:, 0:64], in1=n["A"][:, a0:a0 + 64], op=ALU.add), reads=[ps[4], n["A"]], writes=[sc])
                fw.op("dve", lambda: nc.vector.memset(sc[:, 0:1], BIG), writes=[sc])
                fw.op("dve", lambda: nc.vector.max(out=m8[:, 0:8], in_=sc[:]), reads=[sc], writes=[m8])
                fw.op("dve", lambda: nc.vector.tensor_scalar(out=sc2[:], in0=sc[:], scalar1=m8[:, 7:8], scalar2=-3.0 * BIG, op0=ALU.is_ge, op1=ALU.mult), reads=[sc, m8], writes=[sc2])
                fw.op("dve", lambda: nc.vector.tensor_tensor(out=sc2[:], in0=sc2[:], in1=sc[:], op=ALU.add), reads=[sc2, sc], writes=[sc2])
                fw.op("dve", lambda: nc.vector.max(out=m8[:, 8:16], in_=sc2[:]), reads=[sc2], writes=[m8])
                fw.op("dve", lambda: nc.vector.tensor_scalar(out=negqp[:, 64:128], in0=sc[:], scalar1=m8[:, 15:16], scalar2=None, op0=ALU.is_lt), reads=[sc, m8], writes=[negqp])
                fw.op("pe", lambda: nc.tensor.matmul(ps[4][:, 128:256], lhsT=negqp[:], rhs=ident[:], start=True, stop=True), reads=[negqp, ident], writes=[ps[4]])
                for j in range(4):
                    fw.op("dve", lambda: nc.vector.tensor_scalar(out=qb[64:128, j, i * 128:(i + 1) * 128], in0=ps[4][64:128, 128:256], scalar1=NEGM, scalar2=None, op0=ALU.mult), reads=[ps[4]], writes=[qb])
                def attend(kT, vtok, kts, tables, with_sel, p_num, p_den):
                    nonlocal evi
                    for x, kt in enumerate(kts):
                        sb = ps[evi % 2]; e_ = Ef[evi % 2]; p_ = Pb[evi % 2]
                        first, last = (x == 0), (x == len(kts) - 1)
                        if with_sel:
                            fw.op("pe", lambda: nc.tensor.matmul(sb[:], lhsT=kT[:, kt * 128:(kt + 1) * 128], rhs=Q128, start=True, stop=True), reads=[kT, qb], writes=[sb])
                        else:
                            fw.op("pe", lambda: nc.tensor.matmul(sb[:], lhsT=kT[0:64, kt * 128:(kt + 1) * 128], rhs=Q, start=True, stop=True), reads=[kT, qb], writes=[sb])
                        tab = tables.get(kt)
                        if tab is None:
                            fw.op("act", lambda: nc.scalar.activation(out=p_[:], in_=sb[:], func=AF.Exp, scale=0.125), reads=[sb], writes=[p_])
                        else:
                            fw.op("act", lambda: nc.scalar.activation(out=e_[:], in_=sb[:], func=AF.Exp, scale=0.125), reads=[sb], writes=[e_])
                            tb_, tap = tab
                            fw.op("dve", lambda: nc.vector.tensor_tensor(out=p_[:].rearrange("p (h q) -> p h q", h=4), in0=e_[:].rearrange("p (h q) -> p h q", h=4), in1=tap, op=ALU.mult),
                                  reads=[e_, tb_], writes=[p_])
                        fw.op("pe", lambda: nc.tensor.matmul(p_num[0:65, :], lhsT=vtok[:, kt, :], rhs=p_[:], start=first, stop=last), reads=[vtok, p_], writes=[p_num])
                        evi += 1
                    fw.op("dve", lambda: nc.vector.tensor_copy(out=nd[:], in_=p_num[0:65, :]), reads=[p_num], writes=[nd])
                    fw.op("pe", lambda: nc.tensor.matmul(p_den[0:64, :], lhsT=n["r64"][:], rhs=nd[:], start=True, stop=True), reads=[n["r64"], nd], writes=[p_den])
                Ed_g = n["Ed"][:, g * 4:(g + 1) * 4, :]; Ep_g = n["Ep"][:, g * 4:(g + 1) * 4, :]
                W4b = n["W4"][:].unsqueeze(1).to_broadcast([128, 4, 128])
                tabs = {i: (n["Ed"], Ed_g)}
                if i >= 1:
                    tabs[i - 1] = (n["Ep"], Ep_g)
                def combine(br, p_num, p_den, first):
                    for j in range(4):
                        r = (g * 4 + j) * 3 + br
                        fw.op("pe", lambda: nc.tensor.matmul(ps[2][0:64, j * 128:(j + 1) * 128], lhsT=n["selall"][:, r * 64:(r + 1) * 64], rhs=sgT[:, i * 128:(i + 1) * 128], start=True, stop=True),
                              reads=[n["selall"], sgT], writes=[ps[2]])
                    fw.op("dve", lambda: nc.vector.tensor_scalar(out=wt[:], in0=p_den, scalar1=1e-30, scalar2=None, op0=ALU.max), reads=[ps_of[id(p_den)]], writes=[wt])
                    fw.op("dve", lambda: nc.vector.reciprocal(out=wt[:], in_=wt[:]), reads=[wt], writes=[wt])
                    fw.op("dve", lambda: nc.vector.tensor_tensor(out=wt[:], in0=wt[:], in1=ps[2][0:64, :], op=ALU.mult), reads=[wt, ps[2]], writes=[wt])
                    if first:
                        fw.op("dve", lambda: nc.vector.tensor_tensor(out=acc[:], in0=p_num, in1=wt[:], op=ALU.mult), reads=[ps_of[id(p_num)], wt], writes=[acc])
                    else:
                        fw.op("dve", lambda: nc.vector.tensor_tensor(out=ot[:], in0=p_num, in1=wt[:], op=ALU.mult), reads=[ps_of[id(p_num)], wt], writes=[ot])
                        fw.op("pool", lambda: nc.gpsimd.tensor_tensor(out=acc[:], in0=acc[:], in1=ot[:], op=ALU.add), reads=[acc, ot], writes=[acc])
                ps_of = {}
                numc = ps[5][0:64, :]; denc = ps[3][0:64, :]
                ps_of[id(numc)] = ps[5]; ps_of[id(denc)] = ps[3]
                combine(0, numc, denc, True)
                yield
                attend(ksel, vseltok, list(range(i + 1)), tabs, True, ps[6], ps[3])
                nums = ps[6][0:64, :]; dens = ps[3][0:64, :]
                ps_of[id(nums)] = ps[6]; ps_of[id(dens)] = ps[3]
                combine(1, nums, dens, False)
                wk = [kt for kt in range(i - 4, i + 1) if kt >= 0]
                wtabs = dict(tabs)
                if i >= 4:
                    wtabs[i - 4] = (n["W4"], W4b)
                attend(kwin, vwintok, wk, wtabs, False, ps[7], ps[4])
                numw = ps[7][0:64, :]; denw = ps[4][0:64, :]
                ps_of[id(numw)] = ps[7]; ps_of[id(denw)] = ps[4]
                combine(2, numw, denw, False)
                y_ = yb[i % 2]
                fw.op("dve", lambda: nc.vector.tensor_copy(out=y_[:], in_=acc[:]), reads=[acc], writes=[y_])
                fw.dma("sp", YT.t.ap()[1].rearrange("(h d) t -> d h t", d=64)[:, g * 4:(g + 1) * 4, i * 128:(i + 1) * 128], y_[:].rearrange("p (h q) -> p h q", h=4),
                       reads=[y_], writes=[k.YTtok[1][g * 2][i // 4], k.YTtok[1][g * 2 + 1][i // 4]])
            prev_g = None
            for i in IQ:
                cur_g = tile_gen(i)
                next(cur_g)
                if prev_g is not None:
                    for _ in prev_g:
                        pass
                prev_g = cur_g
            if prev_g is not None:
                for _ in prev_g:
                    pass


N_LAYERS = 4
N_CORES = 4


def build_all(nc, fw, n_layers=N_LAYERS):
    k = K()
    alloc_tokens(k)
    setup_common(nc, fw, k, n_layers)
    setup_hgrn(nc, fw, k, n_layers)
    setup_rwkv(nc, fw, k)
    setup_nsa(nc, fw, k)
    setup_p5(nc, fw, k)
    fw.barrier()
    k.gsrc = Buf(None, "gsrc")
    PT = fw.dram("PT", [NP, T], F32)
    YT = fw.dram("YT", [3, 512, T], BF16)
    UT = fw.dram("UT", [DFF, T], BF16)
    XT1 = fw.dram("XT1", [D, T], F32)
    XT2 = fw.dram("XT2", [D, T], F32)
    OUT = fw.dram("OUT", [D, T], F32, kind="ExternalOutput")
    x0_tok = [Buf(None, f"x0_{t}") for t in range(NTB)]
    x1_tok = [Buf(None, f"x1_{t}") for t in range(NTB)]
    x2_tok = [Buf(None, f"x2_{t}") for t in range(NTB)]
    o_tok = [Buf(None, f"o_{t}") for t in range(NTB)]
    XTin, xin_tok = k.xT_in, x0_tok
    for l in range(n_layers):
        in_proj_phase(nc, fw, k, l, XTin, xin_tok, PT)
        hgrn(nc, fw, k, l, PT, YT)
        nsa(nc, fw, k, l, PT, YT)
        rwkv(nc, fw, k, l, PT, YT)
        proj_out(nc, fw, k, l, PT, YT, XTin, xin_tok, XT1, x1_tok)
        ffn(nc, fw, k, l, XT1, x1_tok, XT2, x2_tok, UT)
        XTin, xin_tok = XT2, x2_tok
    final_norm(nc, fw, k, XTin, xin_tok, OUT, o_tok)
    fw.finish(o_tok)
    return k


def kernel(**inputs):
    inp = {k_: np.asarray(v) for k_, v in inputs.items()}
    consts = {**host_consts(), **hgrn_consts(), **rwkv_consts(), **nsa_consts(inp["rel_bias"].astype(np.float32))}
    shared = {
        "ada_w": inp["ada_w"], "ada_b": inp["ada_b"], "norm1_w": inp["norm1_w"], "norm2_w": inp["norm2_w"],
        "w_in_p": pad_w_in(inp["w_in"]),
        "hgrn_lb_logits": inp["hgrn_lb_logits"], "hgrn_norm_w": inp["hgrn_norm_w"],
        "w_branch": inp["w_branch"], "w_out": inp["w_out"], "ffn_w1": inp["ffn_w1"], "ffn_w3": inp["ffn_w3"], "ffn_w2": inp["ffn_w2"],
        "final_norm_w": inp["final_norm_w"],
    }
    for nm in ("rw_mu", "rw_w0", "rw_w2", "rw_a0", "rw_a2", "rw_g2", "rw_k_k", "rw_k_a", "rw_lnx_w", "rw_lnx_b"):
        shared[nm] = inp[nm]
    shared["rw_r_k"] = np.ascontiguousarray(inp["rw_r_k"].reshape(4, 512))
    for nm in ("nsa_pe_k", "nsa_cmp_w1_k", "nsa_cmp_w2_k", "nsa_pe_v", "nsa_cmp_w1_v", "nsa_cmp_w2_v"):
        shared[nm] = inp[nm]
    shared.update(consts)
    shared = {k_: np.ascontiguousarray(v, dtype=np.float32) for k_, v in shared.items()}
    B = inp["x"].shape[0]
    in_maps = []
    for b in range(B):
        m = dict(shared)
        m["xT"] = np.ascontiguousarray(inp["x"][b].T.astype(np.float32))
        m["c8"] = np.ascontiguousarray(inp["c"][b].reshape(8, 128).T.astype(np.float32))
        in_maps.append(m)
    nc = bass.Bass("TRN2", target_bir_lowering=False)
    with ExitStack() as es:
        fw = FW(nc, es)
        build_all(nc, fw)
    res = run_bass_kernel_spmd(nc, in_maps, core_ids=list(range(B)))
    out = np.stack([np.asarray(res.results[b]["OUT"]).T for b in range(B)], axis=0)
    return np.ascontiguousarray(out.astype(np.float32))
```
